# Optimizing a Trainium2 kernel written in Bass

```python
import math
import jax, jax.numpy as jnp
from jax import lax
import numpy as np

D_MODEL = 1024
BATCH = 4
SEQ = 4096
DEPTH = 1

CONV_DIM = 1024
CONV_KERNEL = 31
N_HEADS = 8
QK_NOPE_DIM = 128
QK_ROPE_DIM = 64
V_HEAD_DIM = 128
Q_LORA_RANK = 384
KV_LORA_RANK = 256
ROPE_THETA = 10000.0
Q_BLOCK = 128
D_FF = 4 * D_MODEL
N_BRANCHES = 2
EPS = 1e-6

IN_WIDTHS = (2 * CONV_DIM, Q_LORA_RANK, KV_LORA_RANK, QK_ROPE_DIM, N_BRANCHES * D_MODEL)
IN_DIM = sum(IN_WIDTHS)
IN_SPLIT = tuple(int(v) for v in np.cumsum(IN_WIDTHS)[:-1])

kernel_name = "hybrid_conformer_mla_gated_block"


def rms_norm(x, g):
    xf = x.astype(jnp.float32)
    y = xf * lax.rsqrt(jnp.mean(xf * xf, axis=-1, keepdims=True) + EPS)
    return (y * g.astype(jnp.float32)).astype(x.dtype)


def layer_norm(x, g, b):
    xf = x.astype(jnp.float32)
    mu = jnp.mean(xf, axis=-1, keepdims=True)
    var = jnp.mean(jnp.square(xf - mu), axis=-1, keepdims=True)
    y = (xf - mu) * lax.rsqrt(var + EPS)
    return (y * g.astype(jnp.float32) + b.astype(jnp.float32)).astype(x.dtype)


def rope_tables(positions):
    inv_freq = 1.0 / (ROPE_THETA ** (jnp.arange(0, QK_ROPE_DIM, 2, dtype=jnp.float32) / QK_ROPE_DIM))
    ang = positions.astype(jnp.float32)[..., None] * inv_freq
    return jnp.cos(ang), jnp.sin(ang)


def apply_rope(t, cos, sin):
    t1, t2 = jnp.split(t, 2, axis=-1)
    cos = cos.astype(t.dtype)
    sin = sin.astype(t.dtype)
    return jnp.concatenate([t1 * cos - t2 * sin, t2 * cos + t1 * sin], axis=-1)


def causal_depthwise_conv(u, w, b):
    return lax.conv_general_dilated(
        u, w[:, None, :].astype(u.dtype), window_strides=(1,),
        padding=[(CONV_KERNEL - 1, 0)],
        dimension_numbers=("NWC", "WIO", "NWC"),
        feature_group_count=CONV_DIM) + b


def mla_attention(q_nope, q_rope, k_nope, k_rope, v):
    b, s, h, _ = q_nope.shape
    nb = s // Q_BLOCK
    scale = 1.0 / math.sqrt(QK_NOPE_DIM + QK_ROPE_DIM)
    qn = jnp.moveaxis(q_nope.reshape(b, nb, Q_BLOCK, h, QK_NOPE_DIM), 1, 0)
    qr = jnp.moveaxis(q_rope.reshape(b, nb, Q_BLOCK, h, QK_ROPE_DIM), 1, 0)
    k_idx = jnp.arange(s)

    def one_block(args):
        qn_b, qr_b, blk = args
        sc = (jnp.einsum("bqhd,bkhd->bhqk", qn_b, k_nope)
              + jnp.einsum("bqhr,bkr->bhqk", qr_b, k_rope)).astype(jnp.float32) * scale
        q_idx = blk * Q_BLOCK + jnp.arange(Q_BLOCK)
        causal = q_idx[:, None] >= k_idx[None, :]
        sc = jnp.where(causal[None, None], sc, -1e30)
        p = jax.nn.softmax(sc, axis=-1).astype(v.dtype)
        return jnp.einsum("bhqk,bkhd->bqhd", p, v)

    out = lax.map(one_block, (qn, qr, jnp.arange(nb)))
    return jnp.moveaxis(out, 0, 1).reshape(b, s, h * V_HEAD_DIM)


def token_mixer(h, cos, sin, w_in, conv_w, conv_b, conv_norm_g, conv_norm_b, w_conv_out,
                q_norm_g, w_uq, kv_norm_g, w_ukv, w_attn_out, w_out):
    b, s, _ = h.shape
    z = h @ w_in
    u_glu, q_lat, kv_lat, k_r, gate_logits = jnp.split(z, IN_SPLIT, axis=-1)

    ua, ub = jnp.split(u_glu, 2, axis=-1)
    u = ua * jax.nn.sigmoid(ub)
    u = causal_depthwise_conv(u, conv_w, conv_b)
    u = jax.nn.silu(layer_norm(u, conv_norm_g, conv_norm_b))
    y_a = u @ w_conv_out

    q = (rms_norm(q_lat, q_norm_g) @ w_uq).reshape(b, s, N_HEADS, QK_NOPE_DIM + QK_ROPE_DIM)
    q_nope, q_rope = jnp.split(q, [QK_NOPE_DIM], axis=-1)
    q_rope = apply_rope(q_rope, cos[:, :, None, :], sin[:, :, None, :])
    kv = (rms_norm(kv_lat, kv_norm_g) @ w_ukv).reshape(b, s, N_HEADS, QK_NOPE_DIM + V_HEAD_DIM)
    k_nope, v = jnp.split(kv, [QK_NOPE_DIM], axis=-1)
    k_rope = apply_rope(k_r, cos, sin)
    o = mla_attention(q_nope, q_rope, k_nope, k_rope, v)
    y_b = o @ w_attn_out

    g_a, g_b = jnp.split(jax.nn.sigmoid(gate_logits), N_BRANCHES, axis=-1)
    return (g_a * y_a + g_b * y_b) @ w_out


def setup_inputs(seed: int = 0) -> dict:
    key = jax.random.key(seed)
    ks = jax.random.split(key, 32)
    L = DEPTH

    def nrm(k, shape, fan_in, mult=1.0):
        return jax.random.normal(k, shape, jnp.float32) * (mult * fan_in ** -0.5)

    def gain(k, shape):
        return 1.0 + 0.05 * jax.random.normal(k, shape, jnp.float32)

    offsets = jax.random.randint(ks[2], (BATCH, 1), 0, 1024, dtype=jnp.int32)
    positions = offsets + jnp.arange(SEQ, dtype=jnp.int32)[None, :]
    return {
        "x": jax.random.normal(ks[0], (BATCH, SEQ, D_MODEL), jnp.float32),
        "c": jax.random.normal(ks[1], (BATCH, D_MODEL), jnp.float32),
        "positions": positions,
        "w_ada": nrm(ks[3], (L, D_MODEL, 6 * D_MODEL), D_MODEL, 0.2),
        "b_ada": 0.01 * jax.random.normal(ks[4], (L, 6 * D_MODEL), jnp.float32),
        "g_pre_mix": gain(ks[5], (L, D_MODEL)),
        "g_post_mix": gain(ks[6], (L, D_MODEL)),
        "g_pre_mlp": gain(ks[7], (L, D_MODEL)),
        "g_post_mlp": gain(ks[8], (L, D_MODEL)),
        "w_in": nrm(ks[9], (L, D_MODEL, IN_DIM), D_MODEL),
        "conv_w": nrm(ks[10], (L, CONV_KERNEL, CONV_DIM), CONV_KERNEL),
        "conv_b": 0.01 * jax.random.normal(ks[11], (L, CONV_DIM), jnp.float32),
        "conv_norm_g": gain(ks[12], (L, CONV_DIM)),
        "conv_norm_b": 0.01 * jax.random.normal(ks[13], (L, CONV_DIM), jnp.float32),
        "w_conv_out": nrm(ks[14], (L, CONV_DIM, D_MODEL), CONV_DIM),
        "q_norm_g": gain(ks[15], (L, Q_LORA_RANK)),
        "w_uq": nrm(ks[16], (L, Q_LORA_RANK, N_HEADS * (QK_NOPE_DIM + QK_ROPE_DIM)), Q_LORA_RANK),
        "kv_norm_g": gain(ks[17], (L, KV_LORA_RANK)),
        "w_ukv": nrm(ks[18], (L, KV_LORA_RANK, N_HEADS * (QK_NOPE_DIM + V_HEAD_DIM)), KV_LORA_RANK),
        "w_attn_out": nrm(ks[19], (L, N_HEADS * V_HEAD_DIM, D_MODEL), N_HEADS * V_HEAD_DIM),
        "w_out": nrm(ks[20], (L, D_MODEL, D_MODEL), D_MODEL),
        "w_mlp_in": nrm(ks[21], (L, D_MODEL, D_FF), D_MODEL),
        "w_mlp_out": nrm(ks[22], (L, D_FF, D_MODEL), D_FF),
    }


def reference(x, c, positions, w_ada, b_ada, g_pre_mix, g_post_mix, g_pre_mlp, g_post_mlp,
              w_in, conv_w, conv_b, conv_norm_g, conv_norm_b, w_conv_out,
              q_norm_g, w_uq, kv_norm_g, w_ukv, w_attn_out, w_out, w_mlp_in, w_mlp_out):
    cos, sin = rope_tables(positions)
    c_act = jax.nn.silu(c)
    for l in range(DEPTH):
        mod = c_act @ w_ada[l] + b_ada[l]
        shift1, scale1, gate1, shift2, scale2, gate2 = [m[:, None, :] for m in jnp.split(mod, 6, axis=-1)]

        h = rms_norm(x, g_pre_mix[l]) * (1.0 + scale1) + shift1
        y = token_mixer(h, cos, sin, w_in[l], conv_w[l], conv_b[l], conv_norm_g[l], conv_norm_b[l],
                        w_conv_out[l], q_norm_g[l], w_uq[l], kv_norm_g[l], w_ukv[l],
                        w_attn_out[l], w_out[l])
        x = x + gate1 * rms_norm(y, g_post_mix[l])

        h = rms_norm(x, g_pre_mlp[l]) * (1.0 + scale2) + shift2
        y = jnp.square(jax.nn.relu(h @ w_mlp_in[l])) @ w_mlp_out[l]
        x = x + gate2 * rms_norm(y, g_post_mlp[l])
    return x
```

```python
import contextlib
import math
import numpy as np
import concourse.bass as bass
import concourse.mybir as mybir
from concourse.bass_utils import run_bass_kernel_spmd

F32 = mybir.dt.float32
BF = mybir.dt.bfloat16
I32 = mybir.dt.int32
AF = mybir.ActivationFunctionType
ALU = mybir.AluOpType

D = 1024
KC = 8
NOWN = 2048
EPS = 1e-6
NEG = -30000.0
SCALE = 1.0 / math.sqrt(192.0)
TWO_PI = 2.0 * math.pi
C1 = 6.28125
C2 = TWO_PI - 6.28125

ENGS = ("pe", "act", "dve", "pool", "sp")
NDMA = 12


class _Rec:
    def __init__(self):
        self.call = None

    def __getattr__(self, name):
        def f(*a, **k):
            self.call = (name, a, k)
            return self
        return f


def _bind(fn):
    if fn is None:
        return None
    r = _Rec()
    fn(r)
    name, a, k = r.call
    return lambda eng: getattr(eng, name)(*a, **k)


class Sched:
    def __init__(self):
        self.q = {e: [] for e in ENGS}
        self.cnt = {e: 0 for e in ENGS}
        self.waited = {e: {} for e in ENGS}
        self.state = {}
        self.dma_tot = [0] * NDMA
        self.dma_rr = 0
        self.all_dma_tokens = {}

    def _entries(self, buf, key, create):
        d = self.state.setdefault(id(buf), {})
        if key is None:
            if create and None not in d:
                d[None] = {"w": None, "r": {}}
            return list(d.values()) if not create else list(d.values())
        out = []
        if key not in d and create:
            d[key] = {"w": None, "r": {}}
        if key in d:
            out.append(d[key])
        if None in d:
            out.append(d[None])
        return out

    def _deps(self, reads, writes):
        deps = {}

        def add(tok):
            if tok is None:
                return
            s, v = tok
            if deps.get(s, 0) < v:
                deps[s] = v

        for (b, k) in reads:
            for st in self._entries(b, k, False):
                add(st["w"])
        for (b, k) in writes:
            for st in self._entries(b, k, False):
                add(st["w"])
                for s, v in st["r"].items():
                    add((s, v))
        return deps

    def _commit(self, reads, writes, tok):
        for (b, k) in reads:
            d = self.state.setdefault(id(b), {})
            if k not in d:
                d[k] = {"w": None, "r": {}}
            st = d[k]
            s, v = tok
            if st["r"].get(s, 0) < v:
                st["r"][s] = v
        for (b, k) in writes:
            d = self.state.setdefault(id(b), {})
            if k is None:
                d.clear()
            d[k] = {"w": tok, "r": {}}

    def _norm(self, lst):
        out = []
        for x in lst:
            if isinstance(x, tuple):
                out.append(x)
            else:
                out.append((x, None))
        return out

    def op(self, eng, fn, reads=(), writes=(), inc=True):
        fn = _bind(fn)
        reads = self._norm(reads)
        writes = self._norm(writes)
        deps = self._deps(reads, writes)
        waits = []
        for s, v in deps.items():
            if s == eng and eng == "pe":
                continue
            if self.waited[eng].get(s, 0) >= v:
                continue
            self.waited[eng][s] = v
            waits.append((s, v))
        if inc:
            self.cnt[eng] += 1
            tok = (eng, self.cnt[eng])
            self.q[eng].append((waits, fn, (eng, 1)))
        else:
            tok = (eng, self.cnt[eng] + 1)
            self.q[eng].append((waits, fn, None))
        self._commit(reads, writes, tok)
        return tok

    def dma(self, eng, fn, reads=(), writes=()):
        fn = _bind(fn)
        reads = self._norm(reads)
        writes = self._norm(writes)
        j = self.dma_rr
        self.dma_rr = (self.dma_rr + 1) % NDMA
        sem = "d%d" % j
        deps = self._deps(reads, writes)
        if self.dma_tot[j] > 0:
            if deps.get(sem, 0) < self.dma_tot[j]:
                deps[sem] = self.dma_tot[j]
        waits = []
        for s, v in deps.items():
            if self.waited[eng].get(s, 0) >= v:
                continue
            self.waited[eng][s] = v
            waits.append((s, v))
        self.dma_tot[j] += 16
        tok = (sem, self.dma_tot[j])
        self.q[eng].append((waits, fn, (sem, 16)))
        self._commit(reads, writes, tok)
        return tok

    def barrier(self):
        toks = [(e, self.cnt[e]) for e in ENGS if self.cnt[e] > 0]
        toks += [("d%d" % j, self.dma_tot[j]) for j in range(NDMA) if self.dma_tot[j] > 0]
        for e in ENGS:
            self.wait_all(e, [t for t in toks if t[0] != e])
        self.state = {}

    def wait_all(self, eng, toks):
        waits = []
        for (s, v) in toks:
            if self.waited[eng].get(s, 0) >= v:
                continue
            self.waited[eng][s] = v
            waits.append((s, v))
        self.q[eng].append((waits, None, None))


def build_nc(debug=None, stop=None):
    nc = bass.Bass("TRN2", target_bir_lowering=False)
    S = Sched()

    def din(name, shape, dt=F32):
        return nc.dram_tensor(name, list(shape), dt, kind="ExternalInput").ap()

    x_own = din("x_own", [NOWN, D])
    x_oth = din("x_oth", [NOWN, D])
    x_halo = din("x_halo", [512, D])
    pos_own = nc.dram_tensor("pos_own", [NOWN], I32, kind="ExternalInput")
    pos_oth = nc.dram_tensor("pos_oth", [NOWN], I32, kind="ExternalInput")
    c_in = din("c", [D])
    pairmask_in = din("pairmask", [128, 128])
    trimask_in = din("trimask", [128, 128])
    ident_in = din("ident", [128, 128])
    halomask_in = din("halomask", [128, 1])
    invf_in = din("invf", [64, 1])
    sgn_in = din("sgn", [64, 1])
    w_ada = din("w_ada", [D, 6 * D])
    b_ada = din("b_ada", [6 * D])
    g_pre_mix = din("g_pre_mix", [D])
    g_post_mix = din("g_post_mix", [D])
    g_pre_mlp = din("g_pre_mlp", [D])
    g_post_mlp = din("g_post_mlp", [D])
    w_in = din("w_in", [D, 4800])
    w_kr_sw = din("w_kr_sw", [D, 64])
    conv_w = din("conv_w", [31, D])
    conv_b = din("conv_b", [D])
    conv_norm_g = din("conv_norm_g", [D])
    conv_norm_b = din("conv_norm_b", [D])
    w_conv_out = din("w_conv_out", [D, D])
    q_norm_g = din("q_norm_g", [384])
    w_uq = din("w_uq", [384, 1536])
    w_uq_sw = din("w_uq_sw", [384, 512])
    kv_norm_g = din("kv_norm_g", [256])
    w_ukv = din("w_ukv", [256, 2048])
    w_attn_out = din("w_attn_out", [D, D])
    w_out = din("w_out", [D, D])
    w_mlp_in = din("w_mlp_in", [D, 4 * D])
    w_mlp_out = din("w_mlp_out", [4 * D, D])
    out = nc.dram_tensor("out", [NOWN, D], F32, kind="ExternalOutput").ap()
    wq = nc.dram_tensor("wq", [60, 128, 2048], BF, kind="Internal").ap()
    wq_key = object()
    dbg = None

    es = contextlib.ExitStack()
    with es:
        def sb(name, shape, dt=F32):
            return es.enter_context(nc.sbuf_tensor("s_" + name, list(shape), dt))

        sems = {}
        for e in ENGS:
            sems[e] = es.enter_context(nc.semaphore("sem_" + e))
        for j in range(NDMA):
            sems["d%d" % j] = es.enter_context(nc.semaphore("sem_d%d" % j))

        PD = [es.enter_context(nc.psum_tensor("pd%d" % i, [128, 1024], F32)) for i in range(4)]

        def bank(i):
            t = PD[i // 2]
            h = i % 2
            return t, h

        def bk(i):
            t, h = bank(i)
            return (t, h)

        def bap(i, c0=0, c1=512):
            t, h = bank(i)
            return t[:, h * 512 + c0: h * 512 + c1]

        def bap_bf(i):
            t, h = bank(i)
            return t.bitcast(BF)[:, h * 1024:(h + 1) * 1024]

        ident_f = sb("ident_f", [128, 128])
        ident_b = sb("ident_b", [128, 128], BF)
        ones_b = sb("ones_b", [128, 128], BF)
        tri_b = sb("tri_b", [128, 128], BF)
        pair_b = sb("pair_b", [128, 128], BF)
        halom = sb("halom", [128, 1])
        invf = sb("invf", [64, 1])
        sgn = sb("sgn", [64, 1])
        modT = sb("modT", [128, 48])
        vecs = sb("vecs", [128, 128])
        cwT = sb("cwT", [128, 8, 31])
        V_GPRE, V_GPOST, V_GPRE2, V_GPOST2 = 0, 8, 16, 24
        V_CB, V_CG, V_CNB = 32, 40, 48
        V_QG, V_KVG = 56, 59
        der = sb("der", [128, 48])
        NST = 2
        wst = [sb("wst%d" % i, [128, 8, 256]) for i in range(NST)]
        wbf = [sb("wbf%d" % i, [128, 8, 256], BF) for i in range(NST)]
        wrr = [0]
        xblk = [sb("xblk%d" % i, [128, D]) for i in range(2)]
        xnb = [sb("xnb%d" % i, [128, D], BF) for i in range(2)]
        small = [sb("small%d" % i, [128, 4]) for i in range(4)]
        small_rr = [0]
        tmpf = [sb("tmpf%d" % i, [128, 512]) for i in range(4)]
        tmpf_rr = [0]
        tmpb = [sb("tmpb%d" % i, [128, 512], BF) for i in range(3)]
        tmpb_rr = [0]
        rstd_t = [sb("rstd%d" % i, [128, 512]) for i in range(2)]
        rstd_rr = [0]

        def nxt(lst, rr):
            t = lst[rr[0] % len(lst)]
            rr[0] += 1
            return t

        def A(fn, **kw):
            return S.op("act", fn, **kw)

        def V(fn, **kw):
            return S.op("dve", fn, **kw)

        def G(fn, **kw):
            return S.op("pool", fn, **kw)

        def P(fn, **kw):
            return S.op("pe", fn, **kw)

        def dma_in(dst_ap, src_ap, dst_buf, key=None, eng="sp", nonc=False):
            def f(e, dst_ap=dst_ap, src_ap=src_ap):
                if nonc:
                    return e.dma_start(out=dst_ap, in_=src_ap, allow_slow_non_contiguous=True)
                return e.dma_start(out=dst_ap, in_=src_ap)
            return S.dma(eng, f, writes=[(dst_buf, key)])

        def mm(out_ap, lhsT, rhs, start, stop, reads, wkey, last):
            def f(e):
                return e.matmul(out_ap, lhsT, rhs, start=start, stop=stop)
            return S.op("pe", f, reads=reads, writes=[wkey], inc=last)

        def load_w(src, rows, c0, ncols, r0=0, cast=None):
            i = wrr[0] % NST
            wrr[0] += 1
            kc = rows // 128
            st, wb = wst[i], wbf[i]
            src_ap = src[r0:r0 + rows, c0:c0 + ncols].rearrange("(k p) c -> p k c", p=128)
            dma_in(st[:, 0:kc, 0:ncols], src_ap, st)
            if cast == "pool" or (cast is None and wrr[0] % 2 == 0):
                G(lambda e: e.tensor_copy(wb[:, 0:kc, 0:ncols], st[:, 0:kc, 0:ncols]), reads=[st], writes=[wb])
            else:
                V(lambda e: e.tensor_copy(wb[:, 0:kc, 0:ncols], st[:, 0:kc, 0:ncols]), reads=[st], writes=[wb])
            return wb

        def _unused():
            pass

        out_toks = []

        def run_phases():
            dma_in(ident_f[:], ident_in, ident_f)
            dma_in(halom[:], halomask_in, halom)
            dma_in(invf[:], invf_in, invf)
            dma_in(sgn[:], sgn_in, sgn)
            t0 = tmpf[0]
            t1 = tmpf[1]
            dma_in(t0[:, 0:128], trimask_in, t0)
            dma_in(t1[:, 0:128], pairmask_in, t1)
            V(lambda e: e.tensor_copy(ident_b[:], ident_f[:]), reads=[ident_f], writes=[ident_b])
            V(lambda e: e.memset(ones_b[:], 1.0), writes=[ones_b])
            V(lambda e: e.tensor_copy(tri_b[:], t0[:, 0:128]), reads=[t0], writes=[tri_b])
            V(lambda e: e.tensor_copy(pair_b[:], t1[:, 0:128]), reads=[t1], writes=[pair_b])
            tmpf_rr[0] = 2
            if stop == -1:
                return
            stg = tmpf[2]
            V(lambda e: e.memset(stg[:, 0:128], 0.0), writes=[stg])
            for col, src, n in ((V_GPRE, g_pre_mix, D), (V_GPOST, g_post_mix, D), (V_GPRE2, g_pre_mlp, D),
                                (V_GPOST2, g_post_mlp, D), (V_CB, conv_b, D), (V_CG, conv_norm_g, D),
                                (V_CNB, conv_norm_b, D), (V_QG, q_norm_g, 384), (V_KVG, kv_norm_g, 256)):
                dma_in(stg[col:col + n // 128, 0:128], src.rearrange("(k p) -> k p", p=128), stg)
            dma_in(stg[64:112, 0:128], b_ada.rearrange("(k p) -> k p", p=128), stg)
            dma_in(stg[112:120, 0:128], c_in.rearrange("(k p) -> k p", p=128), stg)
            P(lambda e: e.transpose(bap(1, 0, 120), stg[0:120, 0:128], ident_f[0:120, 0:120]),
              reads=[stg, ident_f], writes=[bk(1)])
            V(lambda e: e.tensor_copy(vecs[:, 0:120], bap(1, 0, 120)), reads=[bk(1)], writes=[vecs])
            if stop == -2:
                return
            badaT = vecs[:, 64:112]
            cT = vecs[:, 112:120]
            cwn = xblk[0]
            dma_in(cwn[0:31, :], conv_w, cwn)
            for c in range(8):
                P(lambda e, c=c: e.transpose(bap(2, c * 32, c * 32 + 31), cwn[0:31, c * 128:(c + 1) * 128], ident_f[0:31, 0:31]),
                  reads=[cwn, ident_f], writes=[bk(2)])
            V(lambda e: e.tensor_copy(cwT[:], bap(2, 0, 256).rearrange("p (c k) -> p c k", k=32)[:, :, 0:31]),
              reads=[bk(2)], writes=[cwT])
            if stop == -3:
                return
            scb = sb("scb", [128, 8], BF)
            A(lambda e: e.activation(scb[:], cT, AF.Silu), reads=[vecs], writes=[scb])
            if stop == -4:
                return
            def mod_part(p0, p1, MODB, cast=None):
                for pc in range(p0, p1):
                    wb = load_w(w_ada, D, pc * 256, 256, cast=cast)
                    for jj in range(2):
                        j = pc * 2 + jj
                        for k in range(8):
                            mm(bap(MODB, j, j + 1), wb[:, k, jj * 128:(jj + 1) * 128], scb[:, k:k + 1],
                               k == 0, k == 7, [wb, scb], bk(MODB), k == 7)
                V(lambda e: e.tensor_tensor(modT[:, p0 * 2:p1 * 2], bap(MODB, p0 * 2, p1 * 2), vecs[:, 64 + p0 * 2:64 + p1 * 2], ALU.add),
                  reads=[bk(MODB), vecs], writes=[(modT, p0)])

            mod_part(0, 8, 0)
            if stop == -5:
                return
            V(lambda e: e.scalar_tensor_tensor(der[:, 0:8], modT[:, 8:16], 1.0, vecs[:, V_GPRE:V_GPRE + 8], ALU.add, ALU.mult),
              reads=[modT, vecs], writes=[(der, 0)])
            V(lambda e: e.tensor_copy(der[:, 8:16], modT[:, 0:8]), reads=[modT], writes=[(der, 8)])

            def mod_late():
                mod_part(8, 24, 7, cast="pool")
                V(lambda e: e.tensor_tensor(der[:, 16:24], modT[:, 16:24], vecs[:, V_GPOST:V_GPOST + 8], ALU.mult),
                  reads=[modT, vecs], writes=[(der, 16)])
                V(lambda e: e.scalar_tensor_tensor(der[:, 24:32], modT[:, 32:40], 1.0, vecs[:, V_GPRE2:V_GPRE2 + 8], ALU.add, ALU.mult),
                  reads=[modT, vecs], writes=[(der, 24)])
                V(lambda e: e.tensor_copy(der[:, 32:40], modT[:, 24:32]), reads=[modT], writes=[(der, 32)])
                V(lambda e: e.tensor_tensor(der[:, 40:48], modT[:, 40:48], vecs[:, V_GPOST2:V_GPOST2 + 8], ALU.mult),
                  reads=[modT, vecs], writes=[(der, 40)])
            DER_ALL = [(der, 0), (der, 8), (der, 16), (der, 24), (der, 32), (der, 40)]

            TPB = [0, 1]
            tp_rr = [0]
            xb_rr = [0]

            def xb_prep(src_rows_ap, nrows):
                i = xb_rr[0] % 2
                xb_rr[0] += 1
                xb = xblk[i]
                xn = xnb[i]
                dma_in(xb[0:nrows, :], src_rows_ap, xb)
                sm = nxt(small, small_rr)
                A(lambda e: e.activation(xn[0:nrows, :], xb[0:nrows, :], AF.Square, accum_out=sm[0:nrows, 0:1]),
                  reads=[xb], writes=[xn, (sm, 0)])
                A(lambda e: e.activation(sm[0:nrows, 3:4], sm[0:nrows, 0:1], AF.Sqrt, bias=EPS, scale=1.0 / D),
                  reads=[(sm, 0)], writes=[(sm, 3)])
                V(lambda e: e.reciprocal(sm[0:nrows, 2:3], sm[0:nrows, 3:4]), reads=[(sm, 3)], writes=[(sm, 2)])
                V(lambda e: e.tensor_scalar(xn[0:nrows, :], xb[0:nrows, :], sm[0:nrows, 2:3], None, ALU.mult),
                  reads=[xb, (sm, 2)], writes=[xn])
                return (xn, nrows)

            def xb_trans(hd, hT, col0, gsc, shc):
                xn, nrows = hd
                b = TPB[tp_rr[0] % 2]
                tp_rr[0] += 1
                tpv = bap_bf(b)
                for k in range(8):
                    P(lambda e, k=k: e.transpose(tpv[:, k * 128:k * 128 + nrows], xn[0:nrows, k * 128:(k + 1) * 128],
                                                  ident_b[0:nrows, 0:nrows]),
                      reads=[xn, ident_b], writes=[bk(b)], inc=(k == 7))
                for k in range(8):
                    if b == TPB[0]:
                        V(lambda e, k=k: e.tensor_scalar(hT[:, k, col0:col0 + nrows], tpv[:, k * 128:k * 128 + nrows],
                                                         der[:, gsc + k:gsc + k + 1], der[:, shc + k:shc + k + 1],
                                                         ALU.mult, ALU.add),
                          reads=[bk(b), (der, gsc), (der, shc)], writes=[(hT, k)])
                    else:
                        A(lambda e, k=k: e.activation(hT[:, k, col0:col0 + nrows], tpv[:, k * 128:k * 128 + nrows],
                                                      AF.Identity, bias=der[:, shc + k:shc + k + 1],
                                                      scale=der[:, gsc + k:gsc + k + 1]),
                          reads=[bk(b), (der, gsc), (der, shc)], writes=[(hT, k)])

            def x_block_to_hT(src_rows_ap, nrows, hT, col0, gsc, shc):
                xb_trans(xb_prep(src_rows_ap, nrows), hT, col0, gsc, shc)

            def rstd_from_ps(ps_bank, nfeat, ncols=512):
                r = nxt(rstd_t, rstd_rr)
                jt = nxt(tmpf, tmpf_rr)
                A(lambda e: e.activation(jt[:, 0:ncols], bap(ps_bank, 0, ncols), AF.Sqrt, bias=EPS, scale=1.0 / nfeat),
                  reads=[bk(ps_bank)], writes=[jt])
                V(lambda e: e.reciprocal(r[:, 0:ncols], jt[:, 0:ncols]), reads=[jt], writes=[r])
                return r

            if stop == 0:
                return
            oT = sb("oT", [128, 8, NOWN], BF)
            ph12 = es.enter_context(contextlib.ExitStack())
            ph1 = es.enter_context(contextlib.ExitStack())

            def sb12(name, shape, dt=F32):
                return ph12.enter_context(nc.sbuf_tensor("s_" + name, list(shape), dt))

            def sb1(name, shape, dt=F32):
                return ph1.enter_context(nc.sbuf_tensor("s_" + name, list(shape), dt))

            kvn = [sb12("kvn_own", [128, 2, NOWN], BF), sb12("kvn_oth", [128, 2, NOWN], BF)]
            krT = [sb12("kr_own", [64, NOWN], BF), sb12("kr_oth", [64, NOWN], BF)]
            qn = sb12("qn", [128, 3, NOWN], BF)
            CS = sb12("cs_own", [64, 2, NOWN])
            hTs = [sb1("hT%d" % i, [128, 8, 640], BF) for i in range(2)]
            wlat = sb1("wlat", [128, 8, 768], BF)
            cs_tmp = sb1("cs_tmp", [64, 2, 512])
            posi = sb1("posi", [64, 512], I32)
            angs = [sb1("ang%d" % i, [64, 512]) for i in range(4)]
            ni_t = sb1("ni_t", [64, 512], I32)

            for pc, (src, c0, n, d0) in enumerate(((w_in, 2048, 256, 0), (w_in, 2304, 256, 256), (w_in, 2560, 192, 512),
                                                   (w_kr_sw, 0, 64, 704))):
                wb = load_w(src, D, c0, n)
                G(lambda e, wb=wb, n=n, d0=d0: e.tensor_copy(wlat[:, :, d0:d0 + n], wb[:, :, 0:n]),
                  reads=[wb], writes=[(wlat, pc)])
            WL = [(wlat, i) for i in range(4)]
            if stop == 10:
                return

            def rope_tables(pos_t, c0, dst, dcol):
                src = bass.AP(pos_t, c0, [[0, 64], [1, 512]])
                dma_in(posi[:], src, posi)
                a0, a1, a2, a3 = angs
                V(lambda e: e.tensor_copy(a0[:], posi[:]), reads=[posi], writes=[a0])
                V(lambda e: e.tensor_scalar(a0[:], a0[:], invf[:, 0:1], None, ALU.mult), reads=[a0, invf], writes=[a0])
                V(lambda e: e.tensor_scalar(a1[:], a0[:], 1.0 / TWO_PI, None, ALU.mult), reads=[a0], writes=[a1])
                V(lambda e: e.tensor_copy(ni_t[:], a1[:]), reads=[a1], writes=[ni_t])
                V(lambda e: e.tensor_copy(a1[:], ni_t[:]), reads=[ni_t], writes=[a1])
                V(lambda e: e.scalar_tensor_tensor(a2[:], a1[:], -C1, a0[:], ALU.mult, ALU.add), reads=[a1, a0], writes=[a2])
                V(lambda e: e.scalar_tensor_tensor(a2[:], a1[:], -C2, a2[:], ALU.mult, ALU.add), reads=[a1, a2], writes=[a2])
                V(lambda e: e.tensor_scalar(a3[:], a2[:], math.pi, -TWO_PI, ALU.is_gt, ALU.mult), reads=[a2], writes=[a3])
                V(lambda e: e.tensor_tensor(a2[:], a2[:], a3[:], ALU.add), reads=[a2, a3], writes=[a2])
                V(lambda e: e.tensor_scalar(a3[:], a2[:], -math.pi, TWO_PI, ALU.is_lt, ALU.mult), reads=[a2], writes=[a3])
                V(lambda e: e.tensor_tensor(a2[:], a2[:], a3[:], ALU.add), reads=[a2, a3], writes=[a2])
                V(lambda e: e.tensor_scalar(a1[:], a2[:], math.pi / 2, None, ALU.add), reads=[a2], writes=[a1])
                V(lambda e: e.tensor_scalar(a3[:], a1[:], math.pi, -TWO_PI, ALU.is_gt, ALU.mult), reads=[a1], writes=[a3])
                V(lambda e: e.tensor_tensor(a1[:], a1[:], a3[:], ALU.add), reads=[a1, a3], writes=[a1])
                V(lambda e: e.tensor_scalar(a1[:], a1[:], math.pi, -math.pi, ALU.min, ALU.max), reads=[a1], writes=[a1])
                V(lambda e: e.tensor_scalar(a2[:], a2[:], math.pi, -math.pi, ALU.min, ALU.max), reads=[a2], writes=[a2])
                A(lambda e: e.activation(dst[:, 0, dcol:dcol + 512], a1[:], AF.Sin), reads=[a1], writes=[(dst, dcol)])
                A(lambda e: e.activation(dst[:, 1, dcol:dcol + 512], a2[:], AF.Sin, scale=sgn[:, 0:1]),
                  reads=[a2, sgn], writes=[(dst, dcol)])

            def prep1(j_):
                i_, b_ = j_ // 4, j_ % 4
                grp_, t_ = i_ // 4, i_ % 4
                xsrc = x_own if grp_ == 0 else x_oth
                r0 = t_ * 512 + b_ * 128
                return xb_prep(xsrc[r0:r0 + 128, :], 128)

            hd1 = [prep1(0)]
            for grp in range(2):
                pos_t = pos_own if grp == 0 else pos_oth
                for t in range(4):
                    hT = hTs[(grp * 4 + t) % 2]
                    for b in range(4):
                        j_ = (grp * 4 + t) * 4 + b
                        nh = prep1(j_ + 1) if j_ + 1 < 32 else None
                        xb_trans(hd1[0], hT, b * 128, 0, 8)
                        hd1[0] = nh
                    if grp == 0:
                        rope_tables(pos_t, t * 512, CS, t * 512)
                        cs, cc = CS, t * 512
                    else:
                        rope_tables(pos_t, t * 512, cs_tmp, 0)
                        cs, cc = cs_tmp, 0
                    if stop == 12:
                        return
                    hk = [(hT, k) for k in range(8)]
                    for m in range(2):
                        for k in range(8):
                            mm(bap(2 + m), wlat[:, k, 384 + m * 128:384 + (m + 1) * 128], hT[:, k, 0:512],
                               k == 0, k == 7, hk + WL, bk(2 + m), k == 7)
                    for m in range(2):
                        for k in range(8):
                            mm(bap(5 + m)[0:64, :], wlat[:, k, 640 + m * 64:640 + (m + 1) * 64], hT[:, k, 0:512],
                               k == 0, k == 7, hk + WL, bk(5 + m), k == 7)
                    sq = []
                    for m in range(2):
                        s_ = nxt(tmpb, tmpb_rr)
                        A(lambda e, m=m, s_=s_: e.activation(s_[:], bap(2 + m), AF.Square), reads=[bk(2 + m)], writes=[s_])
                        sq.append(s_)
                    for m in range(2):
                        mm(bap(4), ones_b[:], sq[m][:], m == 0, m == 1, [ones_b, sq[m]], bk(4), True)
                    r = rstd_from_ps(4, 256)
                    for m in range(2):
                        V(lambda e, m=m, r=r: e.scalar_tensor_tensor(kvn[grp][:, m, t * 512:(t + 1) * 512], bap(2 + m),
                                                                     vecs[:, V_KVG + m:V_KVG + m + 1], r[:], ALU.mult, ALU.mult),
                          reads=[bk(2 + m), r, vecs], writes=[(kvn[grp], t)])
                    ta = nxt(tmpf, tmpf_rr)
                    tb_ = nxt(tmpf, tmpf_rr)
                    V(lambda e, ta=ta, cs=cs, cc=cc: e.tensor_tensor(ta[0:64, :], bap(5)[0:64, :], cs[:, 0, cc:cc + 512], ALU.mult),
                      reads=[bk(5), (cs, cc)], writes=[ta])
                    V(lambda e, tb_=tb_, cs=cs, cc=cc: e.tensor_tensor(tb_[0:64, :], bap(6)[0:64, :], cs[:, 1, cc:cc + 512], ALU.mult),
                      reads=[bk(6), (cs, cc)], writes=[tb_])
                    V(lambda e, ta=ta, tb_=tb_: e.tensor_tensor(krT[grp][:, t * 512:(t + 1) * 512], ta[0:64, :], tb_[0:64, :], ALU.add),
                      reads=[ta, tb_], writes=[(krT[grp], t)])
                    if stop == 13:
                        return
                    if grp == 0:
                        QB = [2, 3, 7]
                        for m in range(3):
                            for k in range(8):
                                mm(bap(QB[m]), wlat[:, k, m * 128:(m + 1) * 128], hT[:, k, 0:512],
                                   k == 0, k == 7, hk + WL, bk(QB[m]), k == 7)
                        sq = []
                        for m in range(3):
                            s_ = nxt(tmpb, tmpb_rr)
                            A(lambda e, m=m, s_=s_: e.activation(s_[:], bap(QB[m]), AF.Square), reads=[bk(QB[m])], writes=[s_])
                            sq.append(s_)
                        for m in range(3):
                            mm(bap(4), ones_b[:], sq[m][:], m == 0, m == 2, [ones_b, sq[m]], bk(4), True)
                        r = rstd_from_ps(4, 384)
                        for m in range(3):
                            V(lambda e, m=m, r=r: e.scalar_tensor_tensor(qn[:, m, t * 512:(t + 1) * 512], bap(QB[m]),
                                                                         vecs[:, V_QG + m:V_QG + m + 1], r[:], ALU.mult, ALU.mult),
                              reads=[bk(QB[m]), r, vecs], writes=[(qn, t)])
                    if stop == 14:
                        return

            S.barrier()
            ph1.close()
            if stop == 1:
                ph12.close()
                return

            wuq = sb12("wuq", [128, 3, 2048], BF)
            wukv = sb12("wukv", [128, 2, 2048], BF)
            for pc in range(6):
                wb = load_w(w_uq, 384, pc * 256, 256)
                G(lambda e, wb=wb, pc=pc: e.tensor_copy(wuq[:, :, pc * 256:(pc + 1) * 256], wb[:, 0:3, :]),
                  reads=[wb], writes=[(wuq, pc)])
            for pc in range(2):
                wb = load_w(w_uq_sw, 384, pc * 256, 256)
                G(lambda e, wb=wb, pc=pc: e.tensor_copy(wuq[:, :, 1536 + pc * 256:1536 + (pc + 1) * 256], wb[:, 0:3, :]),
                  reads=[wb], writes=[(wuq, 6 + pc)])
            for pc in range(8):
                wb = load_w(w_ukv, 256, pc * 256, 256)
                G(lambda e, wb=wb, pc=pc: e.tensor_copy(wukv[:, :, pc * 256:(pc + 1) * 256], wb[:, 0:2, :]),
                  reads=[wb], writes=[(wukv, pc)])
            KhT = [[sb12("kh%d_%d" % (i, g), [128, NOWN], BF) for g in range(2)] for i in range(1)]
            Vh = [[sb12("vh%d_%d" % (i, g), [128, 16, 128], BF) for g in range(2)] for i in range(1)]
            Qh = [sb12("qh%d" % i, [128, NOWN], BF) for i in range(1)]
            Qr = [sb12("qr%d" % i, [64, NOWN], BF) for i in range(1)]
            Pt = [sb12("pt%d" % i, [128, 512], BF) for i in range(4)]
            pt_rr = [0]
            SB_ = [0, 1, 2]
            s_rr = [0]
            OB = [3, 5]
            LB = [4, 6]
            HBS = [7, 3, 4]
            hb_rr = [0]

            def nhb():
                b_ = HBS[hb_rr[0] % 3]
                hb_rr[0] += 1
                return b_
            evac_rr = [0]

            def evac_copy(dst_ap, src_bank_ap, reads, writes):
                if evac_rr[0] % 2 == 0:
                    V(lambda e: e.tensor_copy(dst_ap, src_bank_ap), reads=reads, writes=writes)
                else:
                    A(lambda e: e.activation(dst_ap, src_bank_ap, AF.Copy), reads=reads, writes=writes)
                evac_rr[0] += 1

            def build_head(h):
                i = 0
                for grp in range(2):
                    for t in range(4):
                        HB = nhb()
                        for k in range(2):
                            mm(bap(HB), wukv[:, k, h * 256:h * 256 + 128], kvn[grp][:, k, t * 512:(t + 1) * 512],
                               k == 0, k == 1, [wukv, kvn[grp]], bk(HB), k == 1)
                        evac_copy(KhT[i][grp][:, t * 512:(t + 1) * 512], bap(HB), [bk(HB)], [(KhT[i][grp], t)])
                    for t in range(4):
                        HB = nhb()
                        for b in range(4):
                            blk = t * 4 + b
                            for k in range(2):
                                mm(bap(HB, b * 128, (b + 1) * 128), kvn[grp][:, k, blk * 128:(blk + 1) * 128],
                                   wukv[:, k, h * 256 + 128:h * 256 + 256],
                                   k == 0, k == 1, [wukv, kvn[grp]], bk(HB), (k == 1 and b == 3))
                        evac_copy(Vh[i][grp][:, t * 4:(t + 1) * 4, :], bap(HB).rearrange("p (b d) -> p b d", d=128),
                                  [bk(HB)], [(Vh[i][grp], t)])
                for t in range(4):
                    HB = nhb()
                    for k in range(3):
                        mm(bap(HB), wuq[:, k, h * 192:h * 192 + 128], qn[:, k, t * 512:(t + 1) * 512],
                           k == 0, k == 2, [wuq, qn], bk(HB), k == 2)
                    evac_copy(Qh[i][:, t * 512:(t + 1) * 512], bap(HB), [bk(HB)], [(Qh[i], t)])
                for t in range(4):
                    HB = nhb()
                    for k in range(3):
                        mm(bap(HB)[0:64, :], wuq[:, k, h * 192 + 128:h * 192 + 192], qn[:, k, t * 512:(t + 1) * 512],
                           k == 0, k == 2, [wuq, qn], bk(HB), k == 2)
                    ta = nxt(tmpf, tmpf_rr)
                    V(lambda e, ta=ta, t=t: e.tensor_tensor(ta[0:64, :], bap(HB)[0:64, :], CS[:, 0, t * 512:(t + 1) * 512], ALU.mult),
                      reads=[bk(HB), CS], writes=[ta])
                    HB = nhb()
                    for k in range(3):
                        mm(bap(HB)[0:64, :], wuq[:, k, 1536 + h * 64:1536 + (h + 1) * 64], qn[:, k, t * 512:(t + 1) * 512],
                           k == 0, k == 2, [wuq, qn], bk(HB), k == 2)
                    tb_ = nxt(tmpf, tmpf_rr)
                    V(lambda e, tb_=tb_, t=t: e.tensor_tensor(tb_[0:64, :], bap(HB)[0:64, :], CS[:, 1, t * 512:(t + 1) * 512], ALU.mult),
                      reads=[bk(HB), CS], writes=[tb_])
                    V(lambda e, ta=ta, tb_=tb_, t=t: e.tensor_tensor(Qr[i][:, t * 512:(t + 1) * 512], ta[0:64, :], tb_[0:64, :], ALU.add),
                      reads=[ta, tb_], writes=[(Qr[i], t)])

            def attend_head(h):
                i = 0
                for g in range(4):
                    ob = OB[g % 2]
                    lb = LB[g % 2]
                    visits = [(J, grp) for J in range(4 * g + 4) for grp in range(2)]
                    pend = []

                    def do_pv(v, first, last):
                        J, grp, c0, pt = v
                        mm(bap(ob, c0, 512), Vh[i][grp][:, J, :], pt[:, c0:512], first, last,
                           [Vh[i][grp], pt], bk(ob), True)
                        mm(bap(lb, c0, 512), ones_b[:], pt[:, c0:512], first, last,
                           [ones_b, pt], bk(lb), True)

                    npv = [0]
                    for vi, (J, grp) in enumerate(visits):
                        j = J - 4 * g
                        c0 = 128 * max(j, 0)
                        sbk = SB_[s_rr[0] % 3]
                        s_rr[0] += 1
                        q0 = g * 512 + c0
                        q1 = (g + 1) * 512
                        masked = j >= 0
                        mm(bap(sbk, c0, 512), KhT[i][grp][:, J * 128:(J + 1) * 128], Qh[i][:, q0:q1],
                           True, False, [KhT[i][grp], Qh[i]], bk(sbk), False)
                        mm(bap(sbk, c0, 512), krT[grp][:, J * 128:(J + 1) * 128], Qr[i][:, q0:q1],
                           False, not masked, [krT[grp], Qr[i]], bk(sbk), not masked)
                        if masked:
                            mk = tri_b if grp == 0 else pair_b
                            mm(bap(sbk, c0, c0 + 128), ident_b[:], mk[:], False, True, [ident_b, mk], bk(sbk), True)
                        pt = nxt(Pt, pt_rr)
                        A(lambda e, pt=pt, sbk=sbk, c0=c0: e.activation(pt[:, c0:512], bap(sbk, c0, 512), AF.Exp, scale=SCALE),
                          reads=[bk(sbk)], writes=[pt])
                        pend.append((J, grp, c0, pt))
                        if len(pend) > 2:
                            v = pend.pop(0)
                            do_pv(v, npv[0] == 0, False)
                            npv[0] += 1
                    while pend:
                        v = pend.pop(0)
                        do_pv(v, npv[0] == 0, len(pend) == 0)
                        npv[0] += 1
                    rl = nxt(rstd_t, rstd_rr)
                    V(lambda e, rl=rl, lb=lb: e.reciprocal(rl[:], bap(lb)), reads=[bk(lb)], writes=[rl])
                    V(lambda e, rl=rl, ob=ob, g=g: e.tensor_tensor(oT[:, h, g * 512:(g + 1) * 512], bap(ob), rl[:], ALU.mult),
                      reads=[bk(ob), rl], writes=[(oT, (h, g))])

            prep = []
            for pc in range(4):
                prep += [(w_in, pc * 256, 0), (w_in, 1024 + pc * 256, 0)]
            for pc in range(4):
                prep += [(w_conv_out, pc * 256, 0)]
            for pc in range(4):
                prep += [(w_attn_out, pc * 256, 0), (w_in, 2752 + pc * 256, 0), (w_in, 3776 + pc * 256, 0)]
            for pc in range(4):
                prep += [(w_out, pc * 256, 0)]
            for pc in range(16):
                prep += [(w_mlp_in, pc * 256, 0)]
            for pc in range(4):
                for rr_ in range(4):
                    prep += [(w_mlp_out, pc * 256, rr_ * 1024)]
            prep_tok = {}

            def do_prep():
                for i, (src, c0, r0) in enumerate(prep):
                    wb = load_w(src, D, c0, 256, r0=r0)
                    S.dma("sp", lambda e, wb=wb, i=i: e.dma_start(out=wq[i], in_=wb[:].rearrange("p k c -> p (k c)")),
                          reads=[wb], writes=[(wq_key, i)])

            build_head(0)
            for h in range(8):
                attend_head(h)
                if h == 0:
                    mod_late()
                    do_prep()
                if h + 1 < 8:
                    build_head(h + 1)

            S.barrier()
            ph12.close()
            if stop == 2:
                return
            wpool = [wbf[0][:], wbf[1][:]]
            for i_ in range(NST):
                fl = wst[i_].bitcast(BF)[:].rearrange("p k c -> p (k c)")
                wpool += [fl[:, 0:2048].rearrange("p (k c) -> p k c", c=256), fl[:, 2048:4096].rearrange("p (k c) -> p k c", c=256)]
            wp_rr = [0]
            pidx = {(src_.tensor.name, c0_, r0_): i_ for i_, (src_, c0_, r0_) in enumerate(prep)}

            def load_wq(src, rows, c0, ncols, r0=0):
                i = pidx[(src.tensor.name, c0, r0)]
                buf = wpool[wp_rr[0] % len(wpool)]
                wp_rr[0] += 1
                S.dma("sp", lambda e: e.dma_start(out=buf.rearrange("p k c -> p (k c)"), in_=wq[i]), writes=[buf])
                return buf

            xT = sb("xT", [128, 8, 512])
            yT = sb("yT", [128, 8, 512])
            hTe = sb("hTe", [128, 8, 640], BF)
            uext = sb("uext", [128, 8, 640], BF)
            arena = sb("arena", [128, 16384], BF)
            hid = arena[:, :].rearrange("p (j t) -> p j t", t=512)
            ucv = arena.bitcast(F32)[:, 0:4096].rearrange("p (c t) -> p c t", t=512)
            diag = [arena[:, 8192 + i * 3968:8192 + (i + 1) * 3968].rearrange("p (k m) -> p k m", m=128) for i in range(2)]
            sh8 = sb("sh8", [128, 8, 512], BF)
            actT = sh8
            mT = sh8
            h2T = sh8
            yaT = sb("yaT", [128, 8, 512], BF)
            oblk = xblk
            stat_s = sb("stat_s", [128, 512])
            stat_n = sb("stat_n", [128, 512])
            sb_sig = sb("sb_sig", [128, 640])

            def stats_accum(ps_b, src_ap, reads, idx, n):
                s_ = nxt(tmpb, tmpb_rr)
                A(lambda e: e.activation(s_[:], src_ap, AF.Square), reads=reads, writes=[s_])
                mm(bap(ps_b), ones_b[:], s_[:], idx == 0, idx == n - 1, [ones_b, s_], bk(ps_b), True)

            def fh_blocks(g_):
                lst = []
                for b in range(4):
                    blk = g_ * 4 + b
                    lst.append((x_halo[blk * 32:(blk + 1) * 32, :], 32, b * 160))
                    lst.append((x_own[blk * 128:(blk + 1) * 128, :], 128, b * 160 + 32))
                return lst

            def front_h(g_):
                lst = fh_blocks(g_)
                hd = xb_prep(lst[0][0], lst[0][1])
                for i_ in range(8):
                    nh = xb_prep(lst[i_ + 1][0], lst[i_ + 1][1]) if i_ + 1 < 8 else None
                    xb_trans(hd, hTe, lst[i_][2], 0, 8)
                    hd = nh

            front_h(0)
            for g in range(4):
                for b in range(4):
                    blk = g * 4 + b
                    xb = xblk[xb_rr[0] % 2]
                    xb_rr[0] += 1
                    dma_in(xb[:], x_own[blk * 128:(blk + 1) * 128, :], xb)
                    for half in range(2):
                        tb_i = 2 + half
                        for kk in range(4):
                            k = half * 4 + kk
                            P(lambda e, k=k, kk=kk, tb_i=tb_i, xb=xb: e.transpose(bap(tb_i, kk * 128, (kk + 1) * 128),
                                                                                    xb[:, k * 128:(k + 1) * 128], ident_f[:]),
                              reads=[xb, ident_f], writes=[bk(tb_i)], inc=(kk == 3))
                        evac_copy(xT[:, half * 4:(half + 1) * 4, b * 128:(b + 1) * 128],
                                  bap(tb_i).rearrange("p (k t) -> p k t", t=128), [bk(tb_i)], [(xT, (half, b))])
                hk = [(hTe, k) for k in range(8)]
                def glu_chunk(c, wa, wb2, cc):
                    for (wt, d) in ((wa, 0), (wb2, 1)):
                        for (n0, n1, hb) in ((0, 512, 0), (512, 640, 1)):
                            for k in range(8):
                                mm(PD[d][:, hb * 512:hb * 512 + (n1 - n0)], wt[:, k, cc * 128:(cc + 1) * 128],
                                   hTe[:, k, n0:n1], k == 0, k == 7, hk + [wt], (PD[d], hb), k == 7)
                    sg = sb_sig
                    A(lambda e: e.activation(sg[:, 0:640], PD[1][:, 0:640], AF.Sigmoid),
                      reads=[(PD[1], 0), (PD[1], 1)], writes=[sg])
                    V(lambda e: e.tensor_tensor(uext[:, c, :], PD[0][:, 0:640], sg[:, 0:640], ALU.mult),
                      reads=[(PD[0], 0), (PD[0], 1), sg], writes=[(uext, c)])
                    if g == 0:
                        V(lambda e: e.tensor_scalar(uext[:, c, 0:32], uext[:, c, 0:32], halom[:, 0:1], None, ALU.mult),
                          reads=[(uext, c), halom], writes=[(uext, c)])

                def conv_chunk(c):
                    dg = diag[c % 2]
                    for k in range(31):
                        G(lambda e: e.tensor_scalar(dg[:, k, :], ident_b[:], cwT[:, c, k:k + 1], 1.0, ALU.mult, ALU.mult),
                          reads=[ident_b, cwT], writes=[(arena, None) if (c == 0 and k == 0) else (arena, ("d", c % 2, k))])
                    uv = uext[:, c, :].rearrange("p (b w) -> p b w", w=160)
                    cb = 4 + (c % 2)
                    for k in range(31):
                        mm(bap(cb).rearrange("p (b w) -> p b w", w=128), dg[:, k, :], uv[:, :, 2 + k:2 + k + 128],
                           k == 0, k == 30, [(arena, ("d", c % 2, k)), (uext, c)], bk(cb), k == 30)
                    A(lambda e: e.activation(ucv[:, c, :], bap(cb), AF.Identity, bias=vecs[:, V_CB + c:V_CB + c + 1]),
                      reads=[bk(cb), vecs], writes=[(arena, ("u", c))])
                    ub_ = nxt(tmpb, tmpb_rr)
                    V(lambda e: e.tensor_copy(ub_[:], ucv[:, c, :]), reads=[(arena, ("u", c))], writes=[ub_])
                    mm(bap(6), ones_b[:], ub_[:], c == 0, c == 7, [ones_b, ub_], bk(6), True)
                    stats_accum(7, ucv[:, c, :], [(arena, ("u", c))], c, 8)

                for pc in range(4):
                    wa = load_wq(w_in, D, pc * 256, 256)
                    wb2 = load_wq(w_in, D, 1024 + pc * 256, 256)
                    for cc in range(2):
                        c = pc * 2 + cc
                        glu_chunk(c, wa, wb2, cc)
                        if c >= 1:
                            conv_chunk(c - 1)
                conv_chunk(7)
                mean = stat_s
                nmr = stat_n
                A(lambda e: e.activation(mean[:], bap(6), AF.Copy, scale=1.0 / D), reads=[bk(6)], writes=[mean])
                jt = nxt(tmpf, tmpf_rr)
                V(lambda e, jt=jt: e.tensor_tensor(jt[:], mean[:], mean[:], ALU.mult), reads=[mean], writes=[jt])
                jt2 = nxt(tmpf, tmpf_rr)
                V(lambda e, jt=jt, jt2=jt2: e.scalar_tensor_tensor(jt2[:], bap(7), 1.0 / D, jt[:], ALU.mult, ALU.subtract),
                  reads=[bk(7), jt], writes=[jt2])
                V(lambda e, jt2=jt2: e.tensor_scalar(jt2[:], jt2[:], 0.0, None, ALU.max), reads=[jt2], writes=[jt2])
                A(lambda e, jt=jt, jt2=jt2: e.activation(jt[:], jt2[:], AF.Sqrt, bias=EPS), reads=[jt2], writes=[jt])
                rln = nxt(rstd_t, rstd_rr)
                V(lambda e, jt=jt, rln=rln: e.reciprocal(rln[:], jt[:]), reads=[jt], writes=[rln])
                V(lambda e, rln=rln: e.scalar_tensor_tensor(nmr[:], mean[:], -1.0, rln[:], ALU.mult, ALU.mult),
                  reads=[mean, rln], writes=[nmr])
                for c in range(8):
                    jt = nxt(tmpf, tmpf_rr)
                    V(lambda e, c=c, jt=jt, rln=rln: e.tensor_tensor(jt[:], ucv[:, c, :], rln[:], ALU.mult),
                      reads=[(arena, ("u", c)), rln], writes=[jt])
                    V(lambda e, jt=jt: e.tensor_tensor(jt[:], jt[:], nmr[:], ALU.add), reads=[jt, nmr], writes=[jt])
                    A(lambda e, c=c, jt=jt: e.activation(actT[:, c, :], jt[:], AF.Silu,
                                                         bias=vecs[:, V_CNB + c:V_CNB + c + 1], scale=vecs[:, V_CG + c:V_CG + c + 1]),
                      reads=[jt, vecs, vecs], writes=[(actT, c)])
                ak = [(actT, k) for k in range(8)]
                for pc in range(4):
                    wb = load_wq(w_conv_out, D, pc * 256, 256)
                    for cc in range(2):
                        m = pc * 2 + cc
                        bb = 4 + (m % 2)
                        for k in range(8):
                            mm(bap(bb), wb[:, k, cc * 128:(cc + 1) * 128], actT[:, k, :], k == 0, k == 7, ak + [wb], bk(bb), k == 7)
                        evac_copy(yaT[:, m, :], bap(bb), [bk(bb)], [(yaT, m)])
                hown = lambda k: hTe[:, k, :].rearrange("p (b w) -> p b w", w=160)[:, :, 32:160]
                for pc in range(4):
                    wao = load_wq(w_attn_out, D, pc * 256, 256)
                    for cc in range(2):
                        for hh in range(8):
                            mm(bap(2 + cc), wao[:, hh, cc * 128:(cc + 1) * 128], oT[:, hh, g * 512:(g + 1) * 512],
                               hh == 0, hh == 7, [oT, wao], bk(2 + cc), hh == 7)
                    wga = load_wq(w_in, D, 2752 + pc * 256, 256)
                    sas = []
                    for cc in range(2):
                        m = pc * 2 + cc
                        for k in range(8):
                            mm(bap(cc).rearrange("p (b w) -> p b w", w=128), wga[:, k, cc * 128:(cc + 1) * 128], hown(k),
                               k == 0, k == 7, hk + [wga], bk(cc), k == 7)
                        sa = nxt(tmpf, tmpf_rr)
                        A(lambda e, sa=sa, cc=cc: e.activation(sa[:], bap(cc), AF.Sigmoid), reads=[bk(cc)], writes=[sa])
                        V(lambda e, sa=sa, m=m: e.tensor_tensor(sa[:], sa[:], yaT[:, m, :], ALU.mult),
                          reads=[sa, (yaT, m)], writes=[sa])
                        sas.append(sa)
                    wgb = load_wq(w_in, D, 3776 + pc * 256, 256)
                    for cc in range(2):
                        m = pc * 2 + cc
                        for k in range(8):
                            mm(bap(cc).rearrange("p (b w) -> p b w", w=128), wgb[:, k, cc * 128:(cc + 1) * 128], hown(k),
                               k == 0, k == 7, hk + [wgb], bk(cc), k == 7)
                        sb2 = nxt(tmpf, tmpf_rr)
                        A(lambda e, sb2=sb2, cc=cc: e.activation(sb2[:], bap(cc), AF.Sigmoid), reads=[bk(cc)], writes=[sb2])
                        V(lambda e, sb2=sb2, cc=cc: e.tensor_tensor(sb2[:], bap(2 + cc), sb2[:], ALU.mult),
                          reads=[bk(2 + cc), sb2], writes=[sb2])
                        V(lambda e, sa=sas[cc], sb2=sb2, m=m: e.tensor_tensor(mT[:, m, :], sa[:], sb2[:], ALU.add),
                          reads=[sas[cc], sb2], writes=[(mT, m)])
                mk_ = [(mT, k) for k in range(8)]
                for pc in range(4):
                    wb = load_wq(w_out, D, pc * 256, 256)
                    for cc in range(2):
                        m = pc * 2 + cc
                        bb = 4 + (m % 2)
                        for k in range(8):
                            mm(bap(bb), wb[:, k, cc * 128:(cc + 1) * 128], mT[:, k, :], k == 0, k == 7, mk_ + [wb], bk(bb), k == 7)
                        A(lambda e, m=m, bb=bb: e.activation(yT[:, m, :], bap(bb), AF.Copy), reads=[bk(bb)], writes=[(yT, m)])
                        stats_accum(6, yT[:, m, :], [(yT, m)], m, 8)
                if debug == "m" and g == 3:
                    V(lambda e: e.tensor_copy(hTe[:, :, 0:512], mT[:]), reads=[mT], writes=[hTe])
                    V(lambda e: e.tensor_copy(uext[:, :, 0:512], yT[:]), reads=[yT], writes=[uext])
                r1 = rstd_from_ps(6, D)
                for k in range(8):
                    jt = nxt(tmpf, tmpf_rr)
                    V(lambda e, k=k, jt=jt, r1=r1: e.tensor_tensor(jt[:], yT[:, k, :], r1[:], ALU.mult),
                      reads=[(yT, k), r1], writes=[jt])
                    V(lambda e, k=k, jt=jt: e.scalar_tensor_tensor(xT[:, k, :], jt[:], der[:, 16 + k:17 + k], xT[:, k, :], ALU.mult, ALU.add),
                      reads=[jt, (der, 16), xT], writes=[xT])
                    stats_accum(7, xT[:, k, :], [xT], k, 8)
                r2 = rstd_from_ps(7, D)
                for k in range(8):
                    jt = nxt(tmpf, tmpf_rr)
                    V(lambda e, k=k, jt=jt, r2=r2: e.scalar_tensor_tensor(jt[:], xT[:, k, :], der[:, 24 + k:25 + k], r2[:], ALU.mult, ALU.mult),
                      reads=[xT, (der, 24), r2], writes=[jt])
                    A(lambda e, k=k, jt=jt: e.activation(h2T[:, k, :], jt[:], AF.Identity, bias=der[:, 32 + k:33 + k]),
                      reads=[jt, (der, 32)], writes=[(h2T, k)])
                h2k = [(h2T, k) for k in range(8)]
                nlst = fh_blocks(g + 1) if g + 1 < 4 else None
                nhd = {}
                for pc in range(16):
                    if nlst is not None:
                        if pc % 2 == 0:
                            nhd[pc // 2] = xb_prep(nlst[pc // 2][0], nlst[pc // 2][1])
                        else:
                            xb_trans(nhd[pc // 2], hTe, nlst[pc // 2][2], 0, 8)
                    wb = load_wq(w_mlp_in, D, pc * 256, 256)
                    for cc in range(2):
                        j = pc * 2 + cc
                        bb = 4 + (j % 4)
                        for k in range(8):
                            mm(bap(bb), wb[:, k, cc * 128:(cc + 1) * 128], h2T[:, k, :], k == 0, k == 7, h2k + [wb], bk(bb), k == 7)
                        jt = nxt(tmpf, tmpf_rr)
                        A(lambda e, jt=jt, bb=bb: e.activation(jt[:], bap(bb), AF.Relu), reads=[bk(bb)], writes=[jt])
                        V(lambda e, jt=jt, j=j: e.tensor_tensor(hid[:, j, :], jt[:], jt[:], ALU.mult), reads=[jt],
                          writes=[(arena, None) if j == 0 else (arena, ("h", j))])
                for pc in range(4):
                    for cc in range(2):
                        pass
                    wbs = []
                    for rr_ in range(4):
                        wb = load_wq(w_mlp_out, D, pc * 256, 256, r0=rr_ * 1024)
                        for cc in range(2):
                            bb = 4 + cc
                            for k in range(8):
                                kk = rr_ * 8 + k
                                mm(bap(bb), wb[:, k, cc * 128:(cc + 1) * 128], hid[:, kk, :], kk == 0, kk == 31,
                                   [arena, wb], bk(bb), (k == 7))
                    for cc in range(2):
                        m = pc * 2 + cc
                        bb = 4 + cc
                        A(lambda e, m=m, bb=bb: e.activation(yT[:, m, :], bap(bb), AF.Copy), reads=[bk(bb)], writes=[(yT, m)])
                        stats_accum(6, yT[:, m, :], [(yT, m)], m, 8)
                r3 = rstd_from_ps(6, D)
                for k in range(8):
                    jt = nxt(tmpf, tmpf_rr)
                    V(lambda e, k=k, jt=jt, r3=r3: e.tensor_tensor(jt[:], yT[:, k, :], r3[:], ALU.mult),
                      reads=[(yT, k), r3], writes=[jt])
                    V(lambda e, k=k, jt=jt: e.scalar_tensor_tensor(yT[:, k, :], jt[:], der[:, 40 + k:41 + k], xT[:, k, :], ALU.mult, ALU.add),
                      reads=[jt, (der, 40), xT], writes=[(yT, k)])
                for b in range(4):
                    ob_ = oblk[b % 2]
                    for half in range(2):
                        tb_i = 2 + half
                        for kk in range(4):
                            k = half * 4 + kk
                            P(lambda e, k=k, kk=kk, tb_i=tb_i, b=b: e.transpose(bap(tb_i, kk * 128, (kk + 1) * 128),
                                                                                 yT[:, k, b * 128:(b + 1) * 128], ident_f[:]),
                              reads=[yT, ident_f], writes=[bk(tb_i)], inc=(kk == 3))
                        evac_copy(ob_[:, half * 512:(half + 1) * 512], bap(tb_i), [bk(tb_i)], [(ob_, half)])
                    blk = g * 4 + b
                    tok = S.dma("act", lambda e, ob_=ob_, blk=blk: e.dma_start(out=out[blk * 128:(blk + 1) * 128, :], in_=ob_[:]),
                                reads=[ob_])
                    out_toks.append(tok)


        run_phases()
        if stop is not None:
            out_toks.append(S.dma("act", lambda e: e.dma_start(out=out[0:128, :], in_=xblk[0][:]), reads=[xblk[0]]))
        last = {}
        for (s, v) in out_toks:
            last[s] = max(last.get(s, 0), v)
        S.wait_all("act", list(last.items()))

        with nc.Block() as block:
            def emit(engname, eng):
                for (waits, fn, inc) in S.q[engname]:
                    for (s, v) in waits:
                        eng.wait_ge(sems[s], v)
                    if fn is None:
                        continue
                    ins = fn(eng)
                    if inc is not None:
                        ins.then_inc(sems[inc[0]], inc[1])

            @block.sync
            def _(e):
                emit("sp", e)

            @block.tensor
            def _(e):
                emit("pe", e)

            @block.scalar
            def _(e):
                emit("act", e)

            @block.vector
            def _(e):
                emit("dve", e)

            @block.gpsimd
            def _(e):
                emit("pool", e)
    return nc


def _prep_inputs(inputs):
    x = np.asarray(inputs["x"], np.float32)
    pos = np.asarray(inputs["positions"], np.int32)
    c = np.asarray(inputs["c"], np.float32)
    w_in = np.ascontiguousarray(np.asarray(inputs["w_in"], np.float32)[0])
    w_uq = np.ascontiguousarray(np.asarray(inputs["w_uq"], np.float32)[0])
    kr = w_in[:, 2688:2752]
    w_kr_sw = np.ascontiguousarray(np.concatenate([kr[:, 32:64], kr[:, 0:32]], axis=1))
    uq3 = w_uq.reshape(384, 8, 192)[:, :, 128:192]
    w_uq_sw = np.ascontiguousarray(np.concatenate([uq3[:, :, 32:64], uq3[:, :, 0:32]], axis=2).reshape(384, 512))
    k_idx = np.arange(128)[:, None]
    q_idx = np.arange(128)[None, :]
    trimask = np.where(k_idx <= q_idx, 0.0, NEG).astype(np.float32)
    ident = np.eye(128, dtype=np.float32)
    inv = (1.0 / (np.float32(10000.0) ** (np.arange(0, 64, 2, dtype=np.float32) / np.float32(64)))).astype(np.float32)
    invf = np.concatenate([inv, inv]).reshape(64, 1).astype(np.float32)
    sgn = np.concatenate([-np.ones(32), np.ones(32)]).reshape(64, 1).astype(np.float32)
    shared = {
        "trimask": trimask, "ident": ident, "invf": invf, "sgn": sgn,
        "w_ada": np.ascontiguousarray(inputs["w_ada"][0], np.float32),
        "b_ada": np.ascontiguousarray(inputs["b_ada"][0], np.float32),
        "g_pre_mix": np.ascontiguousarray(inputs["g_pre_mix"][0], np.float32),
        "g_post_mix": np.ascontiguousarray(inputs["g_post_mix"][0], np.float32),
        "g_pre_mlp": np.ascontiguousarray(inputs["g_pre_mlp"][0], np.float32),
        "g_post_mlp": np.ascontiguousarray(inputs["g_post_mlp"][0], np.float32),
        "w_in": w_in, "w_kr_sw": w_kr_sw,
        "conv_w": np.ascontiguousarray(inputs["conv_w"][0], np.float32),
        "conv_b": np.ascontiguousarray(inputs["conv_b"][0], np.float32),
        "conv_norm_g": np.ascontiguousarray(inputs["conv_norm_g"][0], np.float32),
        "conv_norm_b": np.ascontiguousarray(inputs["conv_norm_b"][0], np.float32),
        "w_conv_out": np.ascontiguousarray(inputs["w_conv_out"][0], np.float32),
        "q_norm_g": np.ascontiguousarray(inputs["q_norm_g"][0], np.float32),
        "w_uq": w_uq, "w_uq_sw": w_uq_sw,
        "kv_norm_g": np.ascontiguousarray(inputs["kv_norm_g"][0], np.float32),
        "w_ukv": np.ascontiguousarray(inputs["w_ukv"][0], np.float32),
        "w_attn_out": np.ascontiguousarray(inputs["w_attn_out"][0], np.float32),
        "w_out": np.ascontiguousarray(inputs["w_out"][0], np.float32),
        "w_mlp_in": np.ascontiguousarray(inputs["w_mlp_in"][0], np.float32),
        "w_mlp_out": np.ascontiguousarray(inputs["w_mlp_out"][0], np.float32),
    }
    in_maps = []
    for core in range(8):
        b, p = core // 2, core % 2
        xb = x[b].reshape(32, 128, D)
        pb = pos[b].reshape(32, 128)
        own = [2 * i + p for i in range(16)]
        oth = [2 * i + 1 - p for i in range(16)]
        halo = np.zeros((16, 32, D), np.float32)
        for i in range(16):
            st = own[i] * 128
            if st > 0:
                halo[i] = x[b, st - 32:st]
        m = dict(shared)
        m["x_own"] = np.ascontiguousarray(xb[own].reshape(NOWN, D))
        m["x_oth"] = np.ascontiguousarray(xb[oth].reshape(NOWN, D))
        m["x_halo"] = np.ascontiguousarray(halo.reshape(512, D))
        m["pos_own"] = np.ascontiguousarray(pb[own].reshape(NOWN))
        m["pos_oth"] = np.ascontiguousarray(pb[oth].reshape(NOWN))
        m["c"] = np.ascontiguousarray(c[b])
        m["pairmask"] = np.full((128, 128), 0.0 if p == 1 else NEG, np.float32)
        m["halomask"] = np.full((128, 1), 1.0 if p == 1 else 0.0, np.float32)
        in_maps.append(m)
    return in_maps


def kernel(**inputs):
    in_maps = _prep_inputs(inputs)
    nc = build_nc()
    res = run_bass_kernel_spmd(nc, in_maps, core_ids=list(range(8)))
    outf = np.zeros((4, 32, 128, D), np.float32)
    for core in range(8):
        b, p = core // 2, core % 2
        o = np.asarray(res.results[core]["out"]).reshape(16, 128, D)
        for i in range(16):
            outf[b, 2 * i + p] = o[i]
    return outf.reshape(4, 4096, D)
```

```python
import contextlib
import math
import numpy as np
import concourse.bass as bass
import concourse.mybir as mybir
from concourse.bass_utils import run_bass_kernel_spmd

F32 = mybir.dt.float32
BF = mybir.dt.bfloat16
I32 = mybir.dt.int32
AF = mybir.ActivationFunctionType
ALU = mybir.AluOpType

D = 1024
KC = 8
NOWN = 2048
EPS = 1e-6
NEG = -30000.0
SCALE = 1.0 / math.sqrt(192.0)
TWO_PI = 2.0 * math.pi
C1 = 6.28125
C2 = TWO_PI - 6.28125

ENGS = ("pe", "act", "dve", "pool", "sp")
NDMA = 12


class _Rec:
    def __init__(self):
        self.call = None

    def __getattr__(self, name):
        def f(*a, **k):
            self.call = (name, a, k)
            return self
        return f


def _bind(fn):
    if fn is None:
        return None
    r = _Rec()
    fn(r)
    name, a, k = r.call
    return lambda eng: getattr(eng, name)(*a, **k)


class Sched:
    def __init__(self):
        self.q = {e: [] for e in ENGS}
        self.cnt = {e: 0 for e in ENGS}
        self.waited = {e: {} for e in ENGS}
        self.state = {}
        self.dma_tot = [0] * NDMA
        self.dma_rr = 0
        self.all_dma_tokens = {}

    def _entries(self, buf, key, create):
        d = self.state.setdefault(id(buf), {})
        if key is None:
            if create and None not in d:
                d[None] = {"w": None, "r": {}}
            return list(d.values()) if not create else list(d.values())
        out = []
        if key not in d and create:
            d[key] = {"w": None, "r": {}}
        if key in d:
            out.append(d[key])
        if None in d:
            out.append(d[None])
        return out

    def _deps(self, reads, writes):
        deps = {}

        def add(tok):
            if tok is None:
                return
            s, v = tok
            if deps.get(s, 0) < v:
                deps[s] = v

        for (b, k) in reads:
            for st in self._entries(b, k, False):
                add(st["w"])
        for (b, k) in writes:
            for st in self._entries(b, k, False):
                add(st["w"])
                for s, v in st["r"].items():
                    add((s, v))
        return deps

    def _commit(self, reads, writes, tok):
        for (b, k) in reads:
            d = self.state.setdefault(id(b), {})
            if k not in d:
                d[k] = {"w": None, "r": {}}
            st = d[k]
            s, v = tok
            if st["r"].get(s, 0) < v:
                st["r"][s] = v
        for (b, k) in writes:
            d = self.state.setdefault(id(b), {})
            if k is None:
                d.clear()
            d[k] = {"w": tok, "r": {}}

    def _norm(self, lst):
        out = []
        for x in lst:
            if isinstance(x, tuple):
                out.append(x)
            else:
                out.append((x, None))
        return out

    def op(self, eng, fn, reads=(), writes=(), inc=True):
        fn = _bind(fn)
        reads = self._norm(reads)
        writes = self._norm(writes)
        deps = self._deps(reads, writes)
        waits = []
        for s, v in deps.items():
            if s == eng and eng == "pe":
                continue
            if self.waited[eng].get(s, 0) >= v:
                continue
            self.waited[eng][s] = v
            waits.append((s, v))
        if inc:
            self.cnt[eng] += 1
            tok = (eng, self.cnt[eng])
            self.q[eng].append((waits, fn, (eng, 1)))
        else:
            tok = (eng, self.cnt[eng] + 1)
            self.q[eng].append((waits, fn, None))
        self._commit(reads, writes, tok)
        return tok

    def dma(self, eng, fn, reads=(), writes=()):
        fn = _bind(fn)
        reads = self._norm(reads)
        writes = self._norm(writes)
        j = self.dma_rr
        self.dma_rr = (self.dma_rr + 1) % NDMA
        sem = "d%d" % j
        deps = self._deps(reads, writes)
        if self.dma_tot[j] > 0:
            if deps.get(sem, 0) < self.dma_tot[j]:
                deps[sem] = self.dma_tot[j]
        waits = []
        for s, v in deps.items():
            if self.waited[eng].get(s, 0) >= v:
                continue
            self.waited[eng][s] = v
            waits.append((s, v))
        self.dma_tot[j] += 16
        tok = (sem, self.dma_tot[j])
        self.q[eng].append((waits, fn, (sem, 16)))
        self._commit(reads, writes, tok)
        return tok

    def barrier(self):
        toks = [(e, self.cnt[e]) for e in ENGS if self.cnt[e] > 0]
        toks += [("d%d" % j, self.dma_tot[j]) for j in range(NDMA) if self.dma_tot[j] > 0]
        for e in ENGS:
            self.wait_all(e, [t for t in toks if t[0] != e])
        self.state = {}

    def wait_all(self, eng, toks):
        waits = []
        for (s, v) in toks:
            if self.waited[eng].get(s, 0) >= v:
                continue
            self.waited[eng][s] = v
            waits.append((s, v))
        self.q[eng].append((waits, None, None))


def build_nc(debug=None, stop=None):
    nc = bass.Bass("TRN2", target_bir_lowering=False)
    S = Sched()

    def din(name, shape, dt=F32):
        return nc.dram_tensor(name, list(shape), dt, kind="ExternalInput").ap()

    x_own = din("x_own", [NOWN, D])
    x_oth = din("x_oth", [NOWN, D])
    x_halo = din("x_halo", [512, D])
    pos_own = nc.dram_tensor("pos_own", [NOWN], I32, kind="ExternalInput")
    pos_oth = nc.dram_tensor("pos_oth", [NOWN], I32, kind="ExternalInput")
    c_in = din("c", [D])
    pairmask_in = din("pairmask", [128, 128])
    trimask_in = din("trimask", [128, 128])
    ident_in = din("ident", [128, 128])
    halomask_in = din("halomask", [128, 1])
    invf_in = din("invf", [64, 1])
    sgn_in = din("sgn", [64, 1])
    w_ada = din("w_ada", [D, 6 * D])
    b_ada = din("b_ada", [6 * D])
    g_pre_mix = din("g_pre_mix", [D])
    g_post_mix = din("g_post_mix", [D])
    g_pre_mlp = din("g_pre_mlp", [D])
    g_post_mlp = din("g_post_mlp", [D])
    w_in = din("w_in", [D, 4800])
    w_kr_sw = din("w_kr_sw", [D, 64])
    conv_w = din("conv_w", [31, D])
    conv_b = din("conv_b", [D])
    conv_norm_g = din("conv_norm_g", [D])
    conv_norm_b = din("conv_norm_b", [D])
    w_conv_out = din("w_conv_out", [D, D])
    q_norm_g = din("q_norm_g", [384])
    w_uq = din("w_uq", [384, 1536])
    w_uq_sw = din("w_uq_sw", [384, 512])
    kv_norm_g = din("kv_norm_g", [256])
    w_ukv = din("w_ukv", [256, 2048])
    w_attn_out = din("w_attn_out", [D, D])
    w_out = din("w_out", [D, D])
    w_mlp_in = din("w_mlp_in", [D, 4 * D])
    w_mlp_out = din("w_mlp_out", [4 * D, D])
    out = nc.dram_tensor("out", [NOWN, D], F32, kind="ExternalOutput").ap()
    wq = nc.dram_tensor("wq", [60, 128, 2048], BF, kind="Internal").ap()
    wq_key = object()
    dbg = None

    es = contextlib.ExitStack()
    with es:
        def sb(name, shape, dt=F32):
            return es.enter_context(nc.sbuf_tensor("s_" + name, list(shape), dt))

        sems = {}
        for e in ENGS:
            sems[e] = es.enter_context(nc.semaphore("sem_" + e))
        for j in range(NDMA):
            sems["d%d" % j] = es.enter_context(nc.semaphore("sem_d%d" % j))

        PD = [es.enter_context(nc.psum_tensor("pd%d" % i, [128, 1024], F32)) for i in range(4)]

        def bank(i):
            t = PD[i // 2]
            h = i % 2
            return t, h

        def bk(i):
            t, h = bank(i)
            return (t, h)

        def bap(i, c0=0, c1=512):
            t, h = bank(i)
            return t[:, h * 512 + c0: h * 512 + c1]

        def bap_bf(i):
            t, h = bank(i)
            return t.bitcast(BF)[:, h * 1024:(h + 1) * 1024]

        ident_f = sb("ident_f", [128, 128])
        ident_b = sb("ident_b", [128, 128], BF)
        ones_b = sb("ones_b", [128, 128], BF)
        tri_b = sb("tri_b", [128, 128], BF)
        pair_b = sb("pair_b", [128, 128], BF)
        halom = sb("halom", [128, 1])
        invf = sb("invf", [64, 1])
        sgn = sb("sgn", [64, 1])
        modT = sb("modT", [128, 48])
        vecs = sb("vecs", [128, 128])
        cwT = sb("cwT", [128, 8, 31])
        V_GPRE, V_GPOST, V_GPRE2, V_GPOST2 = 0, 8, 16, 24
        V_CB, V_CG, V_CNB = 32, 40, 48
        V_QG, V_KVG = 56, 59
        der = sb("der", [128, 48])
        NST = 2
        wst = [sb("wst%d" % i, [128, 8, 256]) for i in range(NST)]
        wbf = [sb("wbf%d" % i, [128, 8, 256], BF) for i in range(NST)]
        wrr = [0]
        xblk = [sb("xblk%d" % i, [128, D]) for i in range(2)]
        xnb = [sb("xnb%d" % i, [128, D], BF) for i in range(2)]
        small = [sb("small%d" % i, [128, 4]) for i in range(4)]
        small_rr = [0]
        tmpf = [sb("tmpf%d" % i, [128, 512]) for i in range(4)]
        tmpf_rr = [0]
        tmpb = [sb("tmpb%d" % i, [128, 512], BF) for i in range(3)]
        tmpb_rr = [0]
        rstd_t = [sb("rstd%d" % i, [128, 512]) for i in range(2)]
        rstd_rr = [0]

        def nxt(lst, rr):
            t = lst[rr[0] % len(lst)]
            rr[0] += 1
            return t

        def A(fn, **kw):
            return S.op("act", fn, **kw)

        def V(fn, **kw):
            return S.op("dve", fn, **kw)

        def G(fn, **kw):
            return S.op("pool", fn, **kw)

        def P(fn, **kw):
            return S.op("pe", fn, **kw)

        def dma_in(dst_ap, src_ap, dst_buf, key=None, eng="sp", nonc=False):
            def f(e, dst_ap=dst_ap, src_ap=src_ap):
                if nonc:
                    return e.dma_start(out=dst_ap, in_=src_ap, allow_slow_non_contiguous=True)
                return e.dma_start(out=dst_ap, in_=src_ap)
            return S.dma(eng, f, writes=[(dst_buf, key)])

        def mm(out_ap, lhsT, rhs, start, stop, reads, wkey, last):
            def f(e):
                return e.matmul(out_ap, lhsT, rhs, start=start, stop=stop)
            return S.op("pe", f, reads=reads, writes=[wkey], inc=last)

        def load_w(src, rows, c0, ncols, r0=0, cast=None):
            i = wrr[0] % NST
            wrr[0] += 1
            kc = rows // 128
            st, wb = wst[i], wbf[i]
            src_ap = src[r0:r0 + rows, c0:c0 + ncols].rearrange("(k p) c -> p k c", p=128)
            dma_in(st[:, 0:kc, 0:ncols], src_ap, st)
            if cast == "pool" or (cast is None and wrr[0] % 2 == 0):
                G(lambda e: e.tensor_copy(wb[:, 0:kc, 0:ncols], st[:, 0:kc, 0:ncols]), reads=[st], writes=[wb])
            else:
                V(lambda e: e.tensor_copy(wb[:, 0:kc, 0:ncols], st[:, 0:kc, 0:ncols]), reads=[st], writes=[wb])
            return wb

        def _unused():
            pass

        out_toks = []

        def run_phases():
            dma_in(ident_f[:], ident_in, ident_f)
            dma_in(halom[:], halomask_in, halom)
            dma_in(invf[:], invf_in, invf)
            dma_in(sgn[:], sgn_in, sgn)
            t0 = tmpf[0]
            t1 = tmpf[1]
            dma_in(t0[:, 0:128], trimask_in, t0)
            dma_in(t1[:, 0:128], pairmask_in, t1)
            V(lambda e: e.tensor_copy(ident_b[:], ident_f[:]), reads=[ident_f], writes=[ident_b])
            V(lambda e: e.memset(ones_b[:], 1.0), writes=[ones_b])
            V(lambda e: e.tensor_copy(tri_b[:], t0[:, 0:128]), reads=[t0], writes=[tri_b])
            V(lambda e: e.tensor_copy(pair_b[:], t1[:, 0:128]), reads=[t1], writes=[pair_b])
            tmpf_rr[0] = 2
            if stop == -1:
                return
            stg = tmpf[2]
            V(lambda e: e.memset(stg[:, 0:128], 0.0), writes=[stg])
            for col, src, n in ((V_GPRE, g_pre_mix, D), (V_GPOST, g_post_mix, D), (V_GPRE2, g_pre_mlp, D),
                                (V_GPOST2, g_post_mlp, D), (V_CB, conv_b, D), (V_CG, conv_norm_g, D),
                                (V_CNB, conv_norm_b, D), (V_QG, q_norm_g, 384), (V_KVG, kv_norm_g, 256)):
                dma_in(stg[col:col + n // 128, 0:128], src.rearrange("(k p) -> k p", p=128), stg)
            dma_in(stg[64:112, 0:128], b_ada.rearrange("(k p) -> k p", p=128), stg)
            dma_in(stg[112:120, 0:128], c_in.rearrange("(k p) -> k p", p=128), stg)
            P(lambda e: e.transpose(bap(1, 0, 120), stg[0:120, 0:128], ident_f[0:120, 0:120]),
              reads=[stg, ident_f], writes=[bk(1)])
            V(lambda e: e.tensor_copy(vecs[:, 0:120], bap(1, 0, 120)), reads=[bk(1)], writes=[vecs])
            if stop == -2:
                return
            badaT = vecs[:, 64:112]
            cT = vecs[:, 112:120]
            cwn = xblk[0]
            dma_in(cwn[0:31, :], conv_w, cwn)
            for c in range(8):
                P(lambda e, c=c: e.transpose(bap(2, c * 32, c * 32 + 31), cwn[0:31, c * 128:(c + 1) * 128], ident_f[0:31, 0:31]),
                  reads=[cwn, ident_f], writes=[bk(2)])
            V(lambda e: e.tensor_copy(cwT[:], bap(2, 0, 256).rearrange("p (c k) -> p c k", k=32)[:, :, 0:31]),
              reads=[bk(2)], writes=[cwT])
            if stop == -3:
                return
            scb = sb("scb", [128, 8], BF)
            A(lambda e: e.activation(scb[:], cT, AF.Silu), reads=[vecs], writes=[scb])
            if stop == -4:
                return
            def mod_part(p0, p1, MODB, cast=None):
                for pc in range(p0, p1):
                    wb = load_w(w_ada, D, pc * 256, 256, cast=cast)
                    for jj in range(2):
                        j = pc * 2 + jj
                        for k in range(8):
                            mm(bap(MODB, j, j + 1), wb[:, k, jj * 128:(jj + 1) * 128], scb[:, k:k + 1],
                               k == 0, k == 7, [wb, scb], bk(MODB), k == 7)
                V(lambda e: e.tensor_tensor(modT[:, p0 * 2:p1 * 2], bap(MODB, p0 * 2, p1 * 2), vecs[:, 64 + p0 * 2:64 + p1 * 2], ALU.add),
                  reads=[bk(MODB), vecs], writes=[(modT, p0)])

            mod_part(0, 8, 0)
            if stop == -5:
                return
            V(lambda e: e.scalar_tensor_tensor(der[:, 0:8], modT[:, 8:16], 1.0, vecs[:, V_GPRE:V_GPRE + 8], ALU.add, ALU.mult),
              reads=[modT, vecs], writes=[(der, 0)])
            V(lambda e: e.tensor_copy(der[:, 8:16], modT[:, 0:8]), reads=[modT], writes=[(der, 8)])

            def mod_late():
                mod_part(8, 24, 7, cast="pool")
                V(lambda e: e.tensor_tensor(der[:, 16:24], modT[:, 16:24], vecs[:, V_GPOST:V_GPOST + 8], ALU.mult),
                  reads=[modT, vecs], writes=[(der, 16)])
                V(lambda e: e.scalar_tensor_tensor(der[:, 24:32], modT[:, 32:40], 1.0, vecs[:, V_GPRE2:V_GPRE2 + 8], ALU.add, ALU.mult),
                  reads=[modT, vecs], writes=[(der, 24)])
                V(lambda e: e.tensor_copy(der[:, 32:40], modT[:, 24:32]), reads=[modT], writes=[(der, 32)])
                V(lambda e: e.tensor_tensor(der[:, 40:48], modT[:, 40:48], vecs[:, V_GPOST2:V_GPOST2 + 8], ALU.mult),
                  reads=[modT, vecs], writes=[(der, 40)])
            DER_ALL = [(der, 0), (der, 8), (der, 16), (der, 24), (der, 32), (der, 40)]

            TPB = [0, 1]
            tp_rr = [0]
            xb_rr = [0]

            def xb_prep(src_rows_ap, nrows):
                i = xb_rr[0] % 2
                xb_rr[0] += 1
                xb = xblk[i]
                xn = xnb[i]
                dma_in(xb[0:nrows, :], src_rows_ap, xb)
                sm = nxt(small, small_rr)
                A(lambda e: e.activation(xn[0:nrows, :], xb[0:nrows, :], AF.Square, accum_out=sm[0:nrows, 0:1]),
                  reads=[xb], writes=[xn, (sm, 0)])
                A(lambda e: e.activation(sm[0:nrows, 3:4], sm[0:nrows, 0:1], AF.Sqrt, bias=EPS, scale=1.0 / D),
                  reads=[(sm, 0)], writes=[(sm, 3)])
                V(lambda e: e.reciprocal(sm[0:nrows, 2:3], sm[0:nrows, 3:4]), reads=[(sm, 3)], writes=[(sm, 2)])
                V(lambda e: e.tensor_scalar(xn[0:nrows, :], xb[0:nrows, :], sm[0:nrows, 2:3], None, ALU.mult),
                  reads=[xb, (sm, 2)], writes=[xn])
                return (xn, nrows)

            def xb_trans(hd, hT, col0, gsc, shc):
                xn, nrows = hd
                b = TPB[tp_rr[0] % 2]
                tp_rr[0] += 1
                tpv = bap_bf(b)
                for k in range(8):
                    P(lambda e, k=k: e.transpose(tpv[:, k * 128:k * 128 + nrows], xn[0:nrows, k * 128:(k + 1) * 128],
                                                  ident_b[0:nrows, 0:nrows]),
                      reads=[xn, ident_b], writes=[bk(b)], inc=(k == 7))
                for k in range(8):
                    if b == TPB[0]:
                        V(lambda e, k=k: e.tensor_scalar(hT[:, k, col0:col0 + nrows], tpv[:, k * 128:k * 128 + nrows],
                                                         der[:, gsc + k:gsc + k + 1], der[:, shc + k:shc + k + 1],
                                                         ALU.mult, ALU.add),
                          reads=[bk(b), (der, gsc), (der, shc)], writes=[(hT, k)])
                    else:
                        A(lambda e, k=k: e.activation(hT[:, k, col0:col0 + nrows], tpv[:, k * 128:k * 128 + nrows],
                                                      AF.Identity, bias=der[:, shc + k:shc + k + 1],
                                                      scale=der[:, gsc + k:gsc + k + 1]),
                          reads=[bk(b), (der, gsc), (der, shc)], writes=[(hT, k)])

            def x_block_to_hT(src_rows_ap, nrows, hT, col0, gsc, shc):
                xb_trans(xb_prep(src_rows_ap, nrows), hT, col0, gsc, shc)

            def rstd_from_ps(ps_bank, nfeat, ncols=512):
                r = nxt(rstd_t, rstd_rr)
                jt = nxt(tmpf, tmpf_rr)
                A(lambda e: e.activation(jt[:, 0:ncols], bap(ps_bank, 0, ncols), AF.Sqrt, bias=EPS, scale=1.0 / nfeat),
                  reads=[bk(ps_bank)], writes=[jt])
                V(lambda e: e.reciprocal(r[:, 0:ncols], jt[:, 0:ncols]), reads=[jt], writes=[r])
                return r

            if stop == 0:
                return
            oT = sb("oT", [128, 8, NOWN], BF)
            ph12 = es.enter_context(contextlib.ExitStack())
            ph1 = es.enter_context(contextlib.ExitStack())

            def sb12(name, shape, dt=F32):
                return ph12.enter_context(nc.sbuf_tensor("s_" + name, list(shape), dt))

            def sb1(name, shape, dt=F32):
                return ph1.enter_context(nc.sbuf_tensor("s_" + name, list(shape), dt))

            kvn = [sb12("kvn_own", [128, 2, NOWN], BF), sb12("kvn_oth", [128, 2, NOWN], BF)]
            krT = [sb12("kr_own", [64, NOWN], BF), sb12("kr_oth", [64, NOWN], BF)]
            qn = sb12("qn", [128, 3, NOWN], BF)
            CS = sb12("cs_own", [64, 2, NOWN])
            hTs = [sb1("hT%d" % i, [128, 8, 640], BF) for i in range(2)]
            wlat = sb1("wlat", [128, 8, 768], BF)
            cs_tmp = sb1("cs_tmp", [64, 2, 512])
            posi = sb1("posi", [64, 512], I32)
            angs = [sb1("ang%d" % i, [64, 512]) for i in range(4)]
            ni_t = sb1("ni_t", [64, 512], I32)

            for pc, (src, c0, n, d0) in enumerate(((w_in, 2048, 256, 0), (w_in, 2304, 256, 256), (w_in, 2560, 192, 512),
                                                   (w_kr_sw, 0, 64, 704))):
                wb = load_w(src, D, c0, n)
                G(lambda e, wb=wb, n=n, d0=d0: e.tensor_copy(wlat[:, :, d0:d0 + n], wb[:, :, 0:n]),
                  reads=[wb], writes=[(wlat, pc)])
            WL = [(wlat, i) for i in range(4)]
            if stop == 10:
                return

            def rope_tables(pos_t, c0, dst, dcol):
                src = bass.AP(pos_t, c0, [[0, 64], [1, 512]])
                dma_in(posi[:], src, posi)
                a0, a1, a2, a3 = angs
                V(lambda e: e.tensor_copy(a0[:], posi[:]), reads=[posi], writes=[a0])
                V(lambda e: e.tensor_scalar(a0[:], a0[:], invf[:, 0:1], None, ALU.mult), reads=[a0, invf], writes=[a0])
                V(lambda e: e.tensor_scalar(a1[:], a0[:], 1.0 / TWO_PI, None, ALU.mult), reads=[a0], writes=[a1])
                V(lambda e: e.tensor_copy(ni_t[:], a1[:]), reads=[a1], writes=[ni_t])
                V(lambda e: e.tensor_copy(a1[:], ni_t[:]), reads=[ni_t], writes=[a1])
                V(lambda e: e.scalar_tensor_tensor(a2[:], a1[:], -C1, a0[:], ALU.mult, ALU.add), reads=[a1, a0], writes=[a2])
                V(lambda e: e.scalar_tensor_tensor(a2[:], a1[:], -C2, a2[:], ALU.mult, ALU.add), reads=[a1, a2], writes=[a2])
                V(lambda e: e.tensor_scalar(a3[:], a2[:], math.pi, -TWO_PI, ALU.is_gt, ALU.mult), reads=[a2], writes=[a3])
                V(lambda e: e.tensor_tensor(a2[:], a2[:], a3[:], ALU.add), reads=[a2, a3], writes=[a2])
                V(lambda e: e.tensor_scalar(a3[:], a2[:], -math.pi, TWO_PI, ALU.is_lt, ALU.mult), reads=[a2], writes=[a3])
                V(lambda e: e.tensor_tensor(a2[:], a2[:], a3[:], ALU.add), reads=[a2, a3], writes=[a2])
                V(lambda e: e.tensor_scalar(a1[:], a2[:], math.pi / 2, None, ALU.add), reads=[a2], writes=[a1])
                V(lambda e: e.tensor_scalar(a3[:], a1[:], math.pi, -TWO_PI, ALU.is_gt, ALU.mult), reads=[a1], writes=[a3])
                V(lambda e: e.tensor_tensor(a1[:], a1[:], a3[:], ALU.add), reads=[a1, a3], writes=[a1])
                V(lambda e: e.tensor_scalar(a1[:], a1[:], math.pi, -math.pi, ALU.min, ALU.max), reads=[a1], writes=[a1])
                V(lambda e: e.tensor_scalar(a2[:], a2[:], math.pi, -math.pi, ALU.min, ALU.max), reads=[a2], writes=[a2])
                A(lambda e: e.activation(dst[:, 0, dcol:dcol + 512], a1[:], AF.Sin), reads=[a1], writes=[(dst, dcol)])
                A(lambda e: e.activation(dst[:, 1, dcol:dcol + 512], a2[:], AF.Sin, scale=sgn[:, 0:1]),
                  reads=[a2, sgn], writes=[(dst, dcol)])

            def prep1(j_):
                i_, b_ = j_ // 4, j_ % 4
                grp_, t_ = i_ // 4, i_ % 4
                xsrc = x_own if grp_ == 0 else x_oth
                r0 = t_ * 512 + b_ * 128
                return xb_prep(xsrc[r0:r0 + 128, :], 128)

            hd1 = [prep1(0)]
            for grp in range(2):
                pos_t = pos_own if grp == 0 else pos_oth
                for t in range(4):
                    hT = hTs[(grp * 4 + t) % 2]
                    for b in range(4):
                        j_ = (grp * 4 + t) * 4 + b
                        nh = prep1(j_ + 1) if j_ + 1 < 32 else None
                        xb_trans(hd1[0], hT, b * 128, 0, 8)
                        hd1[0] = nh
                    if grp == 0:
                        rope_tables(pos_t, t * 512, CS, t * 512)
                        cs, cc = CS, t * 512
                    else:
                        rope_tables(pos_t, t * 512, cs_tmp, 0)
                        cs, cc = cs_tmp, 0
                    if stop == 12:
                        return
                    hk = [(hT, k) for k in range(8)]
                    for m in range(2):
                        for k in range(8):
                            mm(bap(2 + m), wlat[:, k, 384 + m * 128:384 + (m + 1) * 128], hT[:, k, 0:512],
                               k == 0, k == 7, hk + WL, bk(2 + m), k == 7)
                    for m in range(2):
                        for k in range(8):
                            mm(bap(5 + m)[0:64, :], wlat[:, k, 640 + m * 64:640 + (m + 1) * 64], hT[:, k, 0:512],
                               k == 0, k == 7, hk + WL, bk(5 + m), k == 7)
                    sq = []
                    for m in range(2):
                        s_ = nxt(tmpb, tmpb_rr)
                        A(lambda e, m=m, s_=s_: e.activation(s_[:], bap(2 + m), AF.Square), reads=[bk(2 + m)], writes=[s_])
                        sq.append(s_)
                    for m in range(2):
                        mm(bap(4), ones_b[:], sq[m][:], m == 0, m == 1, [ones_b, sq[m]], bk(4), True)
                    r = rstd_from_ps(4, 256)
                    for m in range(2):
                        V(lambda e, m=m, r=r: e.scalar_tensor_tensor(kvn[grp][:, m, t * 512:(t + 1) * 512], bap(2 + m),
                                                                     vecs[:, V_KVG + m:V_KVG + m + 1], r[:], ALU.mult, ALU.mult),
                          reads=[bk(2 + m), r, vecs], writes=[(kvn[grp], t)])
                    ta = nxt(tmpf, tmpf_rr)
                    tb_ = nxt(tmpf, tmpf_rr)
                    V(lambda e, ta=ta, cs=cs, cc=cc: e.tensor_tensor(ta[0:64, :], bap(5)[0:64, :], cs[:, 0, cc:cc + 512], ALU.mult),
                      reads=[bk(5), (cs, cc)], writes=[ta])
                    V(lambda e, tb_=tb_, cs=cs, cc=cc: e.tensor_tensor(tb_[0:64, :], bap(6)[0:64, :], cs[:, 1, cc:cc + 512], ALU.mult),
                      reads=[bk(6), (cs, cc)], writes=[tb_])
                    V(lambda e, ta=ta, tb_=tb_: e.tensor_tensor(krT[grp][:, t * 512:(t + 1) * 512], ta[0:64, :], tb_[0:64, :], ALU.add),
                      reads=[ta, tb_], writes=[(krT[grp], t)])
                    if stop == 13:
                        return
                    if grp == 0:
                        QB = [2, 3, 7]
                        for m in range(3):
                            for k in range(8):
                                mm(bap(QB[m]), wlat[:, k, m * 128:(m + 1) * 128], hT[:, k, 0:512],
                                   k == 0, k == 7, hk + WL, bk(QB[m]), k == 7)
                        sq = []
                        for m in range(3):
                            s_ = nxt(tmpb, tmpb_rr)
                            A(lambda e, m=m, s_=s_: e.activation(s_[:], bap(QB[m]), AF.Square), reads=[bk(QB[m])], writes=[s_])
                            sq.append(s_)
                        for m in range(3):
                            mm(bap(4), ones_b[:], sq[m][:], m == 0, m == 2, [ones_b, sq[m]], bk(4), True)
                        r = rstd_from_ps(4, 384)
                        for m in range(3):
                            V(lambda e, m=m, r=r: e.scalar_tensor_tensor(qn[:, m, t * 512:(t + 1) * 512], bap(QB[m]),
                                                                         vecs[:, V_QG + m:V_QG + m + 1], r[:], ALU.mult, ALU.mult),
                              reads=[bk(QB[m]), r, vecs], writes=[(qn, t)])
                    if stop == 14:
                        return

            S.barrier()
            ph1.close()
            if stop == 1:
                ph12.close()
                return

            wuq = sb12("wuq", [128, 3, 2048], BF)
            wukv = sb12("wukv", [128, 2, 2048], BF)
            for pc in range(6):
                wb = load_w(w_uq, 384, pc * 256, 256)
                G(lambda e, wb=wb, pc=pc: e.tensor_copy(wuq[:, :, pc * 256:(pc + 1) * 256], wb[:, 0:3, :]),
                  reads=[wb], writes=[(wuq, pc)])
            for pc in range(2):
                wb = load_w(w_uq_sw, 384, pc * 256, 256)
                G(lambda e, wb=wb, pc=pc: e.tensor_copy(wuq[:, :, 1536 + pc * 256:1536 + (pc + 1) * 256], wb[:, 0:3, :]),
                  reads=[wb], writes=[(wuq, 6 + pc)])
            for pc in range(8):
                wb = load_w(w_ukv, 256, pc * 256, 256)
                G(lambda e, wb=wb, pc=pc: e.tensor_copy(wukv[:, :, pc * 256:(pc + 1) * 256], wb[:, 0:2, :]),
                  reads=[wb], writes=[(wukv, pc)])
            KhT = [[sb12("kh%d_%d" % (i, g), [128, NOWN], BF) for g in range(2)] for i in range(1)]
            Vh = [[sb12("vh%d_%d" % (i, g), [128, 16, 128], BF) for g in range(2)] for i in range(1)]
            Qh = [sb12("qh%d" % i, [128, NOWN], BF) for i in range(1)]
            Qr = [sb12("qr%d" % i, [64, NOWN], BF) for i in range(1)]
            Pt = [sb12("pt%d" % i, [128, 512], BF) for i in range(4)]
            pt_rr = [0]
            SB_ = [0, 1, 2]
            s_rr = [0]
            OB = [3, 5]
            LB = [4, 6]
            HBS = [7, 3, 4]
            hb_rr = [0]

            def nhb():
                b_ = HBS[hb_rr[0] % 3]
                hb_rr[0] += 1
                return b_
            evac_rr = [0]

            def evac_copy(dst_ap, src_bank_ap, reads, writes):
                if evac_rr[0] % 2 == 0:
                    V(lambda e: e.tensor_copy(dst_ap, src_bank_ap), reads=reads, writes=writes)
                else:
                    A(lambda e: e.activation(dst_ap, src_bank_ap, AF.Copy), reads=reads, writes=writes)
                evac_rr[0] += 1

            def build_head(h):
                i = 0
                for grp in range(2):
                    for t in range(4):
                        HB = nhb()
                        for k in range(2):
                            mm(bap(HB), wukv[:, k, h * 256:h * 256 + 128], kvn[grp][:, k, t * 512:(t + 1) * 512],
                               k == 0, k == 1, [wukv, kvn[grp]], bk(HB), k == 1)
                        evac_copy(KhT[i][grp][:, t * 512:(t + 1) * 512], bap(HB), [bk(HB)], [(KhT[i][grp], t)])
                    for t in range(4):
                        HB = nhb()
                        for b in range(4):
                            blk = t * 4 + b
                            for k in range(2):
                                mm(bap(HB, b * 128, (b + 1) * 128), kvn[grp][:, k, blk * 128:(blk + 1) * 128],
                                   wukv[:, k, h * 256 + 128:h * 256 + 256],
                                   k == 0, k == 1, [wukv, kvn[grp]], bk(HB), (k == 1 and b == 3))
                        evac_copy(Vh[i][grp][:, t * 4:(t + 1) * 4, :], bap(HB).rearrange("p (b d) -> p b d", d=128),
                                  [bk(HB)], [(Vh[i][grp], t)])
                for t in range(4):
                    HB = nhb()
                    for k in range(3):
                        mm(bap(HB), wuq[:, k, h * 192:h * 192 + 128], qn[:, k, t * 512:(t + 1) * 512],
                           k == 0, k == 2, [wuq, qn], bk(HB), k == 2)
                    evac_copy(Qh[i][:, t * 512:(t + 1) * 512], bap(HB), [bk(HB)], [(Qh[i], t)])
                for t in range(4):
                    HB = nhb()
                    for k in range(3):
                        mm(bap(HB)[0:64, :], wuq[:, k, h * 192 + 128:h * 192 + 192], qn[:, k, t * 512:(t + 1) * 512],
                           k == 0, k == 2, [wuq, qn], bk(HB), k == 2)
                    ta = nxt(tmpf, tmpf_rr)
                    V(lambda e, ta=ta, t=t: e.tensor_tensor(ta[0:64, :], bap(HB)[0:64, :], CS[:, 0, t * 512:(t + 1) * 512], ALU.mult),
                      reads=[bk(HB), CS], writes=[ta])
                    HB = nhb()
                    for k in range(3):
                        mm(bap(HB)[0:64, :], wuq[:, k, 1536 + h * 64:1536 + (h + 1) * 64], qn[:, k, t * 512:(t + 1) * 512],
                           k == 0, k == 2, [wuq, qn], bk(HB), k == 2)
                    tb_ = nxt(tmpf, tmpf_rr)
                    V(lambda e, tb_=tb_, t=t: e.tensor_tensor(tb_[0:64, :], bap(HB)[0:64, :], CS[:, 1, t * 512:(t + 1) * 512], ALU.mult),
                      reads=[bk(HB), CS], writes=[tb_])
                    V(lambda e, ta=ta, tb_=tb_, t=t: e.tensor_tensor(Qr[i][:, t * 512:(t + 1) * 512], ta[0:64, :], tb_[0:64, :], ALU.add),
                      reads=[ta, tb_], writes=[(Qr[i], t)])

            def attend_head(h):
                i = 0
                for g in range(4):
                    ob = OB[g % 2]
                    lb = LB[g % 2]
                    visits = [(J, grp) for J in range(4 * g + 4) for grp in range(2)]
                    pend = []

                    def do_pv(v, first, last):
                        J, grp, c0, pt = v
                        mm(bap(ob, c0, 512), Vh[i][grp][:, J, :], pt[:, c0:512], first, last,
                           [Vh[i][grp], pt], bk(ob), True)
                        mm(bap(lb, c0, 512), ones_b[:], pt[:, c0:512], first, last,
                           [ones_b, pt], bk(lb), True)

                    npv = [0]
                    for vi, (J, grp) in enumerate(visits):
                        j = J - 4 * g
                        c0 = 128 * max(j, 0)
                        sbk = SB_[s_rr[0] % 3]
                        s_rr[0] += 1
                        q0 = g * 512 + c0
                        q1 = (g + 1) * 512
                        masked = j >= 0
                        mm(bap(sbk, c0, 512), KhT[i][grp][:, J * 128:(J + 1) * 128], Qh[i][:, q0:q1],
                           True, False, [KhT[i][grp], Qh[i]], bk(sbk), False)
                        mm(bap(sbk, c0, 512), krT[grp][:, J * 128:(J + 1) * 128], Qr[i][:, q0:q1],
                           False, not masked, [krT[grp], Qr[i]], bk(sbk), not masked)
                        if masked:
                            mk = tri_b if grp == 0 else pair_b
                            mm(bap(sbk, c0, c0 + 128), ident_b[:], mk[:], False, True, [ident_b, mk], bk(sbk), True)
                        pt = nxt(Pt, pt_rr)
                        A(lambda e, pt=pt, sbk=sbk, c0=c0: e.activation(pt[:, c0:512], bap(sbk, c0, 512), AF.Exp, scale=SCALE),
                          reads=[bk(sbk)], writes=[pt])
                        pend.append((J, grp, c0, pt))
                        if len(pend) > 2:
                            v = pend.pop(0)
                            do_pv(v, npv[0] == 0, False)
                            npv[0] += 1
                    while pend:
                        v = pend.pop(0)
                        do_pv(v, npv[0] == 0, len(pend) == 0)
                        npv[0] += 1
                    rl = nxt(rstd_t, rstd_rr)
                    V(lambda e, rl=rl, lb=lb: e.reciprocal(rl[:], bap(lb)), reads=[bk(lb)], writes=[rl])
                    V(lambda e, rl=rl, ob=ob, g=g: e.tensor_tensor(oT[:, h, g * 512:(g + 1) * 512], bap(ob), rl[:], ALU.mult),
                      reads=[bk(ob), rl], writes=[(oT, (h, g))])

            prep = []
            for pc in range(4):
                prep += [(w_in, pc * 256, 0), (w_in, 1024 + pc * 256, 0)]
            for pc in range(4):
                prep += [(w_conv_out, pc * 256, 0)]
            for pc in range(4):
                prep += [(w_attn_out, pc * 256, 0), (w_in, 2752 + pc * 256, 0), (w_in, 3776 + pc * 256, 0)]
            for pc in range(4):
                prep += [(w_out, pc * 256, 0)]
            for pc in range(16):
                prep += [(w_mlp_in, pc * 256, 0)]
            for pc in range(4):
                for rr_ in range(4):
                    prep += [(w_mlp_out, pc * 256, rr_ * 1024)]
            prep_tok = {}

            def do_prep():
                for i, (src, c0, r0) in enumerate(prep):
                    wb = load_w(src, D, c0, 256, r0=r0, cast="pool")
                    S.dma("sp", lambda e, wb=wb, i=i: e.dma_start(out=wq[i], in_=wb[:].rearrange("p k c -> p (k c)")),
                          reads=[wb], writes=[(wq_key, i)])

            do_prep()
            build_head(0)
            for h in range(8):
                attend_head(h)
                if h + 1 < 8:
                    build_head(h + 1)
            mod_late()

            S.barrier()
            ph12.close()
            if stop == 2:
                return
            wpool = [wbf[0][:], wbf[1][:]]
            for i_ in range(NST):
                fl = wst[i_].bitcast(BF)[:].rearrange("p k c -> p (k c)")
                wpool += [fl[:, 0:2048].rearrange("p (k c) -> p k c", c=256), fl[:, 2048:4096].rearrange("p (k c) -> p k c", c=256)]
            wp_rr = [0]
            pidx = {(src_.tensor.name, c0_, r0_): i_ for i_, (src_, c0_, r0_) in enumerate(prep)}

            def load_wq(src, rows, c0, ncols, r0=0):
                i = pidx[(src.tensor.name, c0, r0)]
                buf = wpool[wp_rr[0] % len(wpool)]
                wp_rr[0] += 1
                S.dma("sp", lambda e: e.dma_start(out=buf.rearrange("p k c -> p (k c)"), in_=wq[i]), writes=[buf])
                return buf

            xT = sb("xT", [128, 8, 512])
            yT = sb("yT", [128, 8, 512])
            hTe = sb("hTe", [128, 8, 640], BF)
            uext = sb("uext", [128, 8, 640], BF)
            arena = sb("arena", [128, 16384], BF)
            hid = arena[:, :].rearrange("p (j t) -> p j t", t=512)
            ucv = arena.bitcast(F32)[:, 0:4096].rearrange("p (c t) -> p c t", t=512)
            diag = [arena[:, 8192 + i * 3968:8192 + (i + 1) * 3968].rearrange("p (k m) -> p k m", m=128) for i in range(2)]
            sh8 = sb("sh8", [128, 8, 512], BF)
            actT = sh8
            mT = sh8
            h2T = sh8
            yaT = sb("yaT", [128, 8, 512], BF)
            oblk = xblk
            stat_s = sb("stat_s", [128, 512])
            stat_n = sb("stat_n", [128, 512])
            sb_sig = sb("sb_sig", [128, 640])

            def stats_accum(ps_b, src_ap, reads, idx, n):
                s_ = nxt(tmpb, tmpb_rr)
                A(lambda e: e.activation(s_[:], src_ap, AF.Square), reads=reads, writes=[s_])
                mm(bap(ps_b), ones_b[:], s_[:], idx == 0, idx == n - 1, [ones_b, s_], bk(ps_b), True)

            def fh_blocks(g_):
                lst = []
                for b in range(4):
                    blk = g_ * 4 + b
                    lst.append((x_halo[blk * 32:(blk + 1) * 32, :], 32, b * 160))
                    lst.append((x_own[blk * 128:(blk + 1) * 128, :], 128, b * 160 + 32))
                return lst

            def front_h(g_):
                lst = fh_blocks(g_)
                hd = xb_prep(lst[0][0], lst[0][1])
                for i_ in range(8):
                    nh = xb_prep(lst[i_ + 1][0], lst[i_ + 1][1]) if i_ + 1 < 8 else None
                    xb_trans(hd, hTe, lst[i_][2], 0, 8)
                    hd = nh

            front_h(0)
            for g in range(4):
                for b in range(4):
                    blk = g * 4 + b
                    xb = xblk[xb_rr[0] % 2]
                    xb_rr[0] += 1
                    dma_in(xb[:], x_own[blk * 128:(blk + 1) * 128, :], xb)
                    for half in range(2):
                        tb_i = 2 + half
                        for kk in range(4):
                            k = half * 4 + kk
                            P(lambda e, k=k, kk=kk, tb_i=tb_i, xb=xb: e.transpose(bap(tb_i, kk * 128, (kk + 1) * 128),
                                                                                    xb[:, k * 128:(k + 1) * 128], ident_f[:]),
                              reads=[xb, ident_f], writes=[bk(tb_i)], inc=(kk == 3))
                        evac_copy(xT[:, half * 4:(half + 1) * 4, b * 128:(b + 1) * 128],
                                  bap(tb_i).rearrange("p (k t) -> p k t", t=128), [bk(tb_i)], [(xT, (half, b))])
                hk = [(hTe, k) for k in range(8)]
                def glu_chunk(c, wa, wb2, cc):
                    for (wt, d) in ((wa, 0), (wb2, 1)):
                        for (n0, n1, hb) in ((0, 512, 0), (512, 640, 1)):
                            for k in range(8):
                                mm(PD[d][:, hb * 512:hb * 512 + (n1 - n0)], wt[:, k, cc * 128:(cc + 1) * 128],
                                   hTe[:, k, n0:n1], k == 0, k == 7, hk + [wt], (PD[d], hb), k == 7)
                    sg = sb_sig
                    A(lambda e: e.activation(sg[:, 0:640], PD[1][:, 0:640], AF.Sigmoid),
                      reads=[(PD[1], 0), (PD[1], 1)], writes=[sg])
                    V(lambda e: e.tensor_tensor(uext[:, c, :], PD[0][:, 0:640], sg[:, 0:640], ALU.mult),
                      reads=[(PD[0], 0), (PD[0], 1), sg], writes=[(uext, c)])
                    if g == 0:
                        V(lambda e: e.tensor_scalar(uext[:, c, 0:32], uext[:, c, 0:32], halom[:, 0:1], None, ALU.mult),
                          reads=[(uext, c), halom], writes=[(uext, c)])

                def conv_chunk(c):
                    dg = diag[c % 2]
                    for k in range(31):
                        G(lambda e: e.tensor_scalar(dg[:, k, :], ident_b[:], cwT[:, c, k:k + 1], 1.0, ALU.mult, ALU.mult),
                          reads=[ident_b, cwT], writes=[(arena, None) if (c == 0 and k == 0) else (arena, ("d", c % 2, k))])
                    uv = uext[:, c, :].rearrange("p (b w) -> p b w", w=160)
                    cb = 4 + (c % 2)
                    for k in range(31):
                        mm(bap(cb).rearrange("p (b w) -> p b w", w=128), dg[:, k, :], uv[:, :, 2 + k:2 + k + 128],
                           k == 0, k == 30, [(arena, ("d", c % 2, k)), (uext, c)], bk(cb), k == 30)
                    A(lambda e: e.activation(ucv[:, c, :], bap(cb), AF.Identity, bias=vecs[:, V_CB + c:V_CB + c + 1]),
                      reads=[bk(cb), vecs], writes=[(arena, ("u", c))])
                    ub_ = nxt(tmpb, tmpb_rr)
                    V(lambda e: e.tensor_copy(ub_[:], ucv[:, c, :]), reads=[(arena, ("u", c))], writes=[ub_])
                    mm(bap(6), ones_b[:], ub_[:], c == 0, c == 7, [ones_b, ub_], bk(6), True)
                    stats_accum(7, ucv[:, c, :], [(arena, ("u", c))], c, 8)

                for pc in range(4):
                    wa = load_wq(w_in, D, pc * 256, 256)
                    wb2 = load_wq(w_in, D, 1024 + pc * 256, 256)
                    for cc in range(2):
                        c = pc * 2 + cc
                        glu_chunk(c, wa, wb2, cc)
                        if c >= 1:
                            conv_chunk(c - 1)
                conv_chunk(7)
                mean = stat_s
                nmr = stat_n
                A(lambda e: e.activation(mean[:], bap(6), AF.Copy, scale=1.0 / D), reads=[bk(6)], writes=[mean])
                jt = nxt(tmpf, tmpf_rr)
                V(lambda e, jt=jt: e.tensor_tensor(jt[:], mean[:], mean[:], ALU.mult), reads=[mean], writes=[jt])
                jt2 = nxt(tmpf, tmpf_rr)
                V(lambda e, jt=jt, jt2=jt2: e.scalar_tensor_tensor(jt2[:], bap(7), 1.0 / D, jt[:], ALU.mult, ALU.subtract),
                  reads=[bk(7), jt], writes=[jt2])
                V(lambda e, jt2=jt2: e.tensor_scalar(jt2[:], jt2[:], 0.0, None, ALU.max), reads=[jt2], writes=[jt2])
                A(lambda e, jt=jt, jt2=jt2: e.activation(jt[:], jt2[:], AF.Sqrt, bias=EPS), reads=[jt2], writes=[jt])
                rln = nxt(rstd_t, rstd_rr)
                V(lambda e, jt=jt, rln=rln: e.reciprocal(rln[:], jt[:]), reads=[jt], writes=[rln])
                V(lambda e, rln=rln: e.scalar_tensor_tensor(nmr[:], mean[:], -1.0, rln[:], ALU.mult, ALU.mult),
                  reads=[mean, rln], writes=[nmr])
                for c in range(8):
                    jt = nxt(tmpf, tmpf_rr)
                    V(lambda e, c=c, jt=jt, rln=rln: e.tensor_tensor(jt[:], ucv[:, c, :], rln[:], ALU.mult),
                      reads=[(arena, ("u", c)), rln], writes=[jt])
                    V(lambda e, jt=jt: e.tensor_tensor(jt[:], jt[:], nmr[:], ALU.add), reads=[jt, nmr], writes=[jt])
                    A(lambda e, c=c, jt=jt: e.activation(actT[:, c, :], jt[:], AF.Silu,
                                                         bias=vecs[:, V_CNB + c:V_CNB + c + 1], scale=vecs[:, V_CG + c:V_CG + c + 1]),
                      reads=[jt, vecs, vecs], writes=[(actT, c)])
                ak = [(actT, k) for k in range(8)]
                for pc in range(4):
                    wb = load_wq(w_conv_out, D, pc * 256, 256)
                    for cc in range(2):
                        m = pc * 2 + cc
                        bb = 4 + (m % 2)
                        for k in range(8):
                            mm(bap(bb), wb[:, k, cc * 128:(cc + 1) * 128], actT[:, k, :], k == 0, k == 7, ak + [wb], bk(bb), k == 7)
                        evac_copy(yaT[:, m, :], bap(bb), [bk(bb)], [(yaT, m)])
                hown = lambda k: hTe[:, k, :].rearrange("p (b w) -> p b w", w=160)[:, :, 32:160]
                for pc in range(4):
                    wao = load_wq(w_attn_out, D, pc * 256, 256)
                    for cc in range(2):
                        for hh in range(8):
                            mm(bap(2 + cc), wao[:, hh, cc * 128:(cc + 1) * 128], oT[:, hh, g * 512:(g + 1) * 512],
                               hh == 0, hh == 7, [oT, wao], bk(2 + cc), hh == 7)
                    wga = load_wq(w_in, D, 2752 + pc * 256, 256)
                    sas = []
                    for cc in range(2):
                        m = pc * 2 + cc
                        for k in range(8):
                            mm(bap(cc).rearrange("p (b w) -> p b w", w=128), wga[:, k, cc * 128:(cc + 1) * 128], hown(k),
                               k == 0, k == 7, hk + [wga], bk(cc), k == 7)
                        sa = nxt(tmpf, tmpf_rr)
                        A(lambda e, sa=sa, cc=cc: e.activation(sa[:], bap(cc), AF.Sigmoid), reads=[bk(cc)], writes=[sa])
                        V(lambda e, sa=sa, m=m: e.tensor_tensor(sa[:], sa[:], yaT[:, m, :], ALU.mult),
                          reads=[sa, (yaT, m)], writes=[sa])
                        sas.append(sa)
                    wgb = load_wq(w_in, D, 3776 + pc * 256, 256)
                    for cc in range(2):
                        m = pc * 2 + cc
                        for k in range(8):
                            mm(bap(cc).rearrange("p (b w) -> p b w", w=128), wgb[:, k, cc * 128:(cc + 1) * 128], hown(k),
                               k == 0, k == 7, hk + [wgb], bk(cc), k == 7)
                        sb2 = nxt(tmpf, tmpf_rr)
                        A(lambda e, sb2=sb2, cc=cc: e.activation(sb2[:], bap(cc), AF.Sigmoid), reads=[bk(cc)], writes=[sb2])
                        V(lambda e, sb2=sb2, cc=cc: e.tensor_tensor(sb2[:], bap(2 + cc), sb2[:], ALU.mult),
                          reads=[bk(2 + cc), sb2], writes=[sb2])
                        V(lambda e, sa=sas[cc], sb2=sb2, m=m: e.tensor_tensor(mT[:, m, :], sa[:], sb2[:], ALU.add),
                          reads=[sas[cc], sb2], writes=[(mT, m)])
                mk_ = [(mT, k) for k in range(8)]
                for pc in range(4):
                    wb = load_wq(w_out, D, pc * 256, 256)
                    for cc in range(2):
                        m = pc * 2 + cc
                        bb = 4 + (m % 2)
                        for k in range(8):
                            mm(bap(bb), wb[:, k, cc * 128:(cc + 1) * 128], mT[:, k, :], k == 0, k == 7, mk_ + [wb], bk(bb), k == 7)
                        A(lambda e, m=m, bb=bb: e.activation(yT[:, m, :], bap(bb), AF.Copy), reads=[bk(bb)], writes=[(yT, m)])
                        stats_accum(6, yT[:, m, :], [(yT, m)], m, 8)
                if debug == "m" and g == 3:
                    V(lambda e: e.tensor_copy(hTe[:, :, 0:512], mT[:]), reads=[mT], writes=[hTe])
                    V(lambda e: e.tensor_copy(uext[:, :, 0:512], yT[:]), reads=[yT], writes=[uext])
                r1 = rstd_from_ps(6, D)
                for k in range(8):
                    jt = nxt(tmpf, tmpf_rr)
                    V(lambda e, k=k, jt=jt, r1=r1: e.tensor_tensor(jt[:], yT[:, k, :], r1[:], ALU.mult),
                      reads=[(yT, k), r1], writes=[jt])
                    V(lambda e, k=k, jt=jt: e.scalar_tensor_tensor(xT[:, k, :], jt[:], der[:, 16 + k:17 + k], xT[:, k, :], ALU.mult, ALU.add),
                      reads=[jt, (der, 16), xT], writes=[xT])
                    stats_accum(7, xT[:, k, :], [xT], k, 8)
                r2 = rstd_from_ps(7, D)
                for k in range(8):
                    jt = nxt(tmpf, tmpf_rr)
                    V(lambda e, k=k, jt=jt, r2=r2: e.scalar_tensor_tensor(jt[:], xT[:, k, :], der[:, 24 + k:25 + k], r2[:], ALU.mult, ALU.mult),
                      reads=[xT, (der, 24), r2], writes=[jt])
                    A(lambda e, k=k, jt=jt: e.activation(h2T[:, k, :], jt[:], AF.Identity, bias=der[:, 32 + k:33 + k]),
                      reads=[jt, (der, 32)], writes=[(h2T, k)])
                h2k = [(h2T, k) for k in range(8)]
                nlst = fh_blocks(g + 1) if g + 1 < 4 else None
                nhd = {}
                for pc in range(16):
                    if nlst is not None:
                        if pc % 2 == 0:
                            nhd[pc // 2] = xb_prep(nlst[pc // 2][0], nlst[pc // 2][1])
                        else:
                            xb_trans(nhd[pc // 2], hTe, nlst[pc // 2][2], 0, 8)
                    wb = load_wq(w_mlp_in, D, pc * 256, 256)
                    for cc in range(2):
                        j = pc * 2 + cc
                        bb = 4 + (j % 4)
                        for k in range(8):
                            mm(bap(bb), wb[:, k, cc * 128:(cc + 1) * 128], h2T[:, k, :], k == 0, k == 7, h2k + [wb], bk(bb), k == 7)
                        jt = nxt(tmpf, tmpf_rr)
                        A(lambda e, jt=jt, bb=bb: e.activation(jt[:], bap(bb), AF.Relu), reads=[bk(bb)], writes=[jt])
                        V(lambda e, jt=jt, j=j: e.tensor_tensor(hid[:, j, :], jt[:], jt[:], ALU.mult), reads=[jt],
                          writes=[(arena, None) if j == 0 else (arena, ("h", j))])
                for pc in range(4):
                    for cc in range(2):
                        pass
                    wbs = []
                    for rr_ in range(4):
                        wb = load_wq(w_mlp_out, D, pc * 256, 256, r0=rr_ * 1024)
                        for cc in range(2):
                            bb = 4 + cc
                            for k in range(8):
                                kk = rr_ * 8 + k
                                mm(bap(bb), wb[:, k, cc * 128:(cc + 1) * 128], hid[:, kk, :], kk == 0, kk == 31,
                                   [arena, wb], bk(bb), (k == 7))
                    for cc in range(2):
                        m = pc * 2 + cc
                        bb = 4 + cc
                        A(lambda e, m=m, bb=bb: e.activation(yT[:, m, :], bap(bb), AF.Copy), reads=[bk(bb)], writes=[(yT, m)])
                        stats_accum(6, yT[:, m, :], [(yT, m)], m, 8)
                r3 = rstd_from_ps(6, D)
                for k in range(8):
                    jt = nxt(tmpf, tmpf_rr)
                    V(lambda e, k=k, jt=jt, r3=r3: e.tensor_tensor(jt[:], yT[:, k, :], r3[:], ALU.mult),
                      reads=[(yT, k), r3], writes=[jt])
                    V(lambda e, k=k, jt=jt: e.scalar_tensor_tensor(yT[:, k, :], jt[:], der[:, 40 + k:41 + k], xT[:, k, :], ALU.mult, ALU.add),
                      reads=[jt, (der, 40), xT], writes=[(yT, k)])
                for b in range(4):
                    ob_ = oblk[b % 2]
                    for half in range(2):
                        tb_i = 2 + half
                        for kk in range(4):
                            k = half * 4 + kk
                            P(lambda e, k=k, kk=kk, tb_i=tb_i, b=b: e.transpose(bap(tb_i, kk * 128, (kk + 1) * 128),
                                                                                 yT[:, k, b * 128:(b + 1) * 128], ident_f[:]),
                              reads=[yT, ident_f], writes=[bk(tb_i)], inc=(kk == 3))
                        evac_copy(ob_[:, half * 512:(half + 1) * 512], bap(tb_i), [bk(tb_i)], [(ob_, half)])
                    blk = g * 4 + b
                    tok = S.dma("act", lambda e, ob_=ob_, blk=blk: e.dma_start(out=out[blk * 128:(blk + 1) * 128, :], in_=ob_[:]),
                                reads=[ob_])
                    out_toks.append(tok)


        run_phases()
        if stop is not None:
            out_toks.append(S.dma("act", lambda e: e.dma_start(out=out[0:128, :], in_=xblk[0][:]), reads=[xblk[0]]))
        last = {}
        for (s, v) in out_toks:
            last[s] = max(last.get(s, 0), v)
        S.wait_all("act", list(last.items()))

        with nc.Block() as block:
            def emit(engname, eng):
                for (waits, fn, inc) in S.q[engname]:
                    for (s, v) in waits:
                        eng.wait_ge(sems[s], v)
                    if fn is None:
                        continue
                    ins = fn(eng)
                    if inc is not None:
                        ins.then_inc(sems[inc[0]], inc[1])

            @block.sync
            def _(e):
                emit("sp", e)

            @block.tensor
            def _(e):
                emit("pe", e)

            @block.scalar
            def _(e):
                emit("act", e)

            @block.vector
            def _(e):
                emit("dve", e)

            @block.gpsimd
            def _(e):
                emit("pool", e)
    return nc


def _prep_inputs(inputs):
    x = np.asarray(inputs["x"], np.float32)
    pos = np.asarray(inputs["positions"], np.int32)
    c = np.asarray(inputs["c"], np.float32)
    w_in = np.ascontiguousarray(np.asarray(inputs["w_in"], np.float32)[0])
    w_uq = np.ascontiguousarray(np.asarray(inputs["w_uq"], np.float32)[0])
    kr = w_in[:, 2688:2752]
    w_kr_sw = np.ascontiguousarray(np.concatenate([kr[:, 32:64], kr[:, 0:32]], axis=1))
    uq3 = w_uq.reshape(384, 8, 192)[:, :, 128:192]
    w_uq_sw = np.ascontiguousarray(np.concatenate([uq3[:, :, 32:64], uq3[:, :, 0:32]], axis=2).reshape(384, 512))
    k_idx = np.arange(128)[:, None]
    q_idx = np.arange(128)[None, :]
    trimask = np.where(k_idx <= q_idx, 0.0, NEG).astype(np.float32)
    ident = np.eye(128, dtype=np.float32)
    inv = (1.0 / (np.float32(10000.0) ** (np.arange(0, 64, 2, dtype=np.float32) / np.float32(64)))).astype(np.float32)
    invf = np.concatenate([inv, inv]).reshape(64, 1).astype(np.float32)
    sgn = np.concatenate([-np.ones(32), np.ones(32)]).reshape(64, 1).astype(np.float32)
    shared = {
        "trimask": trimask, "ident": ident, "invf": invf, "sgn": sgn,
        "w_ada": np.ascontiguousarray(inputs["w_ada"][0], np.float32),
        "b_ada": np.ascontiguousarray(inputs["b_ada"][0], np.float32),
        "g_pre_mix": np.ascontiguousarray(inputs["g_pre_mix"][0], np.float32),
        "g_post_mix": np.ascontiguousarray(inputs["g_post_mix"][0], np.float32),
        "g_pre_mlp": np.ascontiguousarray(inputs["g_pre_mlp"][0], np.float32),
        "g_post_mlp": np.ascontiguousarray(inputs["g_post_mlp"][0], np.float32),
        "w_in": w_in, "w_kr_sw": w_kr_sw,
        "conv_w": np.ascontiguousarray(inputs["conv_w"][0], np.float32),
        "conv_b": np.ascontiguousarray(inputs["conv_b"][0], np.float32),
        "conv_norm_g": np.ascontiguousarray(inputs["conv_norm_g"][0], np.float32),
        "conv_norm_b": np.ascontiguousarray(inputs["conv_norm_b"][0], np.float32),
        "w_conv_out": np.ascontiguousarray(inputs["w_conv_out"][0], np.float32),
        "q_norm_g": np.ascontiguousarray(inputs["q_norm_g"][0], np.float32),
        "w_uq": w_uq, "w_uq_sw": w_uq_sw,
        "kv_norm_g": np.ascontiguousarray(inputs["kv_norm_g"][0], np.float32),
        "w_ukv": np.ascontiguousarray(inputs["w_ukv"][0], np.float32),
        "w_attn_out": np.ascontiguousarray(inputs["w_attn_out"][0], np.float32),
        "w_out": np.ascontiguousarray(inputs["w_out"][0], np.float32),
        "w_mlp_in": np.ascontiguousarray(inputs["w_mlp_in"][0], np.float32),
        "w_mlp_out": np.ascontiguousarray(inputs["w_mlp_out"][0], np.float32),
    }
    in_maps = []
    for core in range(8):
        b, p = core // 2, core % 2
        xb = x[b].reshape(32, 128, D)
        pb = pos[b].reshape(32, 128)
        own = [2 * i + p for i in range(16)]
        oth = [2 * i + 1 - p for i in range(16)]
        halo = np.zeros((16, 32, D), np.float32)
        for i in range(16):
            st = own[i] * 128
            if st > 0:
                halo[i] = x[b, st - 32:st]
        m = dict(shared)
        m["x_own"] = np.ascontiguousarray(xb[own].reshape(NOWN, D))
        m["x_oth"] = np.ascontiguousarray(xb[oth].reshape(NOWN, D))
        m["x_halo"] = np.ascontiguousarray(halo.reshape(512, D))
        m["pos_own"] = np.ascontiguousarray(pb[own].reshape(NOWN))
        m["pos_oth"] = np.ascontiguousarray(pb[oth].reshape(NOWN))
        m["c"] = np.ascontiguousarray(c[b])
        m["pairmask"] = np.full((128, 128), 0.0 if p == 1 else NEG, np.float32)
        m["halomask"] = np.full((128, 1), 1.0 if p == 1 else 0.0, np.float32)
        in_maps.append(m)
    return in_maps


def kernel(**inputs):
    in_maps = _prep_inputs(inputs)
    nc = build_nc()
    res = run_bass_kernel_spmd(nc, in_maps, core_ids=list(range(8)))
    outf = np.zeros((4, 32, 128, D), np.float32)
    for core in range(8):
        b, p = core // 2, core % 2
        o = np.asarray(res.results[core]["out"]).reshape(16, 128, D)
        for i in range(16):
            outf[b, 2 * i + p] = o[i]
    return outf.reshape(4, 4096, D)
```

```python
import contextlib
import math
import numpy as np
import concourse.bass as bass
import concourse.mybir as mybir
from concourse.bass_utils import run_bass_kernel_spmd

F32 = mybir.dt.float32
BF = mybir.dt.bfloat16
I32 = mybir.dt.int32
AF = mybir.ActivationFunctionType
ALU = mybir.AluOpType

D = 1024
KC = 8
NOWN = 2048
EPS = 1e-6
NEG = -30000.0
SCALE = 1.0 / math.sqrt(192.0)
TWO_PI = 2.0 * math.pi
C1 = 6.28125
C2 = TWO_PI - 6.28125

ENGS = ("pe", "act", "dve", "pool", "sp")
NDMA = 12


class _Rec:
    def __init__(self):
        self.call = None

    def __getattr__(self, name):
        def f(*a, **k):
            self.call = (name, a, k)
            return self
        return f


def _bind(fn):
    if fn is None:
        return None
    r = _Rec()
    fn(r)
    name, a, k = r.call
    return lambda eng: getattr(eng, name)(*a, **k)


class Sched:
    def __init__(self):
        self.q = {e: [] for e in ENGS}
        self.cnt = {e: 0 for e in ENGS}
        self.waited = {e: {} for e in ENGS}
        self.state = {}
        self.dma_tot = [0] * NDMA
        self.dma_rr = 0
        self.all_dma_tokens = {}

    def _entries(self, buf, key, create):
        d = self.state.setdefault(id(buf), {})
        if key is None:
            if create and None not in d:
                d[None] = {"w": None, "r": {}}
            return list(d.values()) if not create else list(d.values())
        out = []
        if key not in d and create:
            d[key] = {"w": None, "r": {}}
        if key in d:
            out.append(d[key])
        if None in d:
            out.append(d[None])
        return out

    def _deps(self, reads, writes):
        deps = {}

        def add(tok):
            if tok is None:
                return
            s, v = tok
            if deps.get(s, 0) < v:
                deps[s] = v

        for (b, k) in reads:
            for st in self._entries(b, k, False):
                add(st["w"])
        for (b, k) in writes:
            for st in self._entries(b, k, False):
                add(st["w"])
                for s, v in st["r"].items():
                    add((s, v))
        return deps

    def _commit(self, reads, writes, tok):
        for (b, k) in reads:
            d = self.state.setdefault(id(b), {})
            if k not in d:
                d[k] = {"w": None, "r": {}}
            st = d[k]
            s, v = tok
            if st["r"].get(s, 0) < v:
                st["r"][s] = v
        for (b, k) in writes:
            d = self.state.setdefault(id(b), {})
            if k is None:
                d.clear()
            d[k] = {"w": tok, "r": {}}

    def _norm(self, lst):
        out = []
        for x in lst:
            if isinstance(x, tuple):
                out.append(x)
            else:
                out.append((x, None))
        return out

    def op(self, eng, fn, reads=(), writes=(), inc=True):
        fn = _bind(fn)
        reads = self._norm(reads)
        writes = self._norm(writes)
        deps = self._deps(reads, writes)
        waits = []
        for s, v in deps.items():
            if s == eng and eng == "pe":
                continue
            if self.waited[eng].get(s, 0) >= v:
                continue
            self.waited[eng][s] = v
            waits.append((s, v))
        if inc:
            self.cnt[eng] += 1
            tok = (eng, self.cnt[eng])
            self.q[eng].append((waits, fn, (eng, 1)))
        else:
            tok = (eng, self.cnt[eng] + 1)
            self.q[eng].append((waits, fn, None))
        self._commit(reads, writes, tok)
        return tok

    def dma(self, eng, fn, reads=(), writes=()):
        fn = _bind(fn)
        reads = self._norm(reads)
        writes = self._norm(writes)
        j = self.dma_rr
        self.dma_rr = (self.dma_rr + 1) % NDMA
        sem = "d%d" % j
        deps = self._deps(reads, writes)
        if self.dma_tot[j] > 0:
            if deps.get(sem, 0) < self.dma_tot[j]:
                deps[sem] = self.dma_tot[j]
        waits = []
        for s, v in deps.items():
            if self.waited[eng].get(s, 0) >= v:
                continue
            self.waited[eng][s] = v
            waits.append((s, v))
        self.dma_tot[j] += 16
        tok = (sem, self.dma_tot[j])
        self.q[eng].append((waits, fn, (sem, 16)))
        self._commit(reads, writes, tok)
        return tok

    def barrier(self):
        toks = [(e, self.cnt[e]) for e in ENGS if self.cnt[e] > 0]
        toks += [("d%d" % j, self.dma_tot[j]) for j in range(NDMA) if self.dma_tot[j] > 0]
        for e in ENGS:
            self.wait_all(e, [t for t in toks if t[0] != e])
        self.state = {}

    def wait_all(self, eng, toks):
        waits = []
        for (s, v) in toks:
            if self.waited[eng].get(s, 0) >= v:
                continue
            self.waited[eng][s] = v
            waits.append((s, v))
        self.q[eng].append((waits, None, None))


def build_nc(debug=None, stop=None):
    nc = bass.Bass("TRN2", target_bir_lowering=False)
    S = Sched()

    def din(name, shape, dt=F32):
        return nc.dram_tensor(name, list(shape), dt, kind="ExternalInput").ap()

    x_own = din("x_own", [NOWN, D])
    x_oth = din("x_oth", [NOWN, D])
    x_halo = din("x_halo", [512, D])
    pos_own = nc.dram_tensor("pos_own", [NOWN], I32, kind="ExternalInput")
    pos_oth = nc.dram_tensor("pos_oth", [NOWN], I32, kind="ExternalInput")
    c_in = din("c", [D])
    pairmask_in = din("pairmask", [128, 128])
    trimask_in = din("trimask", [128, 128])
    ident_in = din("ident", [128, 128])
    halomask_in = din("halomask", [128, 1])
    invf_in = din("invf", [64, 1])
    sgn_in = din("sgn", [64, 1])
    w_ada = din("w_ada", [D, 6 * D])
    b_ada = din("b_ada", [6 * D])
    g_pre_mix = din("g_pre_mix", [D])
    g_post_mix = din("g_post_mix", [D])
    g_pre_mlp = din("g_pre_mlp", [D])
    g_post_mlp = din("g_post_mlp", [D])
    w_in = din("w_in", [D, 4800])
    w_kr_sw = din("w_kr_sw", [D, 64])
    conv_w = din("conv_w", [31, D])
    conv_b = din("conv_b", [D])
    conv_norm_g = din("conv_norm_g", [D])
    conv_norm_b = din("conv_norm_b", [D])
    w_conv_out = din("w_conv_out", [D, D])
    q_norm_g = din("q_norm_g", [384])
    w_uq = din("w_uq", [384, 1536])
    w_uq_sw = din("w_uq_sw", [384, 512])
    kv_norm_g = din("kv_norm_g", [256])
    w_ukv = din("w_ukv", [256, 2048])
    w_attn_out = din("w_attn_out", [D, D])
    w_out = din("w_out", [D, D])
    w_mlp_in = din("w_mlp_in", [D, 4 * D])
    w_mlp_out = din("w_mlp_out", [4 * D, D])
    out = nc.dram_tensor("out", [NOWN, D], F32, kind="ExternalOutput").ap()
    wq = nc.dram_tensor("wq", [60, 128, 2048], BF, kind="Internal").ap()
    wq_key = object()
    dbg = None

    es = contextlib.ExitStack()
    with es:
        def sb(name, shape, dt=F32):
            return es.enter_context(nc.sbuf_tensor("s_" + name, list(shape), dt))

        sems = {}
        for e in ENGS:
            sems[e] = es.enter_context(nc.semaphore("sem_" + e))
        for j in range(NDMA):
            sems["d%d" % j] = es.enter_context(nc.semaphore("sem_d%d" % j))

        PD = [es.enter_context(nc.psum_tensor("pd%d" % i, [128, 1024], F32)) for i in range(4)]

        def bank(i):
            t = PD[i // 2]
            h = i % 2
            return t, h

        def bk(i):
            t, h = bank(i)
            return (t, h)

        def bap(i, c0=0, c1=512):
            t, h = bank(i)
            return t[:, h * 512 + c0: h * 512 + c1]

        def bap_bf(i):
            t, h = bank(i)
            return t.bitcast(BF)[:, h * 1024:(h + 1) * 1024]

        ident_f = sb("ident_f", [128, 128])
        ident_b = sb("ident_b", [128, 128], BF)
        ones_b = sb("ones_b", [128, 128], BF)
        tri_b = sb("tri_b", [128, 128], BF)
        pair_b = sb("pair_b", [128, 128], BF)
        halom = sb("halom", [128, 1])
        invf = sb("invf", [64, 1])
        sgn = sb("sgn", [64, 1])
        modT = sb("modT", [128, 48])
        vecs = sb("vecs", [128, 128])
        cwT = sb("cwT", [128, 8, 31])
        V_GPRE, V_GPOST, V_GPRE2, V_GPOST2 = 0, 8, 16, 24
        V_CB, V_CG, V_CNB = 32, 40, 48
        V_QG, V_KVG = 56, 59
        der = sb("der", [128, 48])
        NST = 2
        wst = [sb("wst%d" % i, [128, 8, 256]) for i in range(NST)]
        wbf = [sb("wbf%d" % i, [128, 8, 256], BF) for i in range(NST)]
        wrr = [0]
        xblk = [sb("xblk%d" % i, [128, D]) for i in range(2)]
        xnb = [sb("xnb%d" % i, [128, D], BF) for i in range(2)]
        small = [sb("small%d" % i, [128, 4]) for i in range(4)]
        small_rr = [0]
        tmpf = [sb("tmpf%d" % i, [128, 512]) for i in range(4)]
        tmpf_rr = [0]
        tmpb = [sb("tmpb%d" % i, [128, 512], BF) for i in range(6)]
        tmpb_rr = [0]
        rstd_t = [sb("rstd%d" % i, [128, 512]) for i in range(2)]
        rstd_rr = [0]

        def nxt(lst, rr):
            t = lst[rr[0] % len(lst)]
            rr[0] += 1
            return t

        def A(fn, **kw):
            return S.op("act", fn, **kw)

        def V(fn, **kw):
            return S.op("dve", fn, **kw)

        def G(fn, **kw):
            return S.op("pool", fn, **kw)

        def P(fn, **kw):
            return S.op("pe", fn, **kw)

        def dma_in(dst_ap, src_ap, dst_buf, key=None, eng="sp", nonc=False):
            def f(e, dst_ap=dst_ap, src_ap=src_ap):
                if nonc:
                    return e.dma_start(out=dst_ap, in_=src_ap, allow_slow_non_contiguous=True)
                return e.dma_start(out=dst_ap, in_=src_ap)
            return S.dma(eng, f, writes=[(dst_buf, key)])

        def mm(out_ap, lhsT, rhs, start, stop, reads, wkey, last):
            def f(e):
                return e.matmul(out_ap, lhsT, rhs, start=start, stop=stop)
            return S.op("pe", f, reads=reads, writes=[wkey], inc=last)

        def load_w(src, rows, c0, ncols, r0=0, cast=None):
            i = wrr[0] % NST
            wrr[0] += 1
            kc = rows // 128
            st, wb = wst[i], wbf[i]
            src_ap = src[r0:r0 + rows, c0:c0 + ncols].rearrange("(k p) c -> p k c", p=128)
            dma_in(st[:, 0:kc, 0:ncols], src_ap, st)
            if cast == "pool" or (cast is None and wrr[0] % 2 == 0):
                G(lambda e: e.tensor_copy(wb[:, 0:kc, 0:ncols], st[:, 0:kc, 0:ncols]), reads=[st], writes=[wb])
            else:
                V(lambda e: e.tensor_copy(wb[:, 0:kc, 0:ncols], st[:, 0:kc, 0:ncols]), reads=[st], writes=[wb])
            return wb

        def _unused():
            pass

        out_toks = []

        def run_phases():
            dma_in(ident_f[:], ident_in, ident_f)
            dma_in(halom[:], halomask_in, halom)
            dma_in(invf[:], invf_in, invf)
            dma_in(sgn[:], sgn_in, sgn)
            t0 = tmpf[0]
            t1 = tmpf[1]
            dma_in(t0[:, 0:128], trimask_in, t0)
            dma_in(t1[:, 0:128], pairmask_in, t1)
            V(lambda e: e.tensor_copy(ident_b[:], ident_f[:]), reads=[ident_f], writes=[ident_b])
            V(lambda e: e.memset(ones_b[:], 1.0), writes=[ones_b])
            V(lambda e: e.tensor_copy(tri_b[:], t0[:, 0:128]), reads=[t0], writes=[tri_b])
            V(lambda e: e.tensor_copy(pair_b[:], t1[:, 0:128]), reads=[t1], writes=[pair_b])
            tmpf_rr[0] = 2
            if stop == -1:
                return
            stg = tmpf[2]
            V(lambda e: e.memset(stg[:, 0:128], 0.0), writes=[stg])
            for col, src, n in ((V_GPRE, g_pre_mix, D), (V_GPOST, g_post_mix, D), (V_GPRE2, g_pre_mlp, D),
                                (V_GPOST2, g_post_mlp, D), (V_CB, conv_b, D), (V_CG, conv_norm_g, D),
                                (V_CNB, conv_norm_b, D), (V_QG, q_norm_g, 384), (V_KVG, kv_norm_g, 256)):
                dma_in(stg[col:col + n // 128, 0:128], src.rearrange("(k p) -> k p", p=128), stg)
            dma_in(stg[64:112, 0:128], b_ada.rearrange("(k p) -> k p", p=128), stg)
            dma_in(stg[112:120, 0:128], c_in.rearrange("(k p) -> k p", p=128), stg)
            P(lambda e: e.transpose(bap(1, 0, 120), stg[0:120, 0:128], ident_f[0:120, 0:120]),
              reads=[stg, ident_f], writes=[bk(1)])
            V(lambda e: e.tensor_copy(vecs[:, 0:120], bap(1, 0, 120)), reads=[bk(1)], writes=[vecs])
            if stop == -2:
                return
            badaT = vecs[:, 64:112]
            cT = vecs[:, 112:120]
            cwn = xblk[0]
            dma_in(cwn[0:31, :], conv_w, cwn)
            for c in range(8):
                P(lambda e, c=c: e.transpose(bap(2, c * 32, c * 32 + 31), cwn[0:31, c * 128:(c + 1) * 128], ident_f[0:31, 0:31]),
                  reads=[cwn, ident_f], writes=[bk(2)])
            V(lambda e: e.tensor_copy(cwT[:], bap(2, 0, 256).rearrange("p (c k) -> p c k", k=32)[:, :, 0:31]),
              reads=[bk(2)], writes=[cwT])
            if stop == -3:
                return
            scb = sb("scb", [128, 8], BF)
            A(lambda e: e.activation(scb[:], cT, AF.Silu), reads=[vecs], writes=[scb])
            if stop == -4:
                return
            def mod_part(p0, p1, MODB, cast=None):
                for pc in range(p0, p1):
                    wb = load_w(w_ada, D, pc * 256, 256, cast=cast)
                    for jj in range(2):
                        j = pc * 2 + jj
                        for k in range(8):
                            mm(bap(MODB, j, j + 1), wb[:, k, jj * 128:(jj + 1) * 128], scb[:, k:k + 1],
                               k == 0, k == 7, [wb, scb], bk(MODB), k == 7)
                V(lambda e: e.tensor_tensor(modT[:, p0 * 2:p1 * 2], bap(MODB, p0 * 2, p1 * 2), vecs[:, 64 + p0 * 2:64 + p1 * 2], ALU.add),
                  reads=[bk(MODB), vecs], writes=[(modT, p0)])

            mod_part(0, 8, 0)
            if stop == -5:
                return
            V(lambda e: e.scalar_tensor_tensor(der[:, 0:8], modT[:, 8:16], 1.0, vecs[:, V_GPRE:V_GPRE + 8], ALU.add, ALU.mult),
              reads=[modT, vecs], writes=[(der, 0)])
            V(lambda e: e.tensor_copy(der[:, 8:16], modT[:, 0:8]), reads=[modT], writes=[(der, 8)])

            def mod_late():
                mod_part(8, 24, 7, cast="pool")
                V(lambda e: e.tensor_tensor(der[:, 16:24], modT[:, 16:24], vecs[:, V_GPOST:V_GPOST + 8], ALU.mult),
                  reads=[modT, vecs], writes=[(der, 16)])
                V(lambda e: e.scalar_tensor_tensor(der[:, 24:32], modT[:, 32:40], 1.0, vecs[:, V_GPRE2:V_GPRE2 + 8], ALU.add, ALU.mult),
                  reads=[modT, vecs], writes=[(der, 24)])
                V(lambda e: e.tensor_copy(der[:, 32:40], modT[:, 24:32]), reads=[modT], writes=[(der, 32)])
                V(lambda e: e.tensor_tensor(der[:, 40:48], modT[:, 40:48], vecs[:, V_GPOST2:V_GPOST2 + 8], ALU.mult),
                  reads=[modT, vecs], writes=[(der, 40)])
            DER_ALL = [(der, 0), (der, 8), (der, 16), (der, 24), (der, 32), (der, 40)]

            TPB = [0, 1]
            tp_rr = [0]
            xb_rr = [0]

            def xb_prep(src_rows_ap, nrows):
                i = xb_rr[0] % 2
                xb_rr[0] += 1
                xb = xblk[i]
                xn = xnb[i]
                dma_in(xb[0:nrows, :], src_rows_ap, xb)
                sm = nxt(small, small_rr)
                A(lambda e: e.activation(xn[0:nrows, :], xb[0:nrows, :], AF.Square, accum_out=sm[0:nrows, 0:1]),
                  reads=[xb], writes=[xn, (sm, 0)])
                A(lambda e: e.activation(sm[0:nrows, 3:4], sm[0:nrows, 0:1], AF.Sqrt, bias=EPS, scale=1.0 / D),
                  reads=[(sm, 0)], writes=[(sm, 3)])
                V(lambda e: e.reciprocal(sm[0:nrows, 2:3], sm[0:nrows, 3:4]), reads=[(sm, 3)], writes=[(sm, 2)])
                V(lambda e: e.tensor_scalar(xn[0:nrows, :], xb[0:nrows, :], sm[0:nrows, 2:3], None, ALU.mult),
                  reads=[xb, (sm, 2)], writes=[xn])
                return (xn, nrows)

            def xb_trans(hd, hT, col0, gsc, shc):
                xn, nrows = hd
                b = TPB[tp_rr[0] % 2]
                tp_rr[0] += 1
                tpv = bap_bf(b)
                for k in range(8):
                    P(lambda e, k=k: e.transpose(tpv[:, k * 128:k * 128 + nrows], xn[0:nrows, k * 128:(k + 1) * 128],
                                                  ident_b[0:nrows, 0:nrows]),
                      reads=[xn, ident_b], writes=[bk(b)], inc=(k == 7))
                for k in range(8):
                    if b == TPB[0]:
                        V(lambda e, k=k: e.tensor_scalar(hT[:, k, col0:col0 + nrows], tpv[:, k * 128:k * 128 + nrows],
                                                         der[:, gsc + k:gsc + k + 1], der[:, shc + k:shc + k + 1],
                                                         ALU.mult, ALU.add),
                          reads=[bk(b), (der, gsc), (der, shc)], writes=[(hT, k)])
                    else:
                        A(lambda e, k=k: e.activation(hT[:, k, col0:col0 + nrows], tpv[:, k * 128:k * 128 + nrows],
                                                      AF.Identity, bias=der[:, shc + k:shc + k + 1],
                                                      scale=der[:, gsc + k:gsc + k + 1]),
                          reads=[bk(b), (der, gsc), (der, shc)], writes=[(hT, k)])

            def x_block_to_hT(src_rows_ap, nrows, hT, col0, gsc, shc):
                xb_trans(xb_prep(src_rows_ap, nrows), hT, col0, gsc, shc)

            def rstd_from_ps(ps_bank, nfeat, ncols=512):
                r = nxt(rstd_t, rstd_rr)
                jt = nxt(tmpf, tmpf_rr)
                A(lambda e: e.activation(jt[:, 0:ncols], bap(ps_bank, 0, ncols), AF.Sqrt, bias=EPS, scale=1.0 / nfeat),
                  reads=[bk(ps_bank)], writes=[jt])
                V(lambda e: e.reciprocal(r[:, 0:ncols], jt[:, 0:ncols]), reads=[jt], writes=[r])
                return r

            if stop == 0:
                return
            oT = sb("oT", [128, 8, NOWN], BF)
            ph12 = es.enter_context(contextlib.ExitStack())
            ph1 = es.enter_context(contextlib.ExitStack())

            def sb12(name, shape, dt=F32):
                return ph12.enter_context(nc.sbuf_tensor("s_" + name, list(shape), dt))

            def sb1(name, shape, dt=F32):
                return ph1.enter_context(nc.sbuf_tensor("s_" + name, list(shape), dt))

            kvn = [sb12("kvn_own", [128, 2, NOWN], BF), sb12("kvn_oth", [128, 2, NOWN], BF)]
            krT = [sb12("kr_own", [64, NOWN], BF), sb12("kr_oth", [64, NOWN], BF)]
            qn = sb12("qn", [128, 3, NOWN], BF)
            CS = sb12("cs_own", [64, 2, NOWN])
            hTs = [sb1("hT%d" % i, [128, 8, 640], BF) for i in range(2)]
            wlat = sb1("wlat", [128, 8, 768], BF)
            cs_tmp = sb1("cs_tmp", [64, 2, 512])
            posi = sb1("posi", [64, 512], I32)
            angs = [sb1("ang%d" % i, [64, 512]) for i in range(4)]
            ni_t = sb1("ni_t", [64, 512], I32)

            for pc, (src, c0, n, d0) in enumerate(((w_in, 2048, 256, 0), (w_in, 2304, 256, 256), (w_in, 2560, 192, 512),
                                                   (w_kr_sw, 0, 64, 704))):
                wb = load_w(src, D, c0, n)
                G(lambda e, wb=wb, n=n, d0=d0: e.tensor_copy(wlat[:, :, d0:d0 + n], wb[:, :, 0:n]),
                  reads=[wb], writes=[(wlat, pc)])
            WL = [(wlat, i) for i in range(4)]
            if stop == 10:
                return

            def rope_tables(pos_t, c0, dst, dcol):
                src = bass.AP(pos_t, c0, [[0, 64], [1, 512]])
                dma_in(posi[:], src, posi)
                a0, a1, a2, a3 = angs
                V(lambda e: e.tensor_copy(a0[:], posi[:]), reads=[posi], writes=[a0])
                V(lambda e: e.tensor_scalar(a0[:], a0[:], invf[:, 0:1], None, ALU.mult), reads=[a0, invf], writes=[a0])
                V(lambda e: e.tensor_scalar(a1[:], a0[:], 1.0 / TWO_PI, None, ALU.mult), reads=[a0], writes=[a1])
                V(lambda e: e.tensor_copy(ni_t[:], a1[:]), reads=[a1], writes=[ni_t])
                V(lambda e: e.tensor_copy(a1[:], ni_t[:]), reads=[ni_t], writes=[a1])
                V(lambda e: e.scalar_tensor_tensor(a2[:], a1[:], -C1, a0[:], ALU.mult, ALU.add), reads=[a1, a0], writes=[a2])
                V(lambda e: e.scalar_tensor_tensor(a2[:], a1[:], -C2, a2[:], ALU.mult, ALU.add), reads=[a1, a2], writes=[a2])
                V(lambda e: e.tensor_scalar(a3[:], a2[:], math.pi, -TWO_PI, ALU.is_gt, ALU.mult), reads=[a2], writes=[a3])
                V(lambda e: e.tensor_tensor(a2[:], a2[:], a3[:], ALU.add), reads=[a2, a3], writes=[a2])
                V(lambda e: e.tensor_scalar(a3[:], a2[:], -math.pi, TWO_PI, ALU.is_lt, ALU.mult), reads=[a2], writes=[a3])
                V(lambda e: e.tensor_tensor(a2[:], a2[:], a3[:], ALU.add), reads=[a2, a3], writes=[a2])
                V(lambda e: e.tensor_scalar(a1[:], a2[:], math.pi / 2, None, ALU.add), reads=[a2], writes=[a1])
                V(lambda e: e.tensor_scalar(a3[:], a1[:], math.pi, -TWO_PI, ALU.is_gt, ALU.mult), reads=[a1], writes=[a3])
                V(lambda e: e.tensor_tensor(a1[:], a1[:], a3[:], ALU.add), reads=[a1, a3], writes=[a1])
                V(lambda e: e.tensor_scalar(a1[:], a1[:], math.pi, -math.pi, ALU.min, ALU.max), reads=[a1], writes=[a1])
                V(lambda e: e.tensor_scalar(a2[:], a2[:], math.pi, -math.pi, ALU.min, ALU.max), reads=[a2], writes=[a2])
                A(lambda e: e.activation(dst[:, 0, dcol:dcol + 512], a1[:], AF.Sin), reads=[a1], writes=[(dst, dcol)])
                A(lambda e: e.activation(dst[:, 1, dcol:dcol + 512], a2[:], AF.Sin, scale=sgn[:, 0:1]),
                  reads=[a2, sgn], writes=[(dst, dcol)])

            def prep1(j_):
                i_, b_ = j_ // 4, j_ % 4
                grp_, t_ = i_ // 4, i_ % 4
                xsrc = x_own if grp_ == 0 else x_oth
                r0 = t_ * 512 + b_ * 128
                return xb_prep(xsrc[r0:r0 + 128, :], 128)

            hd1 = [prep1(0)]
            for grp in range(2):
                pos_t = pos_own if grp == 0 else pos_oth
                for t in range(4):
                    hT = hTs[(grp * 4 + t) % 2]
                    for b in range(4):
                        j_ = (grp * 4 + t) * 4 + b
                        nh = prep1(j_ + 1) if j_ + 1 < 32 else None
                        xb_trans(hd1[0], hT, b * 128, 0, 8)
                        hd1[0] = nh
                    if grp == 0:
                        rope_tables(pos_t, t * 512, CS, t * 512)
                        cs, cc = CS, t * 512
                    else:
                        rope_tables(pos_t, t * 512, cs_tmp, 0)
                        cs, cc = cs_tmp, 0
                    if stop == 12:
                        return
                    hk = [(hT, k) for k in range(8)]
                    for m in range(2):
                        for k in range(8):
                            mm(bap(2 + m), wlat[:, k, 384 + m * 128:384 + (m + 1) * 128], hT[:, k, 0:512],
                               k == 0, k == 7, hk + WL, bk(2 + m), k == 7)
                    for m in range(2):
                        for k in range(8):
                            mm(bap(5 + m)[0:64, :], wlat[:, k, 640 + m * 64:640 + (m + 1) * 64], hT[:, k, 0:512],
                               k == 0, k == 7, hk + WL, bk(5 + m), k == 7)
                    sq = []
                    for m in range(2):
                        s_ = nxt(tmpb, tmpb_rr)
                        A(lambda e, m=m, s_=s_: e.activation(s_[:], bap(2 + m), AF.Square), reads=[bk(2 + m)], writes=[s_])
                        sq.append(s_)
                    for m in range(2):
                        mm(bap(4), ones_b[:], sq[m][:], m == 0, m == 1, [ones_b, sq[m]], bk(4), True)
                    r = rstd_from_ps(4, 256)
                    for m in range(2):
                        V(lambda e, m=m, r=r: e.scalar_tensor_tensor(kvn[grp][:, m, t * 512:(t + 1) * 512], bap(2 + m),
                                                                     vecs[:, V_KVG + m:V_KVG + m + 1], r[:], ALU.mult, ALU.mult),
                          reads=[bk(2 + m), r, vecs], writes=[(kvn[grp], t)])
                    ta = nxt(tmpf, tmpf_rr)
                    tb_ = nxt(tmpf, tmpf_rr)
                    V(lambda e, ta=ta, cs=cs, cc=cc: e.tensor_tensor(ta[0:64, :], bap(5)[0:64, :], cs[:, 0, cc:cc + 512], ALU.mult),
                      reads=[bk(5), (cs, cc)], writes=[ta])
                    V(lambda e, tb_=tb_, cs=cs, cc=cc: e.tensor_tensor(tb_[0:64, :], bap(6)[0:64, :], cs[:, 1, cc:cc + 512], ALU.mult),
                      reads=[bk(6), (cs, cc)], writes=[tb_])
                    V(lambda e, ta=ta, tb_=tb_: e.tensor_tensor(krT[grp][:, t * 512:(t + 1) * 512], ta[0:64, :], tb_[0:64, :], ALU.add),
                      reads=[ta, tb_], writes=[(krT[grp], t)])
                    if stop == 13:
                        return
                    if grp == 0:
                        QB = [2, 3, 7]
                        for m in range(3):
                            for k in range(8):
                                mm(bap(QB[m]), wlat[:, k, m * 128:(m + 1) * 128], hT[:, k, 0:512],
                                   k == 0, k == 7, hk + WL, bk(QB[m]), k == 7)
                        sq = []
                        for m in range(3):
                            s_ = nxt(tmpb, tmpb_rr)
                            A(lambda e, m=m, s_=s_: e.activation(s_[:], bap(QB[m]), AF.Square), reads=[bk(QB[m])], writes=[s_])
                            sq.append(s_)
                        for m in range(3):
                            mm(bap(4), ones_b[:], sq[m][:], m == 0, m == 2, [ones_b, sq[m]], bk(4), True)
                        r = rstd_from_ps(4, 384)
                        for m in range(3):
                            V(lambda e, m=m, r=r: e.scalar_tensor_tensor(qn[:, m, t * 512:(t + 1) * 512], bap(QB[m]),
                                                                         vecs[:, V_QG + m:V_QG + m + 1], r[:], ALU.mult, ALU.mult),
                              reads=[bk(QB[m]), r, vecs], writes=[(qn, t)])
                    if stop == 14:
                        return

            S.barrier()
            ph1.close()
            if stop == 1:
                ph12.close()
                return

            wuq = sb12("wuq", [128, 3, 2048], BF)
            wukv = sb12("wukv", [128, 2, 2048], BF)
            for pc in range(6):
                wb = load_w(w_uq, 384, pc * 256, 256)
                G(lambda e, wb=wb, pc=pc: e.tensor_copy(wuq[:, :, pc * 256:(pc + 1) * 256], wb[:, 0:3, :]),
                  reads=[wb], writes=[(wuq, pc)])
            for pc in range(2):
                wb = load_w(w_uq_sw, 384, pc * 256, 256)
                G(lambda e, wb=wb, pc=pc: e.tensor_copy(wuq[:, :, 1536 + pc * 256:1536 + (pc + 1) * 256], wb[:, 0:3, :]),
                  reads=[wb], writes=[(wuq, 6 + pc)])
            for pc in range(8):
                wb = load_w(w_ukv, 256, pc * 256, 256)
                G(lambda e, wb=wb, pc=pc: e.tensor_copy(wukv[:, :, pc * 256:(pc + 1) * 256], wb[:, 0:2, :]),
                  reads=[wb], writes=[(wukv, pc)])
            KhT = [[sb12("kh%d_%d" % (i, g), [128, NOWN], BF) for g in range(2)] for i in range(1)]
            Vh = [[sb12("vh%d_%d" % (i, g), [128, 16, 128], BF) for g in range(2)] for i in range(1)]
            Qh = [sb12("qh%d" % i, [128, NOWN], BF) for i in range(1)]
            Qr = [sb12("qr%d" % i, [64, NOWN], BF) for i in range(1)]
            Pt = [sb12("pt%d" % i, [128, 512], BF) for i in range(4)]
            pt_rr = [0]
            SB_ = [0, 1, 2]
            s_rr = [0]
            OB = [3, 5]
            LB = [4, 6]
            HBS = [7, 3, 4]
            hb_rr = [0]

            def nhb():
                b_ = HBS[hb_rr[0] % 3]
                hb_rr[0] += 1
                return b_
            evac_rr = [0]

            def evac_copy(dst_ap, src_bank_ap, reads, writes):
                if evac_rr[0] % 2 == 0:
                    V(lambda e: e.tensor_copy(dst_ap, src_bank_ap), reads=reads, writes=writes)
                else:
                    A(lambda e: e.activation(dst_ap, src_bank_ap, AF.Copy), reads=reads, writes=writes)
                evac_rr[0] += 1

            def build_head(h):
                i = 0
                for grp in range(2):
                    for t in range(4):
                        HB = nhb()
                        for k in range(2):
                            mm(bap(HB), wukv[:, k, h * 256:h * 256 + 128], kvn[grp][:, k, t * 512:(t + 1) * 512],
                               k == 0, k == 1, [wukv, kvn[grp]], bk(HB), k == 1)
                        evac_copy(KhT[i][grp][:, t * 512:(t + 1) * 512], bap(HB), [bk(HB)], [(KhT[i][grp], t)])
                    for t in range(4):
                        HB = nhb()
                        for b in range(4):
                            blk = t * 4 + b
                            for k in range(2):
                                mm(bap(HB, b * 128, (b + 1) * 128), kvn[grp][:, k, blk * 128:(blk + 1) * 128],
                                   wukv[:, k, h * 256 + 128:h * 256 + 256],
                                   k == 0, k == 1, [wukv, kvn[grp]], bk(HB), (k == 1 and b == 3))
                        evac_copy(Vh[i][grp][:, t * 4:(t + 1) * 4, :], bap(HB).rearrange("p (b d) -> p b d", d=128),
                                  [bk(HB)], [(Vh[i][grp], t)])
                for t in range(4):
                    HB = nhb()
                    for k in range(3):
                        mm(bap(HB), wuq[:, k, h * 192:h * 192 + 128], qn[:, k, t * 512:(t + 1) * 512],
                           k == 0, k == 2, [wuq, qn], bk(HB), k == 2)
                    evac_copy(Qh[i][:, t * 512:(t + 1) * 512], bap(HB), [bk(HB)], [(Qh[i], t)])
                for t in range(4):
                    HB = nhb()
                    for k in range(3):
                        mm(bap(HB)[0:64, :], wuq[:, k, h * 192 + 128:h * 192 + 192], qn[:, k, t * 512:(t + 1) * 512],
                           k == 0, k == 2, [wuq, qn], bk(HB), k == 2)
                    ta = nxt(tmpf, tmpf_rr)
                    V(lambda e, ta=ta, t=t: e.tensor_tensor(ta[0:64, :], bap(HB)[0:64, :], CS[:, 0, t * 512:(t + 1) * 512], ALU.mult),
                      reads=[bk(HB), CS], writes=[ta])
                    HB = nhb()
                    for k in range(3):
                        mm(bap(HB)[0:64, :], wuq[:, k, 1536 + h * 64:1536 + (h + 1) * 64], qn[:, k, t * 512:(t + 1) * 512],
                           k == 0, k == 2, [wuq, qn], bk(HB), k == 2)
                    tb_ = nxt(tmpf, tmpf_rr)
                    V(lambda e, tb_=tb_, t=t: e.tensor_tensor(tb_[0:64, :], bap(HB)[0:64, :], CS[:, 1, t * 512:(t + 1) * 512], ALU.mult),
                      reads=[bk(HB), CS], writes=[tb_])
                    V(lambda e, ta=ta, tb_=tb_, t=t: e.tensor_tensor(Qr[i][:, t * 512:(t + 1) * 512], ta[0:64, :], tb_[0:64, :], ALU.add),
                      reads=[ta, tb_], writes=[(Qr[i], t)])

            def attend_head(h):
                i = 0
                for g in range(4):
                    ob = OB[g % 2]
                    lb = LB[g % 2]
                    visits = [(J, grp) for J in range(4 * g + 4) for grp in range(2)]
                    pend = []

                    def do_pv(v, first, last):
                        J, grp, c0, pt = v
                        mm(bap(ob, c0, 512), Vh[i][grp][:, J, :], pt[:, c0:512], first, last,
                           [Vh[i][grp], pt], bk(ob), True)
                        mm(bap(lb, c0, 512), ones_b[:], pt[:, c0:512], first, last,
                           [ones_b, pt], bk(lb), True)

                    npv = [0]
                    for vi, (J, grp) in enumerate(visits):
                        j = J - 4 * g
                        c0 = 128 * max(j, 0)
                        sbk = SB_[s_rr[0] % 3]
                        s_rr[0] += 1
                        q0 = g * 512 + c0
                        q1 = (g + 1) * 512
                        masked = j >= 0
                        mm(bap(sbk, c0, 512), KhT[i][grp][:, J * 128:(J + 1) * 128], Qh[i][:, q0:q1],
                           True, False, [KhT[i][grp], Qh[i]], bk(sbk), False)
                        mm(bap(sbk, c0, 512), krT[grp][:, J * 128:(J + 1) * 128], Qr[i][:, q0:q1],
                           False, not masked, [krT[grp], Qr[i]], bk(sbk), not masked)
                        if masked:
                            mk = tri_b if grp == 0 else pair_b
                            mm(bap(sbk, c0, c0 + 128), ident_b[:], mk[:], False, True, [ident_b, mk], bk(sbk), True)
                        pt = nxt(Pt, pt_rr)
                        A(lambda e, pt=pt, sbk=sbk, c0=c0: e.activation(pt[:, c0:512], bap(sbk, c0, 512), AF.Exp, scale=SCALE),
                          reads=[bk(sbk)], writes=[pt])
                        pend.append((J, grp, c0, pt))
                        if len(pend) > 2:
                            v = pend.pop(0)
                            do_pv(v, npv[0] == 0, False)
                            npv[0] += 1
                    while pend:
                        v = pend.pop(0)
                        do_pv(v, npv[0] == 0, len(pend) == 0)
                        npv[0] += 1
                    rl = nxt(rstd_t, rstd_rr)
                    V(lambda e, rl=rl, lb=lb: e.reciprocal(rl[:], bap(lb)), reads=[bk(lb)], writes=[rl])
                    V(lambda e, rl=rl, ob=ob, g=g: e.tensor_tensor(oT[:, h, g * 512:(g + 1) * 512], bap(ob), rl[:], ALU.mult),
                      reads=[bk(ob), rl], writes=[(oT, (h, g))])

            prep = []
            for pc in range(4):
                prep += [(w_in, pc * 256, 0), (w_in, 1024 + pc * 256, 0)]
            for pc in range(4):
                prep += [(w_conv_out, pc * 256, 0)]
            for pc in range(4):
                prep += [(w_attn_out, pc * 256, 0), (w_in, 2752 + pc * 256, 0), (w_in, 3776 + pc * 256, 0)]
            for pc in range(4):
                prep += [(w_out, pc * 256, 0)]
            for pc in range(16):
                prep += [(w_mlp_in, pc * 256, 0)]
            for pc in range(4):
                for rr_ in range(4):
                    prep += [(w_mlp_out, pc * 256, rr_ * 1024)]
            prep_tok = {}

            def do_prep():
                for i, (src, c0, r0) in enumerate(prep):
                    wb = load_w(src, D, c0, 256, r0=r0, cast="pool")
                    S.dma("sp", lambda e, wb=wb, i=i: e.dma_start(out=wq[i], in_=wb[:].rearrange("p k c -> p (k c)")),
                          reads=[wb], writes=[(wq_key, i)])

            do_prep()
            build_head(0)
            for h in range(8):
                attend_head(h)
                if h + 1 < 8:
                    build_head(h + 1)
            mod_late()

            S.barrier()
            ph12.close()
            if stop == 2:
                return
            wpool = [wbf[0][:], wbf[1][:]]
            for i_ in range(NST):
                fl = wst[i_].bitcast(BF)[:].rearrange("p k c -> p (k c)")
                wpool += [fl[:, 0:2048].rearrange("p (k c) -> p k c", c=256), fl[:, 2048:4096].rearrange("p (k c) -> p k c", c=256)]
            wp_rr = [0]
            pidx = {(src_.tensor.name, c0_, r0_): i_ for i_, (src_, c0_, r0_) in enumerate(prep)}

            def load_wq(src, rows, c0, ncols, r0=0):
                i = pidx[(src.tensor.name, c0, r0)]
                buf = wpool[wp_rr[0] % len(wpool)]
                wp_rr[0] += 1
                S.dma("sp", lambda e: e.dma_start(out=buf.rearrange("p k c -> p (k c)"), in_=wq[i]), writes=[buf])
                return buf

            xT = sb("xT", [128, 8, 512])
            yT = sb("yT", [128, 8, 512])
            hTe = sb("hTe", [128, 8, 640], BF)
            uext = sb("uext", [128, 8, 640], BF)
            arena = sb("arena", [128, 16384], BF)
            hid = arena[:, :].rearrange("p (j t) -> p j t", t=512)
            ucv = arena.bitcast(F32)[:, 0:4096].rearrange("p (c t) -> p c t", t=512)
            diag = [arena[:, 8192 + i * 3968:8192 + (i + 1) * 3968].rearrange("p (k m) -> p k m", m=128) for i in range(2)]
            sh8 = sb("sh8", [128, 8, 512], BF)
            actT = sh8
            mT = sh8
            h2T = sh8
            yaT = sb("yaT", [128, 8, 512], BF)
            oblk = xblk
            stat_s = sb("stat_s", [128, 512])
            stat_n = sb("stat_n", [128, 512])
            sb_sig = sb("sb_sig", [128, 640])

            deferred = []

            def flush_def():
                for f_ in deferred:
                    f_()
                deferred.clear()

            def stats_accum(ps_b, src_ap, reads, idx, n, defer=False):
                s_ = nxt(tmpb, tmpb_rr)
                A(lambda e: e.activation(s_[:], src_ap, AF.Square), reads=reads, writes=[s_])
                f_ = lambda s_=s_: mm(bap(ps_b), ones_b[:], s_[:], idx == 0, idx == n - 1, [ones_b, s_], bk(ps_b), True)
                if defer:
                    deferred.append(f_)
                else:
                    f_()

            def fh_blocks(g_):
                lst = []
                for b in range(4):
                    blk = g_ * 4 + b
                    lst.append((x_halo[blk * 32:(blk + 1) * 32, :], 32, b * 160))
                    lst.append((x_own[blk * 128:(blk + 1) * 128, :], 128, b * 160 + 32))
                return lst

            def front_h(g_):
                lst = fh_blocks(g_)
                hd = xb_prep(lst[0][0], lst[0][1])
                for i_ in range(8):
                    nh = xb_prep(lst[i_ + 1][0], lst[i_ + 1][1]) if i_ + 1 < 8 else None
                    xb_trans(hd, hTe, lst[i_][2], 0, 8)
                    hd = nh

            front_h(0)
            for g in range(4):
                for b in range(4):
                    blk = g * 4 + b
                    xb = xblk[xb_rr[0] % 2]
                    xb_rr[0] += 1
                    dma_in(xb[:], x_own[blk * 128:(blk + 1) * 128, :], xb)
                    for half in range(2):
                        tb_i = 2 + half
                        for kk in range(4):
                            k = half * 4 + kk
                            P(lambda e, k=k, kk=kk, tb_i=tb_i, xb=xb: e.transpose(bap(tb_i, kk * 128, (kk + 1) * 128),
                                                                                    xb[:, k * 128:(k + 1) * 128], ident_f[:]),
                              reads=[xb, ident_f], writes=[bk(tb_i)], inc=(kk == 3))
                        evac_copy(xT[:, half * 4:(half + 1) * 4, b * 128:(b + 1) * 128],
                                  bap(tb_i).rearrange("p (k t) -> p k t", t=128), [bk(tb_i)], [(xT, (half, b))])
                hk = [(hTe, k) for k in range(8)]
                def glu_chunk(c, wa, wb2, cc):
                    for (wt, d) in ((wa, 0), (wb2, 1)):
                        for (n0, n1, hb) in ((0, 512, 0), (512, 640, 1)):
                            for k in range(8):
                                mm(PD[d][:, hb * 512:hb * 512 + (n1 - n0)], wt[:, k, cc * 128:(cc + 1) * 128],
                                   hTe[:, k, n0:n1], k == 0, k == 7, hk + [wt], (PD[d], hb), k == 7)
                    sg = sb_sig
                    A(lambda e: e.activation(sg[:, 0:640], PD[1][:, 0:640], AF.Sigmoid),
                      reads=[(PD[1], 0), (PD[1], 1)], writes=[sg])
                    V(lambda e: e.tensor_tensor(uext[:, c, :], PD[0][:, 0:640], sg[:, 0:640], ALU.mult),
                      reads=[(PD[0], 0), (PD[0], 1), sg], writes=[(uext, c)])
                    if g == 0:
                        V(lambda e: e.tensor_scalar(uext[:, c, 0:32], uext[:, c, 0:32], halom[:, 0:1], None, ALU.mult),
                          reads=[(uext, c), halom], writes=[(uext, c)])

                def conv_chunk(c):
                    dg = diag[c % 2]
                    for k in range(31):
                        G(lambda e: e.tensor_scalar(dg[:, k, :], ident_b[:], cwT[:, c, k:k + 1], 1.0, ALU.mult, ALU.mult),
                          reads=[ident_b, cwT], writes=[(arena, None) if (c == 0 and k == 0) else (arena, ("d", c % 2, k))])
                    uv = uext[:, c, :].rearrange("p (b w) -> p b w", w=160)
                    cb = 4 + (c % 2)
                    for k in range(31):
                        mm(bap(cb).rearrange("p (b w) -> p b w", w=128), dg[:, k, :], uv[:, :, 2 + k:2 + k + 128],
                           k == 0, k == 30, [(arena, ("d", c % 2, k)), (uext, c)], bk(cb), k == 30)
                    flush_def()
                    A(lambda e: e.activation(ucv[:, c, :], bap(cb), AF.Identity, bias=vecs[:, V_CB + c:V_CB + c + 1]),
                      reads=[bk(cb), vecs], writes=[(arena, ("u", c))])
                    ub_ = nxt(tmpb, tmpb_rr)
                    V(lambda e: e.tensor_copy(ub_[:], ucv[:, c, :]), reads=[(arena, ("u", c))], writes=[ub_])
                    deferred.append(lambda ub_=ub_, c=c: mm(bap(6), ones_b[:], ub_[:], c == 0, c == 7, [ones_b, ub_], bk(6), True))
                    stats_accum(7, ucv[:, c, :], [(arena, ("u", c))], c, 8, defer=True)

                for pc in range(4):
                    wa = load_wq(w_in, D, pc * 256, 256)
                    wb2 = load_wq(w_in, D, 1024 + pc * 256, 256)
                    for cc in range(2):
                        c = pc * 2 + cc
                        glu_chunk(c, wa, wb2, cc)
                        if c >= 1:
                            conv_chunk(c - 1)
                conv_chunk(7)
                flush_def()
                mean = stat_s
                nmr = stat_n
                A(lambda e: e.activation(mean[:], bap(6), AF.Copy, scale=1.0 / D), reads=[bk(6)], writes=[mean])
                jt = nxt(tmpf, tmpf_rr)
                V(lambda e, jt=jt: e.tensor_tensor(jt[:], mean[:], mean[:], ALU.mult), reads=[mean], writes=[jt])
                jt2 = nxt(tmpf, tmpf_rr)
                V(lambda e, jt=jt, jt2=jt2: e.scalar_tensor_tensor(jt2[:], bap(7), 1.0 / D, jt[:], ALU.mult, ALU.subtract),
                  reads=[bk(7), jt], writes=[jt2])
                V(lambda e, jt2=jt2: e.tensor_scalar(jt2[:], jt2[:], 0.0, None, ALU.max), reads=[jt2], writes=[jt2])
                A(lambda e, jt=jt, jt2=jt2: e.activation(jt[:], jt2[:], AF.Sqrt, bias=EPS), reads=[jt2], writes=[jt])
                rln = nxt(rstd_t, rstd_rr)
                V(lambda e, jt=jt, rln=rln: e.reciprocal(rln[:], jt[:]), reads=[jt], writes=[rln])
                V(lambda e, rln=rln: e.scalar_tensor_tensor(nmr[:], mean[:], -1.0, rln[:], ALU.mult, ALU.mult),
                  reads=[mean, rln], writes=[nmr])
                for c in range(8):
                    jt = nxt(tmpf, tmpf_rr)
                    V(lambda e, c=c, jt=jt, rln=rln: e.tensor_tensor(jt[:], ucv[:, c, :], rln[:], ALU.mult),
                      reads=[(arena, ("u", c)), rln], writes=[jt])
                    V(lambda e, jt=jt: e.tensor_tensor(jt[:], jt[:], nmr[:], ALU.add), reads=[jt, nmr], writes=[jt])
                    A(lambda e, c=c, jt=jt: e.activation(actT[:, c, :], jt[:], AF.Silu,
                                                         bias=vecs[:, V_CNB + c:V_CNB + c + 1], scale=vecs[:, V_CG + c:V_CG + c + 1]),
                      reads=[jt, vecs, vecs], writes=[(actT, c)])
                ak = [(actT, k) for k in range(8)]
                for pc in range(4):
                    wb = load_wq(w_conv_out, D, pc * 256, 256)
                    for cc in range(2):
                        m = pc * 2 + cc
                        bb = 4 + (m % 2)
                        for k in range(8):
                            mm(bap(bb), wb[:, k, cc * 128:(cc + 1) * 128], actT[:, k, :], k == 0, k == 7, ak + [wb], bk(bb), k == 7)
                        evac_copy(yaT[:, m, :], bap(bb), [bk(bb)], [(yaT, m)])
                hown = lambda k: hTe[:, k, :].rearrange("p (b w) -> p b w", w=160)[:, :, 32:160]
                for pc in range(4):
                    wao = load_wq(w_attn_out, D, pc * 256, 256)
                    for cc in range(2):
                        for hh in range(8):
                            mm(bap(2 + cc), wao[:, hh, cc * 128:(cc + 1) * 128], oT[:, hh, g * 512:(g + 1) * 512],
                               hh == 0, hh == 7, [oT, wao], bk(2 + cc), hh == 7)
                    wga = load_wq(w_in, D, 2752 + pc * 256, 256)
                    sas = []
                    for cc in range(2):
                        m = pc * 2 + cc
                        for k in range(8):
                            mm(bap(cc).rearrange("p (b w) -> p b w", w=128), wga[:, k, cc * 128:(cc + 1) * 128], hown(k),
                               k == 0, k == 7, hk + [wga], bk(cc), k == 7)
                        sa = nxt(tmpf, tmpf_rr)
                        A(lambda e, sa=sa, cc=cc: e.activation(sa[:], bap(cc), AF.Sigmoid), reads=[bk(cc)], writes=[sa])
                        V(lambda e, sa=sa, m=m: e.tensor_tensor(sa[:], sa[:], yaT[:, m, :], ALU.mult),
                          reads=[sa, (yaT, m)], writes=[sa])
                        sas.append(sa)
                    wgb = load_wq(w_in, D, 3776 + pc * 256, 256)
                    for cc in range(2):
                        m = pc * 2 + cc
                        for k in range(8):
                            mm(bap(cc).rearrange("p (b w) -> p b w", w=128), wgb[:, k, cc * 128:(cc + 1) * 128], hown(k),
                               k == 0, k == 7, hk + [wgb], bk(cc), k == 7)
                        sb2 = nxt(tmpf, tmpf_rr)
                        A(lambda e, sb2=sb2, cc=cc: e.activation(sb2[:], bap(cc), AF.Sigmoid), reads=[bk(cc)], writes=[sb2])
                        V(lambda e, sb2=sb2, cc=cc: e.tensor_tensor(sb2[:], bap(2 + cc), sb2[:], ALU.mult),
                          reads=[bk(2 + cc), sb2], writes=[sb2])
                        V(lambda e, sa=sas[cc], sb2=sb2, m=m: e.tensor_tensor(mT[:, m, :], sa[:], sb2[:], ALU.add),
                          reads=[sas[cc], sb2], writes=[(mT, m)])
                mk_ = [(mT, k) for k in range(8)]
                for pc in range(4):
                    wb = load_wq(w_out, D, pc * 256, 256)
                    for cc in range(2):
                        m = pc * 2 + cc
                        bb = 4 + (m % 2)
                        for k in range(8):
                            mm(bap(bb), wb[:, k, cc * 128:(cc + 1) * 128], mT[:, k, :], k == 0, k == 7, mk_ + [wb], bk(bb), k == 7)
                        flush_def()
                        A(lambda e, m=m, bb=bb: e.activation(yT[:, m, :], bap(bb), AF.Copy), reads=[bk(bb)], writes=[(yT, m)])
                        stats_accum(6, yT[:, m, :], [(yT, m)], m, 8, defer=True)
                flush_def()
                if debug == "m" and g == 3:
                    V(lambda e: e.tensor_copy(hTe[:, :, 0:512], mT[:]), reads=[mT], writes=[hTe])
                    V(lambda e: e.tensor_copy(uext[:, :, 0:512], yT[:]), reads=[yT], writes=[uext])
                r1 = rstd_from_ps(6, D)
                for k in range(8):
                    jt = nxt(tmpf, tmpf_rr)
                    V(lambda e, k=k, jt=jt, r1=r1: e.tensor_tensor(jt[:], yT[:, k, :], r1[:], ALU.mult),
                      reads=[(yT, k), r1], writes=[jt])
                    V(lambda e, k=k, jt=jt: e.scalar_tensor_tensor(xT[:, k, :], jt[:], der[:, 16 + k:17 + k], xT[:, k, :], ALU.mult, ALU.add),
                      reads=[jt, (der, 16), xT], writes=[xT])
                    stats_accum(7, xT[:, k, :], [xT], k, 8)
                r2 = rstd_from_ps(7, D)
                for k in range(8):
                    jt = nxt(tmpf, tmpf_rr)
                    V(lambda e, k=k, jt=jt, r2=r2: e.scalar_tensor_tensor(jt[:], xT[:, k, :], der[:, 24 + k:25 + k], r2[:], ALU.mult, ALU.mult),
                      reads=[xT, (der, 24), r2], writes=[jt])
                    A(lambda e, k=k, jt=jt: e.activation(h2T[:, k, :], jt[:], AF.Identity, bias=der[:, 32 + k:33 + k]),
                      reads=[jt, (der, 32)], writes=[(h2T, k)])
                h2k = [(h2T, k) for k in range(8)]
                nlst = fh_blocks(g + 1) if g + 1 < 4 else None
                nhd = {}
                for pc in range(16):
                    if nlst is not None and pc % 2 == 0:
                        if pc >= 2:
                            xb_trans(nhd[pc // 2 - 1], hTe, nlst[pc // 2 - 1][2], 0, 8)
                        nhd[pc // 2] = xb_prep(nlst[pc // 2][0], nlst[pc // 2][1])
                    wb = load_wq(w_mlp_in, D, pc * 256, 256)
                    for cc in range(2):
                        j = pc * 2 + cc
                        bb = 4 + (j % 4)
                        for k in range(8):
                            mm(bap(bb), wb[:, k, cc * 128:(cc + 1) * 128], h2T[:, k, :], k == 0, k == 7, h2k + [wb], bk(bb), k == 7)
                        jt = nxt(tmpf, tmpf_rr)
                        A(lambda e, jt=jt, bb=bb: e.activation(jt[:], bap(bb), AF.Relu), reads=[bk(bb)], writes=[jt])
                        V(lambda e, jt=jt, j=j: e.tensor_tensor(hid[:, j, :], jt[:], jt[:], ALU.mult), reads=[jt],
                          writes=[(arena, None) if j == 0 else (arena, ("h", j))])
                if nlst is not None:
                    xb_trans(nhd[7], hTe, nlst[7][2], 0, 8)
                for pc in range(4):
                    for cc in range(2):
                        pass
                    wbs = []
                    for rr_ in range(4):
                        wb = load_wq(w_mlp_out, D, pc * 256, 256, r0=rr_ * 1024)
                        for cc in range(2):
                            bb = 4 + cc
                            for k in range(8):
                                kk = rr_ * 8 + k
                                mm(bap(bb), wb[:, k, cc * 128:(cc + 1) * 128], hid[:, kk, :], kk == 0, kk == 31,
                                   [arena, wb], bk(bb), (k == 7))
                    flush_def()
                    for cc in range(2):
                        m = pc * 2 + cc
                        bb = 4 + cc
                        A(lambda e, m=m, bb=bb: e.activation(yT[:, m, :], bap(bb), AF.Copy), reads=[bk(bb)], writes=[(yT, m)])
                        stats_accum(6, yT[:, m, :], [(yT, m)], m, 8, defer=True)
                flush_def()
                r3 = rstd_from_ps(6, D)
                for k in range(8):
                    jt = nxt(tmpf, tmpf_rr)
                    V(lambda e, k=k, jt=jt, r3=r3: e.tensor_tensor(jt[:], yT[:, k, :], r3[:], ALU.mult),
                      reads=[(yT, k), r3], writes=[jt])
                    V(lambda e, k=k, jt=jt: e.scalar_tensor_tensor(yT[:, k, :], jt[:], der[:, 40 + k:41 + k], xT[:, k, :], ALU.mult, ALU.add),
                      reads=[jt, (der, 40), xT], writes=[(yT, k)])
                for b in range(4):
                    ob_ = oblk[b % 2]
                    for half in range(2):
                        tb_i = 2 + half
                        for kk in range(4):
                            k = half * 4 + kk
                            P(lambda e, k=k, kk=kk, tb_i=tb_i, b=b: e.transpose(bap(tb_i, kk * 128, (kk + 1) * 128),
                                                                                 yT[:, k, b * 128:(b + 1) * 128], ident_f[:]),
                              reads=[yT, ident_f], writes=[bk(tb_i)], inc=(kk == 3))
                        evac_copy(ob_[:, half * 512:(half + 1) * 512], bap(tb_i), [bk(tb_i)], [(ob_, half)])
                    blk = g * 4 + b
                    tok = S.dma("act", lambda e, ob_=ob_, blk=blk: e.dma_start(out=out[blk * 128:(blk + 1) * 128, :], in_=ob_[:]),
                                reads=[ob_])
                    out_toks.append(tok)


        run_phases()
        if stop is not None:
            out_toks.append(S.dma("act", lambda e: e.dma_start(out=out[0:128, :], in_=xblk[0][:]), reads=[xblk[0]]))
        last = {}
        for (s, v) in out_toks:
            last[s] = max(last.get(s, 0), v)
        S.wait_all("act", list(last.items()))

        with nc.Block() as block:
            def emit(engname, eng):
                for (waits, fn, inc) in S.q[engname]:
                    for (s, v) in waits:
                        eng.wait_ge(sems[s], v)
                    if fn is None:
                        continue
                    ins = fn(eng)
                    if inc is not None:
                        ins.then_inc(sems[inc[0]], inc[1])

            @block.sync
            def _(e):
                emit("sp", e)

            @block.tensor
            def _(e):
                emit("pe", e)

            @block.scalar
            def _(e):
                emit("act", e)

            @block.vector
            def _(e):
                emit("dve", e)

            @block.gpsimd
            def _(e):
                emit("pool", e)
    return nc


def _prep_inputs(inputs):
    x = np.asarray(inputs["x"], np.float32)
    pos = np.asarray(inputs["positions"], np.int32)
    c = np.asarray(inputs["c"], np.float32)
    w_in = np.ascontiguousarray(np.asarray(inputs["w_in"], np.float32)[0])
    w_uq = np.ascontiguousarray(np.asarray(inputs["w_uq"], np.float32)[0])
    kr = w_in[:, 2688:2752]
    w_kr_sw = np.ascontiguousarray(np.concatenate([kr[:, 32:64], kr[:, 0:32]], axis=1))
    uq3 = w_uq.reshape(384, 8, 192)[:, :, 128:192]
    w_uq_sw = np.ascontiguousarray(np.concatenate([uq3[:, :, 32:64], uq3[:, :, 0:32]], axis=2).reshape(384, 512))
    k_idx = np.arange(128)[:, None]
    q_idx = np.arange(128)[None, :]
    trimask = np.where(k_idx <= q_idx, 0.0, NEG).astype(np.float32)
    ident = np.eye(128, dtype=np.float32)
    inv = (1.0 / (np.float32(10000.0) ** (np.arange(0, 64, 2, dtype=np.float32) / np.float32(64)))).astype(np.float32)
    invf = np.concatenate([inv, inv]).reshape(64, 1).astype(np.float32)
    sgn = np.concatenate([-np.ones(32), np.ones(32)]).reshape(64, 1).astype(np.float32)
    shared = {
        "trimask": trimask, "ident": ident, "invf": invf, "sgn": sgn,
        "w_ada": np.ascontiguousarray(inputs["w_ada"][0], np.float32),
        "b_ada": np.ascontiguousarray(inputs["b_ada"][0], np.float32),
        "g_pre_mix": np.ascontiguousarray(inputs["g_pre_mix"][0], np.float32),
        "g_post_mix": np.ascontiguousarray(inputs["g_post_mix"][0], np.float32),
        "g_pre_mlp": np.ascontiguousarray(inputs["g_pre_mlp"][0], np.float32),
        "g_post_mlp": np.ascontiguousarray(inputs["g_post_mlp"][0], np.float32),
        "w_in": w_in, "w_kr_sw": w_kr_sw,
        "conv_w": np.ascontiguousarray(inputs["conv_w"][0], np.float32),
        "conv_b": np.ascontiguousarray(inputs["conv_b"][0], np.float32),
        "conv_norm_g": np.ascontiguousarray(inputs["conv_norm_g"][0], np.float32),
        "conv_norm_b": np.ascontiguousarray(inputs["conv_norm_b"][0], np.float32),
        "w_conv_out": np.ascontiguousarray(inputs["w_conv_out"][0], np.float32),
        "q_norm_g": np.ascontiguousarray(inputs["q_norm_g"][0], np.float32),
        "w_uq": w_uq, "w_uq_sw": w_uq_sw,
        "kv_norm_g": np.ascontiguousarray(inputs["kv_norm_g"][0], np.float32),
        "w_ukv": np.ascontiguousarray(inputs["w_ukv"][0], np.float32),
        "w_attn_out": np.ascontiguousarray(inputs["w_attn_out"][0], np.float32),
        "w_out": np.ascontiguousarray(inputs["w_out"][0], np.float32),
        "w_mlp_in": np.ascontiguousarray(inputs["w_mlp_in"][0], np.float32),
        "w_mlp_out": np.ascontiguousarray(inputs["w_mlp_out"][0], np.float32),
    }
    in_maps = []
    for core in range(8):
        b, p = core // 2, core % 2
        xb = x[b].reshape(32, 128, D)
        pb = pos[b].reshape(32, 128)
        own = [2 * i + p for i in range(16)]
        oth = [2 * i + 1 - p for i in range(16)]
        halo = np.zeros((16, 32, D), np.float32)
        for i in range(16):
            st = own[i] * 128
            if st > 0:
                halo[i] = x[b, st - 32:st]
        m = dict(shared)
        m["x_own"] = np.ascontiguousarray(xb[own].reshape(NOWN, D))
        m["x_oth"] = np.ascontiguousarray(xb[oth].reshape(NOWN, D))
        m["x_halo"] = np.ascontiguousarray(halo.reshape(512, D))
        m["pos_own"] = np.ascontiguousarray(pb[own].reshape(NOWN))
        m["pos_oth"] = np.ascontiguousarray(pb[oth].reshape(NOWN))
        m["c"] = np.ascontiguousarray(c[b])
        m["pairmask"] = np.full((128, 128), 0.0 if p == 1 else NEG, np.float32)
        m["halomask"] = np.full((128, 1), 1.0 if p == 1 else 0.0, np.float32)
        in_maps.append(m)
    return in_maps


def kernel(**inputs):
    in_maps = _prep_inputs(inputs)
    nc = build_nc()
    res = run_bass_kernel_spmd(nc, in_maps, core_ids=list(range(8)))
    outf = np.zeros((4, 32, 128, D), np.float32)
    for core in range(8):
        b, p = core // 2, core % 2
        o = np.asarray(res.results[core]["out"]).reshape(16, 128, D)
        for i in range(16):
            outf[b, 2 * i + p] = o[i]
    return outf.reshape(4, 4096, D)
```

```python
import contextlib
import math
import numpy as np
import concourse.bass as bass
import concourse.mybir as mybir
from concourse.bass_utils import run_bass_kernel_spmd

F32 = mybir.dt.float32
BF = mybir.dt.bfloat16
I32 = mybir.dt.int32
AF = mybir.ActivationFunctionType
ALU = mybir.AluOpType

D = 1024
KC = 8
NOWN = 2048
EPS = 1e-6
NEG = -30000.0
SCALE = 1.0 / math.sqrt(192.0)
TWO_PI = 2.0 * math.pi
C1 = 6.28125
C2 = TWO_PI - 6.28125

ENGS = ("pe", "act", "dve", "pool", "sp")
NDMA = 12


class _Rec:
    def __init__(self):
        self.call = None

    def __getattr__(self, name):
        def f(*a, **k):
            self.call = (name, a, k)
            return self
        return f


def _bind(fn):
    if fn is None:
        return None
    r = _Rec()
    fn(r)
    name, a, k = r.call
    return lambda eng: getattr(eng, name)(*a, **k)


class Sched:
    def __init__(self):
        self.q = {e: [] for e in ENGS}
        self.cnt = {e: 0 for e in ENGS}
        self.waited = {e: {} for e in ENGS}
        self.state = {}
        self.dma_tot = [0] * NDMA
        self.dma_rr = 0
        self.all_dma_tokens = {}

    def _entries(self, buf, key, create):
        d = self.state.setdefault(id(buf), {})
        if key is None:
            if create and None not in d:
                d[None] = {"w": None, "r": {}}
            return list(d.values()) if not create else list(d.values())
        out = []
        if key not in d and create:
            d[key] = {"w": None, "r": {}}
        if key in d:
            out.append(d[key])
        if None in d:
            out.append(d[None])
        return out

    def _deps(self, reads, writes):
        deps = {}

        def add(tok):
            if tok is None:
                return
            s, v = tok
            if deps.get(s, 0) < v:
                deps[s] = v

        for (b, k) in reads:
            for st in self._entries(b, k, False):
                add(st["w"])
        for (b, k) in writes:
            for st in self._entries(b, k, False):
                add(st["w"])
                for s, v in st["r"].items():
                    add((s, v))
        return deps

    def _commit(self, reads, writes, tok):
        for (b, k) in reads:
            d = self.state.setdefault(id(b), {})
            if k not in d:
                d[k] = {"w": None, "r": {}}
            st = d[k]
            s, v = tok
            if st["r"].get(s, 0) < v:
                st["r"][s] = v
        for (b, k) in writes:
            d = self.state.setdefault(id(b), {})
            if k is None:
                d.clear()
            d[k] = {"w": tok, "r": {}}

    def _norm(self, lst):
        out = []
        for x in lst:
            if isinstance(x, tuple):
                out.append(x)
            else:
                out.append((x, None))
        return out

    def op(self, eng, fn, reads=(), writes=(), inc=True):
        fn = _bind(fn)
        reads = self._norm(reads)
        writes = self._norm(writes)
        deps = self._deps(reads, writes)
        waits = []
        for s, v in deps.items():
            if s == eng and eng == "pe":
                continue
            if self.waited[eng].get(s, 0) >= v:
                continue
            self.waited[eng][s] = v
            waits.append((s, v))
        if inc:
            self.cnt[eng] += 1
            tok = (eng, self.cnt[eng])
            self.q[eng].append((waits, fn, (eng, 1)))
        else:
            tok = (eng, self.cnt[eng] + 1)
            self.q[eng].append((waits, fn, None))
        self._commit(reads, writes, tok)
        return tok

    def dma(self, eng, fn, reads=(), writes=()):
        fn = _bind(fn)
        reads = self._norm(reads)
        writes = self._norm(writes)
        j = self.dma_rr
        self.dma_rr = (self.dma_rr + 1) % NDMA
        sem = "d%d" % j
        deps = self._deps(reads, writes)
        if self.dma_tot[j] > 0:
            if deps.get(sem, 0) < self.dma_tot[j]:
                deps[sem] = self.dma_tot[j]
        waits = []
        for s, v in deps.items():
            if self.waited[eng].get(s, 0) >= v:
                continue
            self.waited[eng][s] = v
            waits.append((s, v))
        self.dma_tot[j] += 16
        tok = (sem, self.dma_tot[j])
        self.q[eng].append((waits, fn, (sem, 16)))
        self._commit(reads, writes, tok)
        return tok

    def barrier(self):
        toks = [(e, self.cnt[e]) for e in ENGS if self.cnt[e] > 0]
        toks += [("d%d" % j, self.dma_tot[j]) for j in range(NDMA) if self.dma_tot[j] > 0]
        for e in ENGS:
            self.wait_all(e, [t for t in toks if t[0] != e])
        self.state = {}

    def wait_all(self, eng, toks):
        waits = []
        for (s, v) in toks:
            if self.waited[eng].get(s, 0) >= v:
                continue
            self.waited[eng][s] = v
            waits.append((s, v))
        self.q[eng].append((waits, None, None))


def build_nc(debug=None, stop=None):
    nc = bass.Bass("TRN2", target_bir_lowering=False)
    S = Sched()

    def din(name, shape, dt=F32):
        return nc.dram_tensor(name, list(shape), dt, kind="ExternalInput").ap()

    x_own = din("x_own", [NOWN, D])
    x_oth = din("x_oth", [NOWN, D])
    x_halo = din("x_halo", [512, D])
    pos_own = nc.dram_tensor("pos_own", [NOWN], I32, kind="ExternalInput")
    pos_oth = nc.dram_tensor("pos_oth", [NOWN], I32, kind="ExternalInput")
    c_in = din("c", [D])
    pairmask_in = din("pairmask", [128, 128])
    trimask_in = din("trimask", [128, 128])
    ident_in = din("ident", [128, 128])
    halomask_in = din("halomask", [128, 1])
    invf_in = din("invf", [64, 1])
    sgn_in = din("sgn", [64, 1])
    w_ada = din("w_ada", [D, 6 * D])
    b_ada = din("b_ada", [6 * D])
    g_pre_mix = din("g_pre_mix", [D])
    g_post_mix = din("g_post_mix", [D])
    g_pre_mlp = din("g_pre_mlp", [D])
    g_post_mlp = din("g_post_mlp", [D])
    w_in = din("w_in", [D, 4800])
    w_kr_sw = din("w_kr_sw", [D, 64])
    conv_w = din("conv_w", [31, D])
    conv_b = din("conv_b", [D])
    conv_norm_g = din("conv_norm_g", [D])
    conv_norm_b = din("conv_norm_b", [D])
    w_conv_out = din("w_conv_out", [D, D])
    q_norm_g = din("q_norm_g", [384])
    w_uq = din("w_uq", [384, 1536])
    w_uq_sw = din("w_uq_sw", [384, 512])
    kv_norm_g = din("kv_norm_g", [256])
    w_ukv = din("w_ukv", [256, 2048])
    w_attn_out = din("w_attn_out", [D, D])
    w_out = din("w_out", [D, D])
    w_mlp_in = din("w_mlp_in", [D, 4 * D])
    w_mlp_out = din("w_mlp_out", [4 * D, D])
    out = nc.dram_tensor("out", [NOWN, D], F32, kind="ExternalOutput").ap()
    wq = nc.dram_tensor("wq", [60, 128, 2048], BF, kind="Internal").ap()
    wq_key = object()
    dbg = None

    es = contextlib.ExitStack()
    with es:
        def sb(name, shape, dt=F32):
            return es.enter_context(nc.sbuf_tensor("s_" + name, list(shape), dt))

        sems = {}
        for e in ENGS:
            sems[e] = es.enter_context(nc.semaphore("sem_" + e))
        for j in range(NDMA):
            sems["d%d" % j] = es.enter_context(nc.semaphore("sem_d%d" % j))

        PD = [es.enter_context(nc.psum_tensor("pd%d" % i, [128, 1024], F32)) for i in range(4)]

        def bank(i):
            t = PD[i // 2]
            h = i % 2
            return t, h

        def bk(i):
            t, h = bank(i)
            return (t, h)

        def bap(i, c0=0, c1=512):
            t, h = bank(i)
            return t[:, h * 512 + c0: h * 512 + c1]

        def bap_bf(i):
            t, h = bank(i)
            return t.bitcast(BF)[:, h * 1024:(h + 1) * 1024]

        ident_f = sb("ident_f", [128, 128])
        ident_b = sb("ident_b", [128, 128], BF)
        ones_b = sb("ones_b", [128, 128], BF)
        tri_b = sb("tri_b", [128, 128], BF)
        pair_b = sb("pair_b", [128, 128], BF)
        halom = sb("halom", [128, 1])
        invf = sb("invf", [64, 1])
        sgn = sb("sgn", [64, 1])
        modT = sb("modT", [128, 48])
        vecs = sb("vecs", [128, 128])
        cwT = sb("cwT", [128, 8, 31])
        V_GPRE, V_GPOST, V_GPRE2, V_GPOST2 = 0, 8, 16, 24
        V_CB, V_CG, V_CNB = 32, 40, 48
        V_QG, V_KVG = 56, 59
        der = sb("der", [128, 48])
        NST = 2
        wst = [sb("wst%d" % i, [128, 8, 256]) for i in range(NST)]
        wbf = [sb("wbf%d" % i, [128, 8, 256], BF) for i in range(NST)]
        wrr = [0]
        xblk = [sb("xblk%d" % i, [128, D]) for i in range(2)]
        xnb = [sb("xnb%d" % i, [128, D], BF) for i in range(2)]
        small = [sb("small%d" % i, [128, 4]) for i in range(4)]
        small_rr = [0]
        tmpf = [sb("tmpf%d" % i, [128, 512]) for i in range(4)]
        tmpf_rr = [0]
        tmpb = [sb("tmpb%d" % i, [128, 512], BF) for i in range(6)]
        tmpb_rr = [0]
        rstd_t = [sb("rstd%d" % i, [128, 512]) for i in range(2)]
        rstd_rr = [0]

        def nxt(lst, rr):
            t = lst[rr[0] % len(lst)]
            rr[0] += 1
            return t

        def A(fn, **kw):
            return S.op("act", fn, **kw)

        def V(fn, **kw):
            return S.op("dve", fn, **kw)

        def G(fn, **kw):
            return S.op("pool", fn, **kw)

        def P(fn, **kw):
            return S.op("pe", fn, **kw)

        def dma_in(dst_ap, src_ap, dst_buf, key=None, eng="sp", nonc=False):
            def f(e, dst_ap=dst_ap, src_ap=src_ap):
                if nonc:
                    return e.dma_start(out=dst_ap, in_=src_ap, allow_slow_non_contiguous=True)
                return e.dma_start(out=dst_ap, in_=src_ap)
            return S.dma(eng, f, writes=[(dst_buf, key)])

        def mm(out_ap, lhsT, rhs, start, stop, reads, wkey, last):
            def f(e):
                return e.matmul(out_ap, lhsT, rhs, start=start, stop=stop)
            return S.op("pe", f, reads=reads, writes=[wkey], inc=last)

        def load_w(src, rows, c0, ncols, r0=0, cast=None):
            i = wrr[0] % NST
            wrr[0] += 1
            kc = rows // 128
            st, wb = wst[i], wbf[i]
            src_ap = src[r0:r0 + rows, c0:c0 + ncols].rearrange("(k p) c -> p k c", p=128)
            dma_in(st[:, 0:kc, 0:ncols], src_ap, st)
            if cast == "pool" or (cast is None and wrr[0] % 2 == 0):
                G(lambda e: e.tensor_copy(wb[:, 0:kc, 0:ncols], st[:, 0:kc, 0:ncols]), reads=[st], writes=[wb])
            else:
                V(lambda e: e.tensor_copy(wb[:, 0:kc, 0:ncols], st[:, 0:kc, 0:ncols]), reads=[st], writes=[wb])
            return wb

        def _unused():
            pass

        out_toks = []

        def run_phases():
            dma_in(ident_f[:], ident_in, ident_f)
            dma_in(halom[:], halomask_in, halom)
            dma_in(invf[:], invf_in, invf)
            dma_in(sgn[:], sgn_in, sgn)
            t0 = tmpf[0]
            t1 = tmpf[1]
            dma_in(t0[:, 0:128], trimask_in, t0)
            dma_in(t1[:, 0:128], pairmask_in, t1)
            V(lambda e: e.tensor_copy(ident_b[:], ident_f[:]), reads=[ident_f], writes=[ident_b])
            V(lambda e: e.memset(ones_b[:], 1.0), writes=[ones_b])
            V(lambda e: e.tensor_copy(tri_b[:], t0[:, 0:128]), reads=[t0], writes=[tri_b])
            V(lambda e: e.tensor_copy(pair_b[:], t1[:, 0:128]), reads=[t1], writes=[pair_b])
            tmpf_rr[0] = 2
            if stop == -1:
                return
            stg = tmpf[2]
            V(lambda e: e.memset(stg[:, 0:128], 0.0), writes=[stg])
            for col, src, n in ((V_GPRE, g_pre_mix, D), (V_GPOST, g_post_mix, D), (V_GPRE2, g_pre_mlp, D),
                                (V_GPOST2, g_post_mlp, D), (V_CB, conv_b, D), (V_CG, conv_norm_g, D),
                                (V_CNB, conv_norm_b, D), (V_QG, q_norm_g, 384), (V_KVG, kv_norm_g, 256)):
                dma_in(stg[col:col + n // 128, 0:128], src.rearrange("(k p) -> k p", p=128), stg)
            dma_in(stg[64:112, 0:128], b_ada.rearrange("(k p) -> k p", p=128), stg)
            dma_in(stg[112:120, 0:128], c_in.rearrange("(k p) -> k p", p=128), stg)
            P(lambda e: e.transpose(bap(1, 0, 120), stg[0:120, 0:128], ident_f[0:120, 0:120]),
              reads=[stg, ident_f], writes=[bk(1)])
            V(lambda e: e.tensor_copy(vecs[:, 0:120], bap(1, 0, 120)), reads=[bk(1)], writes=[vecs])
            if stop == -2:
                return
            badaT = vecs[:, 64:112]
            cT = vecs[:, 112:120]
            cwn = xblk[0]
            dma_in(cwn[0:31, :], conv_w, cwn)
            for c in range(8):
                P(lambda e, c=c: e.transpose(bap(2, c * 32, c * 32 + 31), cwn[0:31, c * 128:(c + 1) * 128], ident_f[0:31, 0:31]),
                  reads=[cwn, ident_f], writes=[bk(2)])
            V(lambda e: e.tensor_copy(cwT[:], bap(2, 0, 256).rearrange("p (c k) -> p c k", k=32)[:, :, 0:31]),
              reads=[bk(2)], writes=[cwT])
            if stop == -3:
                return
            scb = sb("scb", [128, 8], BF)
            A(lambda e: e.activation(scb[:], cT, AF.Silu), reads=[vecs], writes=[scb])
            if stop == -4:
                return
            def mod_part(p0, p1, MODB, cast=None):
                for pc in range(p0, p1):
                    wb = load_w(w_ada, D, pc * 256, 256, cast=cast)
                    for jj in range(2):
                        j = pc * 2 + jj
                        for k in range(8):
                            mm(bap(MODB, j, j + 1), wb[:, k, jj * 128:(jj + 1) * 128], scb[:, k:k + 1],
                               k == 0, k == 7, [wb, scb], bk(MODB), k == 7)
                V(lambda e: e.tensor_tensor(modT[:, p0 * 2:p1 * 2], bap(MODB, p0 * 2, p1 * 2), vecs[:, 64 + p0 * 2:64 + p1 * 2], ALU.add),
                  reads=[bk(MODB), vecs], writes=[(modT, p0)])

            mod_part(0, 8, 0)
            if stop == -5:
                return
            V(lambda e: e.scalar_tensor_tensor(der[:, 0:8], modT[:, 8:16], 1.0, vecs[:, V_GPRE:V_GPRE + 8], ALU.add, ALU.mult),
              reads=[modT, vecs], writes=[(der, 0)])
            V(lambda e: e.tensor_copy(der[:, 8:16], modT[:, 0:8]), reads=[modT], writes=[(der, 8)])

            def mod_late():
                mod_part(8, 24, 7, cast="pool")
                V(lambda e: e.tensor_tensor(der[:, 16:24], modT[:, 16:24], vecs[:, V_GPOST:V_GPOST + 8], ALU.mult),
                  reads=[modT, vecs], writes=[(der, 16)])
                V(lambda e: e.scalar_tensor_tensor(der[:, 24:32], modT[:, 32:40], 1.0, vecs[:, V_GPRE2:V_GPRE2 + 8], ALU.add, ALU.mult),
                  reads=[modT, vecs], writes=[(der, 24)])
                V(lambda e: e.tensor_copy(der[:, 32:40], modT[:, 24:32]), reads=[modT], writes=[(der, 32)])
                V(lambda e: e.tensor_tensor(der[:, 40:48], modT[:, 40:48], vecs[:, V_GPOST2:V_GPOST2 + 8], ALU.mult),
                  reads=[modT, vecs], writes=[(der, 40)])
            DER_ALL = [(der, 0), (der, 8), (der, 16), (der, 24), (der, 32), (der, 40)]

            TPB = [0, 1]
            tp_rr = [0]
            xb_rr = [0]

            def xb_prep(src_rows_ap, nrows):
                i = xb_rr[0] % 2
                xb_rr[0] += 1
                xb = xblk[i]
                xn = xnb[i]
                dma_in(xb[0:nrows, :], src_rows_ap, xb)
                sm = nxt(small, small_rr)
                A(lambda e: e.activation(xn[0:nrows, :], xb[0:nrows, :], AF.Square, accum_out=sm[0:nrows, 0:1]),
                  reads=[xb], writes=[xn, (sm, 0)])
                A(lambda e: e.activation(sm[0:nrows, 3:4], sm[0:nrows, 0:1], AF.Sqrt, bias=EPS, scale=1.0 / D),
                  reads=[(sm, 0)], writes=[(sm, 3)])
                V(lambda e: e.reciprocal(sm[0:nrows, 2:3], sm[0:nrows, 3:4]), reads=[(sm, 3)], writes=[(sm, 2)])
                V(lambda e: e.tensor_scalar(xn[0:nrows, :], xb[0:nrows, :], sm[0:nrows, 2:3], None, ALU.mult),
                  reads=[xb, (sm, 2)], writes=[xn])
                return (xn, nrows)

            def xb_trans(hd, hT, col0, gsc, shc):
                xn, nrows = hd
                b = TPB[tp_rr[0] % 2]
                tp_rr[0] += 1
                tpv = bap_bf(b)
                for k in range(8):
                    P(lambda e, k=k: e.transpose(tpv[:, k * 128:k * 128 + nrows], xn[0:nrows, k * 128:(k + 1) * 128],
                                                  ident_b[0:nrows, 0:nrows]),
                      reads=[xn, ident_b], writes=[bk(b)], inc=(k == 7))
                for k in range(8):
                    if b == TPB[0]:
                        V(lambda e, k=k: e.tensor_scalar(hT[:, k, col0:col0 + nrows], tpv[:, k * 128:k * 128 + nrows],
                                                         der[:, gsc + k:gsc + k + 1], der[:, shc + k:shc + k + 1],
                                                         ALU.mult, ALU.add),
                          reads=[bk(b), (der, gsc), (der, shc)], writes=[(hT, k)])
                    else:
                        A(lambda e, k=k: e.activation(hT[:, k, col0:col0 + nrows], tpv[:, k * 128:k * 128 + nrows],
                                                      AF.Identity, bias=der[:, shc + k:shc + k + 1],
                                                      scale=der[:, gsc + k:gsc + k + 1]),
                          reads=[bk(b), (der, gsc), (der, shc)], writes=[(hT, k)])

            def x_block_to_hT(src_rows_ap, nrows, hT, col0, gsc, shc):
                xb_trans(xb_prep(src_rows_ap, nrows), hT, col0, gsc, shc)

            def rstd_from_ps(ps_bank, nfeat, ncols=512):
                r = nxt(rstd_t, rstd_rr)
                jt = nxt(tmpf, tmpf_rr)
                A(lambda e: e.activation(jt[:, 0:ncols], bap(ps_bank, 0, ncols), AF.Sqrt, bias=EPS, scale=1.0 / nfeat),
                  reads=[bk(ps_bank)], writes=[jt])
                V(lambda e: e.reciprocal(r[:, 0:ncols], jt[:, 0:ncols]), reads=[jt], writes=[r])
                return r

            if stop == 0:
                return
            oT = sb("oT", [128, 8, NOWN], BF)
            ph12 = es.enter_context(contextlib.ExitStack())
            ph1 = es.enter_context(contextlib.ExitStack())

            def sb12(name, shape, dt=F32):
                return ph12.enter_context(nc.sbuf_tensor("s_" + name, list(shape), dt))

            def sb1(name, shape, dt=F32):
                return ph1.enter_context(nc.sbuf_tensor("s_" + name, list(shape), dt))

            kvn = [sb12("kvn_own", [128, 2, NOWN], BF), sb12("kvn_oth", [128, 2, NOWN], BF)]
            krT = [sb12("kr_own", [128, NOWN], BF), sb12("kr_oth", [128, NOWN], BF)]
            for kr_ in krT:
                G(lambda e, kr_=kr_: e.memset(kr_[64:128, :], 0.0), writes=[kr_])
            qn = sb12("qn", [128, 3, NOWN], BF)
            CS = sb12("cs_own", [64, 2, NOWN])
            wuq = sb12("wuq", [128, 3, 2048], BF)
            hTs = [sb1("hT%d" % i, [128, 8, 640], BF) for i in range(2)]
            wlat = sb1("wlat", [128, 8, 768], BF)
            cs_tmp = sb1("cs_tmp", [64, 2, 512])
            posi = sb1("posi", [64, 512], I32)
            angs = [sb1("ang%d" % i, [64, 512]) for i in range(4)]
            ni_t = sb1("ni_t", [64, 512], I32)

            for pc, (src, c0, n, d0) in enumerate(((w_in, 2048, 256, 0), (w_in, 2304, 256, 256), (w_in, 2560, 192, 512),
                                                   (w_kr_sw, 0, 64, 704))):
                wb = load_w(src, D, c0, n)
                G(lambda e, wb=wb, n=n, d0=d0: e.tensor_copy(wlat[:, :, d0:d0 + n], wb[:, :, 0:n]),
                  reads=[wb], writes=[(wlat, pc)])
            WL = [(wlat, i) for i in range(4)]
            if stop == 10:
                return

            def rope_tables(pos_t, c0, dst, dcol):
                src = bass.AP(pos_t, c0, [[0, 64], [1, 512]])
                dma_in(posi[:], src, posi)
                a0, a1, a2, a3 = angs
                V(lambda e: e.tensor_copy(a0[:], posi[:]), reads=[posi], writes=[a0])
                V(lambda e: e.tensor_scalar(a0[:], a0[:], invf[:, 0:1], None, ALU.mult), reads=[a0, invf], writes=[a0])
                V(lambda e: e.tensor_scalar(a1[:], a0[:], 1.0 / TWO_PI, None, ALU.mult), reads=[a0], writes=[a1])
                V(lambda e: e.tensor_copy(ni_t[:], a1[:]), reads=[a1], writes=[ni_t])
                V(lambda e: e.tensor_copy(a1[:], ni_t[:]), reads=[ni_t], writes=[a1])
                V(lambda e: e.scalar_tensor_tensor(a2[:], a1[:], -C1, a0[:], ALU.mult, ALU.add), reads=[a1, a0], writes=[a2])
                V(lambda e: e.scalar_tensor_tensor(a2[:], a1[:], -C2, a2[:], ALU.mult, ALU.add), reads=[a1, a2], writes=[a2])
                V(lambda e: e.tensor_scalar(a3[:], a2[:], math.pi, -TWO_PI, ALU.is_gt, ALU.mult), reads=[a2], writes=[a3])
                V(lambda e: e.tensor_tensor(a2[:], a2[:], a3[:], ALU.add), reads=[a2, a3], writes=[a2])
                V(lambda e: e.tensor_scalar(a3[:], a2[:], -math.pi, TWO_PI, ALU.is_lt, ALU.mult), reads=[a2], writes=[a3])
                V(lambda e: e.tensor_tensor(a2[:], a2[:], a3[:], ALU.add), reads=[a2, a3], writes=[a2])
                V(lambda e: e.tensor_scalar(a1[:], a2[:], math.pi / 2, None, ALU.add), reads=[a2], writes=[a1])
                V(lambda e: e.tensor_scalar(a3[:], a1[:], math.pi, -TWO_PI, ALU.is_gt, ALU.mult), reads=[a1], writes=[a3])
                V(lambda e: e.tensor_tensor(a1[:], a1[:], a3[:], ALU.add), reads=[a1, a3], writes=[a1])
                V(lambda e: e.tensor_scalar(a1[:], a1[:], math.pi, -math.pi, ALU.min, ALU.max), reads=[a1], writes=[a1])
                V(lambda e: e.tensor_scalar(a2[:], a2[:], math.pi, -math.pi, ALU.min, ALU.max), reads=[a2], writes=[a2])
                A(lambda e: e.activation(dst[:, 0, dcol:dcol + 512], a1[:], AF.Sin), reads=[a1], writes=[(dst, dcol)])
                A(lambda e: e.activation(dst[:, 1, dcol:dcol + 512], a2[:], AF.Sin, scale=sgn[:, 0:1]),
                  reads=[a2, sgn], writes=[(dst, dcol)])

            for pc in range(6):
                wb = load_w(w_uq, 384, pc * 256, 256)
                G(lambda e, wb=wb, pc=pc: e.tensor_copy(wuq[:, :, pc * 256:(pc + 1) * 256], wb[:, 0:3, :]),
                  reads=[wb], writes=[(wuq, pc)])
            for pc in range(2):
                wb = load_w(w_uq_sw, 384, pc * 256, 256)
                G(lambda e, wb=wb, pc=pc: e.tensor_copy(wuq[:, :, 1536 + pc * 256:1536 + (pc + 1) * 256], wb[:, 0:3, :]),
                  reads=[wb], writes=[(wuq, 6 + pc)])
            def prep1(j_):
                i_, b_ = j_ // 4, j_ % 4
                grp_, t_ = i_ // 4, i_ % 4
                xsrc = x_own if grp_ == 0 else x_oth
                r0 = t_ * 512 + b_ * 128
                return xb_prep(xsrc[r0:r0 + 128, :], 128)

            hd1 = [prep1(0)]
            for grp in range(2):
                pos_t = pos_own if grp == 0 else pos_oth
                for t in range(4):
                    hT = hTs[(grp * 4 + t) % 2]
                    for b in range(4):
                        j_ = (grp * 4 + t) * 4 + b
                        nh = prep1(j_ + 1) if j_ + 1 < 32 else None
                        xb_trans(hd1[0], hT, b * 128, 0, 8)
                        hd1[0] = nh
                    if grp == 0:
                        rope_tables(pos_t, t * 512, CS, t * 512)
                        cs, cc = CS, t * 512
                    else:
                        rope_tables(pos_t, t * 512, cs_tmp, 0)
                        cs, cc = cs_tmp, 0
                    if stop == 12:
                        return
                    hk = [(hT, k) for k in range(8)]
                    for m in range(2):
                        for k in range(8):
                            mm(bap(2 + m), wlat[:, k, 384 + m * 128:384 + (m + 1) * 128], hT[:, k, 0:512],
                               k == 0, k == 7, hk + WL, bk(2 + m), k == 7)
                    for m in range(2):
                        for k in range(8):
                            mm(bap(5 + m)[0:64, :], wlat[:, k, 640 + m * 64:640 + (m + 1) * 64], hT[:, k, 0:512],
                               k == 0, k == 7, hk + WL, bk(5 + m), k == 7)
                    sq = []
                    for m in range(2):
                        s_ = nxt(tmpb, tmpb_rr)
                        A(lambda e, m=m, s_=s_: e.activation(s_[:], bap(2 + m), AF.Square), reads=[bk(2 + m)], writes=[s_])
                        sq.append(s_)
                    for m in range(2):
                        mm(bap(4), ones_b[:], sq[m][:], m == 0, m == 1, [ones_b, sq[m]], bk(4), True)
                    r = rstd_from_ps(4, 256)
                    for m in range(2):
                        V(lambda e, m=m, r=r: e.scalar_tensor_tensor(kvn[grp][:, m, t * 512:(t + 1) * 512], bap(2 + m),
                                                                     vecs[:, V_KVG + m:V_KVG + m + 1], r[:], ALU.mult, ALU.mult),
                          reads=[bk(2 + m), r, vecs], writes=[(kvn[grp], t)])
                    ta = nxt(tmpf, tmpf_rr)
                    tb_ = nxt(tmpf, tmpf_rr)
                    V(lambda e, ta=ta, cs=cs, cc=cc: e.tensor_tensor(ta[0:64, :], bap(5)[0:64, :], cs[:, 0, cc:cc + 512], ALU.mult),
                      reads=[bk(5), (cs, cc)], writes=[ta])
                    V(lambda e, tb_=tb_, cs=cs, cc=cc: e.tensor_tensor(tb_[0:64, :], bap(6)[0:64, :], cs[:, 1, cc:cc + 512], ALU.mult),
                      reads=[bk(6), (cs, cc)], writes=[tb_])
                    V(lambda e, ta=ta, tb_=tb_: e.tensor_tensor(krT[grp][0:64, t * 512:(t + 1) * 512], ta[0:64, :], tb_[0:64, :], ALU.add),
                      reads=[ta, tb_], writes=[(krT[grp], t)])
                    if stop == 13:
                        return
                    if grp == 0:
                        QB = [2, 3, 7]
                        for m in range(3):
                            for k in range(8):
                                mm(bap(QB[m]), wlat[:, k, m * 128:(m + 1) * 128], hT[:, k, 0:512],
                                   k == 0, k == 7, hk + WL, bk(QB[m]), k == 7)
                        sq = []
                        for m in range(3):
                            s_ = nxt(tmpb, tmpb_rr)
                            A(lambda e, m=m, s_=s_: e.activation(s_[:], bap(QB[m]), AF.Square), reads=[bk(QB[m])], writes=[s_])
                            sq.append(s_)
                        for m in range(3):
                            mm(bap(4), ones_b[:], sq[m][:], m == 0, m == 2, [ones_b, sq[m]], bk(4), True)
                        r = rstd_from_ps(4, 384)
                        for m in range(3):
                            V(lambda e, m=m, r=r: e.scalar_tensor_tensor(qn[:, m, t * 512:(t + 1) * 512], bap(QB[m]),
                                                                         vecs[:, V_QG + m:V_QG + m + 1], r[:], ALU.mult, ALU.mult),
                              reads=[bk(QB[m]), r, vecs], writes=[(qn, t)])
                    if stop == 14:
                        return

            S.barrier()
            ph1.close()
            if stop == 1:
                ph12.close()
                return

            wukv = sb12("wukv", [128, 2, 2048], BF)
            for pc in range(8):
                wb = load_w(w_ukv, 256, pc * 256, 256)
                G(lambda e, wb=wb, pc=pc: e.tensor_copy(wukv[:, :, pc * 256:(pc + 1) * 256], wb[:, 0:2, :]),
                  reads=[wb], writes=[(wukv, pc)])
            KhT = [[sb12("kh%d_%d" % (i, g), [128, NOWN], BF) for g in range(2)] for i in range(1)]
            Vh = [[sb12("vh%d_%d" % (i, g), [128, 16, 128], BF) for g in range(2)] for i in range(1)]
            Qh = [sb12("qh%d" % i, [128, NOWN], BF) for i in range(1)]
            Qr = [sb12("qr%d" % i, [128, NOWN], BF) for i in range(1)]
            G(lambda e: e.memset(Qr[0][64:128, :], 0.0), writes=[Qr[0]])
            Pt = [sb12("pt%d" % i, [128, 512], BF) for i in range(4)]
            pt_rr = [0]
            SB_ = [0, 1, 2]
            s_rr = [0]
            OB = [3, 5]
            LB = [4, 6]
            HBS = [7, 3, 4]
            hb_rr = [0]

            def nhb():
                b_ = HBS[hb_rr[0] % 3]
                hb_rr[0] += 1
                return b_
            evac_rr = [0]

            def evac_copy(dst_ap, src_bank_ap, reads, writes):
                if evac_rr[0] % 2 == 0:
                    V(lambda e: e.tensor_copy(dst_ap, src_bank_ap), reads=reads, writes=writes)
                else:
                    A(lambda e: e.activation(dst_ap, src_bank_ap, AF.Copy), reads=reads, writes=writes)
                evac_rr[0] += 1

            def build_head(h):
                i = 0
                for grp in range(2):
                    for t in range(4):
                        HB = nhb()
                        for k in range(2):
                            mm(bap(HB), wukv[:, k, h * 256:h * 256 + 128], kvn[grp][:, k, t * 512:(t + 1) * 512],
                               k == 0, k == 1, [wukv, kvn[grp]], bk(HB), k == 1)
                        evac_copy(KhT[i][grp][:, t * 512:(t + 1) * 512], bap(HB), [bk(HB)], [(KhT[i][grp], t)])
                    for t in range(4):
                        HB = nhb()
                        for b in range(4):
                            blk = t * 4 + b
                            for k in range(2):
                                mm(bap(HB, b * 128, (b + 1) * 128), kvn[grp][:, k, blk * 128:(blk + 1) * 128],
                                   wukv[:, k, h * 256 + 128:h * 256 + 256],
                                   k == 0, k == 1, [wukv, kvn[grp]], bk(HB), (k == 1 and b == 3))
                        evac_copy(Vh[i][grp][:, t * 4:(t + 1) * 4, :], bap(HB).rearrange("p (b d) -> p b d", d=128),
                                  [bk(HB)], [(Vh[i][grp], t)])
                for t in range(4):
                    HB = nhb()
                    for k in range(3):
                        mm(bap(HB), wuq[:, k, h * 192:h * 192 + 128], qn[:, k, t * 512:(t + 1) * 512],
                           k == 0, k == 2, [wuq, qn], bk(HB), k == 2)
                    evac_copy(Qh[i][:, t * 512:(t + 1) * 512], bap(HB), [bk(HB)], [(Qh[i], t)])
                for t in range(4):
                    HB = nhb()
                    for k in range(3):
                        mm(bap(HB)[0:64, :], wuq[:, k, h * 192 + 128:h * 192 + 192], qn[:, k, t * 512:(t + 1) * 512],
                           k == 0, k == 2, [wuq, qn], bk(HB), k == 2)
                    ta = nxt(tmpf, tmpf_rr)
                    V(lambda e, ta=ta, t=t: e.tensor_tensor(ta[0:64, :], bap(HB)[0:64, :], CS[:, 0, t * 512:(t + 1) * 512], ALU.mult),
                      reads=[bk(HB), CS], writes=[ta])
                    HB = nhb()
                    for k in range(3):
                        mm(bap(HB)[0:64, :], wuq[:, k, 1536 + h * 64:1536 + (h + 1) * 64], qn[:, k, t * 512:(t + 1) * 512],
                           k == 0, k == 2, [wuq, qn], bk(HB), k == 2)
                    tb_ = nxt(tmpf, tmpf_rr)
                    V(lambda e, tb_=tb_, t=t: e.tensor_tensor(tb_[0:64, :], bap(HB)[0:64, :], CS[:, 1, t * 512:(t + 1) * 512], ALU.mult),
                      reads=[bk(HB), CS], writes=[tb_])
                    V(lambda e, ta=ta, tb_=tb_, t=t: e.tensor_tensor(Qr[i][0:64, t * 512:(t + 1) * 512], ta[0:64, :], tb_[0:64, :], ALU.add),
                      reads=[ta, tb_], writes=[(Qr[i], t)])

            def attend_head(h):
                i = 0
                for g in range(4):
                    ob = OB[g % 2]
                    lb = LB[g % 2]
                    visits = [(J, grp) for J in range(4 * g + 4) for grp in range(2)]
                    pend = []

                    def do_pv(v, first, last):
                        J, grp, c0, pt = v
                        mm(bap(ob, c0, 512), Vh[i][grp][:, J, :], pt[:, c0:512], first, last,
                           [Vh[i][grp], pt], bk(ob), True)
                        mm(bap(lb, c0, 512), ones_b[:], pt[:, c0:512], first, last,
                           [ones_b, pt], bk(lb), True)

                    npv = [0]
                    for vi, (J, grp) in enumerate(visits):
                        j = J - 4 * g
                        c0 = 128 * max(j, 0)
                        sbk = SB_[s_rr[0] % 3]
                        s_rr[0] += 1
                        q0 = g * 512 + c0
                        q1 = (g + 1) * 512
                        masked = j >= 0
                        mm(bap(sbk, c0, 512), KhT[i][grp][:, J * 128:(J + 1) * 128], Qh[i][:, q0:q1],
                           True, False, [KhT[i][grp], Qh[i]], bk(sbk), False)
                        mm(bap(sbk, c0, 512), krT[grp][:, J * 128:(J + 1) * 128], Qr[i][:, q0:q1],
                           False, not masked, [krT[grp], Qr[i]], bk(sbk), not masked)
                        if masked:
                            mk = tri_b if grp == 0 else pair_b
                            mm(bap(sbk, c0, c0 + 128), ident_b[:], mk[:], False, True, [ident_b, mk], bk(sbk), True)
                        pt = nxt(Pt, pt_rr)
                        A(lambda e, pt=pt, sbk=sbk, c0=c0: e.activation(pt[:, c0:512], bap(sbk, c0, 512), AF.Exp, scale=SCALE),
                          reads=[bk(sbk)], writes=[pt])
                        pend.append((J, grp, c0, pt))
                        if len(pend) > 2:
                            v = pend.pop(0)
                            do_pv(v, npv[0] == 0, False)
                            npv[0] += 1
                    while pend:
                        v = pend.pop(0)
                        do_pv(v, npv[0] == 0, len(pend) == 0)
                        npv[0] += 1
                    rl = nxt(rstd_t, rstd_rr)
                    V(lambda e, rl=rl, lb=lb: e.reciprocal(rl[:], bap(lb)), reads=[bk(lb)], writes=[rl])
                    V(lambda e, rl=rl, ob=ob, g=g: e.tensor_tensor(oT[:, h, g * 512:(g + 1) * 512], bap(ob), rl[:], ALU.mult),
                      reads=[bk(ob), rl], writes=[(oT, (h, g))])

            prep = []
            for pc in range(4):
                prep += [(w_in, pc * 256, 0), (w_in, 1024 + pc * 256, 0)]
            for pc in range(4):
                prep += [(w_conv_out, pc * 256, 0)]
            for pc in range(4):
                prep += [(w_attn_out, pc * 256, 0), (w_in, 2752 + pc * 256, 0), (w_in, 3776 + pc * 256, 0)]
            for pc in range(4):
                prep += [(w_out, pc * 256, 0)]
            for pc in range(16):
                prep += [(w_mlp_in, pc * 256, 0)]
            for pc in range(4):
                for rr_ in range(4):
                    prep += [(w_mlp_out, pc * 256, rr_ * 1024)]
            prep_tok = {}

            def do_prep():
                for i, (src, c0, r0) in enumerate(prep):
                    wb = load_w(src, D, c0, 256, r0=r0, cast="pool")
                    S.dma("sp", lambda e, wb=wb, i=i: e.dma_start(out=wq[i], in_=wb[:].rearrange("p k c -> p (k c)")),
                          reads=[wb], writes=[(wq_key, i)])

            do_prep()
            build_head(0)
            for h in range(8):
                attend_head(h)
                if h + 1 < 8:
                    build_head(h + 1)
            mod_late()

            S.barrier()
            ph12.close()
            if stop == 2:
                return
            wpool = [wbf[0][:], wbf[1][:]]
            for i_ in range(NST):
                fl = wst[i_].bitcast(BF)[:].rearrange("p k c -> p (k c)")
                wpool += [fl[:, 0:2048].rearrange("p (k c) -> p k c", c=256), fl[:, 2048:4096].rearrange("p (k c) -> p k c", c=256)]
            wp_rr = [0]
            pidx = {(src_.tensor.name, c0_, r0_): i_ for i_, (src_, c0_, r0_) in enumerate(prep)}

            def load_wq(src, rows, c0, ncols, r0=0):
                i = pidx[(src.tensor.name, c0, r0)]
                buf = wpool[wp_rr[0] % len(wpool)]
                wp_rr[0] += 1
                S.dma("sp", lambda e: e.dma_start(out=buf.rearrange("p k c -> p (k c)"), in_=wq[i]), writes=[buf])
                return buf

            xT = sb("xT", [128, 8, 512])
            yT = sb("yT", [128, 8, 512])
            hTe = sb("hTe", [128, 8, 640], BF)
            uext = sb("uext", [128, 8, 640], BF)
            arena = sb("arena", [128, 16384], BF)
            hid = arena[:, :].rearrange("p (j t) -> p j t", t=512)
            ucv = arena.bitcast(F32)[:, 0:4096].rearrange("p (c t) -> p c t", t=512)
            diag = [arena[:, 8192 + i * 3968:8192 + (i + 1) * 3968].rearrange("p (k m) -> p k m", m=128) for i in range(2)]
            sh8 = sb("sh8", [128, 8, 512], BF)
            actT = sh8
            mT = sh8
            h2T = sh8
            yaT = sb("yaT", [128, 8, 512], BF)
            oblk = xblk
            stat_s = sb("stat_s", [128, 512])
            stat_n = sb("stat_n", [128, 512])
            sb_sig = sb("sb_sig", [128, 640])

            deferred = []

            def flush_def():
                for f_ in deferred:
                    f_()
                deferred.clear()

            def stats_accum(ps_b, src_ap, reads, idx, n, defer=False):
                s_ = nxt(tmpb, tmpb_rr)
                A(lambda e: e.activation(s_[:], src_ap, AF.Square), reads=reads, writes=[s_])
                f_ = lambda s_=s_: mm(bap(ps_b), ones_b[:], s_[:], idx == 0, idx == n - 1, [ones_b, s_], bk(ps_b), True)
                if defer:
                    deferred.append(f_)
                else:
                    f_()

            def fh_blocks(g_):
                lst = []
                for b in range(4):
                    blk = g_ * 4 + b
                    lst.append((x_halo[blk * 32:(blk + 1) * 32, :], 32, b * 160))
                    lst.append((x_own[blk * 128:(blk + 1) * 128, :], 128, b * 160 + 32))
                return lst

            def front_h(g_):
                lst = fh_blocks(g_)
                hd = xb_prep(lst[0][0], lst[0][1])
                for i_ in range(8):
                    nh = xb_prep(lst[i_ + 1][0], lst[i_ + 1][1]) if i_ + 1 < 8 else None
                    xb_trans(hd, hTe, lst[i_][2], 0, 8)
                    hd = nh

            front_h(0)
            for g in range(4):
                for b in range(4):
                    blk = g * 4 + b
                    xb = xblk[xb_rr[0] % 2]
                    xb_rr[0] += 1
                    dma_in(xb[:], x_own[blk * 128:(blk + 1) * 128, :], xb)
                    for half in range(2):
                        tb_i = 2 + half
                        for kk in range(4):
                            k = half * 4 + kk
                            P(lambda e, k=k, kk=kk, tb_i=tb_i, xb=xb: e.transpose(bap(tb_i, kk * 128, (kk + 1) * 128),
                                                                                    xb[:, k * 128:(k + 1) * 128], ident_f[:]),
                              reads=[xb, ident_f], writes=[bk(tb_i)], inc=(kk == 3))
                        evac_copy(xT[:, half * 4:(half + 1) * 4, b * 128:(b + 1) * 128],
                                  bap(tb_i).rearrange("p (k t) -> p k t", t=128), [bk(tb_i)], [(xT, (half, b))])
                hk = [(hTe, k) for k in range(8)]
                def glu_chunk(c, wa, wb2, cc):
                    for (wt, d) in ((wa, 0), (wb2, 1)):
                        for (n0, n1, hb) in ((0, 512, 0), (512, 640, 1)):
                            for k in range(8):
                                mm(PD[d][:, hb * 512:hb * 512 + (n1 - n0)], wt[:, k, cc * 128:(cc + 1) * 128],
                                   hTe[:, k, n0:n1], k == 0, k == 7, hk + [wt], (PD[d], hb), k == 7)
                    sg = sb_sig
                    A(lambda e: e.activation(sg[:, 0:640], PD[1][:, 0:640], AF.Sigmoid),
                      reads=[(PD[1], 0), (PD[1], 1)], writes=[sg])
                    V(lambda e: e.tensor_tensor(uext[:, c, :], PD[0][:, 0:640], sg[:, 0:640], ALU.mult),
                      reads=[(PD[0], 0), (PD[0], 1), sg], writes=[(uext, c)])
                    if g == 0:
                        V(lambda e: e.tensor_scalar(uext[:, c, 0:32], uext[:, c, 0:32], halom[:, 0:1], None, ALU.mult),
                          reads=[(uext, c), halom], writes=[(uext, c)])

                def conv_chunk(c):
                    dg = diag[c % 2]
                    for k in range(31):
                        G(lambda e: e.tensor_scalar(dg[:, k, :], ident_b[:], cwT[:, c, k:k + 1], 1.0, ALU.mult, ALU.mult),
                          reads=[ident_b, cwT], writes=[(arena, None) if (c == 0 and k == 0) else (arena, ("d", c % 2, k))])
                    uv = uext[:, c, :].rearrange("p (b w) -> p b w", w=160)
                    cb = 4 + (c % 2)
                    for k in range(31):
                        mm(bap(cb).rearrange("p (b w) -> p b w", w=128), dg[:, k, :], uv[:, :, 2 + k:2 + k + 128],
                           k == 0, k == 30, [(arena, ("d", c % 2, k)), (uext, c)], bk(cb), k == 30)
                    flush_def()
                    A(lambda e: e.activation(ucv[:, c, :], bap(cb), AF.Identity, bias=vecs[:, V_CB + c:V_CB + c + 1]),
                      reads=[bk(cb), vecs], writes=[(arena, ("u", c))])
                    ub_ = nxt(tmpb, tmpb_rr)
                    V(lambda e: e.tensor_copy(ub_[:], ucv[:, c, :]), reads=[(arena, ("u", c))], writes=[ub_])
                    deferred.append(lambda ub_=ub_, c=c: mm(bap(6), ones_b[:], ub_[:], c == 0, c == 7, [ones_b, ub_], bk(6), True))
                    stats_accum(7, ucv[:, c, :], [(arena, ("u", c))], c, 8, defer=True)

                for pc in range(4):
                    wa = load_wq(w_in, D, pc * 256, 256)
                    wb2 = load_wq(w_in, D, 1024 + pc * 256, 256)
                    for cc in range(2):
                        c = pc * 2 + cc
                        glu_chunk(c, wa, wb2, cc)
                        if c >= 1:
                            conv_chunk(c - 1)
                conv_chunk(7)
                flush_def()
                mean = stat_s
                nmr = stat_n
                A(lambda e: e.activation(mean[:], bap(6), AF.Copy, scale=1.0 / D), reads=[bk(6)], writes=[mean])
                jt = nxt(tmpf, tmpf_rr)
                V(lambda e, jt=jt: e.tensor_tensor(jt[:], mean[:], mean[:], ALU.mult), reads=[mean], writes=[jt])
                jt2 = nxt(tmpf, tmpf_rr)
                V(lambda e, jt=jt, jt2=jt2: e.scalar_tensor_tensor(jt2[:], bap(7), 1.0 / D, jt[:], ALU.mult, ALU.subtract),
                  reads=[bk(7), jt], writes=[jt2])
                V(lambda e, jt2=jt2: e.tensor_scalar(jt2[:], jt2[:], 0.0, None, ALU.max), reads=[jt2], writes=[jt2])
                A(lambda e, jt=jt, jt2=jt2: e.activation(jt[:], jt2[:], AF.Sqrt, bias=EPS), reads=[jt2], writes=[jt])
                rln = nxt(rstd_t, rstd_rr)
                V(lambda e, jt=jt, rln=rln: e.reciprocal(rln[:], jt[:]), reads=[jt], writes=[rln])
                V(lambda e, rln=rln: e.scalar_tensor_tensor(nmr[:], mean[:], -1.0, rln[:], ALU.mult, ALU.mult),
                  reads=[mean, rln], writes=[nmr])
                for c in range(8):
                    jt = nxt(tmpf, tmpf_rr)
                    V(lambda e, c=c, jt=jt, rln=rln: e.tensor_tensor(jt[:], ucv[:, c, :], rln[:], ALU.mult),
                      reads=[(arena, ("u", c)), rln], writes=[jt])
                    V(lambda e, jt=jt: e.tensor_tensor(jt[:], jt[:], nmr[:], ALU.add), reads=[jt, nmr], writes=[jt])
                    A(lambda e, c=c, jt=jt: e.activation(actT[:, c, :], jt[:], AF.Silu,
                                                         bias=vecs[:, V_CNB + c:V_CNB + c + 1], scale=vecs[:, V_CG + c:V_CG + c + 1]),
                      reads=[jt, vecs, vecs], writes=[(actT, c)])
                ak = [(actT, k) for k in range(8)]
                for pc in range(4):
                    wb = load_wq(w_conv_out, D, pc * 256, 256)
                    for cc in range(2):
                        m = pc * 2 + cc
                        bb = 4 + (m % 2)
                        for k in range(8):
                            mm(bap(bb), wb[:, k, cc * 128:(cc + 1) * 128], actT[:, k, :], k == 0, k == 7, ak + [wb], bk(bb), k == 7)
                        evac_copy(yaT[:, m, :], bap(bb), [bk(bb)], [(yaT, m)])
                hown = lambda k: hTe[:, k, :].rearrange("p (b w) -> p b w", w=160)[:, :, 32:160]
                for pc in range(4):
                    wao = load_wq(w_attn_out, D, pc * 256, 256)
                    for cc in range(2):
                        for hh in range(8):
                            mm(bap(2 + cc), wao[:, hh, cc * 128:(cc + 1) * 128], oT[:, hh, g * 512:(g + 1) * 512],
                               hh == 0, hh == 7, [oT, wao], bk(2 + cc), hh == 7)
                    wga = load_wq(w_in, D, 2752 + pc * 256, 256)
                    sas = []
                    for cc in range(2):
                        m = pc * 2 + cc
                        for k in range(8):
                            mm(bap(cc).rearrange("p (b w) -> p b w", w=128), wga[:, k, cc * 128:(cc + 1) * 128], hown(k),
                               k == 0, k == 7, hk + [wga], bk(cc), k == 7)
                        sa = nxt(tmpf, tmpf_rr)
                        A(lambda e, sa=sa, cc=cc: e.activation(sa[:], bap(cc), AF.Sigmoid), reads=[bk(cc)], writes=[sa])
                        V(lambda e, sa=sa, m=m: e.tensor_tensor(sa[:], sa[:], yaT[:, m, :], ALU.mult),
                          reads=[sa, (yaT, m)], writes=[sa])
                        sas.append(sa)
                    wgb = load_wq(w_in, D, 3776 + pc * 256, 256)
                    for cc in range(2):
                        m = pc * 2 + cc
                        for k in range(8):
                            mm(bap(cc).rearrange("p (b w) -> p b w", w=128), wgb[:, k, cc * 128:(cc + 1) * 128], hown(k),
                               k == 0, k == 7, hk + [wgb], bk(cc), k == 7)
                        sb2 = nxt(tmpf, tmpf_rr)
                        A(lambda e, sb2=sb2, cc=cc: e.activation(sb2[:], bap(cc), AF.Sigmoid), reads=[bk(cc)], writes=[sb2])
                        V(lambda e, sb2=sb2, cc=cc: e.tensor_tensor(sb2[:], bap(2 + cc), sb2[:], ALU.mult),
                          reads=[bk(2 + cc), sb2], writes=[sb2])
                        V(lambda e, sa=sas[cc], sb2=sb2, m=m: e.tensor_tensor(mT[:, m, :], sa[:], sb2[:], ALU.add),
                          reads=[sas[cc], sb2], writes=[(mT, m)])
                mk_ = [(mT, k) for k in range(8)]
                for pc in range(4):
                    wb = load_wq(w_out, D, pc * 256, 256)
                    for cc in range(2):
                        m = pc * 2 + cc
                        bb = 4 + (m % 2)
                        for k in range(8):
                            mm(bap(bb), wb[:, k, cc * 128:(cc + 1) * 128], mT[:, k, :], k == 0, k == 7, mk_ + [wb], bk(bb), k == 7)
                        flush_def()
                        A(lambda e, m=m, bb=bb: e.activation(yT[:, m, :], bap(bb), AF.Copy), reads=[bk(bb)], writes=[(yT, m)])
                        stats_accum(6, yT[:, m, :], [(yT, m)], m, 8, defer=True)
                flush_def()
                if debug == "m" and g == 3:
                    V(lambda e: e.tensor_copy(hTe[:, :, 0:512], mT[:]), reads=[mT], writes=[hTe])
                    V(lambda e: e.tensor_copy(uext[:, :, 0:512], yT[:]), reads=[yT], writes=[uext])
                r1 = rstd_from_ps(6, D)
                for k in range(8):
                    jt = nxt(tmpf, tmpf_rr)
                    V(lambda e, k=k, jt=jt, r1=r1: e.tensor_tensor(jt[:], yT[:, k, :], r1[:], ALU.mult),
                      reads=[(yT, k), r1], writes=[jt])
                    V(lambda e, k=k, jt=jt: e.scalar_tensor_tensor(xT[:, k, :], jt[:], der[:, 16 + k:17 + k], xT[:, k, :], ALU.mult, ALU.add),
                      reads=[jt, (der, 16), xT], writes=[xT])
                    stats_accum(7, xT[:, k, :], [xT], k, 8)
                r2 = rstd_from_ps(7, D)
                for k in range(8):
                    jt = nxt(tmpf, tmpf_rr)
                    V(lambda e, k=k, jt=jt, r2=r2: e.scalar_tensor_tensor(jt[:], xT[:, k, :], der[:, 24 + k:25 + k], r2[:], ALU.mult, ALU.mult),
                      reads=[xT, (der, 24), r2], writes=[jt])
                    A(lambda e, k=k, jt=jt: e.activation(h2T[:, k, :], jt[:], AF.Identity, bias=der[:, 32 + k:33 + k]),
                      reads=[jt, (der, 32)], writes=[(h2T, k)])
                h2k = [(h2T, k) for k in range(8)]
                nlst = fh_blocks(g + 1) if g + 1 < 4 else None
                nhd = {}
                for pc in range(16):
                    if nlst is not None and pc % 2 == 0:
                        if pc >= 2:
                            xb_trans(nhd[pc // 2 - 1], hTe, nlst[pc // 2 - 1][2], 0, 8)
                        nhd[pc // 2] = xb_prep(nlst[pc // 2][0], nlst[pc // 2][1])
                    wb = load_wq(w_mlp_in, D, pc * 256, 256)
                    for cc in range(2):
                        j = pc * 2 + cc
                        bb = 4 + (j % 4)
                        for k in range(8):
                            mm(bap(bb), wb[:, k, cc * 128:(cc + 1) * 128], h2T[:, k, :], k == 0, k == 7, h2k + [wb], bk(bb), k == 7)
                        jt = nxt(tmpf, tmpf_rr)
                        A(lambda e, jt=jt, bb=bb: e.activation(jt[:], bap(bb), AF.Relu), reads=[bk(bb)], writes=[jt])
                        V(lambda e, jt=jt, j=j: e.tensor_tensor(hid[:, j, :], jt[:], jt[:], ALU.mult), reads=[jt],
                          writes=[(arena, None) if j == 0 else (arena, ("h", j))])
                if nlst is not None:
                    xb_trans(nhd[7], hTe, nlst[7][2], 0, 8)
                for pc in range(4):
                    for cc in range(2):
                        pass
                    wbs = []
                    for rr_ in range(4):
                        wb = load_wq(w_mlp_out, D, pc * 256, 256, r0=rr_ * 1024)
                        for cc in range(2):
                            bb = 4 + cc
                            for k in range(8):
                                kk = rr_ * 8 + k
                                mm(bap(bb), wb[:, k, cc * 128:(cc + 1) * 128], hid[:, kk, :], kk == 0, kk == 31,
                                   [arena, wb], bk(bb), (k == 7))
                    flush_def()
                    for cc in range(2):
                        m = pc * 2 + cc
                        bb = 4 + cc
                        A(lambda e, m=m, bb=bb: e.activation(yT[:, m, :], bap(bb), AF.Copy), reads=[bk(bb)], writes=[(yT, m)])
                        stats_accum(6, yT[:, m, :], [(yT, m)], m, 8, defer=True)
                flush_def()
                r3 = rstd_from_ps(6, D)
                for k in range(8):
                    jt = nxt(tmpf, tmpf_rr)
                    V(lambda e, k=k, jt=jt, r3=r3: e.tensor_tensor(jt[:], yT[:, k, :], r3[:], ALU.mult),
                      reads=[(yT, k), r3], writes=[jt])
                    V(lambda e, k=k, jt=jt: e.scalar_tensor_tensor(yT[:, k, :], jt[:], der[:, 40 + k:41 + k], xT[:, k, :], ALU.mult, ALU.add),
                      reads=[jt, (der, 40), xT], writes=[(yT, k)])
                for b in range(4):
                    ob_ = oblk[b % 2]
                    for half in range(2):
                        tb_i = 2 + half
                        for kk in range(4):
                            k = half * 4 + kk
                            P(lambda e, k=k, kk=kk, tb_i=tb_i, b=b: e.transpose(bap(tb_i, kk * 128, (kk + 1) * 128),
                                                                                 yT[:, k, b * 128:(b + 1) * 128], ident_f[:]),
                              reads=[yT, ident_f], writes=[bk(tb_i)], inc=(kk == 3))
                        evac_copy(ob_[:, half * 512:(half + 1) * 512], bap(tb_i), [bk(tb_i)], [(ob_, half)])
                    blk = g * 4 + b
                    tok = S.dma("act", lambda e, ob_=ob_, blk=blk: e.dma_start(out=out[blk * 128:(blk + 1) * 128, :], in_=ob_[:]),
                                reads=[ob_])
                    out_toks.append(tok)


        run_phases()
        if stop is not None:
            out_toks.append(S.dma("act", lambda e: e.dma_start(out=out[0:128, :], in_=xblk[0][:]), reads=[xblk[0]]))
        last = {}
        for (s, v) in out_toks:
            last[s] = max(last.get(s, 0), v)
        S.wait_all("act", list(last.items()))

        with nc.Block() as block:
            def emit(engname, eng):
                for (waits, fn, inc) in S.q[engname]:
                    for (s, v) in waits:
                        eng.wait_ge(sems[s], v)
                    if fn is None:
                        continue
                    ins = fn(eng)
                    if inc is not None:
                        ins.then_inc(sems[inc[0]], inc[1])

            @block.sync
            def _(e):
                emit("sp", e)

            @block.tensor
            def _(e):
                emit("pe", e)

            @block.scalar
            def _(e):
                emit("act", e)

            @block.vector
            def _(e):
                emit("dve", e)

            @block.gpsimd
            def _(e):
                emit("pool", e)
    return nc


def _prep_inputs(inputs):
    x = np.asarray(inputs["x"], np.float32)
    pos = np.asarray(inputs["positions"], np.int32)
    c = np.asarray(inputs["c"], np.float32)
    w_in = np.ascontiguousarray(np.asarray(inputs["w_in"], np.float32)[0])
    w_uq = np.ascontiguousarray(np.asarray(inputs["w_uq"], np.float32)[0])
    kr = w_in[:, 2688:2752]
    w_kr_sw = np.ascontiguousarray(np.concatenate([kr[:, 32:64], kr[:, 0:32]], axis=1))
    uq3 = w_uq.reshape(384, 8, 192)[:, :, 128:192]
    w_uq_sw = np.ascontiguousarray(np.concatenate([uq3[:, :, 32:64], uq3[:, :, 0:32]], axis=2).reshape(384, 512))
    k_idx = np.arange(128)[:, None]
    q_idx = np.arange(128)[None, :]
    trimask = np.where(k_idx <= q_idx, 0.0, NEG).astype(np.float32)
    ident = np.eye(128, dtype=np.float32)
    inv = (1.0 / (np.float32(10000.0) ** (np.arange(0, 64, 2, dtype=np.float32) / np.float32(64)))).astype(np.float32)
    invf = np.concatenate([inv, inv]).reshape(64, 1).astype(np.float32)
    sgn = np.concatenate([-np.ones(32), np.ones(32)]).reshape(64, 1).astype(np.float32)
    shared = {
        "trimask": trimask, "ident": ident, "invf": invf, "sgn": sgn,
        "w_ada": np.ascontiguousarray(inputs["w_ada"][0], np.float32),
        "b_ada": np.ascontiguousarray(inputs["b_ada"][0], np.float32),
        "g_pre_mix": np.ascontiguousarray(inputs["g_pre_mix"][0], np.float32),
        "g_post_mix": np.ascontiguousarray(inputs["g_post_mix"][0], np.float32),
        "g_pre_mlp": np.ascontiguousarray(inputs["g_pre_mlp"][0], np.float32),
        "g_post_mlp": np.ascontiguousarray(inputs["g_post_mlp"][0], np.float32),
        "w_in": w_in, "w_kr_sw": w_kr_sw,
        "conv_w": np.ascontiguousarray(inputs["conv_w"][0], np.float32),
        "conv_b": np.ascontiguousarray(inputs["conv_b"][0], np.float32),
        "conv_norm_g": np.ascontiguousarray(inputs["conv_norm_g"][0], np.float32),
        "conv_norm_b": np.ascontiguousarray(inputs["conv_norm_b"][0], np.float32),
        "w_conv_out": np.ascontiguousarray(inputs["w_conv_out"][0], np.float32),
        "q_norm_g": np.ascontiguousarray(inputs["q_norm_g"][0], np.float32),
        "w_uq": w_uq, "w_uq_sw": w_uq_sw,
        "kv_norm_g": np.ascontiguousarray(inputs["kv_norm_g"][0], np.float32),
        "w_ukv": np.ascontiguousarray(inputs["w_ukv"][0], np.float32),
        "w_attn_out": np.ascontiguousarray(inputs["w_attn_out"][0], np.float32),
        "w_out": np.ascontiguousarray(inputs["w_out"][0], np.float32),
        "w_mlp_in": np.ascontiguousarray(inputs["w_mlp_in"][0], np.float32),
        "w_mlp_out": np.ascontiguousarray(inputs["w_mlp_out"][0], np.float32),
    }
    in_maps = []
    for core in range(8):
        b, p = core // 2, core % 2
        xb = x[b].reshape(32, 128, D)
        pb = pos[b].reshape(32, 128)
        own = [2 * i + p for i in range(16)]
        oth = [2 * i + 1 - p for i in range(16)]
        halo = np.zeros((16, 32, D), np.float32)
        for i in range(16):
            st = own[i] * 128
            if st > 0:
                halo[i] = x[b, st - 32:st]
        m = dict(shared)
        m["x_own"] = np.ascontiguousarray(xb[own].reshape(NOWN, D))
        m["x_oth"] = np.ascontiguousarray(xb[oth].reshape(NOWN, D))
        m["x_halo"] = np.ascontiguousarray(halo.reshape(512, D))
        m["pos_own"] = np.ascontiguousarray(pb[own].reshape(NOWN))
        m["pos_oth"] = np.ascontiguousarray(pb[oth].reshape(NOWN))
        m["c"] = np.ascontiguousarray(c[b])
        m["pairmask"] = np.full((128, 128), 0.0 if p == 1 else NEG, np.float32)
        m["halomask"] = np.full((128, 1), 1.0 if p == 1 else 0.0, np.float32)
        in_maps.append(m)
    return in_maps


def kernel(**inputs):
    in_maps = _prep_inputs(inputs)
    nc = build_nc()
    res = run_bass_kernel_spmd(nc, in_maps, core_ids=list(range(8)))
    outf = np.zeros((4, 32, 128, D), np.float32)
    for core in range(8):
        b, p = core // 2, core % 2
        o = np.asarray(res.results[core]["out"]).reshape(16, 128, D)
        for i in range(16):
            outf[b, 2 * i + p] = o[i]
    return outf.reshape(4, 4096, D)
```

```python
import contextlib
import math
import numpy as np
import concourse.bass as bass
import concourse.mybir as mybir
from concourse.bass_utils import run_bass_kernel_spmd

F32 = mybir.dt.float32
BF = mybir.dt.bfloat16
I32 = mybir.dt.int32
AF = mybir.ActivationFunctionType
ALU = mybir.AluOpType

D = 1024
KC = 8
NOWN = 2048
EPS = 1e-6
NEG = -30000.0
SCALE = 1.0 / math.sqrt(192.0)
TWO_PI = 2.0 * math.pi
C1 = 6.28125
C2 = TWO_PI - 6.28125

ENGS = ("pe", "act", "dve", "pool", "sp")
NDMA = 12


class _Rec:
    def __init__(self):
        self.call = None

    def __getattr__(self, name):
        def f(*a, **k):
            self.call = (name, a, k)
            return self
        return f


def _bind(fn):
    if fn is None:
        return None
    r = _Rec()
    fn(r)
    name, a, k = r.call
    return lambda eng: getattr(eng, name)(*a, **k)


class Sched:
    def __init__(self):
        self.q = {e: [] for e in ENGS}
        self.cnt = {e: 0 for e in ENGS}
        self.waited = {e: {} for e in ENGS}
        self.state = {}
        self.dma_tot = [0] * NDMA
        self.dma_rr = 0
        self.all_dma_tokens = {}

    def _entries(self, buf, key, create):
        d = self.state.setdefault(id(buf), {})
        if key is None:
            if create and None not in d:
                d[None] = {"w": None, "r": {}}
            return list(d.values()) if not create else list(d.values())
        out = []
        if key not in d and create:
            d[key] = {"w": None, "r": {}}
        if key in d:
            out.append(d[key])
        if None in d:
            out.append(d[None])
        return out

    def _deps(self, reads, writes):
        deps = {}

        def add(tok):
            if tok is None:
                return
            s, v = tok
            if deps.get(s, 0) < v:
                deps[s] = v

        for (b, k) in reads:
            for st in self._entries(b, k, False):
                add(st["w"])
        for (b, k) in writes:
            for st in self._entries(b, k, False):
                add(st["w"])
                for s, v in st["r"].items():
                    add((s, v))
        return deps

    def _commit(self, reads, writes, tok):
        for (b, k) in reads:
            d = self.state.setdefault(id(b), {})
            if k not in d:
                d[k] = {"w": None, "r": {}}
            st = d[k]
            s, v = tok
            if st["r"].get(s, 0) < v:
                st["r"][s] = v
        for (b, k) in writes:
            d = self.state.setdefault(id(b), {})
            if k is None:
                d.clear()
            d[k] = {"w": tok, "r": {}}

    def _norm(self, lst):
        out = []
        for x in lst:
            if isinstance(x, tuple):
                out.append(x)
            else:
                out.append((x, None))
        return out

    def op(self, eng, fn, reads=(), writes=(), inc=True):
        fn = _bind(fn)
        reads = self._norm(reads)
        writes = self._norm(writes)
        deps = self._deps(reads, writes)
        waits = []
        for s, v in deps.items():
            if s == eng and eng == "pe":
                continue
            if self.waited[eng].get(s, 0) >= v:
                continue
            self.waited[eng][s] = v
            waits.append((s, v))
        if inc:
            self.cnt[eng] += 1
            tok = (eng, self.cnt[eng])
            self.q[eng].append((waits, fn, (eng, 1)))
        else:
            tok = (eng, self.cnt[eng] + 1)
            self.q[eng].append((waits, fn, None))
        self._commit(reads, writes, tok)
        return tok

    def dma(self, eng, fn, reads=(), writes=()):
        fn = _bind(fn)
        reads = self._norm(reads)
        writes = self._norm(writes)
        j = self.dma_rr
        self.dma_rr = (self.dma_rr + 1) % NDMA
        sem = "d%d" % j
        deps = self._deps(reads, writes)
        if self.dma_tot[j] > 0:
            if deps.get(sem, 0) < self.dma_tot[j]:
                deps[sem] = self.dma_tot[j]
        waits = []
        for s, v in deps.items():
            if self.waited[eng].get(s, 0) >= v:
                continue
            self.waited[eng][s] = v
            waits.append((s, v))
        self.dma_tot[j] += 16
        tok = (sem, self.dma_tot[j])
        self.q[eng].append((waits, fn, (sem, 16)))
        self._commit(reads, writes, tok)
        return tok

    def barrier(self):
        toks = [(e, self.cnt[e]) for e in ENGS if self.cnt[e] > 0]
        toks += [("d%d" % j, self.dma_tot[j]) for j in range(NDMA) if self.dma_tot[j] > 0]
        for e in ENGS:
            self.wait_all(e, [t for t in toks if t[0] != e])
        self.state = {}

    def wait_all(self, eng, toks):
        waits = []
        for (s, v) in toks:
            if self.waited[eng].get(s, 0) >= v:
                continue
            self.waited[eng][s] = v
            waits.append((s, v))
        self.q[eng].append((waits, None, None))


def build_nc(debug=None, stop=None):
    nc = bass.Bass("TRN2", target_bir_lowering=False)
    S = Sched()

    def din(name, shape, dt=F32):
        return nc.dram_tensor(name, list(shape), dt, kind="ExternalInput").ap()

    x_own = din("x_own", [NOWN, D])
    x_oth = din("x_oth", [NOWN, D])
    x_halo = din("x_halo", [512, D])
    pos_own = nc.dram_tensor("pos_own", [NOWN], I32, kind="ExternalInput")
    pos_oth = nc.dram_tensor("pos_oth", [NOWN], I32, kind="ExternalInput")
    c_in = din("c", [D])
    pairmask_in = din("pairmask", [128, 128])
    trimask_in = din("trimask", [128, 128])
    ident_in = din("ident", [128, 128])
    halomask_in = din("halomask", [128, 1])
    invf_in = din("invf", [64, 1])
    sgn_in = din("sgn", [64, 1])
    w_ada = din("w_ada", [D, 6 * D])
    b_ada = din("b_ada", [6 * D])
    g_pre_mix = din("g_pre_mix", [D])
    g_post_mix = din("g_post_mix", [D])
    g_pre_mlp = din("g_pre_mlp", [D])
    g_post_mlp = din("g_post_mlp", [D])
    w_in = din("w_in", [D, 4800])
    w_kr_sw = din("w_kr_sw", [D, 64])
    conv_w = din("conv_w", [31, D])
    conv_b = din("conv_b", [D])
    conv_norm_g = din("conv_norm_g", [D])
    conv_norm_b = din("conv_norm_b", [D])
    w_conv_out = din("w_conv_out", [D, D])
    q_norm_g = din("q_norm_g", [384])
    w_uq = din("w_uq", [384, 1536])
    w_uq_sw = din("w_uq_sw", [384, 512])
    kv_norm_g = din("kv_norm_g", [256])
    w_ukv = din("w_ukv", [256, 2048])
    w_attn_out = din("w_attn_out", [D, D])
    w_out = din("w_out", [D, D])
    w_mlp_in = din("w_mlp_in", [D, 4 * D])
    w_mlp_out = din("w_mlp_out", [4 * D, D])
    out = nc.dram_tensor("out", [NOWN, D], F32, kind="ExternalOutput").ap()
    wq = nc.dram_tensor("wq", [60, 128, 2048], BF, kind="Internal").ap()
    wq_key = object()
    dbg = None

    es = contextlib.ExitStack()
    with es:
        def sb(name, shape, dt=F32):
            return es.enter_context(nc.sbuf_tensor("s_" + name, list(shape), dt))

        sems = {}
        for e in ENGS:
            sems[e] = es.enter_context(nc.semaphore("sem_" + e))
        for j in range(NDMA):
            sems["d%d" % j] = es.enter_context(nc.semaphore("sem_d%d" % j))

        PD = [es.enter_context(nc.psum_tensor("pd%d" % i, [128, 1024], F32)) for i in range(4)]

        def bank(i):
            t = PD[i // 2]
            h = i % 2
            return t, h

        def bk(i):
            t, h = bank(i)
            return (t, h)

        def bap(i, c0=0, c1=512):
            t, h = bank(i)
            return t[:, h * 512 + c0: h * 512 + c1]

        def bap_bf(i):
            t, h = bank(i)
            return t.bitcast(BF)[:, h * 1024:(h + 1) * 1024]

        ident_f = sb("ident_f", [128, 128])
        ident_b = sb("ident_b", [128, 128], BF)
        ones_b = sb("ones_b", [128, 128], BF)
        tri_b = sb("tri_b", [128, 128], BF)
        pair_b = sb("pair_b", [128, 128], BF)
        halom = sb("halom", [128, 1])
        invf = sb("invf", [64, 1])
        sgn = sb("sgn", [64, 1])
        modT = sb("modT", [128, 48])
        vecs = sb("vecs", [128, 128])
        cwT = sb("cwT", [128, 8, 31])
        V_GPRE, V_GPOST, V_GPRE2, V_GPOST2 = 0, 8, 16, 24
        V_CB, V_CG, V_CNB = 32, 40, 48
        V_QG, V_KVG = 56, 59
        der = sb("der", [128, 48])
        NST = 2
        wst = [sb("wst%d" % i, [128, 8, 256]) for i in range(NST)]
        wbf = [sb("wbf%d" % i, [128, 8, 256], BF) for i in range(NST)]
        wrr = [0]
        xblk = [sb("xblk%d" % i, [128, D]) for i in range(2)]
        xnb = [sb("xnb%d" % i, [128, D], BF) for i in range(2)]
        small = [sb("small%d" % i, [128, 4]) for i in range(4)]
        small_rr = [0]
        tmpf = [sb("tmpf%d" % i, [128, 512]) for i in range(4)]
        tmpf_rr = [0]
        tmpb = [sb("tmpb%d" % i, [128, 512], BF) for i in range(6)]
        tmpb_rr = [0]
        rstd_t = [sb("rstd%d" % i, [128, 512]) for i in range(2)]
        rstd_rr = [0]

        def nxt(lst, rr):
            t = lst[rr[0] % len(lst)]
            rr[0] += 1
            return t

        def A(fn, **kw):
            return S.op("act", fn, **kw)

        def V(fn, **kw):
            return S.op("dve", fn, **kw)

        def G(fn, **kw):
            return S.op("pool", fn, **kw)

        def P(fn, **kw):
            return S.op("pe", fn, **kw)

        def dma_in(dst_ap, src_ap, dst_buf, key=None, eng="sp", nonc=False):
            def f(e, dst_ap=dst_ap, src_ap=src_ap):
                if nonc:
                    return e.dma_start(out=dst_ap, in_=src_ap, allow_slow_non_contiguous=True)
                return e.dma_start(out=dst_ap, in_=src_ap)
            return S.dma(eng, f, writes=[(dst_buf, key)])

        def mm(out_ap, lhsT, rhs, start, stop, reads, wkey, last):
            def f(e):
                return e.matmul(out_ap, lhsT, rhs, start=start, stop=stop)
            return S.op("pe", f, reads=reads, writes=[wkey], inc=last)

        def load_w(src, rows, c0, ncols, r0=0, cast=None):
            i = wrr[0] % NST
            wrr[0] += 1
            kc = rows // 128
            st, wb = wst[i], wbf[i]
            src_ap = src[r0:r0 + rows, c0:c0 + ncols].rearrange("(k p) c -> p k c", p=128)
            dma_in(st[:, 0:kc, 0:ncols], src_ap, st)
            if cast == "pool" or (cast is None and wrr[0] % 2 == 0):
                G(lambda e: e.tensor_copy(wb[:, 0:kc, 0:ncols], st[:, 0:kc, 0:ncols]), reads=[st], writes=[wb])
            else:
                V(lambda e: e.tensor_copy(wb[:, 0:kc, 0:ncols], st[:, 0:kc, 0:ncols]), reads=[st], writes=[wb])
            return wb

        def _unused():
            pass

        out_toks = []

        def run_phases():
            dma_in(ident_f[:], ident_in, ident_f)
            dma_in(halom[:], halomask_in, halom)
            dma_in(invf[:], invf_in, invf)
            dma_in(sgn[:], sgn_in, sgn)
            t0 = tmpf[0]
            t1 = tmpf[1]
            dma_in(t0[:, 0:128], trimask_in, t0)
            dma_in(t1[:, 0:128], pairmask_in, t1)
            V(lambda e: e.tensor_copy(ident_b[:], ident_f[:]), reads=[ident_f], writes=[ident_b])
            V(lambda e: e.memset(ones_b[:], 1.0), writes=[ones_b])
            V(lambda e: e.tensor_copy(tri_b[:], t0[:, 0:128]), reads=[t0], writes=[tri_b])
            V(lambda e: e.tensor_copy(pair_b[:], t1[:, 0:128]), reads=[t1], writes=[pair_b])
            tmpf_rr[0] = 2
            if stop == -1:
                return
            stg = tmpf[2]
            V(lambda e: e.memset(stg[:, 0:128], 0.0), writes=[stg])
            for col, src, n in ((V_GPRE, g_pre_mix, D), (V_GPOST, g_post_mix, D), (V_GPRE2, g_pre_mlp, D),
                                (V_GPOST2, g_post_mlp, D), (V_CB, conv_b, D), (V_CG, conv_norm_g, D),
                                (V_CNB, conv_norm_b, D), (V_QG, q_norm_g, 384), (V_KVG, kv_norm_g, 256)):
                dma_in(stg[col:col + n // 128, 0:128], src.rearrange("(k p) -> k p", p=128), stg)
            dma_in(stg[64:112, 0:128], b_ada.rearrange("(k p) -> k p", p=128), stg)
            dma_in(stg[112:120, 0:128], c_in.rearrange("(k p) -> k p", p=128), stg)
            P(lambda e: e.transpose(bap(1, 0, 120), stg[0:120, 0:128], ident_f[0:120, 0:120]),
              reads=[stg, ident_f], writes=[bk(1)])
            V(lambda e: e.tensor_copy(vecs[:, 0:120], bap(1, 0, 120)), reads=[bk(1)], writes=[vecs])
            if stop == -2:
                return
            badaT = vecs[:, 64:112]
            cT = vecs[:, 112:120]
            cwn = xblk[0]
            dma_in(cwn[0:31, :], conv_w, cwn)
            for c in range(8):
                P(lambda e, c=c: e.transpose(bap(2, c * 32, c * 32 + 31), cwn[0:31, c * 128:(c + 1) * 128], ident_f[0:31, 0:31]),
                  reads=[cwn, ident_f], writes=[bk(2)])
            V(lambda e: e.tensor_copy(cwT[:], bap(2, 0, 256).rearrange("p (c k) -> p c k", k=32)[:, :, 0:31]),
              reads=[bk(2)], writes=[cwT])
            if stop == -3:
                return
            scb = sb("scb", [128, 8])
            A(lambda e: e.activation(scb[:], cT, AF.Silu), reads=[vecs], writes=[scb])
            if stop == -4:
                return
            def load_w32(src, rows, c0, ncols):
                i = wrr[0] % NST
                wrr[0] += 1
                st = wst[i]
                dma_in(st[:, 0:rows // 128, 0:ncols], src[0:rows, c0:c0 + ncols].rearrange("(k p) c -> p k c", p=128), st)
                return st

            def mod_part(p0, p1, MODB, cast=None):
                for pc in range(p0, p1):
                    wb = load_w32(w_ada, D, pc * 256, 256)
                    for jj in range(2):
                        j = pc * 2 + jj
                        for k in range(8):
                            mm(bap(MODB, j, j + 1), wb[:, k, jj * 128:(jj + 1) * 128], scb[:, k:k + 1],
                               k == 0, k == 7, [wb, scb], bk(MODB), k == 7)
                V(lambda e: e.tensor_tensor(modT[:, p0 * 2:p1 * 2], bap(MODB, p0 * 2, p1 * 2), vecs[:, 64 + p0 * 2:64 + p1 * 2], ALU.add),
                  reads=[bk(MODB), vecs], writes=[(modT, p0)])

            mod_part(0, 8, 0)
            if stop == -5:
                return
            V(lambda e: e.scalar_tensor_tensor(der[:, 0:8], modT[:, 8:16], 1.0, vecs[:, V_GPRE:V_GPRE + 8], ALU.add, ALU.mult),
              reads=[modT, vecs], writes=[(der, 0)])
            V(lambda e: e.tensor_copy(der[:, 8:16], modT[:, 0:8]), reads=[modT], writes=[(der, 8)])

            def mod_late():
                mod_part(8, 24, 7, cast="pool")
                V(lambda e: e.tensor_tensor(der[:, 16:24], modT[:, 16:24], vecs[:, V_GPOST:V_GPOST + 8], ALU.mult),
                  reads=[modT, vecs], writes=[(der, 16)])
                V(lambda e: e.scalar_tensor_tensor(der[:, 24:32], modT[:, 32:40], 1.0, vecs[:, V_GPRE2:V_GPRE2 + 8], ALU.add, ALU.mult),
                  reads=[modT, vecs], writes=[(der, 24)])
                V(lambda e: e.tensor_copy(der[:, 32:40], modT[:, 24:32]), reads=[modT], writes=[(der, 32)])
                V(lambda e: e.tensor_tensor(der[:, 40:48], modT[:, 40:48], vecs[:, V_GPOST2:V_GPOST2 + 8], ALU.mult),
                  reads=[modT, vecs], writes=[(der, 40)])
            DER_ALL = [(der, 0), (der, 8), (der, 16), (der, 24), (der, 32), (der, 40)]

            TPB = [0, 1]
            tp_rr = [0]
            xb_rr = [0]

            def xb_prep(src_rows_ap, nrows):
                i = xb_rr[0] % 2
                xb_rr[0] += 1
                xb = xblk[i]
                xn = xnb[i]
                dma_in(xb[0:nrows, :], src_rows_ap, xb)
                sm = nxt(small, small_rr)
                A(lambda e: e.activation(xn[0:nrows, :], xb[0:nrows, :], AF.Square, accum_out=sm[0:nrows, 0:1]),
                  reads=[xb], writes=[xn, (sm, 0)])
                A(lambda e: e.activation(sm[0:nrows, 3:4], sm[0:nrows, 0:1], AF.Sqrt, bias=EPS, scale=1.0 / D),
                  reads=[(sm, 0)], writes=[(sm, 3)])
                V(lambda e: e.reciprocal(sm[0:nrows, 2:3], sm[0:nrows, 3:4]), reads=[(sm, 3)], writes=[(sm, 2)])
                V(lambda e: e.tensor_scalar(xn[0:nrows, :], xb[0:nrows, :], sm[0:nrows, 2:3], None, ALU.mult),
                  reads=[xb, (sm, 2)], writes=[xn])
                return (xn, nrows)

            def xb_trans(hd, hT, col0, gsc, shc):
                xn, nrows = hd
                b = TPB[tp_rr[0] % 2]
                tp_rr[0] += 1
                tpv = bap_bf(b)
                for k in range(8):
                    P(lambda e, k=k: e.transpose(tpv[:, k * 128:k * 128 + nrows], xn[0:nrows, k * 128:(k + 1) * 128],
                                                  ident_b[0:nrows, 0:nrows]),
                      reads=[xn, ident_b], writes=[bk(b)], inc=(k == 7))
                for k in range(8):
                    if b == TPB[0]:
                        V(lambda e, k=k: e.tensor_scalar(hT[:, k, col0:col0 + nrows], tpv[:, k * 128:k * 128 + nrows],
                                                         der[:, gsc + k:gsc + k + 1], der[:, shc + k:shc + k + 1],
                                                         ALU.mult, ALU.add),
                          reads=[bk(b), (der, gsc), (der, shc)], writes=[(hT, k)])
                    else:
                        A(lambda e, k=k: e.activation(hT[:, k, col0:col0 + nrows], tpv[:, k * 128:k * 128 + nrows],
                                                      AF.Identity, bias=der[:, shc + k:shc + k + 1],
                                                      scale=der[:, gsc + k:gsc + k + 1]),
                          reads=[bk(b), (der, gsc), (der, shc)], writes=[(hT, k)])

            def x_block_to_hT(src_rows_ap, nrows, hT, col0, gsc, shc):
                xb_trans(xb_prep(src_rows_ap, nrows), hT, col0, gsc, shc)

            def rstd_from_ps(ps_bank, nfeat, ncols=512):
                r = nxt(rstd_t, rstd_rr)
                jt = nxt(tmpf, tmpf_rr)
                A(lambda e: e.activation(jt[:, 0:ncols], bap(ps_bank, 0, ncols), AF.Sqrt, bias=EPS, scale=1.0 / nfeat),
                  reads=[bk(ps_bank)], writes=[jt])
                V(lambda e: e.reciprocal(r[:, 0:ncols], jt[:, 0:ncols]), reads=[jt], writes=[r])
                return r

            if stop == 0:
                return
            oT = sb("oT", [128, 8, NOWN], BF)
            ph12 = es.enter_context(contextlib.ExitStack())
            ph1 = es.enter_context(contextlib.ExitStack())

            def sb12(name, shape, dt=F32):
                return ph12.enter_context(nc.sbuf_tensor("s_" + name, list(shape), dt))

            def sb1(name, shape, dt=F32):
                return ph1.enter_context(nc.sbuf_tensor("s_" + name, list(shape), dt))

            kvn = [sb12("kvn_own", [128, 2, NOWN], BF), sb12("kvn_oth", [128, 2, NOWN], BF)]
            krT = [sb12("kr_own", [128, NOWN], BF), sb12("kr_oth", [128, NOWN], BF)]
            for kr_ in krT:
                G(lambda e, kr_=kr_: e.memset(kr_[64:128, :], 0.0), writes=[kr_])
            qn = sb12("qn", [128, 3, NOWN], BF)
            CS = sb12("cs_own", [64, 2, NOWN])
            wuq = sb12("wuq", [128, 3, 2048], BF)
            hTs = [sb1("hT%d" % i, [128, 8, 640], BF) for i in range(2)]
            wlat = sb1("wlat", [128, 8, 768], BF)
            cs_tmp = sb1("cs_tmp", [64, 2, 512])
            posi = sb1("posi", [64, 512], I32)
            angs = [sb1("ang%d" % i, [64, 512]) for i in range(4)]
            ni_t = sb1("ni_t", [64, 512], I32)

            for pc, (src, c0, n, d0) in enumerate(((w_in, 2048, 256, 0), (w_in, 2304, 256, 256), (w_in, 2560, 192, 512),
                                                   (w_kr_sw, 0, 64, 704))):
                wb = load_w(src, D, c0, n)
                G(lambda e, wb=wb, n=n, d0=d0: e.tensor_copy(wlat[:, :, d0:d0 + n], wb[:, :, 0:n]),
                  reads=[wb], writes=[(wlat, pc)])
            WL = [(wlat, i) for i in range(4)]
            if stop == 10:
                return

            def rope_tables(pos_t, c0, dst, dcol):
                src = bass.AP(pos_t, c0, [[0, 64], [1, 512]])
                dma_in(posi[:], src, posi)
                a0, a1, a2, a3 = angs
                V(lambda e: e.tensor_copy(a0[:], posi[:]), reads=[posi], writes=[a0])
                V(lambda e: e.tensor_scalar(a0[:], a0[:], invf[:, 0:1], None, ALU.mult), reads=[a0, invf], writes=[a0])
                V(lambda e: e.tensor_scalar(a1[:], a0[:], 1.0 / TWO_PI, None, ALU.mult), reads=[a0], writes=[a1])
                V(lambda e: e.tensor_copy(ni_t[:], a1[:]), reads=[a1], writes=[ni_t])
                V(lambda e: e.tensor_copy(a1[:], ni_t[:]), reads=[ni_t], writes=[a1])
                V(lambda e: e.scalar_tensor_tensor(a2[:], a1[:], -C1, a0[:], ALU.mult, ALU.add), reads=[a1, a0], writes=[a2])
                V(lambda e: e.scalar_tensor_tensor(a2[:], a1[:], -C2, a2[:], ALU.mult, ALU.add), reads=[a1, a2], writes=[a2])
                V(lambda e: e.tensor_scalar(a3[:], a2[:], math.pi, -TWO_PI, ALU.is_gt, ALU.mult), reads=[a2], writes=[a3])
                V(lambda e: e.tensor_tensor(a2[:], a2[:], a3[:], ALU.add), reads=[a2, a3], writes=[a2])
                V(lambda e: e.tensor_scalar(a3[:], a2[:], -math.pi, TWO_PI, ALU.is_lt, ALU.mult), reads=[a2], writes=[a3])
                V(lambda e: e.tensor_tensor(a2[:], a2[:], a3[:], ALU.add), reads=[a2, a3], writes=[a2])
                V(lambda e: e.tensor_scalar(a1[:], a2[:], math.pi / 2, None, ALU.add), reads=[a2], writes=[a1])
                V(lambda e: e.tensor_scalar(a3[:], a1[:], math.pi, -TWO_PI, ALU.is_gt, ALU.mult), reads=[a1], writes=[a3])
                V(lambda e: e.tensor_tensor(a1[:], a1[:], a3[:], ALU.add), reads=[a1, a3], writes=[a1])
                V(lambda e: e.tensor_scalar(a1[:], a1[:], math.pi, -math.pi, ALU.min, ALU.max), reads=[a1], writes=[a1])
                V(lambda e: e.tensor_scalar(a2[:], a2[:], math.pi, -math.pi, ALU.min, ALU.max), reads=[a2], writes=[a2])
                A(lambda e: e.activation(dst[:, 0, dcol:dcol + 512], a1[:], AF.Sin), reads=[a1], writes=[(dst, dcol)])
                A(lambda e: e.activation(dst[:, 1, dcol:dcol + 512], a2[:], AF.Sin, scale=sgn[:, 0:1]),
                  reads=[a2, sgn], writes=[(dst, dcol)])

            for pc in range(6):
                wb = load_w(w_uq, 384, pc * 256, 256)
                G(lambda e, wb=wb, pc=pc: e.tensor_copy(wuq[:, :, pc * 256:(pc + 1) * 256], wb[:, 0:3, :]),
                  reads=[wb], writes=[(wuq, pc)])
            for pc in range(2):
                wb = load_w(w_uq_sw, 384, pc * 256, 256)
                G(lambda e, wb=wb, pc=pc: e.tensor_copy(wuq[:, :, 1536 + pc * 256:1536 + (pc + 1) * 256], wb[:, 0:3, :]),
                  reads=[wb], writes=[(wuq, 6 + pc)])
            def prep1(j_):
                i_, b_ = j_ // 4, j_ % 4
                grp_, t_ = i_ // 4, i_ % 4
                xsrc = x_own if grp_ == 0 else x_oth
                r0 = t_ * 512 + b_ * 128
                return xb_prep(xsrc[r0:r0 + 128, :], 128)

            hd1 = [prep1(0)]
            for grp in range(2):
                pos_t = pos_own if grp == 0 else pos_oth
                for t in range(4):
                    hT = hTs[(grp * 4 + t) % 2]
                    for b in range(4):
                        j_ = (grp * 4 + t) * 4 + b
                        nh = prep1(j_ + 1) if j_ + 1 < 32 else None
                        xb_trans(hd1[0], hT, b * 128, 0, 8)
                        hd1[0] = nh
                    if grp == 0:
                        rope_tables(pos_t, t * 512, CS, t * 512)
                        cs, cc = CS, t * 512
                    else:
                        rope_tables(pos_t, t * 512, cs_tmp, 0)
                        cs, cc = cs_tmp, 0
                    if stop == 12:
                        return
                    hk = [(hT, k) for k in range(8)]
                    for m in range(2):
                        for k in range(8):
                            mm(bap(2 + m), wlat[:, k, 384 + m * 128:384 + (m + 1) * 128], hT[:, k, 0:512],
                               k == 0, k == 7, hk + WL, bk(2 + m), k == 7)
                    for m in range(2):
                        for k in range(8):
                            mm(bap(5 + m)[0:64, :], wlat[:, k, 640 + m * 64:640 + (m + 1) * 64], hT[:, k, 0:512],
                               k == 0, k == 7, hk + WL, bk(5 + m), k == 7)
                    sq = []
                    for m in range(2):
                        s_ = nxt(tmpb, tmpb_rr)
                        A(lambda e, m=m, s_=s_: e.activation(s_[:], bap(2 + m), AF.Square), reads=[bk(2 + m)], writes=[s_])
                        sq.append(s_)
                    for m in range(2):
                        mm(bap(4), ones_b[:], sq[m][:], m == 0, m == 1, [ones_b, sq[m]], bk(4), True)
                    r = rstd_from_ps(4, 256)
                    for m in range(2):
                        V(lambda e, m=m, r=r: e.scalar_tensor_tensor(kvn[grp][:, m, t * 512:(t + 1) * 512], bap(2 + m),
                                                                     vecs[:, V_KVG + m:V_KVG + m + 1], r[:], ALU.mult, ALU.mult),
                          reads=[bk(2 + m), r, vecs], writes=[(kvn[grp], t)])
                    ta = nxt(tmpf, tmpf_rr)
                    tb_ = nxt(tmpf, tmpf_rr)
                    V(lambda e, ta=ta, cs=cs, cc=cc: e.tensor_tensor(ta[0:64, :], bap(5)[0:64, :], cs[:, 0, cc:cc + 512], ALU.mult),
                      reads=[bk(5), (cs, cc)], writes=[ta])
                    V(lambda e, tb_=tb_, cs=cs, cc=cc: e.tensor_tensor(tb_[0:64, :], bap(6)[0:64, :], cs[:, 1, cc:cc + 512], ALU.mult),
                      reads=[bk(6), (cs, cc)], writes=[tb_])
                    V(lambda e, ta=ta, tb_=tb_: e.tensor_tensor(krT[grp][0:64, t * 512:(t + 1) * 512], ta[0:64, :], tb_[0:64, :], ALU.add),
                      reads=[ta, tb_], writes=[(krT[grp], t)])
                    if stop == 13:
                        return
                    if grp == 0:
                        QB = [2, 3, 7]
                        for m in range(3):
                            for k in range(8):
                                mm(bap(QB[m]), wlat[:, k, m * 128:(m + 1) * 128], hT[:, k, 0:512],
                                   k == 0, k == 7, hk + WL, bk(QB[m]), k == 7)
                        sq = []
                        for m in range(3):
                            s_ = nxt(tmpb, tmpb_rr)
                            A(lambda e, m=m, s_=s_: e.activation(s_[:], bap(QB[m]), AF.Square), reads=[bk(QB[m])], writes=[s_])
                            sq.append(s_)
                        for m in range(3):
                            mm(bap(4), ones_b[:], sq[m][:], m == 0, m == 2, [ones_b, sq[m]], bk(4), True)
                        r = rstd_from_ps(4, 384)
                        for m in range(3):
                            V(lambda e, m=m, r=r: e.scalar_tensor_tensor(qn[:, m, t * 512:(t + 1) * 512], bap(QB[m]),
                                                                         vecs[:, V_QG + m:V_QG + m + 1], r[:], ALU.mult, ALU.mult),
                              reads=[bk(QB[m]), r, vecs], writes=[(qn, t)])
                    if stop == 14:
                        return

            S.barrier()
            ph1.close()
            if stop == 1:
                ph12.close()
                return

            wukv = sb12("wukv", [128, 2, 2048], BF)
            for pc in range(8):
                wb = load_w(w_ukv, 256, pc * 256, 256)
                G(lambda e, wb=wb, pc=pc: e.tensor_copy(wukv[:, :, pc * 256:(pc + 1) * 256], wb[:, 0:2, :]),
                  reads=[wb], writes=[(wukv, pc)])
            KhT = [[sb12("kh%d_%d" % (i, g), [128, NOWN], BF) for g in range(2)] for i in range(1)]
            Vh = [[sb12("vh%d_%d" % (i, g), [128, 16, 128], BF) for g in range(2)] for i in range(1)]
            Qh = [sb12("qh%d" % i, [128, NOWN], BF) for i in range(1)]
            Qr = [sb12("qr%d" % i, [128, NOWN], BF) for i in range(1)]
            G(lambda e: e.memset(Qr[0][64:128, :], 0.0), writes=[Qr[0]])
            Pt = [sb12("pt%d" % i, [128, 512], BF) for i in range(4)]
            pt_rr = [0]
            SB_ = [0, 1, 2]
            s_rr = [0]
            OB = [3, 5]
            LB = [4, 6]
            HBS = [7, 3, 4]
            hb_rr = [0]

            def nhb():
                b_ = HBS[hb_rr[0] % 3]
                hb_rr[0] += 1
                return b_
            evac_rr = [0]

            def evac_copy(dst_ap, src_bank_ap, reads, writes):
                if evac_rr[0] % 2 == 0:
                    V(lambda e: e.tensor_copy(dst_ap, src_bank_ap), reads=reads, writes=writes)
                else:
                    A(lambda e: e.activation(dst_ap, src_bank_ap, AF.Copy), reads=reads, writes=writes)
                evac_rr[0] += 1

            def build_head(h):
                i = 0
                for grp in range(2):
                    for t in range(4):
                        HB = nhb()
                        for k in range(2):
                            mm(bap(HB), wukv[:, k, h * 256:h * 256 + 128], kvn[grp][:, k, t * 512:(t + 1) * 512],
                               k == 0, k == 1, [wukv, kvn[grp]], bk(HB), k == 1)
                        evac_copy(KhT[i][grp][:, t * 512:(t + 1) * 512], bap(HB), [bk(HB)], [(KhT[i][grp], t)])
                    for t in range(4):
                        HB = nhb()
                        for b in range(4):
                            blk = t * 4 + b
                            for k in range(2):
                                mm(bap(HB, b * 128, (b + 1) * 128), kvn[grp][:, k, blk * 128:(blk + 1) * 128],
                                   wukv[:, k, h * 256 + 128:h * 256 + 256],
                                   k == 0, k == 1, [wukv, kvn[grp]], bk(HB), (k == 1 and b == 3))
                        evac_copy(Vh[i][grp][:, t * 4:(t + 1) * 4, :], bap(HB).rearrange("p (b d) -> p b d", d=128),
                                  [bk(HB)], [(Vh[i][grp], t)])
                for t in range(4):
                    HB = nhb()
                    for k in range(3):
                        mm(bap(HB), wuq[:, k, h * 192:h * 192 + 128], qn[:, k, t * 512:(t + 1) * 512],
                           k == 0, k == 2, [wuq, qn], bk(HB), k == 2)
                    evac_copy(Qh[i][:, t * 512:(t + 1) * 512], bap(HB), [bk(HB)], [(Qh[i], t)])
                for t in range(4):
                    HB = nhb()
                    for k in range(3):
                        mm(bap(HB)[0:64, :], wuq[:, k, h * 192 + 128:h * 192 + 192], qn[:, k, t * 512:(t + 1) * 512],
                           k == 0, k == 2, [wuq, qn], bk(HB), k == 2)
                    ta = nxt(tmpf, tmpf_rr)
                    V(lambda e, ta=ta, t=t: e.tensor_tensor(ta[0:64, :], bap(HB)[0:64, :], CS[:, 0, t * 512:(t + 1) * 512], ALU.mult),
                      reads=[bk(HB), CS], writes=[ta])
                    HB = nhb()
                    for k in range(3):
                        mm(bap(HB)[0:64, :], wuq[:, k, 1536 + h * 64:1536 + (h + 1) * 64], qn[:, k, t * 512:(t + 1) * 512],
                           k == 0, k == 2, [wuq, qn], bk(HB), k == 2)
                    tb_ = nxt(tmpf, tmpf_rr)
                    V(lambda e, tb_=tb_, t=t: e.tensor_tensor(tb_[0:64, :], bap(HB)[0:64, :], CS[:, 1, t * 512:(t + 1) * 512], ALU.mult),
                      reads=[bk(HB), CS], writes=[tb_])
                    V(lambda e, ta=ta, tb_=tb_, t=t: e.tensor_tensor(Qr[i][0:64, t * 512:(t + 1) * 512], ta[0:64, :], tb_[0:64, :], ALU.add),
                      reads=[ta, tb_], writes=[(Qr[i], t)])

            def attend_head(h):
                i = 0
                for g in range(4):
                    ob = OB[g % 2]
                    lb = LB[g % 2]
                    visits = [(J, grp) for J in range(4 * g + 4) for grp in range(2)]
                    pend = []

                    def do_pv(v, first, last):
                        J, grp, c0, pt = v
                        mm(bap(ob, c0, 512), Vh[i][grp][:, J, :], pt[:, c0:512], first, last,
                           [Vh[i][grp], pt], bk(ob), True)
                        mm(bap(lb, c0, 512), ones_b[:], pt[:, c0:512], first, last,
                           [ones_b, pt], bk(lb), True)

                    npv = [0]
                    for vi, (J, grp) in enumerate(visits):
                        j = J - 4 * g
                        c0 = 128 * max(j, 0)
                        sbk = SB_[s_rr[0] % 3]
                        s_rr[0] += 1
                        q0 = g * 512 + c0
                        q1 = (g + 1) * 512
                        masked = j >= 0
                        mm(bap(sbk, c0, 512), KhT[i][grp][:, J * 128:(J + 1) * 128], Qh[i][:, q0:q1],
                           True, False, [KhT[i][grp], Qh[i]], bk(sbk), False)
                        mm(bap(sbk, c0, 512), krT[grp][:, J * 128:(J + 1) * 128], Qr[i][:, q0:q1],
                           False, not masked, [krT[grp], Qr[i]], bk(sbk), not masked)
                        if masked:
                            mk = tri_b if grp == 0 else pair_b
                            mm(bap(sbk, c0, c0 + 128), ident_b[:], mk[:], False, True, [ident_b, mk], bk(sbk), True)
                        pt = nxt(Pt, pt_rr)
                        A(lambda e, pt=pt, sbk=sbk, c0=c0: e.activation(pt[:, c0:512], bap(sbk, c0, 512), AF.Exp, scale=SCALE),
                          reads=[bk(sbk)], writes=[pt])
                        pend.append((J, grp, c0, pt))
                        if len(pend) > 2:
                            v = pend.pop(0)
                            do_pv(v, npv[0] == 0, False)
                            npv[0] += 1
                    while pend:
                        v = pend.pop(0)
                        do_pv(v, npv[0] == 0, len(pend) == 0)
                        npv[0] += 1
                    rl = nxt(rstd_t, rstd_rr)
                    V(lambda e, rl=rl, lb=lb: e.reciprocal(rl[:], bap(lb)), reads=[bk(lb)], writes=[rl])
                    V(lambda e, rl=rl, ob=ob, g=g: e.tensor_tensor(oT[:, h, g * 512:(g + 1) * 512], bap(ob), rl[:], ALU.mult),
                      reads=[bk(ob), rl], writes=[(oT, (h, g))])

            prep = []
            for pc in range(4):
                prep += [(w_in, pc * 256, 0), (w_in, 1024 + pc * 256, 0)]
            for pc in range(4):
                prep += [(w_conv_out, pc * 256, 0)]
            for pc in range(4):
                prep += [(w_attn_out, pc * 256, 0), (w_in, 2752 + pc * 256, 0), (w_in, 3776 + pc * 256, 0)]
            for pc in range(4):
                prep += [(w_out, pc * 256, 0)]
            for pc in range(16):
                prep += [(w_mlp_in, pc * 256, 0)]
            for pc in range(4):
                for rr_ in range(4):
                    prep += [(w_mlp_out, pc * 256, rr_ * 1024)]
            prep_tok = {}

            def do_prep():
                for i, (src, c0, r0) in enumerate(prep):
                    wb = load_w(src, D, c0, 256, r0=r0, cast="pool")
                    S.dma("sp", lambda e, wb=wb, i=i: e.dma_start(out=wq[i], in_=wb[:].rearrange("p k c -> p (k c)")),
                          reads=[wb], writes=[(wq_key, i)])

            do_prep()
            build_head(0)
            for h in range(8):
                attend_head(h)
                if h + 1 < 8:
                    build_head(h + 1)
            mod_late()

            S.barrier()
            ph12.close()
            if stop == 2:
                return
            wpool = [wbf[0][:], wbf[1][:]]
            for i_ in range(NST):
                fl = wst[i_].bitcast(BF)[:].rearrange("p k c -> p (k c)")
                wpool += [fl[:, 0:2048].rearrange("p (k c) -> p k c", c=256), fl[:, 2048:4096].rearrange("p (k c) -> p k c", c=256)]
            wp_rr = [0]
            pidx = {(src_.tensor.name, c0_, r0_): i_ for i_, (src_, c0_, r0_) in enumerate(prep)}

            def load_wq(src, rows, c0, ncols, r0=0):
                i = pidx[(src.tensor.name, c0, r0)]
                buf = wpool[wp_rr[0] % len(wpool)]
                wp_rr[0] += 1
                S.dma("sp", lambda e: e.dma_start(out=buf.rearrange("p k c -> p (k c)"), in_=wq[i]), writes=[buf])
                return buf

            xT = sb("xT", [128, 8, 512])
            yT = sb("yT", [128, 8, 512])
            hTe = sb("hTe", [128, 8, 640], BF)
            uext = sb("uext", [128, 8, 640], BF)
            arena = sb("arena", [128, 16384], BF)
            hid = arena[:, :].rearrange("p (j t) -> p j t", t=512)
            ucv = arena.bitcast(F32)[:, 0:4096].rearrange("p (c t) -> p c t", t=512)
            diag = [arena[:, 8192 + i * 3968:8192 + (i + 1) * 3968].rearrange("p (k m) -> p k m", m=128) for i in range(2)]
            sh8 = sb("sh8", [128, 8, 512], BF)
            actT = sh8
            mT = sh8
            h2T = sh8
            yaT = sb("yaT", [128, 8, 512], BF)
            oblk = xblk
            stat_s = sb("stat_s", [128, 512])
            stat_n = sb("stat_n", [128, 512])
            sb_sig = sb("sb_sig", [128, 640])

            deferred = []

            def flush_def():
                for f_ in deferred:
                    f_()
                deferred.clear()

            def stats_accum(ps_b, src_ap, reads, idx, n, defer=False):
                s_ = nxt(tmpb, tmpb_rr)
                A(lambda e: e.activation(s_[:], src_ap, AF.Square), reads=reads, writes=[s_])
                f_ = lambda s_=s_: mm(bap(ps_b), ones_b[:], s_[:], idx == 0, idx == n - 1, [ones_b, s_], bk(ps_b), True)
                if defer:
                    deferred.append(f_)
                else:
                    f_()

            def fh_blocks(g_):
                lst = []
                for b in range(4):
                    blk = g_ * 4 + b
                    lst.append((x_halo[blk * 32:(blk + 1) * 32, :], 32, b * 160))
                    lst.append((x_own[blk * 128:(blk + 1) * 128, :], 128, b * 160 + 32))
                return lst

            def front_h(g_):
                lst = fh_blocks(g_)
                hd = xb_prep(lst[0][0], lst[0][1])
                for i_ in range(8):
                    nh = xb_prep(lst[i_ + 1][0], lst[i_ + 1][1]) if i_ + 1 < 8 else None
                    xb_trans(hd, hTe, lst[i_][2], 0, 8)
                    hd = nh

            front_h(0)
            for g in range(4):
                for b in range(4):
                    blk = g * 4 + b
                    xb = xblk[xb_rr[0] % 2]
                    xb_rr[0] += 1
                    dma_in(xb[:], x_own[blk * 128:(blk + 1) * 128, :], xb)
                    for half in range(2):
                        tb_i = 2 + half
                        for kk in range(4):
                            k = half * 4 + kk
                            P(lambda e, k=k, kk=kk, tb_i=tb_i, xb=xb: e.transpose(bap(tb_i, kk * 128, (kk + 1) * 128),
                                                                                    xb[:, k * 128:(k + 1) * 128], ident_f[:]),
                              reads=[xb, ident_f], writes=[bk(tb_i)], inc=(kk == 3))
                        evac_copy(xT[:, half * 4:(half + 1) * 4, b * 128:(b + 1) * 128],
                                  bap(tb_i).rearrange("p (k t) -> p k t", t=128), [bk(tb_i)], [(xT, (half, b))])
                hk = [(hTe, k) for k in range(8)]
                def glu_chunk(c, wa, wb2, cc):
                    for (wt, d) in ((wa, 0), (wb2, 1)):
                        for (n0, n1, hb) in ((0, 512, 0), (512, 640, 1)):
                            for k in range(8):
                                mm(PD[d][:, hb * 512:hb * 512 + (n1 - n0)], wt[:, k, cc * 128:(cc + 1) * 128],
                                   hTe[:, k, n0:n1], k == 0, k == 7, hk + [wt], (PD[d], hb), k == 7)
                    sg = sb_sig
                    A(lambda e: e.activation(sg[:, 0:640], PD[1][:, 0:640], AF.Sigmoid),
                      reads=[(PD[1], 0), (PD[1], 1)], writes=[sg])
                    V(lambda e: e.tensor_tensor(uext[:, c, :], PD[0][:, 0:640], sg[:, 0:640], ALU.mult),
                      reads=[(PD[0], 0), (PD[0], 1), sg], writes=[(uext, c)])
                    if g == 0:
                        V(lambda e: e.tensor_scalar(uext[:, c, 0:32], uext[:, c, 0:32], halom[:, 0:1], None, ALU.mult),
                          reads=[(uext, c), halom], writes=[(uext, c)])

                def conv_chunk(c):
                    dg = diag[c % 2]
                    for k in range(31):
                        G(lambda e: e.tensor_scalar(dg[:, k, :], ident_b[:], cwT[:, c, k:k + 1], 1.0, ALU.mult, ALU.mult),
                          reads=[ident_b, cwT], writes=[(arena, None) if (c == 0 and k == 0) else (arena, ("d", c % 2, k))])
                    uv = uext[:, c, :].rearrange("p (b w) -> p b w", w=160)
                    cb = 4 + (c % 2)
                    for k in range(31):
                        mm(bap(cb).rearrange("p (b w) -> p b w", w=128), dg[:, k, :], uv[:, :, 2 + k:2 + k + 128],
                           k == 0, k == 30, [(arena, ("d", c % 2, k)), (uext, c)], bk(cb), k == 30)
                    flush_def()
                    A(lambda e: e.activation(ucv[:, c, :], bap(cb), AF.Identity, bias=vecs[:, V_CB + c:V_CB + c + 1]),
                      reads=[bk(cb), vecs], writes=[(arena, ("u", c))])
                    ub_ = nxt(tmpb, tmpb_rr)
                    V(lambda e: e.tensor_copy(ub_[:], ucv[:, c, :]), reads=[(arena, ("u", c))], writes=[ub_])
                    deferred.append(lambda ub_=ub_, c=c: mm(bap(6), ones_b[:], ub_[:], c == 0, c == 7, [ones_b, ub_], bk(6), True))
                    stats_accum(7, ucv[:, c, :], [(arena, ("u", c))], c, 8, defer=True)

                for pc in range(4):
                    wa = load_wq(w_in, D, pc * 256, 256)
                    wb2 = load_wq(w_in, D, 1024 + pc * 256, 256)
                    for cc in range(2):
                        c = pc * 2 + cc
                        glu_chunk(c, wa, wb2, cc)
                        if c >= 1:
                            conv_chunk(c - 1)
                conv_chunk(7)
                flush_def()
                mean = stat_s
                nmr = stat_n
                A(lambda e: e.activation(mean[:], bap(6), AF.Copy, scale=1.0 / D), reads=[bk(6)], writes=[mean])
                jt = nxt(tmpf, tmpf_rr)
                V(lambda e, jt=jt: e.tensor_tensor(jt[:], mean[:], mean[:], ALU.mult), reads=[mean], writes=[jt])
                jt2 = nxt(tmpf, tmpf_rr)
                V(lambda e, jt=jt, jt2=jt2: e.scalar_tensor_tensor(jt2[:], bap(7), 1.0 / D, jt[:], ALU.mult, ALU.subtract),
                  reads=[bk(7), jt], writes=[jt2])
                V(lambda e, jt2=jt2: e.tensor_scalar(jt2[:], jt2[:], 0.0, None, ALU.max), reads=[jt2], writes=[jt2])
                A(lambda e, jt=jt, jt2=jt2: e.activation(jt[:], jt2[:], AF.Sqrt, bias=EPS), reads=[jt2], writes=[jt])
                rln = nxt(rstd_t, rstd_rr)
                V(lambda e, jt=jt, rln=rln: e.reciprocal(rln[:], jt[:]), reads=[jt], writes=[rln])
                V(lambda e, rln=rln: e.scalar_tensor_tensor(nmr[:], mean[:], -1.0, rln[:], ALU.mult, ALU.mult),
                  reads=[mean, rln], writes=[nmr])
                for c in range(8):
                    jt = nxt(tmpf, tmpf_rr)
                    V(lambda e, c=c, jt=jt, rln=rln: e.tensor_tensor(jt[:], ucv[:, c, :], rln[:], ALU.mult),
                      reads=[(arena, ("u", c)), rln], writes=[jt])
                    V(lambda e, jt=jt: e.tensor_tensor(jt[:], jt[:], nmr[:], ALU.add), reads=[jt, nmr], writes=[jt])
                    A(lambda e, c=c, jt=jt: e.activation(actT[:, c, :], jt[:], AF.Silu,
                                                         bias=vecs[:, V_CNB + c:V_CNB + c + 1], scale=vecs[:, V_CG + c:V_CG + c + 1]),
                      reads=[jt, vecs, vecs], writes=[(actT, c)])
                ak = [(actT, k) for k in range(8)]
                for pc in range(4):
                    wb = load_wq(w_conv_out, D, pc * 256, 256)
                    for cc in range(2):
                        m = pc * 2 + cc
                        bb = 4 + (m % 2)
                        for k in range(8):
                            mm(bap(bb), wb[:, k, cc * 128:(cc + 1) * 128], actT[:, k, :], k == 0, k == 7, ak + [wb], bk(bb), k == 7)
                        evac_copy(yaT[:, m, :], bap(bb), [bk(bb)], [(yaT, m)])
                hown = lambda k: hTe[:, k, :].rearrange("p (b w) -> p b w", w=160)[:, :, 32:160]
                for pc in range(4):
                    wao = load_wq(w_attn_out, D, pc * 256, 256)
                    for cc in range(2):
                        for hh in range(8):
                            mm(bap(2 + cc), wao[:, hh, cc * 128:(cc + 1) * 128], oT[:, hh, g * 512:(g + 1) * 512],
                               hh == 0, hh == 7, [oT, wao], bk(2 + cc), hh == 7)
                    wga = load_wq(w_in, D, 2752 + pc * 256, 256)
                    sas = []
                    for cc in range(2):
                        m = pc * 2 + cc
                        for k in range(8):
                            mm(bap(cc).rearrange("p (b w) -> p b w", w=128), wga[:, k, cc * 128:(cc + 1) * 128], hown(k),
                               k == 0, k == 7, hk + [wga], bk(cc), k == 7)
                        sa = nxt(tmpf, tmpf_rr)
                        A(lambda e, sa=sa, cc=cc: e.activation(sa[:], bap(cc), AF.Sigmoid), reads=[bk(cc)], writes=[sa])
                        V(lambda e, sa=sa, m=m: e.tensor_tensor(sa[:], sa[:], yaT[:, m, :], ALU.mult),
                          reads=[sa, (yaT, m)], writes=[sa])
                        sas.append(sa)
                    wgb = load_wq(w_in, D, 3776 + pc * 256, 256)
                    for cc in range(2):
                        m = pc * 2 + cc
                        for k in range(8):
                            mm(bap(cc).rearrange("p (b w) -> p b w", w=128), wgb[:, k, cc * 128:(cc + 1) * 128], hown(k),
                               k == 0, k == 7, hk + [wgb], bk(cc), k == 7)
                        sb2 = nxt(tmpf, tmpf_rr)
                        A(lambda e, sb2=sb2, cc=cc: e.activation(sb2[:], bap(cc), AF.Sigmoid), reads=[bk(cc)], writes=[sb2])
                        V(lambda e, sb2=sb2, cc=cc: e.tensor_tensor(sb2[:], bap(2 + cc), sb2[:], ALU.mult),
                          reads=[bk(2 + cc), sb2], writes=[sb2])
                        V(lambda e, sa=sas[cc], sb2=sb2, m=m: e.tensor_tensor(mT[:, m, :], sa[:], sb2[:], ALU.add),
                          reads=[sas[cc], sb2], writes=[(mT, m)])
                mk_ = [(mT, k) for k in range(8)]
                for pc in range(4):
                    wb = load_wq(w_out, D, pc * 256, 256)
                    for cc in range(2):
                        m = pc * 2 + cc
                        bb = 4 + (m % 2)
                        for k in range(8):
                            mm(bap(bb), wb[:, k, cc * 128:(cc + 1) * 128], mT[:, k, :], k == 0, k == 7, mk_ + [wb], bk(bb), k == 7)
                        flush_def()
                        A(lambda e, m=m, bb=bb: e.activation(yT[:, m, :], bap(bb), AF.Copy), reads=[bk(bb)], writes=[(yT, m)])
                        stats_accum(6, yT[:, m, :], [(yT, m)], m, 8, defer=True)
                flush_def()
                if debug == "m" and g == 3:
                    V(lambda e: e.tensor_copy(hTe[:, :, 0:512], mT[:]), reads=[mT], writes=[hTe])
                    V(lambda e: e.tensor_copy(uext[:, :, 0:512], yT[:]), reads=[yT], writes=[uext])
                r1 = rstd_from_ps(6, D)
                for k in range(8):
                    jt = nxt(tmpf, tmpf_rr)
                    V(lambda e, k=k, jt=jt, r1=r1: e.tensor_tensor(jt[:], yT[:, k, :], r1[:], ALU.mult),
                      reads=[(yT, k), r1], writes=[jt])
                    V(lambda e, k=k, jt=jt: e.scalar_tensor_tensor(xT[:, k, :], jt[:], der[:, 16 + k:17 + k], xT[:, k, :], ALU.mult, ALU.add),
                      reads=[jt, (der, 16), xT], writes=[xT])
                    stats_accum(7, xT[:, k, :], [xT], k, 8)
                r2 = rstd_from_ps(7, D)
                for k in range(8):
                    jt = nxt(tmpf, tmpf_rr)
                    V(lambda e, k=k, jt=jt, r2=r2: e.scalar_tensor_tensor(jt[:], xT[:, k, :], der[:, 24 + k:25 + k], r2[:], ALU.mult, ALU.mult),
                      reads=[xT, (der, 24), r2], writes=[jt])
                    A(lambda e, k=k, jt=jt: e.activation(h2T[:, k, :], jt[:], AF.Identity, bias=der[:, 32 + k:33 + k]),
                      reads=[jt, (der, 32)], writes=[(h2T, k)])
                h2k = [(h2T, k) for k in range(8)]
                nlst = fh_blocks(g + 1) if g + 1 < 4 else None
                nhd = {}
                for pc in range(16):
                    if nlst is not None and pc % 2 == 0:
                        if pc >= 2:
                            xb_trans(nhd[pc // 2 - 1], hTe, nlst[pc // 2 - 1][2], 0, 8)
                        nhd[pc // 2] = xb_prep(nlst[pc // 2][0], nlst[pc // 2][1])
                    wb = load_wq(w_mlp_in, D, pc * 256, 256)
                    for cc in range(2):
                        j = pc * 2 + cc
                        bb = 4 + (j % 4)
                        for k in range(8):
                            mm(bap(bb), wb[:, k, cc * 128:(cc + 1) * 128], h2T[:, k, :], k == 0, k == 7, h2k + [wb], bk(bb), k == 7)
                        jt = nxt(tmpf, tmpf_rr)
                        A(lambda e, jt=jt, bb=bb: e.activation(jt[:], bap(bb), AF.Relu), reads=[bk(bb)], writes=[jt])
                        V(lambda e, jt=jt, j=j: e.tensor_tensor(hid[:, j, :], jt[:], jt[:], ALU.mult), reads=[jt],
                          writes=[(arena, None) if j == 0 else (arena, ("h", j))])
                if nlst is not None:
                    xb_trans(nhd[7], hTe, nlst[7][2], 0, 8)
                for pc in range(4):
                    for cc in range(2):
                        pass
                    wbs = []
                    for rr_ in range(4):
                        wb = load_wq(w_mlp_out, D, pc * 256, 256, r0=rr_ * 1024)
                        for cc in range(2):
                            bb = 4 + cc
                            for k in range(8):
                                kk = rr_ * 8 + k
                                mm(bap(bb), wb[:, k, cc * 128:(cc + 1) * 128], hid[:, kk, :], kk == 0, kk == 31,
                                   [arena, wb], bk(bb), (k == 7))
                    flush_def()
                    for cc in range(2):
                        m = pc * 2 + cc
                        bb = 4 + cc
                        A(lambda e, m=m, bb=bb: e.activation(yT[:, m, :], bap(bb), AF.Copy), reads=[bk(bb)], writes=[(yT, m)])
                        stats_accum(6, yT[:, m, :], [(yT, m)], m, 8, defer=True)
                flush_def()
                r3 = rstd_from_ps(6, D)
                for k in range(8):
                    jt = nxt(tmpf, tmpf_rr)
                    V(lambda e, k=k, jt=jt, r3=r3: e.tensor_tensor(jt[:], yT[:, k, :], r3[:], ALU.mult),
                      reads=[(yT, k), r3], writes=[jt])
                    V(lambda e, k=k, jt=jt: e.scalar_tensor_tensor(yT[:, k, :], jt[:], der[:, 40 + k:41 + k], xT[:, k, :], ALU.mult, ALU.add),
                      reads=[jt, (der, 40), xT], writes=[(yT, k)])
                for b in range(4):
                    ob_ = oblk[b % 2]
                    for half in range(2):
                        tb_i = 2 + half
                        for kk in range(4):
                            k = half * 4 + kk
                            P(lambda e, k=k, kk=kk, tb_i=tb_i, b=b: e.transpose(bap(tb_i, kk * 128, (kk + 1) * 128),
                                                                                 yT[:, k, b * 128:(b + 1) * 128], ident_f[:]),
                              reads=[yT, ident_f], writes=[bk(tb_i)], inc=(kk == 3))
                        evac_copy(ob_[:, half * 512:(half + 1) * 512], bap(tb_i), [bk(tb_i)], [(ob_, half)])
                    blk = g * 4 + b
                    tok = S.dma("act", lambda e, ob_=ob_, blk=blk: e.dma_start(out=out[blk * 128:(blk + 1) * 128, :], in_=ob_[:]),
                                reads=[ob_])
                    out_toks.append(tok)


        run_phases()
        if stop is not None:
            out_toks.append(S.dma("act", lambda e: e.dma_start(out=out[0:128, :], in_=xblk[0][:]), reads=[xblk[0]]))
        last = {}
        for (s, v) in out_toks:
            last[s] = max(last.get(s, 0), v)
        S.wait_all("act", list(last.items()))

        with nc.Block() as block:
            def emit(engname, eng):
                for (waits, fn, inc) in S.q[engname]:
                    for (s, v) in waits:
                        eng.wait_ge(sems[s], v)
                    if fn is None:
                        continue
                    ins = fn(eng)
                    if inc is not None:
                        ins.then_inc(sems[inc[0]], inc[1])

            @block.sync
            def _(e):
                emit("sp", e)

            @block.tensor
            def _(e):
                emit("pe", e)

            @block.scalar
            def _(e):
                emit("act", e)

            @block.vector
            def _(e):
                emit("dve", e)

            @block.gpsimd
            def _(e):
                emit("pool", e)
    return nc


def _prep_inputs(inputs):
    x = np.asarray(inputs["x"], np.float32)
    pos = np.asarray(inputs["positions"], np.int32)
    c = np.asarray(inputs["c"], np.float32)
    w_in = np.ascontiguousarray(np.asarray(inputs["w_in"], np.float32)[0])
    w_uq = np.ascontiguousarray(np.asarray(inputs["w_uq"], np.float32)[0])
    kr = w_in[:, 2688:2752]
    w_kr_sw = np.ascontiguousarray(np.concatenate([kr[:, 32:64], kr[:, 0:32]], axis=1))
    uq3 = w_uq.reshape(384, 8, 192)[:, :, 128:192]
    w_uq_sw = np.ascontiguousarray(np.concatenate([uq3[:, :, 32:64], uq3[:, :, 0:32]], axis=2).reshape(384, 512))
    k_idx = np.arange(128)[:, None]
    q_idx = np.arange(128)[None, :]
    trimask = np.where(k_idx <= q_idx, 0.0, NEG).astype(np.float32)
    ident = np.eye(128, dtype=np.float32)
    inv = (1.0 / (np.float32(10000.0) ** (np.arange(0, 64, 2, dtype=np.float32) / np.float32(64)))).astype(np.float32)
    invf = np.concatenate([inv, inv]).reshape(64, 1).astype(np.float32)
    sgn = np.concatenate([-np.ones(32), np.ones(32)]).reshape(64, 1).astype(np.float32)
    shared = {
        "trimask": trimask, "ident": ident, "invf": invf, "sgn": sgn,
        "w_ada": np.ascontiguousarray(inputs["w_ada"][0], np.float32),
        "b_ada": np.ascontiguousarray(inputs["b_ada"][0], np.float32),
        "g_pre_mix": np.ascontiguousarray(inputs["g_pre_mix"][0], np.float32),
        "g_post_mix": np.ascontiguousarray(inputs["g_post_mix"][0], np.float32),
        "g_pre_mlp": np.ascontiguousarray(inputs["g_pre_mlp"][0], np.float32),
        "g_post_mlp": np.ascontiguousarray(inputs["g_post_mlp"][0], np.float32),
        "w_in": w_in, "w_kr_sw": w_kr_sw,
        "conv_w": np.ascontiguousarray(inputs["conv_w"][0], np.float32),
        "conv_b": np.ascontiguousarray(inputs["conv_b"][0], np.float32),
        "conv_norm_g": np.ascontiguousarray(inputs["conv_norm_g"][0], np.float32),
        "conv_norm_b": np.ascontiguousarray(inputs["conv_norm_b"][0], np.float32),
        "w_conv_out": np.ascontiguousarray(inputs["w_conv_out"][0], np.float32),
        "q_norm_g": np.ascontiguousarray(inputs["q_norm_g"][0], np.float32),
        "w_uq": w_uq, "w_uq_sw": w_uq_sw,
        "kv_norm_g": np.ascontiguousarray(inputs["kv_norm_g"][0], np.float32),
        "w_ukv": np.ascontiguousarray(inputs["w_ukv"][0], np.float32),
        "w_attn_out": np.ascontiguousarray(inputs["w_attn_out"][0], np.float32),
        "w_out": np.ascontiguousarray(inputs["w_out"][0], np.float32),
        "w_mlp_in": np.ascontiguousarray(inputs["w_mlp_in"][0], np.float32),
        "w_mlp_out": np.ascontiguousarray(inputs["w_mlp_out"][0], np.float32),
    }
    in_maps = []
    for core in range(8):
        b, p = core // 2, core % 2
        xb = x[b].reshape(32, 128, D)
        pb = pos[b].reshape(32, 128)
        own = [2 * i + p for i in range(16)]
        oth = [2 * i + 1 - p for i in range(16)]
        halo = np.zeros((16, 32, D), np.float32)
        for i in range(16):
            st = own[i] * 128
            if st > 0:
                halo[i] = x[b, st - 32:st]
        m = dict(shared)
        m["x_own"] = np.ascontiguousarray(xb[own].reshape(NOWN, D))
        m["x_oth"] = np.ascontiguousarray(xb[oth].reshape(NOWN, D))
        m["x_halo"] = np.ascontiguousarray(halo.reshape(512, D))
        m["pos_own"] = np.ascontiguousarray(pb[own].reshape(NOWN))
        m["pos_oth"] = np.ascontiguousarray(pb[oth].reshape(NOWN))
        m["c"] = np.ascontiguousarray(c[b])
        m["pairmask"] = np.full((128, 128), 0.0 if p == 1 else NEG, np.float32)
        m["halomask"] = np.full((128, 1), 1.0 if p == 1 else 0.0, np.float32)
        in_maps.append(m)
    return in_maps


def kernel(**inputs):
    in_maps = _prep_inputs(inputs)
    nc = build_nc()
    res = run_bass_kernel_spmd(nc, in_maps, core_ids=list(range(8)))
    outf = np.zeros((4, 32, 128, D), np.float32)
    for core in range(8):
        b, p = core // 2, core % 2
        o = np.asarray(res.results[core]["out"]).reshape(16, 128, D)
        for i in range(16):
            outf[b, 2 * i + p] = o[i]
    return outf.reshape(4, 4096, D)
```

```python
import contextlib
import math
import numpy as np
import concourse.bass as bass
import concourse.mybir as mybir
from concourse.bass_utils import run_bass_kernel_spmd

F32 = mybir.dt.float32
BF = mybir.dt.bfloat16
I32 = mybir.dt.int32
AF = mybir.ActivationFunctionType
ALU = mybir.AluOpType

D = 1024
KC = 8
NOWN = 2048
EPS = 1e-6
NEG = -30000.0
SCALE = 1.0 / math.sqrt(192.0)
TWO_PI = 2.0 * math.pi
C1 = 6.28125
C2 = TWO_PI - 6.28125

ENGS = ("pe", "act", "dve", "pool", "sp")
NDMA = 12


class _Rec:
    def __init__(self):
        self.call = None

    def __getattr__(self, name):
        def f(*a, **k):
            self.call = (name, a, k)
            return self
        return f


def _bind(fn):
    if fn is None:
        return None
    r = _Rec()
    fn(r)
    name, a, k = r.call
    return lambda eng: getattr(eng, name)(*a, **k)


class Sched:
    def __init__(self):
        self.q = {e: [] for e in ENGS}
        self.cnt = {e: 0 for e in ENGS}
        self.waited = {e: {} for e in ENGS}
        self.state = {}
        self.dma_tot = [0] * NDMA
        self.dma_rr = 0
        self.all_dma_tokens = {}

    def _entries(self, buf, key, create):
        d = self.state.setdefault(id(buf), {})
        if key is None:
            if create and None not in d:
                d[None] = {"w": None, "r": {}}
            return list(d.values()) if not create else list(d.values())
        out = []
        if key not in d and create:
            d[key] = {"w": None, "r": {}}
        if key in d:
            out.append(d[key])
        if None in d:
            out.append(d[None])
        return out

    def _deps(self, reads, writes):
        deps = {}

        def add(tok):
            if tok is None:
                return
            s, v = tok
            if deps.get(s, 0) < v:
                deps[s] = v

        for (b, k) in reads:
            for st in self._entries(b, k, False):
                add(st["w"])
        for (b, k) in writes:
            for st in self._entries(b, k, False):
                add(st["w"])
                for s, v in st["r"].items():
                    add((s, v))
        return deps

    def _commit(self, reads, writes, tok):
        for (b, k) in reads:
            d = self.state.setdefault(id(b), {})
            if k not in d:
                d[k] = {"w": None, "r": {}}
            st = d[k]
            s, v = tok
            if st["r"].get(s, 0) < v:
                st["r"][s] = v
        for (b, k) in writes:
            d = self.state.setdefault(id(b), {})
            if k is None:
                d.clear()
            d[k] = {"w": tok, "r": {}}

    def _norm(self, lst):
        out = []
        for x in lst:
            if isinstance(x, tuple):
                out.append(x)
            else:
                out.append((x, None))
        return out

    def op(self, eng, fn, reads=(), writes=(), inc=True):
        fn = _bind(fn)
        reads = self._norm(reads)
        writes = self._norm(writes)
        deps = self._deps(reads, writes)
        waits = []
        for s, v in deps.items():
            if s == eng and eng == "pe":
                continue
            if self.waited[eng].get(s, 0) >= v:
                continue
            self.waited[eng][s] = v
            waits.append((s, v))
        if inc:
            self.cnt[eng] += 1
            tok = (eng, self.cnt[eng])
            self.q[eng].append((waits, fn, (eng, 1)))
        else:
            tok = (eng, self.cnt[eng] + 1)
            self.q[eng].append((waits, fn, None))
        self._commit(reads, writes, tok)
        return tok

    def dma(self, eng, fn, reads=(), writes=()):
        fn = _bind(fn)
        reads = self._norm(reads)
        writes = self._norm(writes)
        j = self.dma_rr
        self.dma_rr = (self.dma_rr + 1) % NDMA
        sem = "d%d" % j
        deps = self._deps(reads, writes)
        if self.dma_tot[j] > 0:
            if deps.get(sem, 0) < self.dma_tot[j]:
                deps[sem] = self.dma_tot[j]
        waits = []
        for s, v in deps.items():
            if self.waited[eng].get(s, 0) >= v:
                continue
            self.waited[eng][s] = v
            waits.append((s, v))
        self.dma_tot[j] += 16
        tok = (sem, self.dma_tot[j])
        self.q[eng].append((waits, fn, (sem, 16)))
        self._commit(reads, writes, tok)
        return tok

    def barrier(self):
        toks = [(e, self.cnt[e]) for e in ENGS if self.cnt[e] > 0]
        toks += [("d%d" % j, self.dma_tot[j]) for j in range(NDMA) if self.dma_tot[j] > 0]
        for e in ENGS:
            self.wait_all(e, [t for t in toks if t[0] != e])
        self.state = {}

    def wait_all(self, eng, toks):
        waits = []
        for (s, v) in toks:
            if self.waited[eng].get(s, 0) >= v:
                continue
            self.waited[eng][s] = v
            waits.append((s, v))
        self.q[eng].append((waits, None, None))


def build_nc(debug=None, stop=None):
    nc = bass.Bass("TRN2", target_bir_lowering=False)
    S = Sched()

    def din(name, shape, dt=F32):
        return nc.dram_tensor(name, list(shape), dt, kind="ExternalInput").ap()

    x_own = din("x_own", [NOWN, D])
    x_oth = din("x_oth", [NOWN, D])
    x_halo = din("x_halo", [512, D])
    pos_own = nc.dram_tensor("pos_own", [NOWN], I32, kind="ExternalInput")
    pos_oth = nc.dram_tensor("pos_oth", [NOWN], I32, kind="ExternalInput")
    c_in = din("c", [D])
    pairmask_in = din("pairmask", [128, 128])
    trimask_in = din("trimask", [128, 128])
    ident_in = din("ident", [128, 128])
    halomask_in = din("halomask", [128, 1])
    invf_in = din("invf", [64, 1])
    sgn_in = din("sgn", [64, 1])
    w_ada = din("w_ada", [D, 6 * D])
    b_ada = din("b_ada", [6 * D])
    g_pre_mix = din("g_pre_mix", [D])
    g_post_mix = din("g_post_mix", [D])
    g_pre_mlp = din("g_pre_mlp", [D])
    g_post_mlp = din("g_post_mlp", [D])
    w_in = din("w_in", [D, 4800])
    w_kr_sw = din("w_kr_sw", [D, 64])
    conv_w = din("conv_w", [31, D])
    conv_b = din("conv_b", [D])
    conv_norm_g = din("conv_norm_g", [D])
    conv_norm_b = din("conv_norm_b", [D])
    w_conv_out = din("w_conv_out", [D, D])
    q_norm_g = din("q_norm_g", [384])
    w_uq = din("w_uq", [384, 1536])
    w_uq_sw = din("w_uq_sw", [384, 512])
    kv_norm_g = din("kv_norm_g", [256])
    w_ukv = din("w_ukv", [256, 2048])
    w_attn_out = din("w_attn_out", [D, D])
    w_out = din("w_out", [D, D])
    w_mlp_in = din("w_mlp_in", [D, 4 * D])
    w_mlp_out = din("w_mlp_out", [4 * D, D])
    out = nc.dram_tensor("out", [NOWN, D], F32, kind="ExternalOutput").ap()
    wq = nc.dram_tensor("wq", [60, 128, 2048], BF, kind="Internal").ap()
    wq_key = object()
    dbg = None

    es = contextlib.ExitStack()
    with es:
        def sb(name, shape, dt=F32):
            return es.enter_context(nc.sbuf_tensor("s_" + name, list(shape), dt))

        sems = {}
        for e in ENGS:
            sems[e] = es.enter_context(nc.semaphore("sem_" + e))
        for j in range(NDMA):
            sems["d%d" % j] = es.enter_context(nc.semaphore("sem_d%d" % j))

        PD = [es.enter_context(nc.psum_tensor("pd%d" % i, [128, 1024], F32)) for i in range(4)]

        def bank(i):
            t = PD[i // 2]
            h = i % 2
            return t, h

        def bk(i):
            t, h = bank(i)
            return (t, h)

        def bap(i, c0=0, c1=512):
            t, h = bank(i)
            return t[:, h * 512 + c0: h * 512 + c1]

        def bap_bf(i):
            t, h = bank(i)
            return t.bitcast(BF)[:, h * 1024:(h + 1) * 1024]

        ident_f = sb("ident_f", [128, 128])
        ident_b = sb("ident_b", [128, 128], BF)
        ones_b = sb("ones_b", [128, 128], BF)
        tri_b = sb("tri_b", [128, 128], BF)
        pair_b = sb("pair_b", [128, 128], BF)
        halom = sb("halom", [128, 1])
        invf = sb("invf", [64, 1])
        sgn = sb("sgn", [64, 1])
        modT = sb("modT", [128, 48])
        vecs = sb("vecs", [128, 128])
        cwT = sb("cwT", [128, 8, 31])
        V_GPRE, V_GPOST, V_GPRE2, V_GPOST2 = 0, 8, 16, 24
        V_CB, V_CG, V_CNB = 32, 40, 48
        V_QG, V_KVG = 56, 59
        der = sb("der", [128, 48])
        NST = 2
        wst = [sb("wst%d" % i, [128, 8, 256]) for i in range(NST)]
        wbf = [sb("wbf%d" % i, [128, 8, 256], BF) for i in range(NST)]
        wrr = [0]
        xblk = [sb("xblk%d" % i, [128, D]) for i in range(2)]
        xnb = [sb("xnb%d" % i, [128, D], BF) for i in range(2)]
        small = [sb("small%d" % i, [128, 4]) for i in range(4)]
        small_rr = [0]
        tmpf = [sb("tmpf%d" % i, [128, 512]) for i in range(4)]
        tmpf_rr = [0]
        tmpb = [sb("tmpb%d" % i, [128, 512], BF) for i in range(6)]
        tmpb_rr = [0]
        rstd_t = [sb("rstd%d" % i, [128, 512]) for i in range(2)]
        rstd_rr = [0]

        def nxt(lst, rr):
            t = lst[rr[0] % len(lst)]
            rr[0] += 1
            return t

        def A(fn, **kw):
            return S.op("act", fn, **kw)

        def V(fn, **kw):
            return S.op("dve", fn, **kw)

        def G(fn, **kw):
            return S.op("pool", fn, **kw)

        def P(fn, **kw):
            return S.op("pe", fn, **kw)

        def dma_in(dst_ap, src_ap, dst_buf, key=None, eng="sp", nonc=False):
            def f(e, dst_ap=dst_ap, src_ap=src_ap):
                if nonc:
                    return e.dma_start(out=dst_ap, in_=src_ap, allow_slow_non_contiguous=True)
                return e.dma_start(out=dst_ap, in_=src_ap)
            return S.dma(eng, f, writes=[(dst_buf, key)])

        def mm(out_ap, lhsT, rhs, start, stop, reads, wkey, last):
            def f(e):
                return e.matmul(out_ap, lhsT, rhs, start=start, stop=stop)
            return S.op("pe", f, reads=reads, writes=[wkey], inc=last)

        def load_w(src, rows, c0, ncols, r0=0, cast=None):
            i = wrr[0] % NST
            wrr[0] += 1
            kc = rows // 128
            st, wb = wst[i], wbf[i]
            src_ap = src[r0:r0 + rows, c0:c0 + ncols].rearrange("(k p) c -> p k c", p=128)
            dma_in(st[:, 0:kc, 0:ncols], src_ap, st)
            if cast == "pool" or (cast is None and wrr[0] % 2 == 0):
                G(lambda e: e.tensor_copy(wb[:, 0:kc, 0:ncols], st[:, 0:kc, 0:ncols]), reads=[st], writes=[wb])
            else:
                V(lambda e: e.tensor_copy(wb[:, 0:kc, 0:ncols], st[:, 0:kc, 0:ncols]), reads=[st], writes=[wb])
            return wb

        def _unused():
            pass

        out_toks = []

        def run_phases():
            dma_in(ident_f[:], ident_in, ident_f)
            dma_in(halom[:], halomask_in, halom)
            dma_in(invf[:], invf_in, invf)
            dma_in(sgn[:], sgn_in, sgn)
            t0 = tmpf[0]
            t1 = tmpf[1]
            dma_in(t0[:, 0:128], trimask_in, t0)
            dma_in(t1[:, 0:128], pairmask_in, t1)
            V(lambda e: e.tensor_copy(ident_b[:], ident_f[:]), reads=[ident_f], writes=[ident_b])
            V(lambda e: e.memset(ones_b[:], 1.0), writes=[ones_b])
            V(lambda e: e.tensor_copy(tri_b[:], t0[:, 0:128]), reads=[t0], writes=[tri_b])
            V(lambda e: e.tensor_copy(pair_b[:], t1[:, 0:128]), reads=[t1], writes=[pair_b])
            tmpf_rr[0] = 2
            if stop == -1:
                return
            stg = tmpf[2]
            V(lambda e: e.memset(stg[:, 0:128], 0.0), writes=[stg])
            for col, src, n in ((V_GPRE, g_pre_mix, D), (V_GPOST, g_post_mix, D), (V_GPRE2, g_pre_mlp, D),
                                (V_GPOST2, g_post_mlp, D), (V_CB, conv_b, D), (V_CG, conv_norm_g, D),
                                (V_CNB, conv_norm_b, D), (V_QG, q_norm_g, 384), (V_KVG, kv_norm_g, 256)):
                dma_in(stg[col:col + n // 128, 0:128], src.rearrange("(k p) -> k p", p=128), stg)
            dma_in(stg[64:112, 0:128], b_ada.rearrange("(k p) -> k p", p=128), stg)
            dma_in(stg[112:120, 0:128], c_in.rearrange("(k p) -> k p", p=128), stg)
            P(lambda e: e.transpose(bap(1, 0, 120), stg[0:120, 0:128], ident_f[0:120, 0:120]),
              reads=[stg, ident_f], writes=[bk(1)])
            V(lambda e: e.tensor_copy(vecs[:, 0:120], bap(1, 0, 120)), reads=[bk(1)], writes=[vecs])
            if stop == -2:
                return
            badaT = vecs[:, 64:112]
            cT = vecs[:, 112:120]
            cwn = xblk[0]
            dma_in(cwn[0:31, :], conv_w, cwn)
            for c in range(8):
                P(lambda e, c=c: e.transpose(bap(2, c * 32, c * 32 + 31), cwn[0:31, c * 128:(c + 1) * 128], ident_f[0:31, 0:31]),
                  reads=[cwn, ident_f], writes=[bk(2)])
            V(lambda e: e.tensor_copy(cwT[:], bap(2, 0, 256).rearrange("p (c k) -> p c k", k=32)[:, :, 0:31]),
              reads=[bk(2)], writes=[cwT])
            if stop == -3:
                return
            scb = sb("scb", [128, 8])
            A(lambda e: e.activation(scb[:], cT, AF.Silu), reads=[vecs], writes=[scb])
            if stop == -4:
                return
            def load_w32(src, rows, c0, ncols):
                i = wrr[0] % NST
                wrr[0] += 1
                st = wst[i]
                dma_in(st[:, 0:rows // 128, 0:ncols], src[0:rows, c0:c0 + ncols].rearrange("(k p) c -> p k c", p=128), st)
                return st

            def mod_part(p0, p1, MODB, cast=None):
                for pc in range(p0, p1):
                    wb = load_w32(w_ada, D, pc * 256, 256)
                    for jj in range(2):
                        j = pc * 2 + jj
                        for k in range(8):
                            mm(bap(MODB, j, j + 1), wb[:, k, jj * 128:(jj + 1) * 128], scb[:, k:k + 1],
                               k == 0, k == 7, [wb, scb], bk(MODB), k == 7)
                V(lambda e: e.tensor_tensor(modT[:, p0 * 2:p1 * 2], bap(MODB, p0 * 2, p1 * 2), vecs[:, 64 + p0 * 2:64 + p1 * 2], ALU.add),
                  reads=[bk(MODB), vecs], writes=[(modT, p0)])

            mod_part(0, 8, 0)
            if stop == -5:
                return
            V(lambda e: e.scalar_tensor_tensor(der[:, 0:8], modT[:, 8:16], 1.0, vecs[:, V_GPRE:V_GPRE + 8], ALU.add, ALU.mult),
              reads=[modT, vecs], writes=[(der, 0)])
            V(lambda e: e.tensor_copy(der[:, 8:16], modT[:, 0:8]), reads=[modT], writes=[(der, 8)])

            def mod_late():
                mod_part(8, 24, 7, cast="pool")
                V(lambda e: e.tensor_tensor(der[:, 16:24], modT[:, 16:24], vecs[:, V_GPOST:V_GPOST + 8], ALU.mult),
                  reads=[modT, vecs], writes=[(der, 16)])
                V(lambda e: e.scalar_tensor_tensor(der[:, 24:32], modT[:, 32:40], 1.0, vecs[:, V_GPRE2:V_GPRE2 + 8], ALU.add, ALU.mult),
                  reads=[modT, vecs], writes=[(der, 24)])
                V(lambda e: e.tensor_copy(der[:, 32:40], modT[:, 24:32]), reads=[modT], writes=[(der, 32)])
                V(lambda e: e.tensor_tensor(der[:, 40:48], modT[:, 40:48], vecs[:, V_GPOST2:V_GPOST2 + 8], ALU.mult),
                  reads=[modT, vecs], writes=[(der, 40)])
            DER_ALL = [(der, 0), (der, 8), (der, 16), (der, 24), (der, 32), (der, 40)]

            TPB = [0, 1]
            tp_rr = [0]
            xb_rr = [0]

            def xb_prep(src_rows_ap, nrows):
                i = xb_rr[0] % 2
                xb_rr[0] += 1
                xb = xblk[i]
                xn = xnb[i]
                dma_in(xb[0:nrows, :], src_rows_ap, xb)
                sm = nxt(small, small_rr)
                A(lambda e: e.activation(xn[0:nrows, :], xb[0:nrows, :], AF.Square, accum_out=sm[0:nrows, 0:1]),
                  reads=[xb], writes=[xn, (sm, 0)])
                A(lambda e: e.activation(sm[0:nrows, 3:4], sm[0:nrows, 0:1], AF.Sqrt, bias=EPS, scale=1.0 / D),
                  reads=[(sm, 0)], writes=[(sm, 3)])
                V(lambda e: e.reciprocal(sm[0:nrows, 2:3], sm[0:nrows, 3:4]), reads=[(sm, 3)], writes=[(sm, 2)])
                V(lambda e: e.tensor_scalar(xn[0:nrows, :], xb[0:nrows, :], sm[0:nrows, 2:3], None, ALU.mult),
                  reads=[xb, (sm, 2)], writes=[xn])
                return (xn, nrows)

            def xb_trans(hd, hT, col0, gsc, shc):
                xn, nrows = hd
                b = TPB[tp_rr[0] % 2]
                tp_rr[0] += 1
                tpv = bap_bf(b)
                for k in range(8):
                    P(lambda e, k=k: e.transpose(tpv[:, k * 128:k * 128 + nrows], xn[0:nrows, k * 128:(k + 1) * 128],
                                                  ident_b[0:nrows, 0:nrows]),
                      reads=[xn, ident_b], writes=[bk(b)], inc=(k == 7))
                for k in range(8):
                    if b == TPB[0]:
                        V(lambda e, k=k: e.tensor_scalar(hT[:, k, col0:col0 + nrows], tpv[:, k * 128:k * 128 + nrows],
                                                         der[:, gsc + k:gsc + k + 1], der[:, shc + k:shc + k + 1],
                                                         ALU.mult, ALU.add),
                          reads=[bk(b), (der, gsc), (der, shc)], writes=[(hT, k)])
                    else:
                        A(lambda e, k=k: e.activation(hT[:, k, col0:col0 + nrows], tpv[:, k * 128:k * 128 + nrows],
                                                      AF.Identity, bias=der[:, shc + k:shc + k + 1],
                                                      scale=der[:, gsc + k:gsc + k + 1]),
                          reads=[bk(b), (der, gsc), (der, shc)], writes=[(hT, k)])

            def x_block_to_hT(src_rows_ap, nrows, hT, col0, gsc, shc):
                xb_trans(xb_prep(src_rows_ap, nrows), hT, col0, gsc, shc)

            def rstd_from_ps(ps_bank, nfeat, ncols=512):
                r = nxt(rstd_t, rstd_rr)
                jt = nxt(tmpf, tmpf_rr)
                A(lambda e: e.activation(jt[:, 0:ncols], bap(ps_bank, 0, ncols), AF.Sqrt, bias=EPS, scale=1.0 / nfeat),
                  reads=[bk(ps_bank)], writes=[jt])
                V(lambda e: e.reciprocal(r[:, 0:ncols], jt[:, 0:ncols]), reads=[jt], writes=[r])
                return r

            if stop == 0:
                return
            oT = sb("oT", [128, 8, NOWN], BF)
            ph12 = es.enter_context(contextlib.ExitStack())
            ph1 = es.enter_context(contextlib.ExitStack())

            def sb12(name, shape, dt=F32):
                return ph12.enter_context(nc.sbuf_tensor("s_" + name, list(shape), dt))

            def sb1(name, shape, dt=F32):
                return ph1.enter_context(nc.sbuf_tensor("s_" + name, list(shape), dt))

            kvn = [sb12("kvn_own", [128, 2, NOWN], BF), sb12("kvn_oth", [128, 2, NOWN], BF)]
            krT = [sb12("kr_own", [128, NOWN], BF), sb12("kr_oth", [128, NOWN], BF)]
            for kr_ in krT:
                G(lambda e, kr_=kr_: e.memset(kr_[64:128, :], 0.0), writes=[kr_])
            qn = sb12("qn", [128, 3, NOWN], BF)
            CS = sb12("cs_own", [64, 2, NOWN])
            wuq = sb12("wuq", [128, 3, 2048], BF)
            hTs = [sb1("hT%d" % i, [128, 8, 640], BF) for i in range(2)]
            wlat = sb1("wlat", [128, 8, 768], BF)
            cs_tmp = sb1("cs_tmp", [64, 2, 512])
            posi = sb1("posi", [64, 512], I32)
            angs = [sb1("ang%d" % i, [64, 512]) for i in range(4)]
            ni_t = sb1("ni_t", [64, 512], I32)

            for pc, (src, c0, n, d0) in enumerate(((w_in, 2048, 256, 0), (w_in, 2304, 256, 256), (w_in, 2560, 192, 512),
                                                   (w_kr_sw, 0, 64, 704))):
                wb = load_w(src, D, c0, n)
                G(lambda e, wb=wb, n=n, d0=d0: e.tensor_copy(wlat[:, :, d0:d0 + n], wb[:, :, 0:n]),
                  reads=[wb], writes=[(wlat, pc)])
            WL = [(wlat, i) for i in range(4)]
            if stop == 10:
                return

            def rope_tables(pos_t, c0, dst, dcol):
                src = bass.AP(pos_t, c0, [[0, 64], [1, 512]])
                dma_in(posi[:], src, posi)
                a0, a1, a2, a3 = angs
                V(lambda e: e.tensor_copy(a0[:], posi[:]), reads=[posi], writes=[a0])
                V(lambda e: e.tensor_scalar(a0[:], a0[:], invf[:, 0:1], None, ALU.mult), reads=[a0, invf], writes=[a0])
                V(lambda e: e.tensor_scalar(a1[:], a0[:], 1.0 / TWO_PI, None, ALU.mult), reads=[a0], writes=[a1])
                V(lambda e: e.tensor_copy(ni_t[:], a1[:]), reads=[a1], writes=[ni_t])
                V(lambda e: e.tensor_copy(a1[:], ni_t[:]), reads=[ni_t], writes=[a1])
                V(lambda e: e.scalar_tensor_tensor(a2[:], a1[:], -C1, a0[:], ALU.mult, ALU.add), reads=[a1, a0], writes=[a2])
                V(lambda e: e.scalar_tensor_tensor(a2[:], a1[:], -C2, a2[:], ALU.mult, ALU.add), reads=[a1, a2], writes=[a2])
                V(lambda e: e.tensor_scalar(a3[:], a2[:], math.pi, -TWO_PI, ALU.is_gt, ALU.mult), reads=[a2], writes=[a3])
                V(lambda e: e.tensor_tensor(a2[:], a2[:], a3[:], ALU.add), reads=[a2, a3], writes=[a2])
                V(lambda e: e.tensor_scalar(a3[:], a2[:], -math.pi, TWO_PI, ALU.is_lt, ALU.mult), reads=[a2], writes=[a3])
                V(lambda e: e.tensor_tensor(a2[:], a2[:], a3[:], ALU.add), reads=[a2, a3], writes=[a2])
                V(lambda e: e.tensor_scalar(a1[:], a2[:], math.pi / 2, None, ALU.add), reads=[a2], writes=[a1])
                V(lambda e: e.tensor_scalar(a3[:], a1[:], math.pi, -TWO_PI, ALU.is_gt, ALU.mult), reads=[a1], writes=[a3])
                V(lambda e: e.tensor_tensor(a1[:], a1[:], a3[:], ALU.add), reads=[a1, a3], writes=[a1])
                V(lambda e: e.tensor_scalar(a1[:], a1[:], math.pi, -math.pi, ALU.min, ALU.max), reads=[a1], writes=[a1])
                V(lambda e: e.tensor_scalar(a2[:], a2[:], math.pi, -math.pi, ALU.min, ALU.max), reads=[a2], writes=[a2])
                A(lambda e: e.activation(dst[:, 0, dcol:dcol + 512], a1[:], AF.Sin), reads=[a1], writes=[(dst, dcol)])
                A(lambda e: e.activation(dst[:, 1, dcol:dcol + 512], a2[:], AF.Sin, scale=sgn[:, 0:1]),
                  reads=[a2, sgn], writes=[(dst, dcol)])

            for pc in range(6):
                wb = load_w(w_uq, 384, pc * 256, 256, cast="pool")
                G(lambda e, wb=wb, pc=pc: e.tensor_copy(wuq[:, :, pc * 256:(pc + 1) * 256], wb[:, 0:3, :]),
                  reads=[wb], writes=[(wuq, pc)])
            for pc in range(2):
                wb = load_w(w_uq_sw, 384, pc * 256, 256, cast="pool")
                G(lambda e, wb=wb, pc=pc: e.tensor_copy(wuq[:, :, 1536 + pc * 256:1536 + (pc + 1) * 256], wb[:, 0:3, :]),
                  reads=[wb], writes=[(wuq, 6 + pc)])
            def prep1(j_):
                i_, b_ = j_ // 4, j_ % 4
                grp_, t_ = i_ // 4, i_ % 4
                xsrc = x_own if grp_ == 0 else x_oth
                r0 = t_ * 512 + b_ * 128
                return xb_prep(xsrc[r0:r0 + 128, :], 128)

            hd1 = [prep1(0)]
            for grp in range(2):
                pos_t = pos_own if grp == 0 else pos_oth
                for t in range(4):
                    hT = hTs[(grp * 4 + t) % 2]
                    for b in range(4):
                        j_ = (grp * 4 + t) * 4 + b
                        nh = prep1(j_ + 1) if j_ + 1 < 32 else None
                        xb_trans(hd1[0], hT, b * 128, 0, 8)
                        hd1[0] = nh
                    if grp == 0:
                        rope_tables(pos_t, t * 512, CS, t * 512)
                        cs, cc = CS, t * 512
                    else:
                        rope_tables(pos_t, t * 512, cs_tmp, 0)
                        cs, cc = cs_tmp, 0
                    if stop == 12:
                        return
                    hk = [(hT, k) for k in range(8)]
                    for m in range(2):
                        for k in range(8):
                            mm(bap(2 + m), wlat[:, k, 384 + m * 128:384 + (m + 1) * 128], hT[:, k, 0:512],
                               k == 0, k == 7, hk + WL, bk(2 + m), k == 7)
                    for m in range(2):
                        for k in range(8):
                            mm(bap(5 + m)[0:64, :], wlat[:, k, 640 + m * 64:640 + (m + 1) * 64], hT[:, k, 0:512],
                               k == 0, k == 7, hk + WL, bk(5 + m), k == 7)
                    sq = []
                    for m in range(2):
                        s_ = nxt(tmpb, tmpb_rr)
                        A(lambda e, m=m, s_=s_: e.activation(s_[:], bap(2 + m), AF.Square), reads=[bk(2 + m)], writes=[s_])
                        sq.append(s_)
                    for m in range(2):
                        mm(bap(4), ones_b[:], sq[m][:], m == 0, m == 1, [ones_b, sq[m]], bk(4), True)
                    r = rstd_from_ps(4, 256)
                    for m in range(2):
                        V(lambda e, m=m, r=r: e.scalar_tensor_tensor(kvn[grp][:, m, t * 512:(t + 1) * 512], bap(2 + m),
                                                                     vecs[:, V_KVG + m:V_KVG + m + 1], r[:], ALU.mult, ALU.mult),
                          reads=[bk(2 + m), r, vecs], writes=[(kvn[grp], t)])
                    ta = nxt(tmpf, tmpf_rr)
                    tb_ = nxt(tmpf, tmpf_rr)
                    V(lambda e, ta=ta, cs=cs, cc=cc: e.tensor_tensor(ta[0:64, :], bap(5)[0:64, :], cs[:, 0, cc:cc + 512], ALU.mult),
                      reads=[bk(5), (cs, cc)], writes=[ta])
                    V(lambda e, tb_=tb_, cs=cs, cc=cc: e.tensor_tensor(tb_[0:64, :], bap(6)[0:64, :], cs[:, 1, cc:cc + 512], ALU.mult),
                      reads=[bk(6), (cs, cc)], writes=[tb_])
                    V(lambda e, ta=ta, tb_=tb_: e.tensor_tensor(krT[grp][0:64, t * 512:(t + 1) * 512], ta[0:64, :], tb_[0:64, :], ALU.add),
                      reads=[ta, tb_], writes=[(krT[grp], t)])
                    if stop == 13:
                        return
                    if grp == 0:
                        QB = [2, 3, 7]
                        for m in range(3):
                            for k in range(8):
                                mm(bap(QB[m]), wlat[:, k, m * 128:(m + 1) * 128], hT[:, k, 0:512],
                                   k == 0, k == 7, hk + WL, bk(QB[m]), k == 7)
                        sq = []
                        for m in range(3):
                            s_ = nxt(tmpb, tmpb_rr)
                            A(lambda e, m=m, s_=s_: e.activation(s_[:], bap(QB[m]), AF.Square), reads=[bk(QB[m])], writes=[s_])
                            sq.append(s_)
                        for m in range(3):
                            mm(bap(4), ones_b[:], sq[m][:], m == 0, m == 2, [ones_b, sq[m]], bk(4), True)
                        r = rstd_from_ps(4, 384)
                        for m in range(3):
                            V(lambda e, m=m, r=r: e.scalar_tensor_tensor(qn[:, m, t * 512:(t + 1) * 512], bap(QB[m]),
                                                                         vecs[:, V_QG + m:V_QG + m + 1], r[:], ALU.mult, ALU.mult),
                              reads=[bk(QB[m]), r, vecs], writes=[(qn, t)])
                    if stop == 14:
                        return

            S.barrier()
            ph1.close()
            if stop == 1:
                ph12.close()
                return

            wukv = sb12("wukv", [128, 2, 2048], BF)
            for pc in range(8):
                wb = load_w(w_ukv, 256, pc * 256, 256, cast="pool")
                G(lambda e, wb=wb, pc=pc: e.tensor_copy(wukv[:, :, pc * 256:(pc + 1) * 256], wb[:, 0:2, :]),
                  reads=[wb], writes=[(wukv, pc)])
            KhT = [[sb12("kh%d_%d" % (i, g), [128, NOWN], BF) for g in range(2)] for i in range(1)]
            Vh = [[sb12("vh%d_%d" % (i, g), [128, 16, 128], BF) for g in range(2)] for i in range(1)]
            Qh = [sb12("qh%d" % i, [128, NOWN], BF) for i in range(1)]
            Qr = [sb12("qr%d" % i, [128, NOWN], BF) for i in range(1)]
            G(lambda e: e.memset(Qr[0][64:128, :], 0.0), writes=[Qr[0]])
            Pt = [sb12("pt%d" % i, [128, 512], BF) for i in range(4)]
            pt_rr = [0]
            SB_ = [0, 1, 2]
            s_rr = [0]
            OB = [3, 5]
            LB = [4, 6]
            HBS = [7, 3, 4]
            hb_rr = [0]

            def nhb():
                b_ = HBS[hb_rr[0] % 3]
                hb_rr[0] += 1
                return b_
            evac_rr = [0]

            def evac_copy(dst_ap, src_bank_ap, reads, writes):
                if evac_rr[0] % 2 == 0:
                    V(lambda e: e.tensor_copy(dst_ap, src_bank_ap), reads=reads, writes=writes)
                else:
                    A(lambda e: e.activation(dst_ap, src_bank_ap, AF.Copy), reads=reads, writes=writes)
                evac_rr[0] += 1

            def build_head(h):
                i = 0
                for grp in range(2):
                    for t in range(4):
                        HB = nhb()
                        for k in range(2):
                            mm(bap(HB), wukv[:, k, h * 256:h * 256 + 128], kvn[grp][:, k, t * 512:(t + 1) * 512],
                               k == 0, k == 1, [wukv, kvn[grp]], bk(HB), k == 1)
                        evac_copy(KhT[i][grp][:, t * 512:(t + 1) * 512], bap(HB), [bk(HB)], [(KhT[i][grp], t)])
                    for t in range(4):
                        HB = nhb()
                        for b in range(4):
                            blk = t * 4 + b
                            for k in range(2):
                                mm(bap(HB, b * 128, (b + 1) * 128), kvn[grp][:, k, blk * 128:(blk + 1) * 128],
                                   wukv[:, k, h * 256 + 128:h * 256 + 256],
                                   k == 0, k == 1, [wukv, kvn[grp]], bk(HB), (k == 1 and b == 3))
                        evac_copy(Vh[i][grp][:, t * 4:(t + 1) * 4, :], bap(HB).rearrange("p (b d) -> p b d", d=128),
                                  [bk(HB)], [(Vh[i][grp], t)])
                for t in range(4):
                    HB = nhb()
                    for k in range(3):
                        mm(bap(HB), wuq[:, k, h * 192:h * 192 + 128], qn[:, k, t * 512:(t + 1) * 512],
                           k == 0, k == 2, [wuq, qn], bk(HB), k == 2)
                    evac_copy(Qh[i][:, t * 512:(t + 1) * 512], bap(HB), [bk(HB)], [(Qh[i], t)])
                for t in range(4):
                    HB = nhb()
                    for k in range(3):
                        mm(bap(HB)[0:64, :], wuq[:, k, h * 192 + 128:h * 192 + 192], qn[:, k, t * 512:(t + 1) * 512],
                           k == 0, k == 2, [wuq, qn], bk(HB), k == 2)
                    ta = nxt(tmpf, tmpf_rr)
                    V(lambda e, ta=ta, t=t: e.tensor_tensor(ta[0:64, :], bap(HB)[0:64, :], CS[:, 0, t * 512:(t + 1) * 512], ALU.mult),
                      reads=[bk(HB), CS], writes=[ta])
                    HB = nhb()
                    for k in range(3):
                        mm(bap(HB)[0:64, :], wuq[:, k, 1536 + h * 64:1536 + (h + 1) * 64], qn[:, k, t * 512:(t + 1) * 512],
                           k == 0, k == 2, [wuq, qn], bk(HB), k == 2)
                    tb_ = nxt(tmpf, tmpf_rr)
                    V(lambda e, tb_=tb_, t=t: e.tensor_tensor(tb_[0:64, :], bap(HB)[0:64, :], CS[:, 1, t * 512:(t + 1) * 512], ALU.mult),
                      reads=[bk(HB), CS], writes=[tb_])
                    V(lambda e, ta=ta, tb_=tb_, t=t: e.tensor_tensor(Qr[i][0:64, t * 512:(t + 1) * 512], ta[0:64, :], tb_[0:64, :], ALU.add),
                      reads=[ta, tb_], writes=[(Qr[i], t)])

            def attend_head(h):
                i = 0
                for g in range(4):
                    ob = OB[g % 2]
                    lb = LB[g % 2]
                    visits = [(J, grp) for J in range(4 * g + 4) for grp in range(2)]
                    pend = []

                    def do_pv(v, first, last):
                        J, grp, c0, pt = v
                        mm(bap(ob, c0, 512), Vh[i][grp][:, J, :], pt[:, c0:512], first, last,
                           [Vh[i][grp], pt], bk(ob), True)
                        mm(bap(lb, c0, 512), ones_b[:], pt[:, c0:512], first, last,
                           [ones_b, pt], bk(lb), True)

                    npv = [0]
                    for vi, (J, grp) in enumerate(visits):
                        j = J - 4 * g
                        c0 = 128 * max(j, 0)
                        sbk = SB_[s_rr[0] % 3]
                        s_rr[0] += 1
                        q0 = g * 512 + c0
                        q1 = (g + 1) * 512
                        masked = j >= 0
                        mm(bap(sbk, c0, 512), KhT[i][grp][:, J * 128:(J + 1) * 128], Qh[i][:, q0:q1],
                           True, False, [KhT[i][grp], Qh[i]], bk(sbk), False)
                        mm(bap(sbk, c0, 512), krT[grp][:, J * 128:(J + 1) * 128], Qr[i][:, q0:q1],
                           False, not masked, [krT[grp], Qr[i]], bk(sbk), not masked)
                        if masked:
                            mk = tri_b if grp == 0 else pair_b
                            mm(bap(sbk, c0, c0 + 128), ident_b[:], mk[:], False, True, [ident_b, mk], bk(sbk), True)
                        pt = nxt(Pt, pt_rr)
                        A(lambda e, pt=pt, sbk=sbk, c0=c0: e.activation(pt[:, c0:512], bap(sbk, c0, 512), AF.Exp, scale=SCALE),
                          reads=[bk(sbk)], writes=[pt])
                        pend.append((J, grp, c0, pt))
                        if len(pend) > 2:
                            v = pend.pop(0)
                            do_pv(v, npv[0] == 0, False)
                            npv[0] += 1
                    while pend:
                        v = pend.pop(0)
                        do_pv(v, npv[0] == 0, len(pend) == 0)
                        npv[0] += 1
                    rl = nxt(rstd_t, rstd_rr)
                    V(lambda e, rl=rl, lb=lb: e.reciprocal(rl[:], bap(lb)), reads=[bk(lb)], writes=[rl])
                    V(lambda e, rl=rl, ob=ob, g=g: e.tensor_tensor(oT[:, h, g * 512:(g + 1) * 512], bap(ob), rl[:], ALU.mult),
                      reads=[bk(ob), rl], writes=[(oT, (h, g))])

            prep = []
            for pc in range(4):
                prep += [(w_in, pc * 256, 0), (w_in, 1024 + pc * 256, 0)]
            for pc in range(4):
                prep += [(w_conv_out, pc * 256, 0)]
            for pc in range(4):
                prep += [(w_attn_out, pc * 256, 0), (w_in, 2752 + pc * 256, 0), (w_in, 3776 + pc * 256, 0)]
            for pc in range(4):
                prep += [(w_out, pc * 256, 0)]
            for pc in range(16):
                prep += [(w_mlp_in, pc * 256, 0)]
            for pc in range(4):
                for rr_ in range(4):
                    prep += [(w_mlp_out, pc * 256, rr_ * 1024)]
            prep_tok = {}

            def do_prep():
                prev = None
                for i, (src, c0, r0) in enumerate(prep):
                    wb = load_w(src, D, c0, 256, r0=r0, cast="pool")
                    if prev is not None:
                        pw, pi = prev
                        S.dma("sp", lambda e, pw=pw, pi=pi: e.dma_start(out=wq[pi], in_=pw[:].rearrange("p k c -> p (k c)")),
                              reads=[pw], writes=[(wq_key, pi)])
                    prev = (wb, i)
                pw, pi = prev
                S.dma("sp", lambda e: e.dma_start(out=wq[pi], in_=pw[:].rearrange("p k c -> p (k c)")),
                      reads=[pw], writes=[(wq_key, pi)])

            do_prep()
            build_head(0)
            for h in range(8):
                attend_head(h)
                if h + 1 < 8:
                    build_head(h + 1)
            mod_late()

            S.barrier()
            ph12.close()
            if stop == 2:
                return
            wpool = [wbf[0][:], wbf[1][:]]
            for i_ in range(NST):
                fl = wst[i_].bitcast(BF)[:].rearrange("p k c -> p (k c)")
                wpool += [fl[:, 0:2048].rearrange("p (k c) -> p k c", c=256), fl[:, 2048:4096].rearrange("p (k c) -> p k c", c=256)]
            wp_rr = [0]
            pidx = {(src_.tensor.name, c0_, r0_): i_ for i_, (src_, c0_, r0_) in enumerate(prep)}

            def load_wq(src, rows, c0, ncols, r0=0):
                i = pidx[(src.tensor.name, c0, r0)]
                buf = wpool[wp_rr[0] % len(wpool)]
                wp_rr[0] += 1
                S.dma("sp", lambda e: e.dma_start(out=buf.rearrange("p k c -> p (k c)"), in_=wq[i]), writes=[buf])
                return buf

            xT = sb("xT", [128, 8, 512])
            yT = sb("yT", [128, 8, 512])
            hTe = sb("hTe", [128, 8, 640], BF)
            uext = sb("uext", [128, 8, 640], BF)
            arena = sb("arena", [128, 16384], BF)
            hid = arena[:, :].rearrange("p (j t) -> p j t", t=512)
            ucv = arena.bitcast(F32)[:, 0:4096].rearrange("p (c t) -> p c t", t=512)
            diag = [arena[:, 8192 + i * 3968:8192 + (i + 1) * 3968].rearrange("p (k m) -> p k m", m=128) for i in range(2)]
            sh8 = sb("sh8", [128, 8, 512], BF)
            actT = sh8
            mT = sh8
            h2T = sh8
            yaT = sb("yaT", [128, 8, 512], BF)
            oblk = xblk
            stat_s = sb("stat_s", [128, 512])
            stat_n = sb("stat_n", [128, 512])
            sb_sig = sb("sb_sig", [128, 640])

            deferred = []

            def flush_def():
                for f_ in deferred:
                    f_()
                deferred.clear()

            def stats_accum(ps_b, src_ap, reads, idx, n, defer=False):
                s_ = nxt(tmpb, tmpb_rr)
                A(lambda e: e.activation(s_[:], src_ap, AF.Square), reads=reads, writes=[s_])
                f_ = lambda s_=s_: mm(bap(ps_b), ones_b[:], s_[:], idx == 0, idx == n - 1, [ones_b, s_], bk(ps_b), True)
                if defer:
                    deferred.append(f_)
                else:
                    f_()

            def fh_blocks(g_):
                lst = []
                for b in range(4):
                    blk = g_ * 4 + b
                    lst.append((x_halo[blk * 32:(blk + 1) * 32, :], 32, b * 160))
                    lst.append((x_own[blk * 128:(blk + 1) * 128, :], 128, b * 160 + 32))
                return lst

            def front_h(g_):
                lst = fh_blocks(g_)
                hd = xb_prep(lst[0][0], lst[0][1])
                for i_ in range(8):
                    nh = xb_prep(lst[i_ + 1][0], lst[i_ + 1][1]) if i_ + 1 < 8 else None
                    xb_trans(hd, hTe, lst[i_][2], 0, 8)
                    hd = nh

            front_h(0)
            for g in range(4):
                for b in range(4):
                    blk = g * 4 + b
                    xb = xblk[xb_rr[0] % 2]
                    xb_rr[0] += 1
                    dma_in(xb[:], x_own[blk * 128:(blk + 1) * 128, :], xb)
                    for half in range(2):
                        tb_i = 2 + half
                        for kk in range(4):
                            k = half * 4 + kk
                            P(lambda e, k=k, kk=kk, tb_i=tb_i, xb=xb: e.transpose(bap(tb_i, kk * 128, (kk + 1) * 128),
                                                                                    xb[:, k * 128:(k + 1) * 128], ident_f[:]),
                              reads=[xb, ident_f], writes=[bk(tb_i)], inc=(kk == 3))
                        evac_copy(xT[:, half * 4:(half + 1) * 4, b * 128:(b + 1) * 128],
                                  bap(tb_i).rearrange("p (k t) -> p k t", t=128), [bk(tb_i)], [(xT, (half, b))])
                hk = [(hTe, k) for k in range(8)]
                def glu_chunk(c, wa, wb2, cc):
                    for (wt, d) in ((wa, 0), (wb2, 1)):
                        for (n0, n1, hb) in ((0, 512, 0), (512, 640, 1)):
                            for k in range(8):
                                mm(PD[d][:, hb * 512:hb * 512 + (n1 - n0)], wt[:, k, cc * 128:(cc + 1) * 128],
                                   hTe[:, k, n0:n1], k == 0, k == 7, hk + [wt], (PD[d], hb), k == 7)
                    sg = sb_sig
                    A(lambda e: e.activation(sg[:, 0:640], PD[1][:, 0:640], AF.Sigmoid),
                      reads=[(PD[1], 0), (PD[1], 1)], writes=[sg])
                    V(lambda e: e.tensor_tensor(uext[:, c, :], PD[0][:, 0:640], sg[:, 0:640], ALU.mult),
                      reads=[(PD[0], 0), (PD[0], 1), sg], writes=[(uext, c)])
                    if g == 0:
                        V(lambda e: e.tensor_scalar(uext[:, c, 0:32], uext[:, c, 0:32], halom[:, 0:1], None, ALU.mult),
                          reads=[(uext, c), halom], writes=[(uext, c)])

                def conv_chunk(c):
                    dg = diag[c % 2]
                    for k in range(31):
                        G(lambda e: e.tensor_scalar(dg[:, k, :], ident_b[:], cwT[:, c, k:k + 1], 1.0, ALU.mult, ALU.mult),
                          reads=[ident_b, cwT], writes=[(arena, None) if (c == 0 and k == 0) else (arena, ("d", c % 2, k))])
                    uv = uext[:, c, :].rearrange("p (b w) -> p b w", w=160)
                    cb = 4 + (c % 2)
                    for k in range(31):
                        mm(bap(cb).rearrange("p (b w) -> p b w", w=128), dg[:, k, :], uv[:, :, 2 + k:2 + k + 128],
                           k == 0, k == 30, [(arena, ("d", c % 2, k)), (uext, c)], bk(cb), k == 30)
                    flush_def()
                    A(lambda e: e.activation(ucv[:, c, :], bap(cb), AF.Identity, bias=vecs[:, V_CB + c:V_CB + c + 1]),
                      reads=[bk(cb), vecs], writes=[(arena, ("u", c))])
                    ub_ = nxt(tmpb, tmpb_rr)
                    V(lambda e: e.tensor_copy(ub_[:], ucv[:, c, :]), reads=[(arena, ("u", c))], writes=[ub_])
                    deferred.append(lambda ub_=ub_, c=c: mm(bap(6), ones_b[:], ub_[:], c == 0, c == 7, [ones_b, ub_], bk(6), True))
                    stats_accum(7, ucv[:, c, :], [(arena, ("u", c))], c, 8, defer=True)

                for pc in range(4):
                    wa = load_wq(w_in, D, pc * 256, 256)
                    wb2 = load_wq(w_in, D, 1024 + pc * 256, 256)
                    for cc in range(2):
                        c = pc * 2 + cc
                        glu_chunk(c, wa, wb2, cc)
                        if c >= 1:
                            conv_chunk(c - 1)
                conv_chunk(7)
                flush_def()
                mean = stat_s
                nmr = stat_n
                A(lambda e: e.activation(mean[:], bap(6), AF.Copy, scale=1.0 / D), reads=[bk(6)], writes=[mean])
                jt = nxt(tmpf, tmpf_rr)
                V(lambda e, jt=jt: e.tensor_tensor(jt[:], mean[:], mean[:], ALU.mult), reads=[mean], writes=[jt])
                jt2 = nxt(tmpf, tmpf_rr)
                V(lambda e, jt=jt, jt2=jt2: e.scalar_tensor_tensor(jt2[:], bap(7), 1.0 / D, jt[:], ALU.mult, ALU.subtract),
                  reads=[bk(7), jt], writes=[jt2])
                V(lambda e, jt2=jt2: e.tensor_scalar(jt2[:], jt2[:], 0.0, None, ALU.max), reads=[jt2], writes=[jt2])
                A(lambda e, jt=jt, jt2=jt2: e.activation(jt[:], jt2[:], AF.Sqrt, bias=EPS), reads=[jt2], writes=[jt])
                rln = nxt(rstd_t, rstd_rr)
                V(lambda e, jt=jt, rln=rln: e.reciprocal(rln[:], jt[:]), reads=[jt], writes=[rln])
                V(lambda e, rln=rln: e.scalar_tensor_tensor(nmr[:], mean[:], -1.0, rln[:], ALU.mult, ALU.mult),
                  reads=[mean, rln], writes=[nmr])
                for c in range(8):
                    jt = nxt(tmpf, tmpf_rr)
                    V(lambda e, c=c, jt=jt, rln=rln: e.tensor_tensor(jt[:], ucv[:, c, :], rln[:], ALU.mult),
                      reads=[(arena, ("u", c)), rln], writes=[jt])
                    V(lambda e, jt=jt: e.tensor_tensor(jt[:], jt[:], nmr[:], ALU.add), reads=[jt, nmr], writes=[jt])
                    A(lambda e, c=c, jt=jt: e.activation(actT[:, c, :], jt[:], AF.Silu,
                                                         bias=vecs[:, V_CNB + c:V_CNB + c + 1], scale=vecs[:, V_CG + c:V_CG + c + 1]),
                      reads=[jt, vecs, vecs], writes=[(actT, c)])
                ak = [(actT, k) for k in range(8)]
                for pc in range(4):
                    wb = load_wq(w_conv_out, D, pc * 256, 256)
                    for cc in range(2):
                        m = pc * 2 + cc
                        bb = 4 + (m % 2)
                        for k in range(8):
                            mm(bap(bb), wb[:, k, cc * 128:(cc + 1) * 128], actT[:, k, :], k == 0, k == 7, ak + [wb], bk(bb), k == 7)
                        evac_copy(yaT[:, m, :], bap(bb), [bk(bb)], [(yaT, m)])
                hown = lambda k: hTe[:, k, :].rearrange("p (b w) -> p b w", w=160)[:, :, 32:160]
                for pc in range(4):
                    wao = load_wq(w_attn_out, D, pc * 256, 256)
                    for cc in range(2):
                        for hh in range(8):
                            mm(bap(2 + cc), wao[:, hh, cc * 128:(cc + 1) * 128], oT[:, hh, g * 512:(g + 1) * 512],
                               hh == 0, hh == 7, [oT, wao], bk(2 + cc), hh == 7)
                    wga = load_wq(w_in, D, 2752 + pc * 256, 256)
                    sas = []
                    for cc in range(2):
                        m = pc * 2 + cc
                        for k in range(8):
                            mm(bap(cc).rearrange("p (b w) -> p b w", w=128), wga[:, k, cc * 128:(cc + 1) * 128], hown(k),
                               k == 0, k == 7, hk + [wga], bk(cc), k == 7)
                        sa = nxt(tmpf, tmpf_rr)
                        A(lambda e, sa=sa, cc=cc: e.activation(sa[:], bap(cc), AF.Sigmoid), reads=[bk(cc)], writes=[sa])
                        V(lambda e, sa=sa, m=m: e.tensor_tensor(sa[:], sa[:], yaT[:, m, :], ALU.mult),
                          reads=[sa, (yaT, m)], writes=[sa])
                        sas.append(sa)
                    wgb = load_wq(w_in, D, 3776 + pc * 256, 256)
                    for cc in range(2):
                        m = pc * 2 + cc
                        for k in range(8):
                            mm(bap(cc).rearrange("p (b w) -> p b w", w=128), wgb[:, k, cc * 128:(cc + 1) * 128], hown(k),
                               k == 0, k == 7, hk + [wgb], bk(cc), k == 7)
                        sb2 = nxt(tmpf, tmpf_rr)
                        A(lambda e, sb2=sb2, cc=cc: e.activation(sb2[:], bap(cc), AF.Sigmoid), reads=[bk(cc)], writes=[sb2])
                        V(lambda e, sb2=sb2, cc=cc: e.tensor_tensor(sb2[:], bap(2 + cc), sb2[:], ALU.mult),
                          reads=[bk(2 + cc), sb2], writes=[sb2])
                        V(lambda e, sa=sas[cc], sb2=sb2, m=m: e.tensor_tensor(mT[:, m, :], sa[:], sb2[:], ALU.add),
                          reads=[sas[cc], sb2], writes=[(mT, m)])
                mk_ = [(mT, k) for k in range(8)]
                for pc in range(4):
                    wb = load_wq(w_out, D, pc * 256, 256)
                    for cc in range(2):
                        m = pc * 2 + cc
                        bb = 4 + (m % 2)
                        for k in range(8):
                            mm(bap(bb), wb[:, k, cc * 128:(cc + 1) * 128], mT[:, k, :], k == 0, k == 7, mk_ + [wb], bk(bb), k == 7)
                        flush_def()
                        A(lambda e, m=m, bb=bb: e.activation(yT[:, m, :], bap(bb), AF.Copy), reads=[bk(bb)], writes=[(yT, m)])
                        stats_accum(6, yT[:, m, :], [(yT, m)], m, 8, defer=True)
                flush_def()
                if debug == "m" and g == 3:
                    V(lambda e: e.tensor_copy(hTe[:, :, 0:512], mT[:]), reads=[mT], writes=[hTe])
                    V(lambda e: e.tensor_copy(uext[:, :, 0:512], yT[:]), reads=[yT], writes=[uext])
                r1 = rstd_from_ps(6, D)
                for k in range(8):
                    jt = nxt(tmpf, tmpf_rr)
                    V(lambda e, k=k, jt=jt, r1=r1: e.tensor_tensor(jt[:], yT[:, k, :], r1[:], ALU.mult),
                      reads=[(yT, k), r1], writes=[jt])
                    V(lambda e, k=k, jt=jt: e.scalar_tensor_tensor(xT[:, k, :], jt[:], der[:, 16 + k:17 + k], xT[:, k, :], ALU.mult, ALU.add),
                      reads=[jt, (der, 16), xT], writes=[xT])
                    stats_accum(7, xT[:, k, :], [xT], k, 8)
                r2 = rstd_from_ps(7, D)
                for k in range(8):
                    jt = nxt(tmpf, tmpf_rr)
                    V(lambda e, k=k, jt=jt, r2=r2: e.scalar_tensor_tensor(jt[:], xT[:, k, :], der[:, 24 + k:25 + k], r2[:], ALU.mult, ALU.mult),
                      reads=[xT, (der, 24), r2], writes=[jt])
                    A(lambda e, k=k, jt=jt: e.activation(h2T[:, k, :], jt[:], AF.Identity, bias=der[:, 32 + k:33 + k]),
                      reads=[jt, (der, 32)], writes=[(h2T, k)])
                h2k = [(h2T, k) for k in range(8)]
                nlst = fh_blocks(g + 1) if g + 1 < 4 else None
                nhd = {}
                for pc in range(16):
                    if nlst is not None and pc % 2 == 0:
                        if pc >= 2:
                            xb_trans(nhd[pc // 2 - 1], hTe, nlst[pc // 2 - 1][2], 0, 8)
                        nhd[pc // 2] = xb_prep(nlst[pc // 2][0], nlst[pc // 2][1])
                    wb = load_wq(w_mlp_in, D, pc * 256, 256)
                    for cc in range(2):
                        j = pc * 2 + cc
                        bb = 4 + (j % 4)
                        for k in range(8):
                            mm(bap(bb), wb[:, k, cc * 128:(cc + 1) * 128], h2T[:, k, :], k == 0, k == 7, h2k + [wb], bk(bb), k == 7)
                        jt = nxt(tmpf, tmpf_rr)
                        A(lambda e, jt=jt, bb=bb: e.activation(jt[:], bap(bb), AF.Relu), reads=[bk(bb)], writes=[jt])
                        V(lambda e, jt=jt, j=j: e.tensor_tensor(hid[:, j, :], jt[:], jt[:], ALU.mult), reads=[jt],
                          writes=[(arena, None) if j == 0 else (arena, ("h", j))])
                if nlst is not None:
                    xb_trans(nhd[7], hTe, nlst[7][2], 0, 8)
                for pc in range(4):
                    for cc in range(2):
                        pass
                    wbs = []
                    for rr_ in range(4):
                        wb = load_wq(w_mlp_out, D, pc * 256, 256, r0=rr_ * 1024)
                        for cc in range(2):
                            bb = 4 + cc
                            for k in range(8):
                                kk = rr_ * 8 + k
                                mm(bap(bb), wb[:, k, cc * 128:(cc + 1) * 128], hid[:, kk, :], kk == 0, kk == 31,
                                   [arena, wb], bk(bb), (k == 7))
                    flush_def()
                    for cc in range(2):
                        m = pc * 2 + cc
                        bb = 4 + cc
                        A(lambda e, m=m, bb=bb: e.activation(yT[:, m, :], bap(bb), AF.Copy), reads=[bk(bb)], writes=[(yT, m)])
                        stats_accum(6, yT[:, m, :], [(yT, m)], m, 8, defer=True)
                flush_def()
                r3 = rstd_from_ps(6, D)
                for k in range(8):
                    jt = nxt(tmpf, tmpf_rr)
                    V(lambda e, k=k, jt=jt, r3=r3: e.tensor_tensor(jt[:], yT[:, k, :], r3[:], ALU.mult),
                      reads=[(yT, k), r3], writes=[jt])
                    V(lambda e, k=k, jt=jt: e.scalar_tensor_tensor(yT[:, k, :], jt[:], der[:, 40 + k:41 + k], xT[:, k, :], ALU.mult, ALU.add),
                      reads=[jt, (der, 40), xT], writes=[(yT, k)])
                for b in range(4):
                    ob_ = oblk[b % 2]
                    for half in range(2):
                        tb_i = 2 + half
                        for kk in range(4):
                            k = half * 4 + kk
                            P(lambda e, k=k, kk=kk, tb_i=tb_i, b=b: e.transpose(bap(tb_i, kk * 128, (kk + 1) * 128),
                                                                                 yT[:, k, b * 128:(b + 1) * 128], ident_f[:]),
                              reads=[yT, ident_f], writes=[bk(tb_i)], inc=(kk == 3))
                        evac_copy(ob_[:, half * 512:(half + 1) * 512], bap(tb_i), [bk(tb_i)], [(ob_, half)])
                    blk = g * 4 + b
                    tok = S.dma("act", lambda e, ob_=ob_, blk=blk: e.dma_start(out=out[blk * 128:(blk + 1) * 128, :], in_=ob_[:]),
                                reads=[ob_])
                    out_toks.append(tok)


        run_phases()
        if stop is not None:
            out_toks.append(S.dma("act", lambda e: e.dma_start(out=out[0:128, :], in_=xblk[0][:]), reads=[xblk[0]]))
        last = {}
        for (s, v) in out_toks:
            last[s] = max(last.get(s, 0), v)
        S.wait_all("act", list(last.items()))

        with nc.Block() as block:
            def emit(engname, eng):
                for (waits, fn, inc) in S.q[engname]:
                    for (s, v) in waits:
                        eng.wait_ge(sems[s], v)
                    if fn is None:
                        continue
                    ins = fn(eng)
                    if inc is not None:
                        ins.then_inc(sems[inc[0]], inc[1])

            @block.sync
            def _(e):
                emit("sp", e)

            @block.tensor
            def _(e):
                emit("pe", e)

            @block.scalar
            def _(e):
                emit("act", e)

            @block.vector
            def _(e):
                emit("dve", e)

            @block.gpsimd
            def _(e):
                emit("pool", e)
    return nc


def _prep_inputs(inputs):
    x = np.asarray(inputs["x"], np.float32)
    pos = np.asarray(inputs["positions"], np.int32)
    c = np.asarray(inputs["c"], np.float32)
    w_in = np.ascontiguousarray(np.asarray(inputs["w_in"], np.float32)[0])
    w_uq = np.ascontiguousarray(np.asarray(inputs["w_uq"], np.float32)[0])
    kr = w_in[:, 2688:2752]
    w_kr_sw = np.ascontiguousarray(np.concatenate([kr[:, 32:64], kr[:, 0:32]], axis=1))
    uq3 = w_uq.reshape(384, 8, 192)[:, :, 128:192]
    w_uq_sw = np.ascontiguousarray(np.concatenate([uq3[:, :, 32:64], uq3[:, :, 0:32]], axis=2).reshape(384, 512))
    k_idx = np.arange(128)[:, None]
    q_idx = np.arange(128)[None, :]
    trimask = np.where(k_idx <= q_idx, 0.0, NEG).astype(np.float32)
    ident = np.eye(128, dtype=np.float32)
    inv = (1.0 / (np.float32(10000.0) ** (np.arange(0, 64, 2, dtype=np.float32) / np.float32(64)))).astype(np.float32)
    invf = np.concatenate([inv, inv]).reshape(64, 1).astype(np.float32)
    sgn = np.concatenate([-np.ones(32), np.ones(32)]).reshape(64, 1).astype(np.float32)
    shared = {
        "trimask": trimask, "ident": ident, "invf": invf, "sgn": sgn,
        "w_ada": np.ascontiguousarray(inputs["w_ada"][0], np.float32),
        "b_ada": np.ascontiguousarray(inputs["b_ada"][0], np.float32),
        "g_pre_mix": np.ascontiguousarray(inputs["g_pre_mix"][0], np.float32),
        "g_post_mix": np.ascontiguousarray(inputs["g_post_mix"][0], np.float32),
        "g_pre_mlp": np.ascontiguousarray(inputs["g_pre_mlp"][0], np.float32),
        "g_post_mlp": np.ascontiguousarray(inputs["g_post_mlp"][0], np.float32),
        "w_in": w_in, "w_kr_sw": w_kr_sw,
        "conv_w": np.ascontiguousarray(inputs["conv_w"][0], np.float32),
        "conv_b": np.ascontiguousarray(inputs["conv_b"][0], np.float32),
        "conv_norm_g": np.ascontiguousarray(inputs["conv_norm_g"][0], np.float32),
        "conv_norm_b": np.ascontiguousarray(inputs["conv_norm_b"][0], np.float32),
        "w_conv_out": np.ascontiguousarray(inputs["w_conv_out"][0], np.float32),
        "q_norm_g": np.ascontiguousarray(inputs["q_norm_g"][0], np.float32),
        "w_uq": w_uq, "w_uq_sw": w_uq_sw,
        "kv_norm_g": np.ascontiguousarray(inputs["kv_norm_g"][0], np.float32),
        "w_ukv": np.ascontiguousarray(inputs["w_ukv"][0], np.float32),
        "w_attn_out": np.ascontiguousarray(inputs["w_attn_out"][0], np.float32),
        "w_out": np.ascontiguousarray(inputs["w_out"][0], np.float32),
        "w_mlp_in": np.ascontiguousarray(inputs["w_mlp_in"][0], np.float32),
        "w_mlp_out": np.ascontiguousarray(inputs["w_mlp_out"][0], np.float32),
    }
    in_maps = []
    for core in range(8):
        b, p = core // 2, core % 2
        xb = x[b].reshape(32, 128, D)
        pb = pos[b].reshape(32, 128)
        own = [2 * i + p for i in range(16)]
        oth = [2 * i + 1 - p for i in range(16)]
        halo = np.zeros((16, 32, D), np.float32)
        for i in range(16):
            st = own[i] * 128
            if st > 0:
                halo[i] = x[b, st - 32:st]
        m = dict(shared)
        m["x_own"] = np.ascontiguousarray(xb[own].reshape(NOWN, D))
        m["x_oth"] = np.ascontiguousarray(xb[oth].reshape(NOWN, D))
        m["x_halo"] = np.ascontiguousarray(halo.reshape(512, D))
        m["pos_own"] = np.ascontiguousarray(pb[own].reshape(NOWN))
        m["pos_oth"] = np.ascontiguousarray(pb[oth].reshape(NOWN))
        m["c"] = np.ascontiguousarray(c[b])
        m["pairmask"] = np.full((128, 128), 0.0 if p == 1 else NEG, np.float32)
        m["halomask"] = np.full((128, 1), 1.0 if p == 1 else 0.0, np.float32)
        in_maps.append(m)
    return in_maps


def kernel(**inputs):
    in_maps = _prep_inputs(inputs)
    nc = build_nc()
    res = run_bass_kernel_spmd(nc, in_maps, core_ids=list(range(8)))
    outf = np.zeros((4, 32, 128, D), np.float32)
    for core in range(8):
        b, p = core // 2, core % 2
        o = np.asarray(res.results[core]["out"]).reshape(16, 128, D)
        for i in range(16):
            outf[b, 2 * i + p] = o[i]
    return outf.reshape(4, 4096, D)
```

```python
import contextlib
import math
import numpy as np
import concourse.bass as bass
import concourse.mybir as mybir
from concourse.bass_utils import run_bass_kernel_spmd

F32 = mybir.dt.float32
BF = mybir.dt.bfloat16
I32 = mybir.dt.int32
AF = mybir.ActivationFunctionType
ALU = mybir.AluOpType

D = 1024
KC = 8
NOWN = 2048
EPS = 1e-6
NEG = -30000.0
SCALE = 1.0 / math.sqrt(192.0)
TWO_PI = 2.0 * math.pi
C1 = 6.28125
C2 = TWO_PI - 6.28125

ENGS = ("pe", "act", "dve", "pool", "sp")
NDMA = 12


class _Rec:
    def __init__(self):
        self.call = None

    def __getattr__(self, name):
        def f(*a, **k):
            self.call = (name, a, k)
            return self
        return f


def _bind(fn):
    if fn is None:
        return None
    r = _Rec()
    fn(r)
    name, a, k = r.call
    return lambda eng: getattr(eng, name)(*a, **k)


class Sched:
    def __init__(self):
        self.q = {e: [] for e in ENGS}
        self.cnt = {e: 0 for e in ENGS}
        self.waited = {e: {} for e in ENGS}
        self.state = {}
        self.dma_tot = [0] * NDMA
        self.dma_rr = 0
        self.all_dma_tokens = {}

    def _entries(self, buf, key, create):
        d = self.state.setdefault(id(buf), {})
        if key is None:
            if create and None not in d:
                d[None] = {"w": None, "r": {}}
            return list(d.values()) if not create else list(d.values())
        out = []
        if key not in d and create:
            d[key] = {"w": None, "r": {}}
        if key in d:
            out.append(d[key])
        if None in d:
            out.append(d[None])
        return out

    def _deps(self, reads, writes):
        deps = {}

        def add(tok):
            if tok is None:
                return
            s, v = tok
            if deps.get(s, 0) < v:
                deps[s] = v

        for (b, k) in reads:
            for st in self._entries(b, k, False):
                add(st["w"])
        for (b, k) in writes:
            for st in self._entries(b, k, False):
                add(st["w"])
                for s, v in st["r"].items():
                    add((s, v))
        return deps

    def _commit(self, reads, writes, tok):
        for (b, k) in reads:
            d = self.state.setdefault(id(b), {})
            if k not in d:
                d[k] = {"w": None, "r": {}}
            st = d[k]
            s, v = tok
            if st["r"].get(s, 0) < v:
                st["r"][s] = v
        for (b, k) in writes:
            d = self.state.setdefault(id(b), {})
            if k is None:
                d.clear()
            d[k] = {"w": tok, "r": {}}

    def _norm(self, lst):
        out = []
        for x in lst:
            if isinstance(x, tuple):
                out.append(x)
            else:
                out.append((x, None))
        return out

    def op(self, eng, fn, reads=(), writes=(), inc=True):
        fn = _bind(fn)
        reads = self._norm(reads)
        writes = self._norm(writes)
        deps = self._deps(reads, writes)
        waits = []
        for s, v in deps.items():
            if s == eng and eng == "pe":
                continue
            if self.waited[eng].get(s, 0) >= v:
                continue
            self.waited[eng][s] = v
            waits.append((s, v))
        if inc:
            self.cnt[eng] += 1
            tok = (eng, self.cnt[eng])
            self.q[eng].append((waits, fn, (eng, 1)))
        else:
            tok = (eng, self.cnt[eng] + 1)
            self.q[eng].append((waits, fn, None))
        self._commit(reads, writes, tok)
        return tok

    def dma(self, eng, fn, reads=(), writes=()):
        fn = _bind(fn)
        reads = self._norm(reads)
        writes = self._norm(writes)
        j = self.dma_rr
        self.dma_rr = (self.dma_rr + 1) % NDMA
        sem = "d%d" % j
        deps = self._deps(reads, writes)
        if self.dma_tot[j] > 0:
            if deps.get(sem, 0) < self.dma_tot[j]:
                deps[sem] = self.dma_tot[j]
        waits = []
        for s, v in deps.items():
            if self.waited[eng].get(s, 0) >= v:
                continue
            self.waited[eng][s] = v
            waits.append((s, v))
        self.dma_tot[j] += 16
        tok = (sem, self.dma_tot[j])
        self.q[eng].append((waits, fn, (sem, 16)))
        self._commit(reads, writes, tok)
        return tok

    def barrier(self):
        toks = [(e, self.cnt[e]) for e in ENGS if self.cnt[e] > 0]
        toks += [("d%d" % j, self.dma_tot[j]) for j in range(NDMA) if self.dma_tot[j] > 0]
        for e in ENGS:
            self.wait_all(e, [t for t in toks if t[0] != e])
        self.state = {}

    def wait_all(self, eng, toks):
        waits = []
        for (s, v) in toks:
            if self.waited[eng].get(s, 0) >= v:
                continue
            self.waited[eng][s] = v
            waits.append((s, v))
        self.q[eng].append((waits, None, None))


def build_nc(debug=None, stop=None):
    nc = bass.Bass("TRN2", target_bir_lowering=False)
    S = Sched()

    def din(name, shape, dt=F32):
        return nc.dram_tensor(name, list(shape), dt, kind="ExternalInput").ap()

    x_own = din("x_own", [NOWN, D])
    x_oth = din("x_oth", [NOWN, D])
    x_halo = din("x_halo", [512, D])
    pos_own = nc.dram_tensor("pos_own", [NOWN], I32, kind="ExternalInput")
    pos_oth = nc.dram_tensor("pos_oth", [NOWN], I32, kind="ExternalInput")
    c_in = din("c", [D])
    pairmask_in = din("pairmask", [128, 128])
    trimask_in = din("trimask", [128, 128])
    ident_in = din("ident", [128, 128])
    halomask_in = din("halomask", [128, 1])
    invf_in = din("invf", [64, 1])
    sgn_in = din("sgn", [64, 1])
    w_ada = din("w_ada", [D, 6 * D])
    b_ada = din("b_ada", [6 * D])
    g_pre_mix = din("g_pre_mix", [D])
    g_post_mix = din("g_post_mix", [D])
    g_pre_mlp = din("g_pre_mlp", [D])
    g_post_mlp = din("g_post_mlp", [D])
    w_in = din("w_in", [D, 4800])
    w_kr_sw = din("w_kr_sw", [D, 64])
    conv_w = din("conv_w", [31, D])
    conv_b = din("conv_b", [D])
    conv_norm_g = din("conv_norm_g", [D])
    conv_norm_b = din("conv_norm_b", [D])
    w_conv_out = din("w_conv_out", [D, D])
    q_norm_g = din("q_norm_g", [384])
    w_uq = din("w_uq", [384, 1536])
    w_uq_sw = din("w_uq_sw", [384, 512])
    kv_norm_g = din("kv_norm_g", [256])
    w_ukv = din("w_ukv", [256, 2048])
    w_attn_out = din("w_attn_out", [D, D])
    w_out = din("w_out", [D, D])
    w_mlp_in = din("w_mlp_in", [D, 4 * D])
    w_mlp_out = din("w_mlp_out", [4 * D, D])
    out = nc.dram_tensor("out", [NOWN, D], F32, kind="ExternalOutput").ap()
    wq = nc.dram_tensor("wq", [60, 128, 2048], BF, kind="Internal").ap()
    wq_key = object()
    dbg = None

    es = contextlib.ExitStack()
    with es:
        def sb(name, shape, dt=F32):
            return es.enter_context(nc.sbuf_tensor("s_" + name, list(shape), dt))

        sems = {}
        for e in ENGS:
            sems[e] = es.enter_context(nc.semaphore("sem_" + e))
        for j in range(NDMA):
            sems["d%d" % j] = es.enter_context(nc.semaphore("sem_d%d" % j))

        PD = [es.enter_context(nc.psum_tensor("pd%d" % i, [128, 1024], F32)) for i in range(4)]

        def bank(i):
            t = PD[i // 2]
            h = i % 2
            return t, h

        def bk(i):
            t, h = bank(i)
            return (t, h)

        def bap(i, c0=0, c1=512):
            t, h = bank(i)
            return t[:, h * 512 + c0: h * 512 + c1]

        def bap_bf(i):
            t, h = bank(i)
            return t.bitcast(BF)[:, h * 1024:(h + 1) * 1024]

        ident_f = sb("ident_f", [128, 128])
        ident_b = sb("ident_b", [128, 128], BF)
        ones_b = sb("ones_b", [128, 128], BF)
        tri_b = sb("tri_b", [128, 128], BF)
        pair_b = sb("pair_b", [128, 128], BF)
        halom = sb("halom", [128, 1])
        invf = sb("invf", [64, 1])
        sgn = sb("sgn", [64, 1])
        modT = sb("modT", [128, 48])
        vecs = sb("vecs", [128, 128])
        cwT = sb("cwT", [128, 8, 31])
        V_GPRE, V_GPOST, V_GPRE2, V_GPOST2 = 0, 8, 16, 24
        V_CB, V_CG, V_CNB = 32, 40, 48
        V_QG, V_KVG = 56, 59
        der = sb("der", [128, 48])
        NST = 2
        wst = [sb("wst%d" % i, [128, 8, 256]) for i in range(NST)]
        wbf = [sb("wbf%d" % i, [128, 8, 256], BF) for i in range(NST)]
        wrr = [0]
        xblk = [sb("xblk%d" % i, [128, D]) for i in range(2)]
        xnb = [sb("xnb%d" % i, [128, D], BF) for i in range(2)]
        small = [sb("small%d" % i, [128, 4]) for i in range(4)]
        small_rr = [0]
        tmpf = [sb("tmpf%d" % i, [128, 512]) for i in range(4)]
        tmpf_rr = [0]
        tmpb = [sb("tmpb%d" % i, [128, 512], BF) for i in range(6)]
        tmpb_rr = [0]
        rstd_t = [sb("rstd%d" % i, [128, 512]) for i in range(2)]
        rstd_rr = [0]

        def nxt(lst, rr):
            t = lst[rr[0] % len(lst)]
            rr[0] += 1
            return t

        def A(fn, **kw):
            return S.op("act", fn, **kw)

        def V(fn, **kw):
            return S.op("dve", fn, **kw)

        def G(fn, **kw):
            return S.op("pool", fn, **kw)

        def P(fn, **kw):
            return S.op("pe", fn, **kw)

        def dma_in(dst_ap, src_ap, dst_buf, key=None, eng="sp", nonc=False):
            def f(e, dst_ap=dst_ap, src_ap=src_ap):
                if nonc:
                    return e.dma_start(out=dst_ap, in_=src_ap, allow_slow_non_contiguous=True)
                return e.dma_start(out=dst_ap, in_=src_ap)
            return S.dma(eng, f, writes=[(dst_buf, key)])

        def mm(out_ap, lhsT, rhs, start, stop, reads, wkey, last):
            def f(e):
                return e.matmul(out_ap, lhsT, rhs, start=start, stop=stop)
            return S.op("pe", f, reads=reads, writes=[wkey], inc=last)

        def load_w(src, rows, c0, ncols, r0=0, cast=None):
            i = wrr[0] % NST
            wrr[0] += 1
            kc = rows // 128
            st, wb = wst[i], wbf[i]
            src_ap = src[r0:r0 + rows, c0:c0 + ncols].rearrange("(k p) c -> p k c", p=128)
            dma_in(st[:, 0:kc, 0:ncols], src_ap, st)
            if cast == "pool" or (cast is None and wrr[0] % 2 == 0):
                G(lambda e: e.tensor_copy(wb[:, 0:kc, 0:ncols], st[:, 0:kc, 0:ncols]), reads=[st], writes=[wb])
            else:
                V(lambda e: e.tensor_copy(wb[:, 0:kc, 0:ncols], st[:, 0:kc, 0:ncols]), reads=[st], writes=[wb])
            return wb

        def _unused():
            pass

        out_toks = []

        def run_phases():
            dma_in(ident_f[:], ident_in, ident_f)
            dma_in(halom[:], halomask_in, halom)
            dma_in(invf[:], invf_in, invf)
            dma_in(sgn[:], sgn_in, sgn)
            t0 = tmpf[0]
            t1 = tmpf[1]
            dma_in(t0[:, 0:128], trimask_in, t0)
            dma_in(t1[:, 0:128], pairmask_in, t1)
            V(lambda e: e.tensor_copy(ident_b[:], ident_f[:]), reads=[ident_f], writes=[ident_b])
            V(lambda e: e.memset(ones_b[:], 1.0), writes=[ones_b])
            V(lambda e: e.tensor_copy(tri_b[:], t0[:, 0:128]), reads=[t0], writes=[tri_b])
            V(lambda e: e.tensor_copy(pair_b[:], t1[:, 0:128]), reads=[t1], writes=[pair_b])
            tmpf_rr[0] = 2
            if stop == -1:
                return
            stg = tmpf[2]
            V(lambda e: e.memset(stg[:, 0:128], 0.0), writes=[stg])
            for col, src, n in ((V_GPRE, g_pre_mix, D), (V_GPOST, g_post_mix, D), (V_GPRE2, g_pre_mlp, D),
                                (V_GPOST2, g_post_mlp, D), (V_CB, conv_b, D), (V_CG, conv_norm_g, D),
                                (V_CNB, conv_norm_b, D), (V_QG, q_norm_g, 384), (V_KVG, kv_norm_g, 256)):
                dma_in(stg[col:col + n // 128, 0:128], src.rearrange("(k p) -> k p", p=128), stg)
            dma_in(stg[64:112, 0:128], b_ada.rearrange("(k p) -> k p", p=128), stg)
            dma_in(stg[112:120, 0:128], c_in.rearrange("(k p) -> k p", p=128), stg)
            P(lambda e: e.transpose(bap(1, 0, 120), stg[0:120, 0:128], ident_f[0:120, 0:120]),
              reads=[stg, ident_f], writes=[bk(1)])
            V(lambda e: e.tensor_copy(vecs[:, 0:120], bap(1, 0, 120)), reads=[bk(1)], writes=[vecs])
            if stop == -2:
                return
            badaT = vecs[:, 64:112]
            cT = vecs[:, 112:120]
            cwn = xblk[0]
            dma_in(cwn[0:31, :], conv_w, cwn)
            for c in range(8):
                P(lambda e, c=c: e.transpose(bap(2, c * 32, c * 32 + 31), cwn[0:31, c * 128:(c + 1) * 128], ident_f[0:31, 0:31]),
                  reads=[cwn, ident_f], writes=[bk(2)])
            V(lambda e: e.tensor_copy(cwT[:], bap(2, 0, 256).rearrange("p (c k) -> p c k", k=32)[:, :, 0:31]),
              reads=[bk(2)], writes=[cwT])
            if stop == -3:
                return
            scb = sb("scb", [128, 8])
            A(lambda e: e.activation(scb[:], cT, AF.Silu), reads=[vecs], writes=[scb])
            if stop == -4:
                return
            def load_w32(src, rows, c0, ncols):
                i = wrr[0] % NST
                wrr[0] += 1
                st = wst[i]
                dma_in(st[:, 0:rows // 128, 0:ncols], src[0:rows, c0:c0 + ncols].rearrange("(k p) c -> p k c", p=128), st)
                return st

            def mod_part(p0, p1, MODB, cast=None):
                for pc in range(p0, p1):
                    wb = load_w32(w_ada, D, pc * 256, 256)
                    for jj in range(2):
                        j = pc * 2 + jj
                        for k in range(8):
                            mm(bap(MODB, j, j + 1), wb[:, k, jj * 128:(jj + 1) * 128], scb[:, k:k + 1],
                               k == 0, k == 7, [wb, scb], bk(MODB), k == 7)
                V(lambda e: e.tensor_tensor(modT[:, p0 * 2:p1 * 2], bap(MODB, p0 * 2, p1 * 2), vecs[:, 64 + p0 * 2:64 + p1 * 2], ALU.add),
                  reads=[bk(MODB), vecs], writes=[(modT, p0)])

            def mod_late():
                mod_part(8, 24, 7, cast="pool")
                V(lambda e: e.tensor_tensor(der[:, 16:24], modT[:, 16:24], vecs[:, V_GPOST:V_GPOST + 8], ALU.mult),
                  reads=[modT, vecs], writes=[(der, 16)])
                V(lambda e: e.scalar_tensor_tensor(der[:, 24:32], modT[:, 32:40], 1.0, vecs[:, V_GPRE2:V_GPRE2 + 8], ALU.add, ALU.mult),
                  reads=[modT, vecs], writes=[(der, 24)])
                V(lambda e: e.tensor_copy(der[:, 32:40], modT[:, 24:32]), reads=[modT], writes=[(der, 32)])
                V(lambda e: e.tensor_tensor(der[:, 40:48], modT[:, 40:48], vecs[:, V_GPOST2:V_GPOST2 + 8], ALU.mult),
                  reads=[modT, vecs], writes=[(der, 40)])
            DER_ALL = [(der, 0), (der, 8), (der, 16), (der, 24), (der, 32), (der, 40)]

            TPB = [0, 1]
            tp_rr = [0]
            xb_rr = [0]

            def xb_prep(src_rows_ap, nrows):
                i = xb_rr[0] % 2
                xb_rr[0] += 1
                xb = xblk[i]
                xn = xnb[i]
                dma_in(xb[0:nrows, :], src_rows_ap, xb)
                sm = nxt(small, small_rr)
                A(lambda e: e.activation(xn[0:nrows, :], xb[0:nrows, :], AF.Square, accum_out=sm[0:nrows, 0:1]),
                  reads=[xb], writes=[xn, (sm, 0)])
                A(lambda e: e.activation(sm[0:nrows, 3:4], sm[0:nrows, 0:1], AF.Sqrt, bias=EPS, scale=1.0 / D),
                  reads=[(sm, 0)], writes=[(sm, 3)])
                V(lambda e: e.reciprocal(sm[0:nrows, 2:3], sm[0:nrows, 3:4]), reads=[(sm, 3)], writes=[(sm, 2)])
                V(lambda e: e.tensor_scalar(xn[0:nrows, :], xb[0:nrows, :], sm[0:nrows, 2:3], None, ALU.mult),
                  reads=[xb, (sm, 2)], writes=[xn])
                return (xn, nrows)

            def xb_trans(hd, hT, col0, gsc, shc):
                xn, nrows = hd
                b = TPB[tp_rr[0] % 2]
                tp_rr[0] += 1
                tpv = bap_bf(b)
                for k in range(8):
                    P(lambda e, k=k: e.transpose(tpv[:, k * 128:k * 128 + nrows], xn[0:nrows, k * 128:(k + 1) * 128],
                                                  ident_b[0:nrows, 0:nrows]),
                      reads=[xn, ident_b], writes=[bk(b)], inc=(k == 7))
                for k in range(8):
                    if b == TPB[0]:
                        V(lambda e, k=k: e.tensor_scalar(hT[:, k, col0:col0 + nrows], tpv[:, k * 128:k * 128 + nrows],
                                                         der[:, gsc + k:gsc + k + 1], der[:, shc + k:shc + k + 1],
                                                         ALU.mult, ALU.add),
                          reads=[bk(b), (der, gsc), (der, shc)], writes=[(hT, k)])
                    else:
                        A(lambda e, k=k: e.activation(hT[:, k, col0:col0 + nrows], tpv[:, k * 128:k * 128 + nrows],
                                                      AF.Identity, bias=der[:, shc + k:shc + k + 1],
                                                      scale=der[:, gsc + k:gsc + k + 1]),
                          reads=[bk(b), (der, gsc), (der, shc)], writes=[(hT, k)])

            def x_block_to_hT(src_rows_ap, nrows, hT, col0, gsc, shc):
                xb_trans(xb_prep(src_rows_ap, nrows), hT, col0, gsc, shc)

            def rstd_from_ps(ps_bank, nfeat, ncols=512):
                r = nxt(rstd_t, rstd_rr)
                jt = nxt(tmpf, tmpf_rr)
                A(lambda e: e.activation(jt[:, 0:ncols], bap(ps_bank, 0, ncols), AF.Sqrt, bias=EPS, scale=1.0 / nfeat),
                  reads=[bk(ps_bank)], writes=[jt])
                V(lambda e: e.reciprocal(r[:, 0:ncols], jt[:, 0:ncols]), reads=[jt], writes=[r])
                return r

            hd_first = xb_prep(x_own[0:128, :], 128)
            mod_part(0, 8, 0)
            if stop == -5:
                return
            V(lambda e: e.scalar_tensor_tensor(der[:, 0:8], modT[:, 8:16], 1.0, vecs[:, V_GPRE:V_GPRE + 8], ALU.add, ALU.mult),
              reads=[modT, vecs], writes=[(der, 0)])
            V(lambda e: e.tensor_copy(der[:, 8:16], modT[:, 0:8]), reads=[modT], writes=[(der, 8)])

            if stop == 0:
                return
            oT = sb("oT", [128, 8, NOWN], BF)
            ph12 = es.enter_context(contextlib.ExitStack())
            ph1 = es.enter_context(contextlib.ExitStack())

            def sb12(name, shape, dt=F32):
                return ph12.enter_context(nc.sbuf_tensor("s_" + name, list(shape), dt))

            def sb1(name, shape, dt=F32):
                return ph1.enter_context(nc.sbuf_tensor("s_" + name, list(shape), dt))

            kvn = [sb12("kvn_own", [128, 2, NOWN], BF), sb12("kvn_oth", [128, 2, NOWN], BF)]
            krT = [sb12("kr_own", [128, NOWN], BF), sb12("kr_oth", [128, NOWN], BF)]
            for kr_ in krT:
                G(lambda e, kr_=kr_: e.memset(kr_[64:128, :], 0.0), writes=[kr_])
            qn = sb12("qn", [128, 3, NOWN], BF)
            CS = sb12("cs_own", [64, 2, NOWN])
            wuq = sb12("wuq", [128, 3, 2048], BF)
            hTs = [sb1("hT%d" % i, [128, 8, 640], BF) for i in range(2)]
            wlat = sb1("wlat", [128, 8, 768], BF)
            cs_tmp = sb1("cs_tmp", [64, 2, 512])
            posi = sb1("posi", [64, 512], I32)
            angs = [sb1("ang%d" % i, [64, 512]) for i in range(4)]
            ni_t = sb1("ni_t", [64, 512], I32)

            for pc, (src, c0, n, d0) in enumerate(((w_in, 2048, 256, 0), (w_in, 2304, 256, 256), (w_in, 2560, 192, 512),
                                                   (w_kr_sw, 0, 64, 704))):
                wb = load_w(src, D, c0, n)
                G(lambda e, wb=wb, n=n, d0=d0: e.tensor_copy(wlat[:, :, d0:d0 + n], wb[:, :, 0:n]),
                  reads=[wb], writes=[(wlat, pc)])
            WL = [(wlat, i) for i in range(4)]
            if stop == 10:
                return

            def rope_tables(pos_t, c0, dst, dcol):
                src = bass.AP(pos_t, c0, [[0, 64], [1, 512]])
                dma_in(posi[:], src, posi)
                a0, a1, a2, a3 = angs
                V(lambda e: e.tensor_copy(a0[:], posi[:]), reads=[posi], writes=[a0])
                V(lambda e: e.tensor_scalar(a0[:], a0[:], invf[:, 0:1], None, ALU.mult), reads=[a0, invf], writes=[a0])
                V(lambda e: e.tensor_scalar(a1[:], a0[:], 1.0 / TWO_PI, None, ALU.mult), reads=[a0], writes=[a1])
                V(lambda e: e.tensor_copy(ni_t[:], a1[:]), reads=[a1], writes=[ni_t])
                V(lambda e: e.tensor_copy(a1[:], ni_t[:]), reads=[ni_t], writes=[a1])
                V(lambda e: e.scalar_tensor_tensor(a2[:], a1[:], -C1, a0[:], ALU.mult, ALU.add), reads=[a1, a0], writes=[a2])
                V(lambda e: e.scalar_tensor_tensor(a2[:], a1[:], -C2, a2[:], ALU.mult, ALU.add), reads=[a1, a2], writes=[a2])
                V(lambda e: e.tensor_scalar(a3[:], a2[:], math.pi, -TWO_PI, ALU.is_gt, ALU.mult), reads=[a2], writes=[a3])
                V(lambda e: e.tensor_tensor(a2[:], a2[:], a3[:], ALU.add), reads=[a2, a3], writes=[a2])
                V(lambda e: e.tensor_scalar(a3[:], a2[:], -math.pi, TWO_PI, ALU.is_lt, ALU.mult), reads=[a2], writes=[a3])
                V(lambda e: e.tensor_tensor(a2[:], a2[:], a3[:], ALU.add), reads=[a2, a3], writes=[a2])
                V(lambda e: e.tensor_scalar(a1[:], a2[:], math.pi / 2, None, ALU.add), reads=[a2], writes=[a1])
                V(lambda e: e.tensor_scalar(a3[:], a1[:], math.pi, -TWO_PI, ALU.is_gt, ALU.mult), reads=[a1], writes=[a3])
                V(lambda e: e.tensor_tensor(a1[:], a1[:], a3[:], ALU.add), reads=[a1, a3], writes=[a1])
                V(lambda e: e.tensor_scalar(a1[:], a1[:], math.pi, -math.pi, ALU.min, ALU.max), reads=[a1], writes=[a1])
                V(lambda e: e.tensor_scalar(a2[:], a2[:], math.pi, -math.pi, ALU.min, ALU.max), reads=[a2], writes=[a2])
                A(lambda e: e.activation(dst[:, 0, dcol:dcol + 512], a1[:], AF.Sin), reads=[a1], writes=[(dst, dcol)])
                A(lambda e: e.activation(dst[:, 1, dcol:dcol + 512], a2[:], AF.Sin, scale=sgn[:, 0:1]),
                  reads=[a2, sgn], writes=[(dst, dcol)])

            def prep1(j_):
                i_, b_ = j_ // 4, j_ % 4
                grp_, t_ = i_ // 4, i_ % 4
                xsrc = x_own if grp_ == 0 else x_oth
                r0 = t_ * 512 + b_ * 128
                return xb_prep(xsrc[r0:r0 + 128, :], 128)

            hd1 = [hd_first]
            for grp in range(2):
                pos_t = pos_own if grp == 0 else pos_oth
                for t in range(4):
                    hT = hTs[(grp * 4 + t) % 2]
                    for b in range(4):
                        j_ = (grp * 4 + t) * 4 + b
                        nh = prep1(j_ + 1) if j_ + 1 < 32 else None
                        xb_trans(hd1[0], hT, b * 128, 0, 8)
                        hd1[0] = nh
                    if grp == 0:
                        rope_tables(pos_t, t * 512, CS, t * 512)
                        cs, cc = CS, t * 512
                    else:
                        rope_tables(pos_t, t * 512, cs_tmp, 0)
                        cs, cc = cs_tmp, 0
                    if stop == 12:
                        return
                    hk = [(hT, k) for k in range(8)]
                    for m in range(2):
                        for k in range(8):
                            mm(bap(2 + m), wlat[:, k, 384 + m * 128:384 + (m + 1) * 128], hT[:, k, 0:512],
                               k == 0, k == 7, hk + WL, bk(2 + m), k == 7)
                    for m in range(2):
                        for k in range(8):
                            mm(bap(5 + m)[0:64, :], wlat[:, k, 640 + m * 64:640 + (m + 1) * 64], hT[:, k, 0:512],
                               k == 0, k == 7, hk + WL, bk(5 + m), k == 7)
                    sq = []
                    for m in range(2):
                        s_ = nxt(tmpb, tmpb_rr)
                        A(lambda e, m=m, s_=s_: e.activation(s_[:], bap(2 + m), AF.Square), reads=[bk(2 + m)], writes=[s_])
                        sq.append(s_)
                    for m in range(2):
                        mm(bap(4), ones_b[:], sq[m][:], m == 0, m == 1, [ones_b, sq[m]], bk(4), True)
                    r = rstd_from_ps(4, 256)
                    for m in range(2):
                        V(lambda e, m=m, r=r: e.scalar_tensor_tensor(kvn[grp][:, m, t * 512:(t + 1) * 512], bap(2 + m),
                                                                     vecs[:, V_KVG + m:V_KVG + m + 1], r[:], ALU.mult, ALU.mult),
                          reads=[bk(2 + m), r, vecs], writes=[(kvn[grp], t)])
                    ta = nxt(tmpf, tmpf_rr)
                    tb_ = nxt(tmpf, tmpf_rr)
                    V(lambda e, ta=ta, cs=cs, cc=cc: e.tensor_tensor(ta[0:64, :], bap(5)[0:64, :], cs[:, 0, cc:cc + 512], ALU.mult),
                      reads=[bk(5), (cs, cc)], writes=[ta])
                    V(lambda e, tb_=tb_, cs=cs, cc=cc: e.tensor_tensor(tb_[0:64, :], bap(6)[0:64, :], cs[:, 1, cc:cc + 512], ALU.mult),
                      reads=[bk(6), (cs, cc)], writes=[tb_])
                    V(lambda e, ta=ta, tb_=tb_: e.tensor_tensor(krT[grp][0:64, t * 512:(t + 1) * 512], ta[0:64, :], tb_[0:64, :], ALU.add),
                      reads=[ta, tb_], writes=[(krT[grp], t)])
                    if stop == 13:
                        return
                    if grp == 0:
                        QB = [2, 3, 7]
                        for m in range(3):
                            for k in range(8):
                                mm(bap(QB[m]), wlat[:, k, m * 128:(m + 1) * 128], hT[:, k, 0:512],
                                   k == 0, k == 7, hk + WL, bk(QB[m]), k == 7)
                        sq = []
                        for m in range(3):
                            s_ = nxt(tmpb, tmpb_rr)
                            A(lambda e, m=m, s_=s_: e.activation(s_[:], bap(QB[m]), AF.Square), reads=[bk(QB[m])], writes=[s_])
                            sq.append(s_)
                        for m in range(3):
                            mm(bap(4), ones_b[:], sq[m][:], m == 0, m == 2, [ones_b, sq[m]], bk(4), True)
                        r = rstd_from_ps(4, 384)
                        for m in range(3):
                            V(lambda e, m=m, r=r: e.scalar_tensor_tensor(qn[:, m, t * 512:(t + 1) * 512], bap(QB[m]),
                                                                         vecs[:, V_QG + m:V_QG + m + 1], r[:], ALU.mult, ALU.mult),
                              reads=[bk(QB[m]), r, vecs], writes=[(qn, t)])
                    if stop == 14:
                        return

            def load_wbig(src, rows, c0, ncols):
                i = wrr[0] % NST
                wrr[0] += 1
                kc = rows // 128
                st, wb = wst[i], wbf[i]
                stv = st[:].rearrange("p k c -> p (k c)")[:, 0:kc * ncols].rearrange("p (k c) -> p k c", c=ncols)
                wbv = wb[:].rearrange("p k c -> p (k c)")[:, 0:kc * ncols].rearrange("p (k c) -> p k c", c=ncols)
                dma_in(stv, src[0:rows, c0:c0 + ncols].rearrange("(k p) c -> p k c", p=128), st)
                G(lambda e: e.tensor_copy(wbv, stv), reads=[st], writes=[wb])
                return wb, wbv

            for pc in range(3):
                wb, wbv = load_wbig(w_uq, 384, pc * 512, 512)
                G(lambda e: e.tensor_copy(wuq[:, :, pc * 512:(pc + 1) * 512], wbv), reads=[wb], writes=[(wuq, pc)])
            wb, wbv = load_wbig(w_uq_sw, 384, 0, 512)
            G(lambda e: e.tensor_copy(wuq[:, :, 1536:2048], wbv), reads=[wb], writes=[(wuq, 3)])
            S.barrier()
            ph1.close()
            if stop == 1:
                ph12.close()
                return

            wukv = sb12("wukv", [128, 2, 2048], BF)
            for pc in range(2):
                wb, wbv = load_wbig(w_ukv, 256, pc * 1024, 1024)
                G(lambda e: e.tensor_copy(wukv[:, :, pc * 1024:(pc + 1) * 1024], wbv), reads=[wb], writes=[(wukv, pc)])
            KhT = [[sb12("kh%d_%d" % (i, g), [128, NOWN], BF) for g in range(2)] for i in range(1)]
            Vh = [[sb12("vh%d_%d" % (i, g), [128, 16, 128], BF) for g in range(2)] for i in range(1)]
            Qh = [sb12("qh%d" % i, [128, NOWN], BF) for i in range(1)]
            Qr = [sb12("qr%d" % i, [128, NOWN], BF) for i in range(1)]
            G(lambda e: e.memset(Qr[0][64:128, :], 0.0), writes=[Qr[0]])
            Pt = [sb12("pt%d" % i, [128, 512], BF) for i in range(4)]
            pt_rr = [0]
            SB_ = [0, 1, 2]
            s_rr = [0]
            OB = [3, 5]
            LB = [4, 6]
            HBS = [7, 3, 4]
            hb_rr = [0]

            def nhb():
                b_ = HBS[hb_rr[0] % 3]
                hb_rr[0] += 1
                return b_
            evac_rr = [0]

            def evac_copy(dst_ap, src_bank_ap, reads, writes):
                if evac_rr[0] % 2 == 0:
                    V(lambda e: e.tensor_copy(dst_ap, src_bank_ap), reads=reads, writes=writes)
                else:
                    A(lambda e: e.activation(dst_ap, src_bank_ap, AF.Copy), reads=reads, writes=writes)
                evac_rr[0] += 1

            def build_head(h):
                i = 0
                for t in range(4):
                    HB = nhb()
                    for k in range(3):
                        mm(bap(HB)[0:64, :], wuq[:, k, h * 192 + 128:h * 192 + 192], qn[:, k, t * 512:(t + 1) * 512],
                           k == 0, k == 2, [wuq, qn], bk(HB), k == 2)
                    ta = nxt(tmpf, tmpf_rr)
                    V(lambda e, ta=ta, t=t: e.tensor_tensor(ta[0:64, :], bap(HB)[0:64, :], CS[:, 0, t * 512:(t + 1) * 512], ALU.mult),
                      reads=[bk(HB), CS], writes=[ta])
                    HB = nhb()
                    for k in range(3):
                        mm(bap(HB)[0:64, :], wuq[:, k, 1536 + h * 64:1536 + (h + 1) * 64], qn[:, k, t * 512:(t + 1) * 512],
                           k == 0, k == 2, [wuq, qn], bk(HB), k == 2)
                    tb_ = nxt(tmpf, tmpf_rr)
                    V(lambda e, tb_=tb_, t=t: e.tensor_tensor(tb_[0:64, :], bap(HB)[0:64, :], CS[:, 1, t * 512:(t + 1) * 512], ALU.mult),
                      reads=[bk(HB), CS], writes=[tb_])
                    V(lambda e, ta=ta, tb_=tb_, t=t: e.tensor_tensor(Qr[i][0:64, t * 512:(t + 1) * 512], ta[0:64, :], tb_[0:64, :], ALU.add),
                      reads=[ta, tb_], writes=[(Qr[i], t)])
                for t in range(4):
                    HB = nhb()
                    for k in range(3):
                        mm(bap(HB), wuq[:, k, h * 192:h * 192 + 128], qn[:, k, t * 512:(t + 1) * 512],
                           k == 0, k == 2, [wuq, qn], bk(HB), k == 2)
                    evac_copy(Qh[i][:, t * 512:(t + 1) * 512], bap(HB), [bk(HB)], [(Qh[i], t)])
                for grp in range(2):
                    for t in range(4):
                        HB = nhb()
                        for k in range(2):
                            mm(bap(HB), wukv[:, k, h * 256:h * 256 + 128], kvn[grp][:, k, t * 512:(t + 1) * 512],
                               k == 0, k == 1, [wukv, kvn[grp]], bk(HB), k == 1)
                        evac_copy(KhT[i][grp][:, t * 512:(t + 1) * 512], bap(HB), [bk(HB)], [(KhT[i][grp], t)])
                    for t in range(4):
                        HB = nhb()
                        for b in range(4):
                            blk = t * 4 + b
                            for k in range(2):
                                mm(bap(HB, b * 128, (b + 1) * 128), kvn[grp][:, k, blk * 128:(blk + 1) * 128],
                                   wukv[:, k, h * 256 + 128:h * 256 + 256],
                                   k == 0, k == 1, [wukv, kvn[grp]], bk(HB), (k == 1 and b == 3))
                        evac_copy(Vh[i][grp][:, t * 4:(t + 1) * 4, :], bap(HB).rearrange("p (b d) -> p b d", d=128),
                                  [bk(HB)], [(Vh[i][grp], t)])

            def attend_head(h):
                i = 0
                for g in range(4):
                    ob = OB[g % 2]
                    lb = LB[g % 2]
                    visits = [(J, grp) for J in range(4 * g + 4) for grp in range(2)]
                    pend = []

                    def do_pv(v, first, last):
                        J, grp, c0, pt = v
                        mm(bap(ob, c0, 512), Vh[i][grp][:, J, :], pt[:, c0:512], first, last,
                           [Vh[i][grp], pt], bk(ob), True)
                        mm(bap(lb, c0, 512), ones_b[:], pt[:, c0:512], first, last,
                           [ones_b, pt], bk(lb), True)

                    npv = [0]
                    for vi, (J, grp) in enumerate(visits):
                        j = J - 4 * g
                        c0 = 128 * max(j, 0)
                        sbk = SB_[s_rr[0] % 3]
                        s_rr[0] += 1
                        q0 = g * 512 + c0
                        q1 = (g + 1) * 512
                        masked = j >= 0
                        mm(bap(sbk, c0, 512), KhT[i][grp][:, J * 128:(J + 1) * 128], Qh[i][:, q0:q1],
                           True, False, [KhT[i][grp], Qh[i]], bk(sbk), False)
                        mm(bap(sbk, c0, 512), krT[grp][:, J * 128:(J + 1) * 128], Qr[i][:, q0:q1],
                           False, not masked, [krT[grp], Qr[i]], bk(sbk), not masked)
                        if masked:
                            mk = tri_b if grp == 0 else pair_b
                            mm(bap(sbk, c0, c0 + 128), ident_b[:], mk[:], False, True, [ident_b, mk], bk(sbk), True)
                        pt = nxt(Pt, pt_rr)
                        A(lambda e, pt=pt, sbk=sbk, c0=c0: e.activation(pt[:, c0:512], bap(sbk, c0, 512), AF.Exp, scale=SCALE),
                          reads=[bk(sbk)], writes=[pt])
                        pend.append((J, grp, c0, pt))
                        if len(pend) > 2:
                            v = pend.pop(0)
                            do_pv(v, npv[0] == 0, False)
                            npv[0] += 1
                    while pend:
                        v = pend.pop(0)
                        do_pv(v, npv[0] == 0, len(pend) == 0)
                        npv[0] += 1
                    rl = nxt(rstd_t, rstd_rr)
                    V(lambda e, rl=rl, lb=lb: e.reciprocal(rl[:], bap(lb)), reads=[bk(lb)], writes=[rl])
                    V(lambda e, rl=rl, ob=ob, g=g: e.tensor_tensor(oT[:, h, g * 512:(g + 1) * 512], bap(ob), rl[:], ALU.mult),
                      reads=[bk(ob), rl], writes=[(oT, (h, g))])

            prep = []
            for pc in range(4):
                prep += [(w_in, pc * 256, 0), (w_in, 1024 + pc * 256, 0)]
            for pc in range(4):
                prep += [(w_conv_out, pc * 256, 0)]
            for pc in range(4):
                prep += [(w_attn_out, pc * 256, 0), (w_in, 2752 + pc * 256, 0), (w_in, 3776 + pc * 256, 0)]
            for pc in range(4):
                prep += [(w_out, pc * 256, 0)]
            for pc in range(16):
                prep += [(w_mlp_in, pc * 256, 0)]
            for pc in range(4):
                for rr_ in range(4):
                    prep += [(w_mlp_out, pc * 256, rr_ * 1024)]
            prep_tok = {}

            def do_prep():
                prev = None
                for i, (src, c0, r0) in enumerate(prep):
                    wb = load_w(src, D, c0, 256, r0=r0, cast="pool")
                    if prev is not None:
                        pw, pi = prev
                        S.dma("sp", lambda e, pw=pw, pi=pi: e.dma_start(out=wq[pi], in_=pw[:].rearrange("p k c -> p (k c)")),
                              reads=[pw], writes=[(wq_key, pi)])
                    prev = (wb, i)
                pw, pi = prev
                S.dma("sp", lambda e: e.dma_start(out=wq[pi], in_=pw[:].rearrange("p k c -> p (k c)")),
                      reads=[pw], writes=[(wq_key, pi)])

            do_prep()
            build_head(0)
            for h in range(8):
                attend_head(h)
                if h + 1 < 8:
                    build_head(h + 1)
            mod_late()

            S.barrier()
            ph12.close()
            if stop == 2:
                return
            wpool = [wbf[0][:], wbf[1][:]]
            for i_ in range(NST):
                fl = wst[i_].bitcast(BF)[:].rearrange("p k c -> p (k c)")
                wpool += [fl[:, 0:2048].rearrange("p (k c) -> p k c", c=256), fl[:, 2048:4096].rearrange("p (k c) -> p k c", c=256)]
            wp_rr = [0]
            pidx = {(src_.tensor.name, c0_, r0_): i_ for i_, (src_, c0_, r0_) in enumerate(prep)}

            def load_wq(src, rows, c0, ncols, r0=0):
                i = pidx[(src.tensor.name, c0, r0)]
                buf = wpool[wp_rr[0] % len(wpool)]
                wp_rr[0] += 1
                S.dma("sp", lambda e: e.dma_start(out=buf.rearrange("p k c -> p (k c)"), in_=wq[i]), writes=[buf])
                return buf

            xT = sb("xT", [128, 8, 512])
            yT = sb("yT", [128, 8, 512])
            hTe = sb("hTe", [128, 8, 640], BF)
            uext = sb("uext", [128, 8, 640], BF)
            arena = sb("arena", [128, 16384], BF)
            hid = arena[:, :].rearrange("p (j t) -> p j t", t=512)
            ucv = arena.bitcast(F32)[:, 0:4096].rearrange("p (c t) -> p c t", t=512)
            diag = [arena[:, 8192 + i * 3968:8192 + (i + 1) * 3968].rearrange("p (k m) -> p k m", m=128) for i in range(2)]
            sh8 = sb("sh8", [128, 8, 512], BF)
            actT = sh8
            mT = sh8
            h2T = sh8
            yaT = sb("yaT", [128, 8, 512], BF)
            oblk = xblk
            stat_s = sb("stat_s", [128, 512])
            stat_n = sb("stat_n", [128, 512])
            sb_sig = sb("sb_sig", [128, 640])

            deferred = []

            def flush_def():
                for f_ in deferred:
                    f_()
                deferred.clear()

            def stats_accum(ps_b, src_ap, reads, idx, n, defer=False):
                s_ = nxt(tmpb, tmpb_rr)
                A(lambda e: e.activation(s_[:], src_ap, AF.Square), reads=reads, writes=[s_])
                f_ = lambda s_=s_: mm(bap(ps_b), ones_b[:], s_[:], idx == 0, idx == n - 1, [ones_b, s_], bk(ps_b), True)
                if defer:
                    deferred.append(f_)
                else:
                    f_()

            def fh_blocks(g_):
                lst = []
                for b in range(4):
                    blk = g_ * 4 + b
                    lst.append((x_halo[blk * 32:(blk + 1) * 32, :], 32, b * 160))
                    lst.append((x_own[blk * 128:(blk + 1) * 128, :], 128, b * 160 + 32))
                return lst

            def front_h(g_):
                lst = fh_blocks(g_)
                hd = xb_prep(lst[0][0], lst[0][1])
                for i_ in range(8):
                    nh = xb_prep(lst[i_ + 1][0], lst[i_ + 1][1]) if i_ + 1 < 8 else None
                    xb_trans(hd, hTe, lst[i_][2], 0, 8)
                    hd = nh

            front_h(0)
            for g in range(4):
                for b in range(4):
                    blk = g * 4 + b
                    xb = xblk[xb_rr[0] % 2]
                    xb_rr[0] += 1
                    dma_in(xb[:], x_own[blk * 128:(blk + 1) * 128, :], xb)
                    for half in range(2):
                        tb_i = 2 + half
                        for kk in range(4):
                            k = half * 4 + kk
                            P(lambda e, k=k, kk=kk, tb_i=tb_i, xb=xb: e.transpose(bap(tb_i, kk * 128, (kk + 1) * 128),
                                                                                    xb[:, k * 128:(k + 1) * 128], ident_f[:]),
                              reads=[xb, ident_f], writes=[bk(tb_i)], inc=(kk == 3))
                        evac_copy(xT[:, half * 4:(half + 1) * 4, b * 128:(b + 1) * 128],
                                  bap(tb_i).rearrange("p (k t) -> p k t", t=128), [bk(tb_i)], [(xT, (half, b))])
                hk = [(hTe, k) for k in range(8)]
                def glu_chunk(c, wa, wb2, cc):
                    for (wt, d) in ((wa, 0), (wb2, 1)):
                        for (n0, n1, hb) in ((0, 512, 0), (512, 640, 1)):
                            for k in range(8):
                                mm(PD[d][:, hb * 512:hb * 512 + (n1 - n0)], wt[:, k, cc * 128:(cc + 1) * 128],
                                   hTe[:, k, n0:n1], k == 0, k == 7, hk + [wt], (PD[d], hb), k == 7)
                    sg = sb_sig
                    A(lambda e: e.activation(sg[:, 0:640], PD[1][:, 0:640], AF.Sigmoid),
                      reads=[(PD[1], 0), (PD[1], 1)], writes=[sg])
                    V(lambda e: e.tensor_tensor(uext[:, c, :], PD[0][:, 0:640], sg[:, 0:640], ALU.mult),
                      reads=[(PD[0], 0), (PD[0], 1), sg], writes=[(uext, c)])
                    if g == 0:
                        V(lambda e: e.tensor_scalar(uext[:, c, 0:32], uext[:, c, 0:32], halom[:, 0:1], None, ALU.mult),
                          reads=[(uext, c), halom], writes=[(uext, c)])

                def conv_chunk(c):
                    dg = diag[c % 2]
                    for k in range(31):
                        G(lambda e: e.tensor_scalar(dg[:, k, :], ident_b[:], cwT[:, c, k:k + 1], 1.0, ALU.mult, ALU.mult),
                          reads=[ident_b, cwT], writes=[(arena, None) if (c == 0 and k == 0) else (arena, ("d", c % 2, k))])
                    uv = uext[:, c, :].rearrange("p (b w) -> p b w", w=160)
                    cb = 4 + (c % 2)
                    for k in range(31):
                        mm(bap(cb).rearrange("p (b w) -> p b w", w=128), dg[:, k, :], uv[:, :, 2 + k:2 + k + 128],
                           k == 0, k == 30, [(arena, ("d", c % 2, k)), (uext, c)], bk(cb), k == 30)
                    flush_def()
                    A(lambda e: e.activation(ucv[:, c, :], bap(cb), AF.Identity, bias=vecs[:, V_CB + c:V_CB + c + 1]),
                      reads=[bk(cb), vecs], writes=[(arena, ("u", c))])
                    ub_ = nxt(tmpb, tmpb_rr)
                    V(lambda e: e.tensor_copy(ub_[:], ucv[:, c, :]), reads=[(arena, ("u", c))], writes=[ub_])
                    deferred.append(lambda ub_=ub_, c=c: mm(bap(6), ones_b[:], ub_[:], c == 0, c == 7, [ones_b, ub_], bk(6), True))
                    stats_accum(7, ucv[:, c, :], [(arena, ("u", c))], c, 8, defer=True)

                for pc in range(4):
                    wa = load_wq(w_in, D, pc * 256, 256)
                    wb2 = load_wq(w_in, D, 1024 + pc * 256, 256)
                    for cc in range(2):
                        c = pc * 2 + cc
                        glu_chunk(c, wa, wb2, cc)
                        if c >= 1:
                            conv_chunk(c - 1)
                conv_chunk(7)
                flush_def()
                mean = stat_s
                nmr = stat_n
                A(lambda e: e.activation(mean[:], bap(6), AF.Copy, scale=1.0 / D), reads=[bk(6)], writes=[mean])
                jt = nxt(tmpf, tmpf_rr)
                V(lambda e, jt=jt: e.tensor_tensor(jt[:], mean[:], mean[:], ALU.mult), reads=[mean], writes=[jt])
                jt2 = nxt(tmpf, tmpf_rr)
                V(lambda e, jt=jt, jt2=jt2: e.scalar_tensor_tensor(jt2[:], bap(7), 1.0 / D, jt[:], ALU.mult, ALU.subtract),
                  reads=[bk(7), jt], writes=[jt2])
                V(lambda e, jt2=jt2: e.tensor_scalar(jt2[:], jt2[:], 0.0, None, ALU.max), reads=[jt2], writes=[jt2])
                A(lambda e, jt=jt, jt2=jt2: e.activation(jt[:], jt2[:], AF.Sqrt, bias=EPS), reads=[jt2], writes=[jt])
                rln = nxt(rstd_t, rstd_rr)
                V(lambda e, jt=jt, rln=rln: e.reciprocal(rln[:], jt[:]), reads=[jt], writes=[rln])
                V(lambda e, rln=rln: e.scalar_tensor_tensor(nmr[:], mean[:], -1.0, rln[:], ALU.mult, ALU.mult),
                  reads=[mean, rln], writes=[nmr])
                for c in range(8):
                    jt = nxt(tmpf, tmpf_rr)
                    V(lambda e, c=c, jt=jt, rln=rln: e.tensor_tensor(jt[:], ucv[:, c, :], rln[:], ALU.mult),
                      reads=[(arena, ("u", c)), rln], writes=[jt])
                    V(lambda e, jt=jt: e.tensor_tensor(jt[:], jt[:], nmr[:], ALU.add), reads=[jt, nmr], writes=[jt])
                    A(lambda e, c=c, jt=jt: e.activation(actT[:, c, :], jt[:], AF.Silu,
                                                         bias=vecs[:, V_CNB + c:V_CNB + c + 1], scale=vecs[:, V_CG + c:V_CG + c + 1]),
                      reads=[jt, vecs, vecs], writes=[(actT, c)])
                ak = [(actT, k) for k in range(8)]
                for pc in range(4):
                    wb = load_wq(w_conv_out, D, pc * 256, 256)
                    for cc in range(2):
                        m = pc * 2 + cc
                        bb = 4 + (m % 2)
                        for k in range(8):
                            mm(bap(bb), wb[:, k, cc * 128:(cc + 1) * 128], actT[:, k, :], k == 0, k == 7, ak + [wb], bk(bb), k == 7)
                        evac_copy(yaT[:, m, :], bap(bb), [bk(bb)], [(yaT, m)])
                hown = lambda k: hTe[:, k, :].rearrange("p (b w) -> p b w", w=160)[:, :, 32:160]
                for pc in range(4):
                    wao = load_wq(w_attn_out, D, pc * 256, 256)
                    for cc in range(2):
                        for hh in range(8):
                            mm(bap(2 + cc), wao[:, hh, cc * 128:(cc + 1) * 128], oT[:, hh, g * 512:(g + 1) * 512],
                               hh == 0, hh == 7, [oT, wao], bk(2 + cc), hh == 7)
                    wga = load_wq(w_in, D, 2752 + pc * 256, 256)
                    sas = []
                    for cc in range(2):
                        m = pc * 2 + cc
                        for k in range(8):
                            mm(bap(cc).rearrange("p (b w) -> p b w", w=128), wga[:, k, cc * 128:(cc + 1) * 128], hown(k),
                               k == 0, k == 7, hk + [wga], bk(cc), k == 7)
                        sa = nxt(tmpf, tmpf_rr)
                        A(lambda e, sa=sa, cc=cc: e.activation(sa[:], bap(cc), AF.Sigmoid), reads=[bk(cc)], writes=[sa])
                        V(lambda e, sa=sa, m=m: e.tensor_tensor(sa[:], sa[:], yaT[:, m, :], ALU.mult),
                          reads=[sa, (yaT, m)], writes=[sa])
                        sas.append(sa)
                    wgb = load_wq(w_in, D, 3776 + pc * 256, 256)
                    for cc in range(2):
                        m = pc * 2 + cc
                        for k in range(8):
                            mm(bap(cc).rearrange("p (b w) -> p b w", w=128), wgb[:, k, cc * 128:(cc + 1) * 128], hown(k),
                               k == 0, k == 7, hk + [wgb], bk(cc), k == 7)
                        sb2 = nxt(tmpf, tmpf_rr)
                        A(lambda e, sb2=sb2, cc=cc: e.activation(sb2[:], bap(cc), AF.Sigmoid), reads=[bk(cc)], writes=[sb2])
                        V(lambda e, sb2=sb2, cc=cc: e.tensor_tensor(sb2[:], bap(2 + cc), sb2[:], ALU.mult),
                          reads=[bk(2 + cc), sb2], writes=[sb2])
                        V(lambda e, sa=sas[cc], sb2=sb2, m=m: e.tensor_tensor(mT[:, m, :], sa[:], sb2[:], ALU.add),
                          reads=[sas[cc], sb2], writes=[(mT, m)])
                mk_ = [(mT, k) for k in range(8)]
                for pc in range(4):
                    wb = load_wq(w_out, D, pc * 256, 256)
                    for cc in range(2):
                        m = pc * 2 + cc
                        bb = 4 + (m % 2)
                        for k in range(8):
                            mm(bap(bb), wb[:, k, cc * 128:(cc + 1) * 128], mT[:, k, :], k == 0, k == 7, mk_ + [wb], bk(bb), k == 7)
                        flush_def()
                        A(lambda e, m=m, bb=bb: e.activation(yT[:, m, :], bap(bb), AF.Copy), reads=[bk(bb)], writes=[(yT, m)])
                        stats_accum(6, yT[:, m, :], [(yT, m)], m, 8, defer=True)
                flush_def()
                if debug == "m" and g == 3:
                    V(lambda e: e.tensor_copy(hTe[:, :, 0:512], mT[:]), reads=[mT], writes=[hTe])
                    V(lambda e: e.tensor_copy(uext[:, :, 0:512], yT[:]), reads=[yT], writes=[uext])
                r1 = rstd_from_ps(6, D)
                for k in range(8):
                    jt = nxt(tmpf, tmpf_rr)
                    V(lambda e, k=k, jt=jt, r1=r1: e.tensor_tensor(jt[:], yT[:, k, :], r1[:], ALU.mult),
                      reads=[(yT, k), r1], writes=[jt])
                    V(lambda e, k=k, jt=jt: e.scalar_tensor_tensor(xT[:, k, :], jt[:], der[:, 16 + k:17 + k], xT[:, k, :], ALU.mult, ALU.add),
                      reads=[jt, (der, 16), xT], writes=[xT])
                    stats_accum(7, xT[:, k, :], [xT], k, 8)
                r2 = rstd_from_ps(7, D)
                for k in range(8):
                    jt = nxt(tmpf, tmpf_rr)
                    V(lambda e, k=k, jt=jt, r2=r2: e.scalar_tensor_tensor(jt[:], xT[:, k, :], der[:, 24 + k:25 + k], r2[:], ALU.mult, ALU.mult),
                      reads=[xT, (der, 24), r2], writes=[jt])
                    A(lambda e, k=k, jt=jt: e.activation(h2T[:, k, :], jt[:], AF.Identity, bias=der[:, 32 + k:33 + k]),
                      reads=[jt, (der, 32)], writes=[(h2T, k)])
                h2k = [(h2T, k) for k in range(8)]
                nlst = fh_blocks(g + 1) if g + 1 < 4 else None
                nhd = {}
                for pc in range(16):
                    if nlst is not None and pc % 2 == 0:
                        if pc >= 2:
                            xb_trans(nhd[pc // 2 - 1], hTe, nlst[pc // 2 - 1][2], 0, 8)
                        nhd[pc // 2] = xb_prep(nlst[pc // 2][0], nlst[pc // 2][1])
                    wb = load_wq(w_mlp_in, D, pc * 256, 256)
                    for cc in range(2):
                        j = pc * 2 + cc
                        bb = 4 + (j % 4)
                        for k in range(8):
                            mm(bap(bb), wb[:, k, cc * 128:(cc + 1) * 128], h2T[:, k, :], k == 0, k == 7, h2k + [wb], bk(bb), k == 7)
                        jt = nxt(tmpf, tmpf_rr)
                        A(lambda e, jt=jt, bb=bb: e.activation(jt[:], bap(bb), AF.Relu), reads=[bk(bb)], writes=[jt])
                        V(lambda e, jt=jt, j=j: e.tensor_tensor(hid[:, j, :], jt[:], jt[:], ALU.mult), reads=[jt],
                          writes=[(arena, None) if j == 0 else (arena, ("h", j))])
                if nlst is not None:
                    xb_trans(nhd[7], hTe, nlst[7][2], 0, 8)
                for pc in range(4):
                    for cc in range(2):
                        pass
                    wbs = []
                    for rr_ in range(4):
                        wb = load_wq(w_mlp_out, D, pc * 256, 256, r0=rr_ * 1024)
                        for cc in range(2):
                            bb = 4 + cc
                            for k in range(8):
                                kk = rr_ * 8 + k
                                mm(bap(bb), wb[:, k, cc * 128:(cc + 1) * 128], hid[:, kk, :], kk == 0, kk == 31,
                                   [arena, wb], bk(bb), (k == 7))
                    flush_def()
                    for cc in range(2):
                        m = pc * 2 + cc
                        bb = 4 + cc
                        A(lambda e, m=m, bb=bb: e.activation(yT[:, m, :], bap(bb), AF.Copy), reads=[bk(bb)], writes=[(yT, m)])
                        stats_accum(6, yT[:, m, :], [(yT, m)], m, 8, defer=True)
                flush_def()
                r3 = rstd_from_ps(6, D)
                for k in range(8):
                    jt = nxt(tmpf, tmpf_rr)
                    V(lambda e, k=k, jt=jt, r3=r3: e.tensor_tensor(jt[:], yT[:, k, :], r3[:], ALU.mult),
                      reads=[(yT, k), r3], writes=[jt])
                    V(lambda e, k=k, jt=jt: e.scalar_tensor_tensor(yT[:, k, :], jt[:], der[:, 40 + k:41 + k], xT[:, k, :], ALU.mult, ALU.add),
                      reads=[jt, (der, 40), xT], writes=[(yT, k)])
                for b in range(4):
                    ob_ = oblk[b % 2]
                    for half in range(2):
                        tb_i = 2 + half
                        for kk in range(4):
                            k = half * 4 + kk
                            P(lambda e, k=k, kk=kk, tb_i=tb_i, b=b: e.transpose(bap(tb_i, kk * 128, (kk + 1) * 128),
                                                                                 yT[:, k, b * 128:(b + 1) * 128], ident_f[:]),
                              reads=[yT, ident_f], writes=[bk(tb_i)], inc=(kk == 3))
                        evac_copy(ob_[:, half * 512:(half + 1) * 512], bap(tb_i), [bk(tb_i)], [(ob_, half)])
                    blk = g * 4 + b
                    tok = S.dma("act", lambda e, ob_=ob_, blk=blk: e.dma_start(out=out[blk * 128:(blk + 1) * 128, :], in_=ob_[:]),
                                reads=[ob_])
                    out_toks.append(tok)


        run_phases()
        if stop is not None:
            out_toks.append(S.dma("act", lambda e: e.dma_start(out=out[0:128, :], in_=xblk[0][:]), reads=[xblk[0]]))
        last = {}
        for (s, v) in out_toks:
            last[s] = max(last.get(s, 0), v)
        S.wait_all("act", list(last.items()))

        with nc.Block() as block:
            def emit(engname, eng):
                for (waits, fn, inc) in S.q[engname]:
                    for (s, v) in waits:
                        eng.wait_ge(sems[s], v)
                    if fn is None:
                        continue
                    ins = fn(eng)
                    if inc is not None:
                        ins.then_inc(sems[inc[0]], inc[1])

            @block.sync
            def _(e):
                emit("sp", e)

            @block.tensor
            def _(e):
                emit("pe", e)

            @block.scalar
            def _(e):
                emit("act", e)

            @block.vector
            def _(e):
                emit("dve", e)

            @block.gpsimd
            def _(e):
                emit("pool", e)
    return nc


def _prep_inputs(inputs):
    x = np.asarray(inputs["x"], np.float32)
    pos = np.asarray(inputs["positions"], np.int32)
    c = np.asarray(inputs["c"], np.float32)
    w_in = np.ascontiguousarray(np.asarray(inputs["w_in"], np.float32)[0])
    w_uq = np.ascontiguousarray(np.asarray(inputs["w_uq"], np.float32)[0])
    kr = w_in[:, 2688:2752]
    w_kr_sw = np.ascontiguousarray(np.concatenate([kr[:, 32:64], kr[:, 0:32]], axis=1))
    uq3 = w_uq.reshape(384, 8, 192)[:, :, 128:192]
    w_uq_sw = np.ascontiguousarray(np.concatenate([uq3[:, :, 32:64], uq3[:, :, 0:32]], axis=2).reshape(384, 512))
    k_idx = np.arange(128)[:, None]
    q_idx = np.arange(128)[None, :]
    trimask = np.where(k_idx <= q_idx, 0.0, NEG).astype(np.float32)
    ident = np.eye(128, dtype=np.float32)
    inv = (1.0 / (np.float32(10000.0) ** (np.arange(0, 64, 2, dtype=np.float32) / np.float32(64)))).astype(np.float32)
    invf = np.concatenate([inv, inv]).reshape(64, 1).astype(np.float32)
    sgn = np.concatenate([-np.ones(32), np.ones(32)]).reshape(64, 1).astype(np.float32)
    shared = {
        "trimask": trimask, "ident": ident, "invf": invf, "sgn": sgn,
        "w_ada": np.ascontiguousarray(inputs["w_ada"][0], np.float32),
        "b_ada": np.ascontiguousarray(inputs["b_ada"][0], np.float32),
        "g_pre_mix": np.ascontiguousarray(inputs["g_pre_mix"][0], np.float32),
        "g_post_mix": np.ascontiguousarray(inputs["g_post_mix"][0], np.float32),
        "g_pre_mlp": np.ascontiguousarray(inputs["g_pre_mlp"][0], np.float32),
        "g_post_mlp": np.ascontiguousarray(inputs["g_post_mlp"][0], np.float32),
        "w_in": w_in, "w_kr_sw": w_kr_sw,
        "conv_w": np.ascontiguousarray(inputs["conv_w"][0], np.float32),
        "conv_b": np.ascontiguousarray(inputs["conv_b"][0], np.float32),
        "conv_norm_g": np.ascontiguousarray(inputs["conv_norm_g"][0], np.float32),
        "conv_norm_b": np.ascontiguousarray(inputs["conv_norm_b"][0], np.float32),
        "w_conv_out": np.ascontiguousarray(inputs["w_conv_out"][0], np.float32),
        "q_norm_g": np.ascontiguousarray(inputs["q_norm_g"][0], np.float32),
        "w_uq": w_uq, "w_uq_sw": w_uq_sw,
        "kv_norm_g": np.ascontiguousarray(inputs["kv_norm_g"][0], np.float32),
        "w_ukv": np.ascontiguousarray(inputs["w_ukv"][0], np.float32),
        "w_attn_out": np.ascontiguousarray(inputs["w_attn_out"][0], np.float32),
        "w_out": np.ascontiguousarray(inputs["w_out"][0], np.float32),
        "w_mlp_in": np.ascontiguousarray(inputs["w_mlp_in"][0], np.float32),
        "w_mlp_out": np.ascontiguousarray(inputs["w_mlp_out"][0], np.float32),
    }
    in_maps = []
    for core in range(8):
        b, p = core // 2, core % 2
        xb = x[b].reshape(32, 128, D)
        pb = pos[b].reshape(32, 128)
        own = [2 * i + p for i in range(16)]
        oth = [2 * i + 1 - p for i in range(16)]
        halo = np.zeros((16, 32, D), np.float32)
        for i in range(16):
            st = own[i] * 128
            if st > 0:
                halo[i] = x[b, st - 32:st]
        m = dict(shared)
        m["x_own"] = np.ascontiguousarray(xb[own].reshape(NOWN, D))
        m["x_oth"] = np.ascontiguousarray(xb[oth].reshape(NOWN, D))
        m["x_halo"] = np.ascontiguousarray(halo.reshape(512, D))
        m["pos_own"] = np.ascontiguousarray(pb[own].reshape(NOWN))
        m["pos_oth"] = np.ascontiguousarray(pb[oth].reshape(NOWN))
        m["c"] = np.ascontiguousarray(c[b])
        m["pairmask"] = np.full((128, 128), 0.0 if p == 1 else NEG, np.float32)
        m["halomask"] = np.full((128, 1), 1.0 if p == 1 else 0.0, np.float32)
        in_maps.append(m)
    return in_maps


def kernel(**inputs):
    in_maps = _prep_inputs(inputs)
    nc = build_nc()
    res = run_bass_kernel_spmd(nc, in_maps, core_ids=list(range(8)))
    outf = np.zeros((4, 32, 128, D), np.float32)
    for core in range(8):
        b, p = core // 2, core % 2
        o = np.asarray(res.results[core]["out"]).reshape(16, 128, D)
        for i in range(16):
            outf[b, 2 * i + p] = o[i]
    return outf.reshape(4, 4096, D)
```

```python
import contextlib
import math
import numpy as np
import concourse.bass as bass
import concourse.mybir as mybir
from concourse.bass_utils import run_bass_kernel_spmd

F32 = mybir.dt.float32
BF = mybir.dt.bfloat16
I32 = mybir.dt.int32
AF = mybir.ActivationFunctionType
ALU = mybir.AluOpType

D = 1024
KC = 8
NOWN = 2048
EPS = 1e-6
NEG = -30000.0
SCALE = 1.0 / math.sqrt(192.0)
TWO_PI = 2.0 * math.pi
C1 = 6.28125
C2 = TWO_PI - 6.28125

ENGS = ("pe", "act", "dve", "pool", "sp")
NDMA = 12


class _Rec:
    def __init__(self):
        self.call = None

    def __getattr__(self, name):
        def f(*a, **k):
            self.call = (name, a, k)
            return self
        return f


def _bind(fn):
    if fn is None:
        return None
    r = _Rec()
    fn(r)
    name, a, k = r.call
    return lambda eng: getattr(eng, name)(*a, **k)


class Sched:
    def __init__(self):
        self.q = {e: [] for e in ENGS}
        self.cnt = {e: 0 for e in ENGS}
        self.waited = {e: {} for e in ENGS}
        self.state = {}
        self.dma_tot = [0] * NDMA
        self.dma_rr = 0
        self.all_dma_tokens = {}

    def _entries(self, buf, key, create):
        d = self.state.setdefault(id(buf), {})
        if key is None:
            if create and None not in d:
                d[None] = {"w": None, "r": {}}
            return list(d.values()) if not create else list(d.values())
        out = []
        if key not in d and create:
            d[key] = {"w": None, "r": {}}
        if key in d:
            out.append(d[key])
        if None in d:
            out.append(d[None])
        return out

    def _deps(self, reads, writes):
        deps = {}

        def add(tok):
            if tok is None:
                return
            s, v = tok
            if deps.get(s, 0) < v:
                deps[s] = v

        for (b, k) in reads:
            for st in self._entries(b, k, False):
                add(st["w"])
        for (b, k) in writes:
            for st in self._entries(b, k, False):
                add(st["w"])
                for s, v in st["r"].items():
                    add((s, v))
        return deps

    def _commit(self, reads, writes, tok):
        for (b, k) in reads:
            d = self.state.setdefault(id(b), {})
            if k not in d:
                d[k] = {"w": None, "r": {}}
            st = d[k]
            s, v = tok
            if st["r"].get(s, 0) < v:
                st["r"][s] = v
        for (b, k) in writes:
            d = self.state.setdefault(id(b), {})
            if k is None:
                d.clear()
            d[k] = {"w": tok, "r": {}}

    def _norm(self, lst):
        out = []
        for x in lst:
            if isinstance(x, tuple):
                out.append(x)
            else:
                out.append((x, None))
        return out

    def op(self, eng, fn, reads=(), writes=(), inc=True):
        fn = _bind(fn)
        reads = self._norm(reads)
        writes = self._norm(writes)
        deps = self._deps(reads, writes)
        waits = []
        for s, v in deps.items():
            if s == eng and eng == "pe":
                continue
            if self.waited[eng].get(s, 0) >= v:
                continue
            self.waited[eng][s] = v
            waits.append((s, v))
        if inc:
            self.cnt[eng] += 1
            tok = (eng, self.cnt[eng])
            self.q[eng].append((waits, fn, (eng, 1)))
        else:
            tok = (eng, self.cnt[eng] + 1)
            self.q[eng].append((waits, fn, None))
        self._commit(reads, writes, tok)
        return tok

    def dma(self, eng, fn, reads=(), writes=()):
        fn = _bind(fn)
        reads = self._norm(reads)
        writes = self._norm(writes)
        j = self.dma_rr
        self.dma_rr = (self.dma_rr + 1) % NDMA
        sem = "d%d" % j
        deps = self._deps(reads, writes)
        if self.dma_tot[j] > 0:
            if deps.get(sem, 0) < self.dma_tot[j]:
                deps[sem] = self.dma_tot[j]
        waits = []
        for s, v in deps.items():
            if self.waited[eng].get(s, 0) >= v:
                continue
            self.waited[eng][s] = v
            waits.append((s, v))
        self.dma_tot[j] += 16
        tok = (sem, self.dma_tot[j])
        self.q[eng].append((waits, fn, (sem, 16)))
        self._commit(reads, writes, tok)
        return tok

    def barrier(self):
        toks = [(e, self.cnt[e]) for e in ENGS if self.cnt[e] > 0]
        toks += [("d%d" % j, self.dma_tot[j]) for j in range(NDMA) if self.dma_tot[j] > 0]
        for e in ENGS:
            self.wait_all(e, [t for t in toks if t[0] != e])
        self.state = {}

    def wait_all(self, eng, toks):
        waits = []
        for (s, v) in toks:
            if self.waited[eng].get(s, 0) >= v:
                continue
            self.waited[eng][s] = v
            waits.append((s, v))
        self.q[eng].append((waits, None, None))


def build_nc(debug=None, stop=None):
    nc = bass.Bass("TRN2", target_bir_lowering=False)
    S = Sched()

    def din(name, shape, dt=F32):
        return nc.dram_tensor(name, list(shape), dt, kind="ExternalInput").ap()

    x_own = din("x_own", [NOWN, D])
    x_oth = din("x_oth", [NOWN, D])
    x_halo = din("x_halo", [512, D])
    pos_own = nc.dram_tensor("pos_own", [NOWN], I32, kind="ExternalInput")
    pos_oth = nc.dram_tensor("pos_oth", [NOWN], I32, kind="ExternalInput")
    c_in = din("c", [D])
    pairmask_in = din("pairmask", [128, 128])
    trimask_in = din("trimask", [128, 128])
    ident_in = din("ident", [128, 128])
    halomask_in = din("halomask", [128, 1])
    invf_in = din("invf", [64, 1])
    sgn_in = din("sgn", [64, 1])
    w_ada = din("w_ada", [D, 6 * D])
    b_ada = din("b_ada", [6 * D])
    g_pre_mix = din("g_pre_mix", [D])
    g_post_mix = din("g_post_mix", [D])
    g_pre_mlp = din("g_pre_mlp", [D])
    g_post_mlp = din("g_post_mlp", [D])
    w_in = din("w_in", [D, 4800])
    w_kr_sw = din("w_kr_sw", [D, 64])
    conv_w = din("conv_w", [31, D])
    conv_b = din("conv_b", [D])
    conv_norm_g = din("conv_norm_g", [D])
    conv_norm_b = din("conv_norm_b", [D])
    w_conv_out = din("w_conv_out", [D, D])
    q_norm_g = din("q_norm_g", [384])
    w_uq = din("w_uq", [384, 1536])
    w_uq_sw = din("w_uq_sw", [384, 512])
    kv_norm_g = din("kv_norm_g", [256])
    w_ukv = din("w_ukv", [256, 2048])
    w_attn_out = din("w_attn_out", [D, D])
    w_out = din("w_out", [D, D])
    w_mlp_in = din("w_mlp_in", [D, 4 * D])
    w_mlp_out = din("w_mlp_out", [4 * D, D])
    out = nc.dram_tensor("out", [NOWN, D], F32, kind="ExternalOutput").ap()
    wq = nc.dram_tensor("wq", [60, 128, 2048], BF, kind="Internal").ap()
    wq_key = object()
    dbg = None

    es = contextlib.ExitStack()
    with es:
        def sb(name, shape, dt=F32):
            return es.enter_context(nc.sbuf_tensor("s_" + name, list(shape), dt))

        sems = {}
        for e in ENGS:
            sems[e] = es.enter_context(nc.semaphore("sem_" + e))
        for j in range(NDMA):
            sems["d%d" % j] = es.enter_context(nc.semaphore("sem_d%d" % j))

        PD = [es.enter_context(nc.psum_tensor("pd%d" % i, [128, 1024], F32)) for i in range(4)]

        def bank(i):
            t = PD[i // 2]
            h = i % 2
            return t, h

        def bk(i):
            t, h = bank(i)
            return (t, h)

        def bap(i, c0=0, c1=512):
            t, h = bank(i)
            return t[:, h * 512 + c0: h * 512 + c1]

        def bap_bf(i):
            t, h = bank(i)
            return t.bitcast(BF)[:, h * 1024:(h + 1) * 1024]

        ident_f = sb("ident_f", [128, 128])
        ident_b = sb("ident_b", [128, 128], BF)
        ones_b = sb("ones_b", [128, 128], BF)
        tri_b = sb("tri_b", [128, 128], BF)
        pair_b = sb("pair_b", [128, 128], BF)
        halom = sb("halom", [128, 1])
        invf = sb("invf", [64, 1])
        sgn = sb("sgn", [64, 1])
        modT = sb("modT", [128, 48])
        vecs = sb("vecs", [128, 128])
        cwT = sb("cwT", [128, 8, 31])
        V_GPRE, V_GPOST, V_GPRE2, V_GPOST2 = 0, 8, 16, 24
        V_CB, V_CG, V_CNB = 32, 40, 48
        V_QG, V_KVG = 56, 59
        der = sb("der", [128, 48])
        NST = 2
        wst = [sb("wst%d" % i, [128, 8, 256]) for i in range(NST)]
        wbf = [sb("wbf%d" % i, [128, 8, 256], BF) for i in range(NST)]
        wrr = [0]
        xblk = [sb("xblk%d" % i, [128, D]) for i in range(2)]
        xnb = [sb("xnb%d" % i, [128, D], BF) for i in range(2)]
        small = [sb("small%d" % i, [128, 4]) for i in range(4)]
        small_rr = [0]
        tmpf = [sb("tmpf%d" % i, [128, 512]) for i in range(4)]
        tmpf_rr = [0]
        tmpb = [sb("tmpb%d" % i, [128, 512], BF) for i in range(6)]
        tmpb_rr = [0]
        rstd_t = [sb("rstd%d" % i, [128, 512]) for i in range(2)]
        rstd_rr = [0]

        def nxt(lst, rr):
            t = lst[rr[0] % len(lst)]
            rr[0] += 1
            return t

        def A(fn, **kw):
            return S.op("act", fn, **kw)

        def V(fn, **kw):
            return S.op("dve", fn, **kw)

        def G(fn, **kw):
            return S.op("pool", fn, **kw)

        def P(fn, **kw):
            return S.op("pe", fn, **kw)

        def dma_in(dst_ap, src_ap, dst_buf, key=None, eng="sp", nonc=False):
            def f(e, dst_ap=dst_ap, src_ap=src_ap):
                if nonc:
                    return e.dma_start(out=dst_ap, in_=src_ap, allow_slow_non_contiguous=True)
                return e.dma_start(out=dst_ap, in_=src_ap)
            return S.dma(eng, f, writes=[(dst_buf, key)])

        def mm(out_ap, lhsT, rhs, start, stop, reads, wkey, last):
            def f(e):
                return e.matmul(out_ap, lhsT, rhs, start=start, stop=stop)
            return S.op("pe", f, reads=reads, writes=[wkey], inc=last)

        def load_w(src, rows, c0, ncols, r0=0, cast=None):
            i = wrr[0] % NST
            wrr[0] += 1
            kc = rows // 128
            st, wb = wst[i], wbf[i]
            src_ap = src[r0:r0 + rows, c0:c0 + ncols].rearrange("(k p) c -> p k c", p=128)
            dma_in(st[:, 0:kc, 0:ncols], src_ap, st)
            if cast == "pool" or (cast is None and wrr[0] % 2 == 0):
                G(lambda e: e.tensor_copy(wb[:, 0:kc, 0:ncols], st[:, 0:kc, 0:ncols]), reads=[st], writes=[wb])
            else:
                V(lambda e: e.tensor_copy(wb[:, 0:kc, 0:ncols], st[:, 0:kc, 0:ncols]), reads=[st], writes=[wb])
            return wb

        def _unused():
            pass

        out_toks = []

        def run_phases():
            dma_in(ident_f[:], ident_in, ident_f)
            dma_in(halom[:], halomask_in, halom)
            dma_in(invf[:], invf_in, invf)
            dma_in(sgn[:], sgn_in, sgn)
            t0 = tmpf[0]
            t1 = tmpf[1]
            dma_in(t0[:, 0:128], trimask_in, t0)
            dma_in(t1[:, 0:128], pairmask_in, t1)
            V(lambda e: e.tensor_copy(ident_b[:], ident_f[:]), reads=[ident_f], writes=[ident_b])
            V(lambda e: e.memset(ones_b[:], 1.0), writes=[ones_b])
            V(lambda e: e.tensor_copy(tri_b[:], t0[:, 0:128]), reads=[t0], writes=[tri_b])
            V(lambda e: e.tensor_copy(pair_b[:], t1[:, 0:128]), reads=[t1], writes=[pair_b])
            tmpf_rr[0] = 2
            if stop == -1:
                return
            stg = tmpf[2]
            V(lambda e: e.memset(stg[:, 0:128], 0.0), writes=[stg])
            for col, src, n in ((V_GPRE, g_pre_mix, D), (V_GPOST, g_post_mix, D), (V_GPRE2, g_pre_mlp, D),
                                (V_GPOST2, g_post_mlp, D), (V_CB, conv_b, D), (V_CG, conv_norm_g, D),
                                (V_CNB, conv_norm_b, D), (V_QG, q_norm_g, 384), (V_KVG, kv_norm_g, 256)):
                dma_in(stg[col:col + n // 128, 0:128], src.rearrange("(k p) -> k p", p=128), stg)
            dma_in(stg[64:112, 0:128], b_ada.rearrange("(k p) -> k p", p=128), stg)
            dma_in(stg[112:120, 0:128], c_in.rearrange("(k p) -> k p", p=128), stg)
            P(lambda e: e.transpose(bap(1, 0, 120), stg[0:120, 0:128], ident_f[0:120, 0:120]),
              reads=[stg, ident_f], writes=[bk(1)])
            V(lambda e: e.tensor_copy(vecs[:, 0:120], bap(1, 0, 120)), reads=[bk(1)], writes=[vecs])
            if stop == -2:
                return
            badaT = vecs[:, 64:112]
            cT = vecs[:, 112:120]
            cwn = xblk[0]
            dma_in(cwn[0:31, :], conv_w, cwn)
            for c in range(8):
                P(lambda e, c=c: e.transpose(bap(2, c * 32, c * 32 + 31), cwn[0:31, c * 128:(c + 1) * 128], ident_f[0:31, 0:31]),
                  reads=[cwn, ident_f], writes=[bk(2)])
            V(lambda e: e.tensor_copy(cwT[:], bap(2, 0, 256).rearrange("p (c k) -> p c k", k=32)[:, :, 0:31]),
              reads=[bk(2)], writes=[cwT])
            if stop == -3:
                return
            scb = sb("scb", [128, 8])
            A(lambda e: e.activation(scb[:], cT, AF.Silu), reads=[vecs], writes=[scb])
            if stop == -4:
                return
            def load_w32(src, rows, c0, ncols):
                i = wrr[0] % NST
                wrr[0] += 1
                st = wst[i]
                dma_in(st[:, 0:rows // 128, 0:ncols], src[0:rows, c0:c0 + ncols].rearrange("(k p) c -> p k c", p=128), st)
                return st

            def mod_part(p0, p1, MODB, cast=None):
                for pc in range(p0, p1):
                    wb = load_w32(w_ada, D, pc * 256, 256)
                    for jj in range(2):
                        j = pc * 2 + jj
                        for k in range(8):
                            mm(bap(MODB, j, j + 1), wb[:, k, jj * 128:(jj + 1) * 128], scb[:, k:k + 1],
                               k == 0, k == 7, [wb, scb], bk(MODB), k == 7)
                V(lambda e: e.tensor_tensor(modT[:, p0 * 2:p1 * 2], bap(MODB, p0 * 2, p1 * 2), vecs[:, 64 + p0 * 2:64 + p1 * 2], ALU.add),
                  reads=[bk(MODB), vecs], writes=[(modT, p0)])

            def mod_late():
                mod_part(8, 24, 7, cast="pool")
                V(lambda e: e.tensor_tensor(der[:, 16:24], modT[:, 16:24], vecs[:, V_GPOST:V_GPOST + 8], ALU.mult),
                  reads=[modT, vecs], writes=[(der, 16)])
                V(lambda e: e.scalar_tensor_tensor(der[:, 24:32], modT[:, 32:40], 1.0, vecs[:, V_GPRE2:V_GPRE2 + 8], ALU.add, ALU.mult),
                  reads=[modT, vecs], writes=[(der, 24)])
                V(lambda e: e.tensor_copy(der[:, 32:40], modT[:, 24:32]), reads=[modT], writes=[(der, 32)])
                V(lambda e: e.tensor_tensor(der[:, 40:48], modT[:, 40:48], vecs[:, V_GPOST2:V_GPOST2 + 8], ALU.mult),
                  reads=[modT, vecs], writes=[(der, 40)])
            DER_ALL = [(der, 0), (der, 8), (der, 16), (der, 24), (der, 32), (der, 40)]

            TPB = [0, 1]
            tp_rr = [0]
            xb_rr = [0]

            def xb_prep(src_rows_ap, nrows):
                i = xb_rr[0] % 2
                xb_rr[0] += 1
                xb = xblk[i]
                xn = xnb[i]
                dma_in(xb[0:nrows, :], src_rows_ap, xb)
                sm = nxt(small, small_rr)
                A(lambda e: e.activation(xn[0:nrows, :], xb[0:nrows, :], AF.Square, accum_out=sm[0:nrows, 0:1]),
                  reads=[xb], writes=[xn, (sm, 0)])
                A(lambda e: e.activation(sm[0:nrows, 3:4], sm[0:nrows, 0:1], AF.Sqrt, bias=EPS, scale=1.0 / D),
                  reads=[(sm, 0)], writes=[(sm, 3)])
                V(lambda e: e.reciprocal(sm[0:nrows, 2:3], sm[0:nrows, 3:4]), reads=[(sm, 3)], writes=[(sm, 2)])
                V(lambda e: e.tensor_scalar(xn[0:nrows, :], xb[0:nrows, :], sm[0:nrows, 2:3], None, ALU.mult),
                  reads=[xb, (sm, 2)], writes=[xn])
                return (xn, nrows)

            def xb_trans(hd, hT, col0, gsc, shc):
                xn, nrows = hd
                b = TPB[tp_rr[0] % 2]
                tp_rr[0] += 1
                tpv = bap_bf(b)
                for k in range(8):
                    P(lambda e, k=k: e.transpose(tpv[:, k * 128:k * 128 + nrows], xn[0:nrows, k * 128:(k + 1) * 128],
                                                  ident_b[0:nrows, 0:nrows]),
                      reads=[xn, ident_b], writes=[bk(b)], inc=(k == 7))
                for k in range(8):
                    if b == TPB[0]:
                        V(lambda e, k=k: e.tensor_scalar(hT[:, k, col0:col0 + nrows], tpv[:, k * 128:k * 128 + nrows],
                                                         der[:, gsc + k:gsc + k + 1], der[:, shc + k:shc + k + 1],
                                                         ALU.mult, ALU.add),
                          reads=[bk(b), (der, gsc), (der, shc)], writes=[(hT, k)])
                    else:
                        A(lambda e, k=k: e.activation(hT[:, k, col0:col0 + nrows], tpv[:, k * 128:k * 128 + nrows],
                                                      AF.Identity, bias=der[:, shc + k:shc + k + 1],
                                                      scale=der[:, gsc + k:gsc + k + 1]),
                          reads=[bk(b), (der, gsc), (der, shc)], writes=[(hT, k)])

            def x_block_to_hT(src_rows_ap, nrows, hT, col0, gsc, shc):
                xb_trans(xb_prep(src_rows_ap, nrows), hT, col0, gsc, shc)

            def rstd_from_ps(ps_bank, nfeat, ncols=512):
                r = nxt(rstd_t, rstd_rr)
                jt = nxt(tmpf, tmpf_rr)
                A(lambda e: e.activation(jt[:, 0:ncols], bap(ps_bank, 0, ncols), AF.Sqrt, bias=EPS, scale=1.0 / nfeat),
                  reads=[bk(ps_bank)], writes=[jt])
                V(lambda e: e.reciprocal(r[:, 0:ncols], jt[:, 0:ncols]), reads=[jt], writes=[r])
                return r

            hd_first = xb_prep(x_own[0:128, :], 128)
            mod_part(0, 8, 0)
            if stop == -5:
                return
            V(lambda e: e.scalar_tensor_tensor(der[:, 0:8], modT[:, 8:16], 1.0, vecs[:, V_GPRE:V_GPRE + 8], ALU.add, ALU.mult),
              reads=[modT, vecs], writes=[(der, 0)])
            V(lambda e: e.tensor_copy(der[:, 8:16], modT[:, 0:8]), reads=[modT], writes=[(der, 8)])

            if stop == 0:
                return
            oT = sb("oT", [128, 8, NOWN], BF)
            ph12 = es.enter_context(contextlib.ExitStack())
            ph1 = es.enter_context(contextlib.ExitStack())

            def sb12(name, shape, dt=F32):
                return ph12.enter_context(nc.sbuf_tensor("s_" + name, list(shape), dt))

            def sb1(name, shape, dt=F32):
                return ph1.enter_context(nc.sbuf_tensor("s_" + name, list(shape), dt))

            kvn = [sb12("kvn_own", [128, 2, NOWN], BF), sb12("kvn_oth", [128, 2, NOWN], BF)]
            krT = [sb12("kr_own", [128, NOWN], BF), sb12("kr_oth", [128, NOWN], BF)]
            for kr_ in krT:
                G(lambda e, kr_=kr_: e.memset(kr_[64:128, :], 0.0), writes=[kr_])
            qn = sb12("qn", [128, 3, NOWN], BF)
            CS = sb12("cs_own", [64, 2, NOWN])
            wukv = sb12("wukv", [128, 2, 2048], BF)
            wuq = sb12("wuq", [128, 3, 2048], BF)
            hTs = [sb1("hT%d" % i, [128, 8, 640], BF) for i in range(2)]
            wlat = sb1("wlat", [128, 8, 768], BF)
            cs_tmp = sb1("cs_tmp", [64, 2, 512])
            posi = sb1("posi", [64, 512], I32)
            angs = [sb1("ang%d" % i, [64, 512]) for i in range(3)]
            ni_t = sb1("ni_t", [64, 512], I32)

            for pc, (src, c0, n, d0) in enumerate(((w_in, 2048, 256, 0), (w_in, 2304, 256, 256), (w_in, 2560, 192, 512),
                                                   (w_kr_sw, 0, 64, 704))):
                wb = load_w(src, D, c0, n)
                G(lambda e, wb=wb, n=n, d0=d0: e.tensor_copy(wlat[:, :, d0:d0 + n], wb[:, :, 0:n]),
                  reads=[wb], writes=[(wlat, pc)])
            WL = [(wlat, i) for i in range(4)]
            if stop == 10:
                return

            def rope_tables(pos_t, c0, dst, dcol):
                src = bass.AP(pos_t, c0, [[0, 64], [1, 512]])
                dma_in(posi[:], src, posi)
                a0, a1, a2 = angs
                a3 = posi.bitcast(F32)
                V(lambda e: e.tensor_copy(a0[:], posi[:]), reads=[posi], writes=[a0])
                V(lambda e: e.tensor_scalar(a0[:], a0[:], invf[:, 0:1], None, ALU.mult), reads=[a0, invf], writes=[a0])
                V(lambda e: e.tensor_scalar(a1[:], a0[:], 1.0 / TWO_PI, None, ALU.mult), reads=[a0], writes=[a1])
                V(lambda e: e.tensor_copy(ni_t[:], a1[:]), reads=[a1], writes=[ni_t])
                V(lambda e: e.tensor_copy(a1[:], ni_t[:]), reads=[ni_t], writes=[a1])
                V(lambda e: e.scalar_tensor_tensor(a2[:], a1[:], -C1, a0[:], ALU.mult, ALU.add), reads=[a1, a0], writes=[a2])
                V(lambda e: e.scalar_tensor_tensor(a2[:], a1[:], -C2, a2[:], ALU.mult, ALU.add), reads=[a1, a2], writes=[a2])
                V(lambda e: e.tensor_scalar(a3[:], a2[:], math.pi, -TWO_PI, ALU.is_gt, ALU.mult), reads=[a2], writes=[posi])
                V(lambda e: e.tensor_tensor(a2[:], a2[:], a3[:], ALU.add), reads=[a2, posi], writes=[a2])
                V(lambda e: e.tensor_scalar(a3[:], a2[:], -math.pi, TWO_PI, ALU.is_lt, ALU.mult), reads=[a2], writes=[posi])
                V(lambda e: e.tensor_tensor(a2[:], a2[:], a3[:], ALU.add), reads=[a2, posi], writes=[a2])
                V(lambda e: e.tensor_scalar(a1[:], a2[:], math.pi / 2, None, ALU.add), reads=[a2], writes=[a1])
                V(lambda e: e.tensor_scalar(a3[:], a1[:], math.pi, -TWO_PI, ALU.is_gt, ALU.mult), reads=[a1], writes=[posi])
                V(lambda e: e.tensor_tensor(a1[:], a1[:], a3[:], ALU.add), reads=[a1, posi], writes=[a1])
                V(lambda e: e.tensor_scalar(a1[:], a1[:], math.pi, -math.pi, ALU.min, ALU.max), reads=[a1], writes=[a1])
                V(lambda e: e.tensor_scalar(a2[:], a2[:], math.pi, -math.pi, ALU.min, ALU.max), reads=[a2], writes=[a2])
                A(lambda e: e.activation(dst[:, 0, dcol:dcol + 512], a1[:], AF.Sin), reads=[a1], writes=[(dst, dcol)])
                A(lambda e: e.activation(dst[:, 1, dcol:dcol + 512], a2[:], AF.Sin, scale=sgn[:, 0:1]),
                  reads=[a2, sgn], writes=[(dst, dcol)])

            def load_wbig(src, rows, c0, ncols):
                i = wrr[0] % NST
                wrr[0] += 1
                kc = rows // 128
                st, wb = wst[i], wbf[i]
                stv = st[:].rearrange("p k c -> p (k c)")[:, 0:kc * ncols].rearrange("p (k c) -> p k c", c=ncols)
                wbv = wb[:].rearrange("p k c -> p (k c)")[:, 0:kc * ncols].rearrange("p (k c) -> p k c", c=ncols)
                dma_in(stv, src[0:rows, c0:c0 + ncols].rearrange("(k p) c -> p k c", p=128), st)
                G(lambda e: e.tensor_copy(wbv, stv), reads=[st], writes=[wb])
                return wb, wbv

            def _job_uq(pc):
                def f():
                    wb, wbv = load_wbig(w_uq, 384, pc * 512, 512)
                    G(lambda e: e.tensor_copy(wuq[:, :, pc * 512:(pc + 1) * 512], wbv), reads=[wb], writes=[(wuq, pc)])
                return f

            def _job_uqsw():
                wb, wbv = load_wbig(w_uq_sw, 384, 0, 512)
                G(lambda e: e.tensor_copy(wuq[:, :, 1536:2048], wbv), reads=[wb], writes=[(wuq, 3)])

            def _job_ukv(pc):
                def f():
                    wb, wbv = load_wbig(w_ukv, 256, pc * 1024, 1024)
                    G(lambda e: e.tensor_copy(wukv[:, :, pc * 1024:(pc + 1) * 1024], wbv), reads=[wb], writes=[(wukv, pc)])
                return f

            wjobs = [_job_uq(0), _job_uq(1), _job_uq(2), _job_uqsw, _job_ukv(0), _job_ukv(1)]

            def prep1(j_):
                i_, b_ = j_ // 4, j_ % 4
                grp_, t_ = i_ // 4, i_ % 4
                xsrc = x_own if grp_ == 0 else x_oth
                r0 = t_ * 512 + b_ * 128
                return xb_prep(xsrc[r0:r0 + 128, :], 128)

            hd1 = [hd_first]
            for grp in range(2):
                pos_t = pos_own if grp == 0 else pos_oth
                for t in range(4):
                    hT = hTs[(grp * 4 + t) % 2]
                    for b in range(4):
                        j_ = (grp * 4 + t) * 4 + b
                        nh = prep1(j_ + 1) if j_ + 1 < 32 else None
                        xb_trans(hd1[0], hT, b * 128, 0, 8)
                        hd1[0] = nh
                    if grp * 4 + t >= 2 and wjobs:
                        wjobs.pop(0)()
                    if grp == 0:
                        rope_tables(pos_t, t * 512, CS, t * 512)
                        cs, cc = CS, t * 512
                    else:
                        rope_tables(pos_t, t * 512, cs_tmp, 0)
                        cs, cc = cs_tmp, 0
                    if stop == 12:
                        return
                    hk = [(hT, k) for k in range(8)]
                    for m in range(2):
                        for k in range(8):
                            mm(bap(2 + m), wlat[:, k, 384 + m * 128:384 + (m + 1) * 128], hT[:, k, 0:512],
                               k == 0, k == 7, hk + WL, bk(2 + m), k == 7)
                    for m in range(2):
                        for k in range(8):
                            mm(bap(5 + m)[0:64, :], wlat[:, k, 640 + m * 64:640 + (m + 1) * 64], hT[:, k, 0:512],
                               k == 0, k == 7, hk + WL, bk(5 + m), k == 7)
                    sq = []
                    for m in range(2):
                        s_ = nxt(tmpb, tmpb_rr)
                        A(lambda e, m=m, s_=s_: e.activation(s_[:], bap(2 + m), AF.Square), reads=[bk(2 + m)], writes=[s_])
                        sq.append(s_)
                    for m in range(2):
                        mm(bap(4), ones_b[:], sq[m][:], m == 0, m == 1, [ones_b, sq[m]], bk(4), True)
                    r = rstd_from_ps(4, 256)
                    for m in range(2):
                        V(lambda e, m=m, r=r: e.scalar_tensor_tensor(kvn[grp][:, m, t * 512:(t + 1) * 512], bap(2 + m),
                                                                     vecs[:, V_KVG + m:V_KVG + m + 1], r[:], ALU.mult, ALU.mult),
                          reads=[bk(2 + m), r, vecs], writes=[(kvn[grp], t)])
                    ta = nxt(tmpf, tmpf_rr)
                    tb_ = nxt(tmpf, tmpf_rr)
                    V(lambda e, ta=ta, cs=cs, cc=cc: e.tensor_tensor(ta[0:64, :], bap(5)[0:64, :], cs[:, 0, cc:cc + 512], ALU.mult),
                      reads=[bk(5), (cs, cc)], writes=[ta])
                    V(lambda e, tb_=tb_, cs=cs, cc=cc: e.tensor_tensor(tb_[0:64, :], bap(6)[0:64, :], cs[:, 1, cc:cc + 512], ALU.mult),
                      reads=[bk(6), (cs, cc)], writes=[tb_])
                    V(lambda e, ta=ta, tb_=tb_: e.tensor_tensor(krT[grp][0:64, t * 512:(t + 1) * 512], ta[0:64, :], tb_[0:64, :], ALU.add),
                      reads=[ta, tb_], writes=[(krT[grp], t)])
                    if stop == 13:
                        return
                    if grp == 0:
                        QB = [2, 3, 7]
                        for m in range(3):
                            for k in range(8):
                                mm(bap(QB[m]), wlat[:, k, m * 128:(m + 1) * 128], hT[:, k, 0:512],
                                   k == 0, k == 7, hk + WL, bk(QB[m]), k == 7)
                        sq = []
                        for m in range(3):
                            s_ = nxt(tmpb, tmpb_rr)
                            A(lambda e, m=m, s_=s_: e.activation(s_[:], bap(QB[m]), AF.Square), reads=[bk(QB[m])], writes=[s_])
                            sq.append(s_)
                        for m in range(3):
                            mm(bap(4), ones_b[:], sq[m][:], m == 0, m == 2, [ones_b, sq[m]], bk(4), True)
                        r = rstd_from_ps(4, 384)
                        for m in range(3):
                            V(lambda e, m=m, r=r: e.scalar_tensor_tensor(qn[:, m, t * 512:(t + 1) * 512], bap(QB[m]),
                                                                         vecs[:, V_QG + m:V_QG + m + 1], r[:], ALU.mult, ALU.mult),
                              reads=[bk(QB[m]), r, vecs], writes=[(qn, t)])
                    if stop == 14:
                        return

            while wjobs:
                wjobs.pop(0)()
            S.barrier()
            ph1.close()
            if stop == 1:
                ph12.close()
                return

            KhT = [[sb12("kh%d_%d" % (i, g), [128, NOWN], BF) for g in range(2)] for i in range(1)]
            Vh = [[sb12("vh%d_%d" % (i, g), [128, 16, 128], BF) for g in range(2)] for i in range(1)]
            Qh = [sb12("qh%d" % i, [128, NOWN], BF) for i in range(1)]
            Qr = [sb12("qr%d" % i, [128, NOWN], BF) for i in range(1)]
            G(lambda e: e.memset(Qr[0][64:128, :], 0.0), writes=[Qr[0]])
            Pt = [sb12("pt%d" % i, [128, 512], BF) for i in range(4)]
            pt_rr = [0]
            SB_ = [0, 1, 2]
            s_rr = [0]
            OB = [3, 5]
            LB = [4, 6]
            HBS = [7, 3, 4]
            hb_rr = [0]

            def nhb():
                b_ = HBS[hb_rr[0] % 3]
                hb_rr[0] += 1
                return b_
            evac_rr = [0]

            def evac_copy(dst_ap, src_bank_ap, reads, writes):
                if evac_rr[0] % 2 == 0:
                    V(lambda e: e.tensor_copy(dst_ap, src_bank_ap), reads=reads, writes=writes)
                else:
                    A(lambda e: e.activation(dst_ap, src_bank_ap, AF.Copy), reads=reads, writes=writes)
                evac_rr[0] += 1

            def build_head(h):
                i = 0
                for t in range(4):
                    HB = nhb()
                    for k in range(3):
                        mm(bap(HB)[0:64, :], wuq[:, k, h * 192 + 128:h * 192 + 192], qn[:, k, t * 512:(t + 1) * 512],
                           k == 0, k == 2, [wuq, qn], bk(HB), k == 2)
                    ta = nxt(tmpf, tmpf_rr)
                    V(lambda e, ta=ta, t=t: e.tensor_tensor(ta[0:64, :], bap(HB)[0:64, :], CS[:, 0, t * 512:(t + 1) * 512], ALU.mult),
                      reads=[bk(HB), CS], writes=[ta])
                    HB = nhb()
                    for k in range(3):
                        mm(bap(HB)[0:64, :], wuq[:, k, 1536 + h * 64:1536 + (h + 1) * 64], qn[:, k, t * 512:(t + 1) * 512],
                           k == 0, k == 2, [wuq, qn], bk(HB), k == 2)
                    tb_ = nxt(tmpf, tmpf_rr)
                    V(lambda e, tb_=tb_, t=t: e.tensor_tensor(tb_[0:64, :], bap(HB)[0:64, :], CS[:, 1, t * 512:(t + 1) * 512], ALU.mult),
                      reads=[bk(HB), CS], writes=[tb_])
                    V(lambda e, ta=ta, tb_=tb_, t=t: e.tensor_tensor(Qr[i][0:64, t * 512:(t + 1) * 512], ta[0:64, :], tb_[0:64, :], ALU.add),
                      reads=[ta, tb_], writes=[(Qr[i], t)])
                for t in range(4):
                    HB = nhb()
                    for k in range(3):
                        mm(bap(HB), wuq[:, k, h * 192:h * 192 + 128], qn[:, k, t * 512:(t + 1) * 512],
                           k == 0, k == 2, [wuq, qn], bk(HB), k == 2)
                    evac_copy(Qh[i][:, t * 512:(t + 1) * 512], bap(HB), [bk(HB)], [(Qh[i], t)])
                for grp in range(2):
                    for t in range(4):
                        HB = nhb()
                        for k in range(2):
                            mm(bap(HB), wukv[:, k, h * 256:h * 256 + 128], kvn[grp][:, k, t * 512:(t + 1) * 512],
                               k == 0, k == 1, [wukv, kvn[grp]], bk(HB), k == 1)
                        evac_copy(KhT[i][grp][:, t * 512:(t + 1) * 512], bap(HB), [bk(HB)], [(KhT[i][grp], t)])
                    for t in range(4):
                        HB = nhb()
                        for b in range(4):
                            blk = t * 4 + b
                            for k in range(2):
                                mm(bap(HB, b * 128, (b + 1) * 128), kvn[grp][:, k, blk * 128:(blk + 1) * 128],
                                   wukv[:, k, h * 256 + 128:h * 256 + 256],
                                   k == 0, k == 1, [wukv, kvn[grp]], bk(HB), (k == 1 and b == 3))
                        evac_copy(Vh[i][grp][:, t * 4:(t + 1) * 4, :], bap(HB).rearrange("p (b d) -> p b d", d=128),
                                  [bk(HB)], [(Vh[i][grp], t)])

            def attend_head(h):
                i = 0
                for g in range(4):
                    ob = OB[g % 2]
                    lb = LB[g % 2]
                    visits = [(J, grp) for J in range(4 * g + 4) for grp in range(2)]
                    pend = []

                    def do_pv(v, first, last):
                        J, grp, c0, pt = v
                        mm(bap(ob, c0, 512), Vh[i][grp][:, J, :], pt[:, c0:512], first, last,
                           [Vh[i][grp], pt], bk(ob), True)
                        mm(bap(lb, c0, 512), ones_b[:], pt[:, c0:512], first, last,
                           [ones_b, pt], bk(lb), True)

                    npv = [0]
                    for vi, (J, grp) in enumerate(visits):
                        j = J - 4 * g
                        c0 = 128 * max(j, 0)
                        sbk = SB_[s_rr[0] % 3]
                        s_rr[0] += 1
                        q0 = g * 512 + c0
                        q1 = (g + 1) * 512
                        masked = j >= 0
                        mm(bap(sbk, c0, 512), KhT[i][grp][:, J * 128:(J + 1) * 128], Qh[i][:, q0:q1],
                           True, False, [KhT[i][grp], Qh[i]], bk(sbk), False)
                        mm(bap(sbk, c0, 512), krT[grp][:, J * 128:(J + 1) * 128], Qr[i][:, q0:q1],
                           False, not masked, [krT[grp], Qr[i]], bk(sbk), not masked)
                        if masked:
                            mk = tri_b if grp == 0 else pair_b
                            mm(bap(sbk, c0, c0 + 128), ident_b[:], mk[:], False, True, [ident_b, mk], bk(sbk), True)
                        pt = nxt(Pt, pt_rr)
                        A(lambda e, pt=pt, sbk=sbk, c0=c0: e.activation(pt[:, c0:512], bap(sbk, c0, 512), AF.Exp, scale=SCALE),
                          reads=[bk(sbk)], writes=[pt])
                        pend.append((J, grp, c0, pt))
                        if len(pend) > 2:
                            v = pend.pop(0)
                            do_pv(v, npv[0] == 0, False)
                            npv[0] += 1
                    while pend:
                        v = pend.pop(0)
                        do_pv(v, npv[0] == 0, len(pend) == 0)
                        npv[0] += 1
                    rl = nxt(rstd_t, rstd_rr)
                    V(lambda e, rl=rl, lb=lb: e.reciprocal(rl[:], bap(lb)), reads=[bk(lb)], writes=[rl])
                    V(lambda e, rl=rl, ob=ob, g=g: e.tensor_tensor(oT[:, h, g * 512:(g + 1) * 512], bap(ob), rl[:], ALU.mult),
                      reads=[bk(ob), rl], writes=[(oT, (h, g))])

            prep = []
            for pc in range(4):
                prep += [(w_in, pc * 256, 0), (w_in, 1024 + pc * 256, 0)]
            for pc in range(4):
                prep += [(w_conv_out, pc * 256, 0)]
            for pc in range(4):
                prep += [(w_attn_out, pc * 256, 0), (w_in, 2752 + pc * 256, 0), (w_in, 3776 + pc * 256, 0)]
            for pc in range(4):
                prep += [(w_out, pc * 256, 0)]
            for pc in range(16):
                prep += [(w_mlp_in, pc * 256, 0)]
            for pc in range(4):
                for rr_ in range(4):
                    prep += [(w_mlp_out, pc * 256, rr_ * 1024)]
            prep_tok = {}

            def do_prep():
                prev = None
                for i, (src, c0, r0) in enumerate(prep):
                    wb = load_w(src, D, c0, 256, r0=r0, cast="pool")
                    if prev is not None:
                        pw, pi = prev
                        S.dma("sp", lambda e, pw=pw, pi=pi: e.dma_start(out=wq[pi], in_=pw[:].rearrange("p k c -> p (k c)")),
                              reads=[pw], writes=[(wq_key, pi)])
                    prev = (wb, i)
                pw, pi = prev
                S.dma("sp", lambda e: e.dma_start(out=wq[pi], in_=pw[:].rearrange("p k c -> p (k c)")),
                      reads=[pw], writes=[(wq_key, pi)])

            do_prep()
            build_head(0)
            for h in range(8):
                attend_head(h)
                if h + 1 < 8:
                    build_head(h + 1)
            mod_late()

            S.barrier()
            ph12.close()
            if stop == 2:
                return
            wpool = [wbf[0][:], wbf[1][:]]
            for i_ in range(NST):
                fl = wst[i_].bitcast(BF)[:].rearrange("p k c -> p (k c)")
                wpool += [fl[:, 0:2048].rearrange("p (k c) -> p k c", c=256), fl[:, 2048:4096].rearrange("p (k c) -> p k c", c=256)]
            wp_rr = [0]
            pidx = {(src_.tensor.name, c0_, r0_): i_ for i_, (src_, c0_, r0_) in enumerate(prep)}

            def load_wq(src, rows, c0, ncols, r0=0):
                i = pidx[(src.tensor.name, c0, r0)]
                buf = wpool[wp_rr[0] % len(wpool)]
                wp_rr[0] += 1
                S.dma("sp", lambda e: e.dma_start(out=buf.rearrange("p k c -> p (k c)"), in_=wq[i]), writes=[buf])
                return buf

            xT = sb("xT", [128, 8, 512])
            yT = sb("yT", [128, 8, 512])
            hTe = sb("hTe", [128, 8, 640], BF)
            uext = sb("uext", [128, 8, 640], BF)
            arena = sb("arena", [128, 16384], BF)
            hid = arena[:, :].rearrange("p (j t) -> p j t", t=512)
            ucv = arena.bitcast(F32)[:, 0:4096].rearrange("p (c t) -> p c t", t=512)
            diag = [arena[:, 8192 + i * 3968:8192 + (i + 1) * 3968].rearrange("p (k m) -> p k m", m=128) for i in range(2)]
            sh8 = sb("sh8", [128, 8, 512], BF)
            actT = sh8
            mT = sh8
            h2T = sh8
            yaT = sb("yaT", [128, 8, 512], BF)
            oblk = xblk
            stat_s = sb("stat_s", [128, 512])
            stat_n = sb("stat_n", [128, 512])
            sb_sig = sb("sb_sig", [128, 640])

            deferred = []

            def flush_def():
                for f_ in deferred:
                    f_()
                deferred.clear()

            def stats_accum(ps_b, src_ap, reads, idx, n, defer=False):
                s_ = nxt(tmpb, tmpb_rr)
                A(lambda e: e.activation(s_[:], src_ap, AF.Square), reads=reads, writes=[s_])
                f_ = lambda s_=s_: mm(bap(ps_b), ones_b[:], s_[:], idx == 0, idx == n - 1, [ones_b, s_], bk(ps_b), True)
                if defer:
                    deferred.append(f_)
                else:
                    f_()

            def fh_blocks(g_):
                lst = []
                for b in range(4):
                    blk = g_ * 4 + b
                    lst.append((x_halo[blk * 32:(blk + 1) * 32, :], 32, b * 160))
                    lst.append((x_own[blk * 128:(blk + 1) * 128, :], 128, b * 160 + 32))
                return lst

            def front_h(g_):
                lst = fh_blocks(g_)
                hd = xb_prep(lst[0][0], lst[0][1])
                for i_ in range(8):
                    nh = xb_prep(lst[i_ + 1][0], lst[i_ + 1][1]) if i_ + 1 < 8 else None
                    xb_trans(hd, hTe, lst[i_][2], 0, 8)
                    hd = nh

            front_h(0)
            for g in range(4):
                for b in range(4):
                    blk = g * 4 + b
                    xb = xblk[xb_rr[0] % 2]
                    xb_rr[0] += 1
                    dma_in(xb[:], x_own[blk * 128:(blk + 1) * 128, :], xb)
                    for half in range(2):
                        tb_i = 2 + half
                        for kk in range(4):
                            k = half * 4 + kk
                            P(lambda e, k=k, kk=kk, tb_i=tb_i, xb=xb: e.transpose(bap(tb_i, kk * 128, (kk + 1) * 128),
                                                                                    xb[:, k * 128:(k + 1) * 128], ident_f[:]),
                              reads=[xb, ident_f], writes=[bk(tb_i)], inc=(kk == 3))
                        evac_copy(xT[:, half * 4:(half + 1) * 4, b * 128:(b + 1) * 128],
                                  bap(tb_i).rearrange("p (k t) -> p k t", t=128), [bk(tb_i)], [(xT, (half, b))])
                hk = [(hTe, k) for k in range(8)]
                def glu_chunk(c, wa, wb2, cc):
                    for (wt, d) in ((wa, 0), (wb2, 1)):
                        for (n0, n1, hb) in ((0, 512, 0), (512, 640, 1)):
                            for k in range(8):
                                mm(PD[d][:, hb * 512:hb * 512 + (n1 - n0)], wt[:, k, cc * 128:(cc + 1) * 128],
                                   hTe[:, k, n0:n1], k == 0, k == 7, hk + [wt], (PD[d], hb), k == 7)
                    sg = sb_sig
                    A(lambda e: e.activation(sg[:, 0:640], PD[1][:, 0:640], AF.Sigmoid),
                      reads=[(PD[1], 0), (PD[1], 1)], writes=[sg])
                    V(lambda e: e.tensor_tensor(uext[:, c, :], PD[0][:, 0:640], sg[:, 0:640], ALU.mult),
                      reads=[(PD[0], 0), (PD[0], 1), sg], writes=[(uext, c)])
                    if g == 0:
                        V(lambda e: e.tensor_scalar(uext[:, c, 0:32], uext[:, c, 0:32], halom[:, 0:1], None, ALU.mult),
                          reads=[(uext, c), halom], writes=[(uext, c)])

                def conv_chunk(c):
                    dg = diag[c % 2]
                    for k in range(31):
                        G(lambda e: e.tensor_scalar(dg[:, k, :], ident_b[:], cwT[:, c, k:k + 1], 1.0, ALU.mult, ALU.mult),
                          reads=[ident_b, cwT], writes=[(arena, None) if (c == 0 and k == 0) else (arena, ("d", c % 2, k))])
                    uv = uext[:, c, :].rearrange("p (b w) -> p b w", w=160)
                    cb = 4 + (c % 2)
                    for k in range(31):
                        mm(bap(cb).rearrange("p (b w) -> p b w", w=128), dg[:, k, :], uv[:, :, 2 + k:2 + k + 128],
                           k == 0, k == 30, [(arena, ("d", c % 2, k)), (uext, c)], bk(cb), k == 30)
                    flush_def()
                    A(lambda e: e.activation(ucv[:, c, :], bap(cb), AF.Identity, bias=vecs[:, V_CB + c:V_CB + c + 1]),
                      reads=[bk(cb), vecs], writes=[(arena, ("u", c))])
                    ub_ = nxt(tmpb, tmpb_rr)
                    V(lambda e: e.tensor_copy(ub_[:], ucv[:, c, :]), reads=[(arena, ("u", c))], writes=[ub_])
                    deferred.append(lambda ub_=ub_, c=c: mm(bap(6), ones_b[:], ub_[:], c == 0, c == 7, [ones_b, ub_], bk(6), True))
                    stats_accum(7, ucv[:, c, :], [(arena, ("u", c))], c, 8, defer=True)

                for pc in range(4):
                    wa = load_wq(w_in, D, pc * 256, 256)
                    wb2 = load_wq(w_in, D, 1024 + pc * 256, 256)
                    for cc in range(2):
                        c = pc * 2 + cc
                        glu_chunk(c, wa, wb2, cc)
                        if c >= 1:
                            conv_chunk(c - 1)
                conv_chunk(7)
                flush_def()
                mean = stat_s
                nmr = stat_n
                A(lambda e: e.activation(mean[:], bap(6), AF.Copy, scale=1.0 / D), reads=[bk(6)], writes=[mean])
                jt = nxt(tmpf, tmpf_rr)
                V(lambda e, jt=jt: e.tensor_tensor(jt[:], mean[:], mean[:], ALU.mult), reads=[mean], writes=[jt])
                jt2 = nxt(tmpf, tmpf_rr)
                V(lambda e, jt=jt, jt2=jt2: e.scalar_tensor_tensor(jt2[:], bap(7), 1.0 / D, jt[:], ALU.mult, ALU.subtract),
                  reads=[bk(7), jt], writes=[jt2])
                V(lambda e, jt2=jt2: e.tensor_scalar(jt2[:], jt2[:], 0.0, None, ALU.max), reads=[jt2], writes=[jt2])
                A(lambda e, jt=jt, jt2=jt2: e.activation(jt[:], jt2[:], AF.Sqrt, bias=EPS), reads=[jt2], writes=[jt])
                rln = nxt(rstd_t, rstd_rr)
                V(lambda e, jt=jt, rln=rln: e.reciprocal(rln[:], jt[:]), reads=[jt], writes=[rln])
                V(lambda e, rln=rln: e.scalar_tensor_tensor(nmr[:], mean[:], -1.0, rln[:], ALU.mult, ALU.mult),
                  reads=[mean, rln], writes=[nmr])
                for c in range(8):
                    jt = nxt(tmpf, tmpf_rr)
                    V(lambda e, c=c, jt=jt, rln=rln: e.tensor_tensor(jt[:], ucv[:, c, :], rln[:], ALU.mult),
                      reads=[(arena, ("u", c)), rln], writes=[jt])
                    V(lambda e, jt=jt: e.tensor_tensor(jt[:], jt[:], nmr[:], ALU.add), reads=[jt, nmr], writes=[jt])
                    A(lambda e, c=c, jt=jt: e.activation(actT[:, c, :], jt[:], AF.Silu,
                                                         bias=vecs[:, V_CNB + c:V_CNB + c + 1], scale=vecs[:, V_CG + c:V_CG + c + 1]),
                      reads=[jt, vecs, vecs], writes=[(actT, c)])
                ak = [(actT, k) for k in range(8)]
                for pc in range(4):
                    wb = load_wq(w_conv_out, D, pc * 256, 256)
                    for cc in range(2):
                        m = pc * 2 + cc
                        bb = 4 + (m % 2)
                        for k in range(8):
                            mm(bap(bb), wb[:, k, cc * 128:(cc + 1) * 128], actT[:, k, :], k == 0, k == 7, ak + [wb], bk(bb), k == 7)
                        evac_copy(yaT[:, m, :], bap(bb), [bk(bb)], [(yaT, m)])
                hown = lambda k: hTe[:, k, :].rearrange("p (b w) -> p b w", w=160)[:, :, 32:160]
                for pc in range(4):
                    wao = load_wq(w_attn_out, D, pc * 256, 256)
                    for cc in range(2):
                        for hh in range(8):
                            mm(bap(2 + cc), wao[:, hh, cc * 128:(cc + 1) * 128], oT[:, hh, g * 512:(g + 1) * 512],
                               hh == 0, hh == 7, [oT, wao], bk(2 + cc), hh == 7)
                    wga = load_wq(w_in, D, 2752 + pc * 256, 256)
                    sas = []
                    for cc in range(2):
                        m = pc * 2 + cc
                        for k in range(8):
                            mm(bap(cc).rearrange("p (b w) -> p b w", w=128), wga[:, k, cc * 128:(cc + 1) * 128], hown(k),
                               k == 0, k == 7, hk + [wga], bk(cc), k == 7)
                        sa = nxt(tmpf, tmpf_rr)
                        A(lambda e, sa=sa, cc=cc: e.activation(sa[:], bap(cc), AF.Sigmoid), reads=[bk(cc)], writes=[sa])
                        V(lambda e, sa=sa, m=m: e.tensor_tensor(sa[:], sa[:], yaT[:, m, :], ALU.mult),
                          reads=[sa, (yaT, m)], writes=[sa])
                        sas.append(sa)
                    wgb = load_wq(w_in, D, 3776 + pc * 256, 256)
                    for cc in range(2):
                        m = pc * 2 + cc
                        for k in range(8):
                            mm(bap(cc).rearrange("p (b w) -> p b w", w=128), wgb[:, k, cc * 128:(cc + 1) * 128], hown(k),
                               k == 0, k == 7, hk + [wgb], bk(cc), k == 7)
                        sb2 = nxt(tmpf, tmpf_rr)
                        A(lambda e, sb2=sb2, cc=cc: e.activation(sb2[:], bap(cc), AF.Sigmoid), reads=[bk(cc)], writes=[sb2])
                        V(lambda e, sb2=sb2, cc=cc: e.tensor_tensor(sb2[:], bap(2 + cc), sb2[:], ALU.mult),
                          reads=[bk(2 + cc), sb2], writes=[sb2])
                        V(lambda e, sa=sas[cc], sb2=sb2, m=m: e.tensor_tensor(mT[:, m, :], sa[:], sb2[:], ALU.add),
                          reads=[sas[cc], sb2], writes=[(mT, m)])
                mk_ = [(mT, k) for k in range(8)]
                for pc in range(4):
                    wb = load_wq(w_out, D, pc * 256, 256)
                    for cc in range(2):
                        m = pc * 2 + cc
                        bb = 4 + (m % 2)
                        for k in range(8):
                            mm(bap(bb), wb[:, k, cc * 128:(cc + 1) * 128], mT[:, k, :], k == 0, k == 7, mk_ + [wb], bk(bb), k == 7)
                        flush_def()
                        A(lambda e, m=m, bb=bb: e.activation(yT[:, m, :], bap(bb), AF.Copy), reads=[bk(bb)], writes=[(yT, m)])
                        stats_accum(6, yT[:, m, :], [(yT, m)], m, 8, defer=True)
                flush_def()
                if debug == "m" and g == 3:
                    V(lambda e: e.tensor_copy(hTe[:, :, 0:512], mT[:]), reads=[mT], writes=[hTe])
                    V(lambda e: e.tensor_copy(uext[:, :, 0:512], yT[:]), reads=[yT], writes=[uext])
                r1 = rstd_from_ps(6, D)
                for k in range(8):
                    jt = nxt(tmpf, tmpf_rr)
                    V(lambda e, k=k, jt=jt, r1=r1: e.tensor_tensor(jt[:], yT[:, k, :], r1[:], ALU.mult),
                      reads=[(yT, k), r1], writes=[jt])
                    V(lambda e, k=k, jt=jt: e.scalar_tensor_tensor(xT[:, k, :], jt[:], der[:, 16 + k:17 + k], xT[:, k, :], ALU.mult, ALU.add),
                      reads=[jt, (der, 16), xT], writes=[xT])
                    stats_accum(7, xT[:, k, :], [xT], k, 8)
                r2 = rstd_from_ps(7, D)
                for k in range(8):
                    jt = nxt(tmpf, tmpf_rr)
                    V(lambda e, k=k, jt=jt, r2=r2: e.scalar_tensor_tensor(jt[:], xT[:, k, :], der[:, 24 + k:25 + k], r2[:], ALU.mult, ALU.mult),
                      reads=[xT, (der, 24), r2], writes=[jt])
                    A(lambda e, k=k, jt=jt: e.activation(h2T[:, k, :], jt[:], AF.Identity, bias=der[:, 32 + k:33 + k]),
                      reads=[jt, (der, 32)], writes=[(h2T, k)])
                h2k = [(h2T, k) for k in range(8)]
                nlst = fh_blocks(g + 1) if g + 1 < 4 else None
                nhd = {}
                for pc in range(16):
                    if nlst is not None and pc % 2 == 0:
                        if pc >= 2:
                            xb_trans(nhd[pc // 2 - 1], hTe, nlst[pc // 2 - 1][2], 0, 8)
                        nhd[pc // 2] = xb_prep(nlst[pc // 2][0], nlst[pc // 2][1])
                    wb = load_wq(w_mlp_in, D, pc * 256, 256)
                    for cc in range(2):
                        j = pc * 2 + cc
                        bb = 4 + (j % 4)
                        for k in range(8):
                            mm(bap(bb), wb[:, k, cc * 128:(cc + 1) * 128], h2T[:, k, :], k == 0, k == 7, h2k + [wb], bk(bb), k == 7)
                        jt = nxt(tmpf, tmpf_rr)
                        A(lambda e, jt=jt, bb=bb: e.activation(jt[:], bap(bb), AF.Relu), reads=[bk(bb)], writes=[jt])
                        V(lambda e, jt=jt, j=j: e.tensor_tensor(hid[:, j, :], jt[:], jt[:], ALU.mult), reads=[jt],
                          writes=[(arena, None) if j == 0 else (arena, ("h", j))])
                if nlst is not None:
                    xb_trans(nhd[7], hTe, nlst[7][2], 0, 8)
                for pc in range(4):
                    for cc in range(2):
                        pass
                    wbs = []
                    for rr_ in range(4):
                        wb = load_wq(w_mlp_out, D, pc * 256, 256, r0=rr_ * 1024)
                        for cc in range(2):
                            bb = 4 + cc
                            for k in range(8):
                                kk = rr_ * 8 + k
                                mm(bap(bb), wb[:, k, cc * 128:(cc + 1) * 128], hid[:, kk, :], kk == 0, kk == 31,
                                   [arena, wb], bk(bb), (k == 7))
                    flush_def()
                    for cc in range(2):
                        m = pc * 2 + cc
                        bb = 4 + cc
                        A(lambda e, m=m, bb=bb: e.activation(yT[:, m, :], bap(bb), AF.Copy), reads=[bk(bb)], writes=[(yT, m)])
                        stats_accum(6, yT[:, m, :], [(yT, m)], m, 8, defer=True)
                flush_def()
                r3 = rstd_from_ps(6, D)
                for k in range(8):
                    jt = nxt(tmpf, tmpf_rr)
                    V(lambda e, k=k, jt=jt, r3=r3: e.tensor_tensor(jt[:], yT[:, k, :], r3[:], ALU.mult),
                      reads=[(yT, k), r3], writes=[jt])
                    V(lambda e, k=k, jt=jt: e.scalar_tensor_tensor(yT[:, k, :], jt[:], der[:, 40 + k:41 + k], xT[:, k, :], ALU.mult, ALU.add),
                      reads=[jt, (der, 40), xT], writes=[(yT, k)])
                for b in range(4):
                    ob_ = oblk[b % 2]
                    for half in range(2):
                        tb_i = 2 + half
                        for kk in range(4):
                            k = half * 4 + kk
                            P(lambda e, k=k, kk=kk, tb_i=tb_i, b=b: e.transpose(bap(tb_i, kk * 128, (kk + 1) * 128),
                                                                                 yT[:, k, b * 128:(b + 1) * 128], ident_f[:]),
                              reads=[yT, ident_f], writes=[bk(tb_i)], inc=(kk == 3))
                        evac_copy(ob_[:, half * 512:(half + 1) * 512], bap(tb_i), [bk(tb_i)], [(ob_, half)])
                    blk = g * 4 + b
                    tok = S.dma("act", lambda e, ob_=ob_, blk=blk: e.dma_start(out=out[blk * 128:(blk + 1) * 128, :], in_=ob_[:]),
                                reads=[ob_])
                    out_toks.append(tok)


        run_phases()
        if stop is not None:
            out_toks.append(S.dma("act", lambda e: e.dma_start(out=out[0:128, :], in_=xblk[0][:]), reads=[xblk[0]]))
        last = {}
        for (s, v) in out_toks:
            last[s] = max(last.get(s, 0), v)
        S.wait_all("act", list(last.items()))

        with nc.Block() as block:
            def emit(engname, eng):
                for (waits, fn, inc) in S.q[engname]:
                    for (s, v) in waits:
                        eng.wait_ge(sems[s], v)
                    if fn is None:
                        continue
                    ins = fn(eng)
                    if inc is not None:
                        ins.then_inc(sems[inc[0]], inc[1])

            @block.sync
            def _(e):
                emit("sp", e)

            @block.tensor
            def _(e):
                emit("pe", e)

            @block.scalar
            def _(e):
                emit("act", e)

            @block.vector
            def _(e):
                emit("dve", e)

            @block.gpsimd
            def _(e):
                emit("pool", e)
    return nc


def _prep_inputs(inputs):
    x = np.asarray(inputs["x"], np.float32)
    pos = np.asarray(inputs["positions"], np.int32)
    c = np.asarray(inputs["c"], np.float32)
    w_in = np.ascontiguousarray(np.asarray(inputs["w_in"], np.float32)[0])
    w_uq = np.ascontiguousarray(np.asarray(inputs["w_uq"], np.float32)[0])
    kr = w_in[:, 2688:2752]
    w_kr_sw = np.ascontiguousarray(np.concatenate([kr[:, 32:64], kr[:, 0:32]], axis=1))
    uq3 = w_uq.reshape(384, 8, 192)[:, :, 128:192]
    w_uq_sw = np.ascontiguousarray(np.concatenate([uq3[:, :, 32:64], uq3[:, :, 0:32]], axis=2).reshape(384, 512))
    k_idx = np.arange(128)[:, None]
    q_idx = np.arange(128)[None, :]
    trimask = np.where(k_idx <= q_idx, 0.0, NEG).astype(np.float32)
    ident = np.eye(128, dtype=np.float32)
    inv = (1.0 / (np.float32(10000.0) ** (np.arange(0, 64, 2, dtype=np.float32) / np.float32(64)))).astype(np.float32)
    invf = np.concatenate([inv, inv]).reshape(64, 1).astype(np.float32)
    sgn = np.concatenate([-np.ones(32), np.ones(32)]).reshape(64, 1).astype(np.float32)
    shared = {
        "trimask": trimask, "ident": ident, "invf": invf, "sgn": sgn,
        "w_ada": np.ascontiguousarray(inputs["w_ada"][0], np.float32),
        "b_ada": np.ascontiguousarray(inputs["b_ada"][0], np.float32),
        "g_pre_mix": np.ascontiguousarray(inputs["g_pre_mix"][0], np.float32),
        "g_post_mix": np.ascontiguousarray(inputs["g_post_mix"][0], np.float32),
        "g_pre_mlp": np.ascontiguousarray(inputs["g_pre_mlp"][0], np.float32),
        "g_post_mlp": np.ascontiguousarray(inputs["g_post_mlp"][0], np.float32),
        "w_in": w_in, "w_kr_sw": w_kr_sw,
        "conv_w": np.ascontiguousarray(inputs["conv_w"][0], np.float32),
        "conv_b": np.ascontiguousarray(inputs["conv_b"][0], np.float32),
        "conv_norm_g": np.ascontiguousarray(inputs["conv_norm_g"][0], np.float32),
        "conv_norm_b": np.ascontiguousarray(inputs["conv_norm_b"][0], np.float32),
        "w_conv_out": np.ascontiguousarray(inputs["w_conv_out"][0], np.float32),
        "q_norm_g": np.ascontiguousarray(inputs["q_norm_g"][0], np.float32),
        "w_uq": w_uq, "w_uq_sw": w_uq_sw,
        "kv_norm_g": np.ascontiguousarray(inputs["kv_norm_g"][0], np.float32),
        "w_ukv": np.ascontiguousarray(inputs["w_ukv"][0], np.float32),
        "w_attn_out": np.ascontiguousarray(inputs["w_attn_out"][0], np.float32),
        "w_out": np.ascontiguousarray(inputs["w_out"][0], np.float32),
        "w_mlp_in": np.ascontiguousarray(inputs["w_mlp_in"][0], np.float32),
        "w_mlp_out": np.ascontiguousarray(inputs["w_mlp_out"][0], np.float32),
    }
    in_maps = []
    for core in range(8):
        b, p = core // 2, core % 2
        xb = x[b].reshape(32, 128, D)
        pb = pos[b].reshape(32, 128)
        own = [2 * i + p for i in range(16)]
        oth = [2 * i + 1 - p for i in range(16)]
        halo = np.zeros((16, 32, D), np.float32)
        for i in range(16):
            st = own[i] * 128
            if st > 0:
                halo[i] = x[b, st - 32:st]
        m = dict(shared)
        m["x_own"] = np.ascontiguousarray(xb[own].reshape(NOWN, D))
        m["x_oth"] = np.ascontiguousarray(xb[oth].reshape(NOWN, D))
        m["x_halo"] = np.ascontiguousarray(halo.reshape(512, D))
        m["pos_own"] = np.ascontiguousarray(pb[own].reshape(NOWN))
        m["pos_oth"] = np.ascontiguousarray(pb[oth].reshape(NOWN))
        m["c"] = np.ascontiguousarray(c[b])
        m["pairmask"] = np.full((128, 128), 0.0 if p == 1 else NEG, np.float32)
        m["halomask"] = np.full((128, 1), 1.0 if p == 1 else 0.0, np.float32)
        in_maps.append(m)
    return in_maps


def kernel(**inputs):
    in_maps = _prep_inputs(inputs)
    nc = build_nc()
    res = run_bass_kernel_spmd(nc, in_maps, core_ids=list(range(8)))
    outf = np.zeros((4, 32, 128, D), np.float32)
    for core in range(8):
        b, p = core // 2, core % 2
        o = np.asarray(res.results[core]["out"]).reshape(16, 128, D)
        for i in range(16):
            outf[b, 2 * i + p] = o[i]
    return outf.reshape(4, 4096, D)
```

```python
import contextlib
import math
import numpy as np
import concourse.bass as bass
import concourse.mybir as mybir
from concourse.bass_utils import run_bass_kernel_spmd

F32 = mybir.dt.float32
BF = mybir.dt.bfloat16
I32 = mybir.dt.int32
AF = mybir.ActivationFunctionType
ALU = mybir.AluOpType

D = 1024
KC = 8
NOWN = 2048
EPS = 1e-6
NEG = -30000.0
SCALE = 1.0 / math.sqrt(192.0)
TWO_PI = 2.0 * math.pi
C1 = 6.28125
C2 = TWO_PI - 6.28125

ENGS = ("pe", "act", "dve", "pool", "sp")
NDMA = 12


class _Rec:
    def __init__(self):
        self.call = None

    def __getattr__(self, name):
        def f(*a, **k):
            self.call = (name, a, k)
            return self
        return f


def _bind(fn):
    if fn is None:
        return None
    r = _Rec()
    fn(r)
    name, a, k = r.call
    return lambda eng: getattr(eng, name)(*a, **k)


class Sched:
    def __init__(self):
        self.q = {e: [] for e in ENGS}
        self.cnt = {e: 0 for e in ENGS}
        self.waited = {e: {} for e in ENGS}
        self.state = {}
        self.dma_tot = [0] * NDMA
        self.dma_rr = 0
        self.all_dma_tokens = {}

    def _entries(self, buf, key, create):
        d = self.state.setdefault(id(buf), {})
        if key is None:
            if create and None not in d:
                d[None] = {"w": None, "r": {}}
            return list(d.values()) if not create else list(d.values())
        out = []
        if key not in d and create:
            d[key] = {"w": None, "r": {}}
        if key in d:
            out.append(d[key])
        if None in d:
            out.append(d[None])
        return out

    def _deps(self, reads, writes):
        deps = {}

        def add(tok):
            if tok is None:
                return
            s, v = tok
            if deps.get(s, 0) < v:
                deps[s] = v

        for (b, k) in reads:
            for st in self._entries(b, k, False):
                add(st["w"])
        for (b, k) in writes:
            for st in self._entries(b, k, False):
                add(st["w"])
                for s, v in st["r"].items():
                    add((s, v))
        return deps

    def _commit(self, reads, writes, tok):
        for (b, k) in reads:
            d = self.state.setdefault(id(b), {})
            if k not in d:
                d[k] = {"w": None, "r": {}}
            st = d[k]
            s, v = tok
            if st["r"].get(s, 0) < v:
                st["r"][s] = v
        for (b, k) in writes:
            d = self.state.setdefault(id(b), {})
            if k is None:
                d.clear()
            d[k] = {"w": tok, "r": {}}

    def _norm(self, lst):
        out = []
        for x in lst:
            if isinstance(x, tuple):
                out.append(x)
            else:
                out.append((x, None))
        return out

    def op(self, eng, fn, reads=(), writes=(), inc=True):
        fn = _bind(fn)
        reads = self._norm(reads)
        writes = self._norm(writes)
        deps = self._deps(reads, writes)
        waits = []
        for s, v in deps.items():
            if s == eng and eng == "pe":
                continue
            if self.waited[eng].get(s, 0) >= v:
                continue
            self.waited[eng][s] = v
            waits.append((s, v))
        if inc:
            self.cnt[eng] += 1
            tok = (eng, self.cnt[eng])
            self.q[eng].append((waits, fn, (eng, 1)))
        else:
            tok = (eng, self.cnt[eng] + 1)
            self.q[eng].append((waits, fn, None))
        self._commit(reads, writes, tok)
        return tok

    def dma(self, eng, fn, reads=(), writes=()):
        fn = _bind(fn)
        reads = self._norm(reads)
        writes = self._norm(writes)
        j = self.dma_rr
        self.dma_rr = (self.dma_rr + 1) % NDMA
        sem = "d%d" % j
        deps = self._deps(reads, writes)
        if self.dma_tot[j] > 0:
            if deps.get(sem, 0) < self.dma_tot[j]:
                deps[sem] = self.dma_tot[j]
        waits = []
        for s, v in deps.items():
            if self.waited[eng].get(s, 0) >= v:
                continue
            self.waited[eng][s] = v
            waits.append((s, v))
        self.dma_tot[j] += 16
        tok = (sem, self.dma_tot[j])
        self.q[eng].append((waits, fn, (sem, 16)))
        self._commit(reads, writes, tok)
        return tok

    def barrier(self):
        toks = [(e, self.cnt[e]) for e in ENGS if self.cnt[e] > 0]
        toks += [("d%d" % j, self.dma_tot[j]) for j in range(NDMA) if self.dma_tot[j] > 0]
        for e in ENGS:
            self.wait_all(e, [t for t in toks if t[0] != e])
        self.state = {}

    def wait_all(self, eng, toks):
        waits = []
        for (s, v) in toks:
            if self.waited[eng].get(s, 0) >= v:
                continue
            self.waited[eng][s] = v
            waits.append((s, v))
        self.q[eng].append((waits, None, None))


def build_nc(debug=None, stop=None):
    nc = bass.Bass("TRN2", target_bir_lowering=False)
    S = Sched()

    def din(name, shape, dt=F32):
        return nc.dram_tensor(name, list(shape), dt, kind="ExternalInput").ap()

    x_own = din("x_own", [NOWN, D])
    x_oth = din("x_oth", [NOWN, D])
    x_halo = din("x_halo", [512, D])
    pos_own = nc.dram_tensor("pos_own", [NOWN], I32, kind="ExternalInput")
    pos_oth = nc.dram_tensor("pos_oth", [NOWN], I32, kind="ExternalInput")
    c_in = din("c", [D])
    pairmask_in = din("pairmask", [128, 128])
    trimask_in = din("trimask", [128, 128])
    ident_in = din("ident", [128, 128])
    halomask_in = din("halomask", [128, 1])
    invf_in = din("invf", [64, 1])
    sgn_in = din("sgn", [64, 1])
    w_ada = din("w_ada", [D, 6 * D])
    b_ada = din("b_ada", [6 * D])
    g_pre_mix = din("g_pre_mix", [D])
    g_post_mix = din("g_post_mix", [D])
    g_pre_mlp = din("g_pre_mlp", [D])
    g_post_mlp = din("g_post_mlp", [D])
    w_in = din("w_in", [D, 4800])
    w_kr_sw = din("w_kr_sw", [D, 64])
    conv_w = din("conv_w", [31, D])
    conv_b = din("conv_b", [D])
    conv_norm_g = din("conv_norm_g", [D])
    conv_norm_b = din("conv_norm_b", [D])
    w_conv_out = din("w_conv_out", [D, D])
    q_norm_g = din("q_norm_g", [384])
    w_uq = din("w_uq", [384, 1536])
    w_uq_sw = din("w_uq_sw", [384, 512])
    kv_norm_g = din("kv_norm_g", [256])
    w_ukv = din("w_ukv", [256, 2048])
    w_attn_out = din("w_attn_out", [D, D])
    w_out = din("w_out", [D, D])
    w_mlp_in = din("w_mlp_in", [D, 4 * D])
    w_mlp_out = din("w_mlp_out", [4 * D, D])
    out = nc.dram_tensor("out", [NOWN, D], F32, kind="ExternalOutput").ap()
    wq = nc.dram_tensor("wq", [60, 128, 2048], BF, kind="Internal").ap()
    wq_key = object()
    dbg = None

    es = contextlib.ExitStack()
    with es:
        def sb(name, shape, dt=F32):
            return es.enter_context(nc.sbuf_tensor("s_" + name, list(shape), dt))

        sems = {}
        for e in ENGS:
            sems[e] = es.enter_context(nc.semaphore("sem_" + e))
        for j in range(NDMA):
            sems["d%d" % j] = es.enter_context(nc.semaphore("sem_d%d" % j))

        PD = [es.enter_context(nc.psum_tensor("pd%d" % i, [128, 1024], F32)) for i in range(4)]

        def bank(i):
            t = PD[i // 2]
            h = i % 2
            return t, h

        def bk(i):
            t, h = bank(i)
            return (t, h)

        def bap(i, c0=0, c1=512):
            t, h = bank(i)
            return t[:, h * 512 + c0: h * 512 + c1]

        def bap_bf(i):
            t, h = bank(i)
            return t.bitcast(BF)[:, h * 1024:(h + 1) * 1024]

        ident_f = sb("ident_f", [128, 128])
        ident_b = sb("ident_b", [128, 128], BF)
        ones_b = sb("ones_b", [128, 128], BF)
        tri_b = sb("tri_b", [128, 128], BF)
        pair_b = sb("pair_b", [128, 128], BF)
        halom = sb("halom", [128, 1])
        invf = sb("invf", [64, 1])
        sgn = sb("sgn", [64, 1])
        modT = sb("modT", [128, 48])
        vecs = sb("vecs", [128, 128])
        cwT = sb("cwT", [128, 8, 31])
        V_GPRE, V_GPOST, V_GPRE2, V_GPOST2 = 0, 8, 16, 24
        V_CB, V_CG, V_CNB = 32, 40, 48
        V_QG, V_KVG = 56, 59
        der = sb("der", [128, 48])
        NST = 2
        wst = [sb("wst%d" % i, [128, 8, 256]) for i in range(NST)]
        wbf = [sb("wbf%d" % i, [128, 8, 256], BF) for i in range(NST)]
        wrr = [0]
        xblk = [sb("xblk%d" % i, [128, D]) for i in range(2)]
        xnb = [sb("xnb%d" % i, [128, D], BF) for i in range(2)]
        small = [sb("small%d" % i, [128, 4]) for i in range(4)]
        small_rr = [0]
        tmpf = [sb("tmpf%d" % i, [128, 512]) for i in range(4)]
        tmpf_rr = [0]
        tmpb = [sb("tmpb%d" % i, [128, 512], BF) for i in range(6)]
        tmpb_rr = [0]
        rstd_t = [sb("rstd%d" % i, [128, 512]) for i in range(2)]
        rstd_rr = [0]

        def nxt(lst, rr):
            t = lst[rr[0] % len(lst)]
            rr[0] += 1
            return t

        def A(fn, **kw):
            return S.op("act", fn, **kw)

        def V(fn, **kw):
            return S.op("dve", fn, **kw)

        def G(fn, **kw):
            return S.op("pool", fn, **kw)

        def P(fn, **kw):
            return S.op("pe", fn, **kw)

        def dma_in(dst_ap, src_ap, dst_buf, key=None, eng="sp", nonc=False):
            def f(e, dst_ap=dst_ap, src_ap=src_ap):
                if nonc:
                    return e.dma_start(out=dst_ap, in_=src_ap, allow_slow_non_contiguous=True)
                return e.dma_start(out=dst_ap, in_=src_ap)
            return S.dma(eng, f, writes=[(dst_buf, key)])

        def mm(out_ap, lhsT, rhs, start, stop, reads, wkey, last):
            def f(e):
                return e.matmul(out_ap, lhsT, rhs, start=start, stop=stop)
            return S.op("pe", f, reads=reads, writes=[wkey], inc=last)

        def load_w(src, rows, c0, ncols, r0=0, cast=None):
            i = wrr[0] % NST
            wrr[0] += 1
            kc = rows // 128
            st, wb = wst[i], wbf[i]
            src_ap = src[r0:r0 + rows, c0:c0 + ncols].rearrange("(k p) c -> p k c", p=128)
            dma_in(st[:, 0:kc, 0:ncols], src_ap, st)
            if cast == "pool" or (cast is None and wrr[0] % 2 == 0):
                G(lambda e: e.tensor_copy(wb[:, 0:kc, 0:ncols], st[:, 0:kc, 0:ncols]), reads=[st], writes=[wb])
            else:
                V(lambda e: e.tensor_copy(wb[:, 0:kc, 0:ncols], st[:, 0:kc, 0:ncols]), reads=[st], writes=[wb])
            return wb

        def _unused():
            pass

        out_toks = []

        def run_phases():
            dma_in(ident_f[:], ident_in, ident_f)
            dma_in(halom[:], halomask_in, halom)
            dma_in(invf[:], invf_in, invf)
            dma_in(sgn[:], sgn_in, sgn)
            t0 = tmpf[0]
            t1 = tmpf[1]
            dma_in(t0[:, 0:128], trimask_in, t0)
            dma_in(t1[:, 0:128], pairmask_in, t1)
            V(lambda e: e.tensor_copy(ident_b[:], ident_f[:]), reads=[ident_f], writes=[ident_b])
            V(lambda e: e.memset(ones_b[:], 1.0), writes=[ones_b])
            V(lambda e: e.tensor_copy(tri_b[:], t0[:, 0:128]), reads=[t0], writes=[tri_b])
            V(lambda e: e.tensor_copy(pair_b[:], t1[:, 0:128]), reads=[t1], writes=[pair_b])
            tmpf_rr[0] = 2
            if stop == -1:
                return
            stg = tmpf[2]
            V(lambda e: e.memset(stg[:, 0:128], 0.0), writes=[stg])
            for col, src, n in ((V_GPRE, g_pre_mix, D), (V_GPOST, g_post_mix, D), (V_GPRE2, g_pre_mlp, D),
                                (V_GPOST2, g_post_mlp, D), (V_CB, conv_b, D), (V_CG, conv_norm_g, D),
                                (V_CNB, conv_norm_b, D), (V_QG, q_norm_g, 384), (V_KVG, kv_norm_g, 256)):
                dma_in(stg[col:col + n // 128, 0:128], src.rearrange("(k p) -> k p", p=128), stg)
            dma_in(stg[64:112, 0:128], b_ada.rearrange("(k p) -> k p", p=128), stg)
            dma_in(stg[112:120, 0:128], c_in.rearrange("(k p) -> k p", p=128), stg)
            P(lambda e: e.transpose(bap(1, 0, 120), stg[0:120, 0:128], ident_f[0:120, 0:120]),
              reads=[stg, ident_f], writes=[bk(1)])
            V(lambda e: e.tensor_copy(vecs[:, 0:120], bap(1, 0, 120)), reads=[bk(1)], writes=[vecs])
            if stop == -2:
                return
            badaT = vecs[:, 64:112]
            cT = vecs[:, 112:120]
            cwn = xblk[0]
            dma_in(cwn[0:31, :], conv_w, cwn)
            for c in range(8):
                P(lambda e, c=c: e.transpose(bap(2, c * 32, c * 32 + 31), cwn[0:31, c * 128:(c + 1) * 128], ident_f[0:31, 0:31]),
                  reads=[cwn, ident_f], writes=[bk(2)])
            V(lambda e: e.tensor_copy(cwT[:], bap(2, 0, 256).rearrange("p (c k) -> p c k", k=32)[:, :, 0:31]),
              reads=[bk(2)], writes=[cwT])
            if stop == -3:
                return
            scb = sb("scb", [128, 8])
            A(lambda e: e.activation(scb[:], cT, AF.Silu), reads=[vecs], writes=[scb])
            if stop == -4:
                return
            w32_eng = ["act"]

            def load_w32(src, rows, c0, ncols):
                i = wrr[0] % NST
                wrr[0] += 1
                st = wst[i]
                dma_in(st[:, 0:rows // 128, 0:ncols], src[0:rows, c0:c0 + ncols].rearrange("(k p) c -> p k c", p=128), st, eng=w32_eng[0])
                return st

            def mod_part(p0, p1, MODB, cast=None):
                for pc in range(p0, p1):
                    wb = load_w32(w_ada, D, pc * 256, 256)
                    for jj in range(2):
                        j = pc * 2 + jj
                        for k in range(8):
                            mm(bap(MODB, j, j + 1), wb[:, k, jj * 128:(jj + 1) * 128], scb[:, k:k + 1],
                               k == 0, k == 7, [wb, scb], bk(MODB), k == 7)
                V(lambda e: e.tensor_tensor(modT[:, p0 * 2:p1 * 2], bap(MODB, p0 * 2, p1 * 2), vecs[:, 64 + p0 * 2:64 + p1 * 2], ALU.add),
                  reads=[bk(MODB), vecs], writes=[(modT, p0)])

            def mod_late():
                w32_eng[0] = "sp"
                mod_part(8, 24, 7, cast="pool")
                V(lambda e: e.tensor_tensor(der[:, 16:24], modT[:, 16:24], vecs[:, V_GPOST:V_GPOST + 8], ALU.mult),
                  reads=[modT, vecs], writes=[(der, 16)])
                V(lambda e: e.scalar_tensor_tensor(der[:, 24:32], modT[:, 32:40], 1.0, vecs[:, V_GPRE2:V_GPRE2 + 8], ALU.add, ALU.mult),
                  reads=[modT, vecs], writes=[(der, 24)])
                V(lambda e: e.tensor_copy(der[:, 32:40], modT[:, 24:32]), reads=[modT], writes=[(der, 32)])
                V(lambda e: e.tensor_tensor(der[:, 40:48], modT[:, 40:48], vecs[:, V_GPOST2:V_GPOST2 + 8], ALU.mult),
                  reads=[modT, vecs], writes=[(der, 40)])
            DER_ALL = [(der, 0), (der, 8), (der, 16), (der, 24), (der, 32), (der, 40)]

            TPB = [0, 1]
            tp_rr = [0]
            xb_rr = [0]

            def xb_prep(src_rows_ap, nrows):
                i = xb_rr[0] % 2
                xb_rr[0] += 1
                xb = xblk[i]
                xn = xnb[i]
                dma_in(xb[0:nrows, :], src_rows_ap, xb)
                sm = nxt(small, small_rr)
                A(lambda e: e.activation(xn[0:nrows, :], xb[0:nrows, :], AF.Square, accum_out=sm[0:nrows, 0:1]),
                  reads=[xb], writes=[xn, (sm, 0)])
                A(lambda e: e.activation(sm[0:nrows, 3:4], sm[0:nrows, 0:1], AF.Sqrt, bias=EPS, scale=1.0 / D),
                  reads=[(sm, 0)], writes=[(sm, 3)])
                V(lambda e: e.reciprocal(sm[0:nrows, 2:3], sm[0:nrows, 3:4]), reads=[(sm, 3)], writes=[(sm, 2)])
                V(lambda e: e.tensor_scalar(xn[0:nrows, :], xb[0:nrows, :], sm[0:nrows, 2:3], None, ALU.mult),
                  reads=[xb, (sm, 2)], writes=[xn])
                return (xn, nrows)

            def xb_trans(hd, hT, col0, gsc, shc):
                xn, nrows = hd
                b = TPB[tp_rr[0] % 2]
                tp_rr[0] += 1
                tpv = bap_bf(b)
                for k in range(8):
                    P(lambda e, k=k: e.transpose(tpv[:, k * 128:k * 128 + nrows], xn[0:nrows, k * 128:(k + 1) * 128],
                                                  ident_b[0:nrows, 0:nrows]),
                      reads=[xn, ident_b], writes=[bk(b)], inc=(k == 7))
                for k in range(8):
                    if b == TPB[0]:
                        V(lambda e, k=k: e.tensor_scalar(hT[:, k, col0:col0 + nrows], tpv[:, k * 128:k * 128 + nrows],
                                                         der[:, gsc + k:gsc + k + 1], der[:, shc + k:shc + k + 1],
                                                         ALU.mult, ALU.add),
                          reads=[bk(b), (der, gsc), (der, shc)], writes=[(hT, k)])
                    else:
                        A(lambda e, k=k: e.activation(hT[:, k, col0:col0 + nrows], tpv[:, k * 128:k * 128 + nrows],
                                                      AF.Identity, bias=der[:, shc + k:shc + k + 1],
                                                      scale=der[:, gsc + k:gsc + k + 1]),
                          reads=[bk(b), (der, gsc), (der, shc)], writes=[(hT, k)])

            def x_block_to_hT(src_rows_ap, nrows, hT, col0, gsc, shc):
                xb_trans(xb_prep(src_rows_ap, nrows), hT, col0, gsc, shc)

            def rstd_from_ps(ps_bank, nfeat, ncols=512):
                r = nxt(rstd_t, rstd_rr)
                jt = nxt(tmpf, tmpf_rr)
                A(lambda e: e.activation(jt[:, 0:ncols], bap(ps_bank, 0, ncols), AF.Sqrt, bias=EPS, scale=1.0 / nfeat),
                  reads=[bk(ps_bank)], writes=[jt])
                V(lambda e: e.reciprocal(r[:, 0:ncols], jt[:, 0:ncols]), reads=[jt], writes=[r])
                return r

            hd_first = xb_prep(x_own[0:128, :], 128)
            mod_part(0, 8, 0)
            if stop == -5:
                return
            V(lambda e: e.scalar_tensor_tensor(der[:, 0:8], modT[:, 8:16], 1.0, vecs[:, V_GPRE:V_GPRE + 8], ALU.add, ALU.mult),
              reads=[modT, vecs], writes=[(der, 0)])
            V(lambda e: e.tensor_copy(der[:, 8:16], modT[:, 0:8]), reads=[modT], writes=[(der, 8)])

            if stop == 0:
                return
            oT = sb("oT", [128, 8, NOWN], BF)
            ph12 = es.enter_context(contextlib.ExitStack())
            ph1 = es.enter_context(contextlib.ExitStack())

            def sb12(name, shape, dt=F32):
                return ph12.enter_context(nc.sbuf_tensor("s_" + name, list(shape), dt))

            def sb1(name, shape, dt=F32):
                return ph1.enter_context(nc.sbuf_tensor("s_" + name, list(shape), dt))

            kvn = [sb12("kvn_own", [128, 2, NOWN], BF), sb12("kvn_oth", [128, 2, NOWN], BF)]
            krT = [sb12("kr_own", [128, NOWN], BF), sb12("kr_oth", [128, NOWN], BF)]
            for kr_ in krT:
                G(lambda e, kr_=kr_: e.memset(kr_[64:128, :], 0.0), writes=[kr_])
            qn = sb12("qn", [128, 3, NOWN], BF)
            CS = sb12("cs_own", [64, 2, NOWN])
            wukv = sb12("wukv", [128, 2, 2048], BF)
            wuq = sb12("wuq", [128, 3, 2048], BF)
            hTs = [sb1("hT%d" % i, [128, 8, 640], BF) for i in range(2)]
            wlat = sb1("wlat", [128, 8, 768], BF)
            cs_tmp = sb1("cs_tmp", [64, 2, 512])
            posi = sb1("posi", [64, 512], I32)
            angs = [sb1("ang%d" % i, [64, 512]) for i in range(3)]
            ni_t = sb1("ni_t", [64, 512], I32)

            for pc, (src, c0, n, d0) in enumerate(((w_in, 2048, 256, 0), (w_in, 2304, 256, 256), (w_in, 2560, 192, 512),
                                                   (w_kr_sw, 0, 64, 704))):
                wb = load_w(src, D, c0, n)
                G(lambda e, wb=wb, n=n, d0=d0: e.tensor_copy(wlat[:, :, d0:d0 + n], wb[:, :, 0:n]),
                  reads=[wb], writes=[(wlat, pc)])
            WL = [(wlat, i) for i in range(4)]
            if stop == 10:
                return

            def rope_tables(pos_t, c0, dst, dcol):
                src = bass.AP(pos_t, c0, [[0, 64], [1, 512]])
                dma_in(posi[:], src, posi)
                a0, a1, a2 = angs
                a3 = posi.bitcast(F32)
                V(lambda e: e.tensor_copy(a0[:], posi[:]), reads=[posi], writes=[a0])
                V(lambda e: e.tensor_scalar(a0[:], a0[:], invf[:, 0:1], None, ALU.mult), reads=[a0, invf], writes=[a0])
                V(lambda e: e.tensor_scalar(a1[:], a0[:], 1.0 / TWO_PI, None, ALU.mult), reads=[a0], writes=[a1])
                V(lambda e: e.tensor_copy(ni_t[:], a1[:]), reads=[a1], writes=[ni_t])
                V(lambda e: e.tensor_copy(a1[:], ni_t[:]), reads=[ni_t], writes=[a1])
                V(lambda e: e.scalar_tensor_tensor(a2[:], a1[:], -C1, a0[:], ALU.mult, ALU.add), reads=[a1, a0], writes=[a2])
                V(lambda e: e.scalar_tensor_tensor(a2[:], a1[:], -C2, a2[:], ALU.mult, ALU.add), reads=[a1, a2], writes=[a2])
                V(lambda e: e.tensor_scalar(a3[:], a2[:], math.pi, -TWO_PI, ALU.is_gt, ALU.mult), reads=[a2], writes=[posi])
                V(lambda e: e.tensor_tensor(a2[:], a2[:], a3[:], ALU.add), reads=[a2, posi], writes=[a2])
                V(lambda e: e.tensor_scalar(a3[:], a2[:], -math.pi, TWO_PI, ALU.is_lt, ALU.mult), reads=[a2], writes=[posi])
                V(lambda e: e.tensor_tensor(a2[:], a2[:], a3[:], ALU.add), reads=[a2, posi], writes=[a2])
                V(lambda e: e.tensor_scalar(a1[:], a2[:], math.pi / 2, None, ALU.add), reads=[a2], writes=[a1])
                V(lambda e: e.tensor_scalar(a3[:], a1[:], math.pi, -TWO_PI, ALU.is_gt, ALU.mult), reads=[a1], writes=[posi])
                V(lambda e: e.tensor_tensor(a1[:], a1[:], a3[:], ALU.add), reads=[a1, posi], writes=[a1])
                V(lambda e: e.tensor_scalar(a1[:], a1[:], math.pi, -math.pi, ALU.min, ALU.max), reads=[a1], writes=[a1])
                V(lambda e: e.tensor_scalar(a2[:], a2[:], math.pi, -math.pi, ALU.min, ALU.max), reads=[a2], writes=[a2])
                A(lambda e: e.activation(dst[:, 0, dcol:dcol + 512], a1[:], AF.Sin), reads=[a1], writes=[(dst, dcol)])
                A(lambda e: e.activation(dst[:, 1, dcol:dcol + 512], a2[:], AF.Sin, scale=sgn[:, 0:1]),
                  reads=[a2, sgn], writes=[(dst, dcol)])

            def load_wbig(src, rows, c0, ncols):
                i = wrr[0] % NST
                wrr[0] += 1
                kc = rows // 128
                st, wb = wst[i], wbf[i]
                stv = st[:].rearrange("p k c -> p (k c)")[:, 0:kc * ncols].rearrange("p (k c) -> p k c", c=ncols)
                wbv = wb[:].rearrange("p k c -> p (k c)")[:, 0:kc * ncols].rearrange("p (k c) -> p k c", c=ncols)
                dma_in(stv, src[0:rows, c0:c0 + ncols].rearrange("(k p) c -> p k c", p=128), st, eng="act")
                G(lambda e: e.tensor_copy(wbv, stv), reads=[st], writes=[wb])
                return wb, wbv

            def _job_uq(pc):
                def f():
                    wb, wbv = load_wbig(w_uq, 384, pc * 512, 512)
                    G(lambda e: e.tensor_copy(wuq[:, :, pc * 512:(pc + 1) * 512], wbv), reads=[wb], writes=[(wuq, pc)])
                return f

            def _job_uqsw():
                wb, wbv = load_wbig(w_uq_sw, 384, 0, 512)
                G(lambda e: e.tensor_copy(wuq[:, :, 1536:2048], wbv), reads=[wb], writes=[(wuq, 3)])

            def _job_ukv(pc):
                def f():
                    wb, wbv = load_wbig(w_ukv, 256, pc * 1024, 1024)
                    G(lambda e: e.tensor_copy(wukv[:, :, pc * 1024:(pc + 1) * 1024], wbv), reads=[wb], writes=[(wukv, pc)])
                return f

            wjobs = [_job_uq(0), _job_uq(1), _job_uq(2), _job_uqsw, _job_ukv(0), _job_ukv(1)]

            def prep1(j_):
                i_, b_ = j_ // 4, j_ % 4
                grp_, t_ = i_ // 4, i_ % 4
                xsrc = x_own if grp_ == 0 else x_oth
                r0 = t_ * 512 + b_ * 128
                return xb_prep(xsrc[r0:r0 + 128, :], 128)

            hd1 = [hd_first]
            for grp in range(2):
                pos_t = pos_own if grp == 0 else pos_oth
                for t in range(4):
                    hT = hTs[(grp * 4 + t) % 2]
                    for b in range(4):
                        j_ = (grp * 4 + t) * 4 + b
                        nh = prep1(j_ + 1) if j_ + 1 < 32 else None
                        xb_trans(hd1[0], hT, b * 128, 0, 8)
                        hd1[0] = nh
                    if grp * 4 + t >= 2 and wjobs:
                        wjobs.pop(0)()
                    if grp == 0:
                        rope_tables(pos_t, t * 512, CS, t * 512)
                        cs, cc = CS, t * 512
                    else:
                        rope_tables(pos_t, t * 512, cs_tmp, 0)
                        cs, cc = cs_tmp, 0
                    if stop == 12:
                        return
                    hk = [(hT, k) for k in range(8)]
                    for m in range(2):
                        for k in range(8):
                            mm(bap(2 + m), wlat[:, k, 384 + m * 128:384 + (m + 1) * 128], hT[:, k, 0:512],
                               k == 0, k == 7, hk + WL, bk(2 + m), k == 7)
                    for m in range(2):
                        for k in range(8):
                            mm(bap(5 + m)[0:64, :], wlat[:, k, 640 + m * 64:640 + (m + 1) * 64], hT[:, k, 0:512],
                               k == 0, k == 7, hk + WL, bk(5 + m), k == 7)
                    sq = []
                    for m in range(2):
                        s_ = nxt(tmpb, tmpb_rr)
                        A(lambda e, m=m, s_=s_: e.activation(s_[:], bap(2 + m), AF.Square), reads=[bk(2 + m)], writes=[s_])
                        sq.append(s_)
                    for m in range(2):
                        mm(bap(4), ones_b[:], sq[m][:], m == 0, m == 1, [ones_b, sq[m]], bk(4), True)
                    r = rstd_from_ps(4, 256)
                    for m in range(2):
                        V(lambda e, m=m, r=r: e.scalar_tensor_tensor(kvn[grp][:, m, t * 512:(t + 1) * 512], bap(2 + m),
                                                                     vecs[:, V_KVG + m:V_KVG + m + 1], r[:], ALU.mult, ALU.mult),
                          reads=[bk(2 + m), r, vecs], writes=[(kvn[grp], t)])
                    ta = nxt(tmpf, tmpf_rr)
                    tb_ = nxt(tmpf, tmpf_rr)
                    V(lambda e, ta=ta, cs=cs, cc=cc: e.tensor_tensor(ta[0:64, :], bap(5)[0:64, :], cs[:, 0, cc:cc + 512], ALU.mult),
                      reads=[bk(5), (cs, cc)], writes=[ta])
                    V(lambda e, tb_=tb_, cs=cs, cc=cc: e.tensor_tensor(tb_[0:64, :], bap(6)[0:64, :], cs[:, 1, cc:cc + 512], ALU.mult),
                      reads=[bk(6), (cs, cc)], writes=[tb_])
                    V(lambda e, ta=ta, tb_=tb_: e.tensor_tensor(krT[grp][0:64, t * 512:(t + 1) * 512], ta[0:64, :], tb_[0:64, :], ALU.add),
                      reads=[ta, tb_], writes=[(krT[grp], t)])
                    if stop == 13:
                        return
                    if grp == 0:
                        QB = [2, 3, 7]
                        for m in range(3):
                            for k in range(8):
                                mm(bap(QB[m]), wlat[:, k, m * 128:(m + 1) * 128], hT[:, k, 0:512],
                                   k == 0, k == 7, hk + WL, bk(QB[m]), k == 7)
                        sq = []
                        for m in range(3):
                            s_ = nxt(tmpb, tmpb_rr)
                            A(lambda e, m=m, s_=s_: e.activation(s_[:], bap(QB[m]), AF.Square), reads=[bk(QB[m])], writes=[s_])
                            sq.append(s_)
                        for m in range(3):
                            mm(bap(4), ones_b[:], sq[m][:], m == 0, m == 2, [ones_b, sq[m]], bk(4), True)
                        r = rstd_from_ps(4, 384)
                        for m in range(3):
                            V(lambda e, m=m, r=r: e.scalar_tensor_tensor(qn[:, m, t * 512:(t + 1) * 512], bap(QB[m]),
                                                                         vecs[:, V_QG + m:V_QG + m + 1], r[:], ALU.mult, ALU.mult),
                              reads=[bk(QB[m]), r, vecs], writes=[(qn, t)])
                    if stop == 14:
                        return

            while wjobs:
                wjobs.pop(0)()
            S.barrier()
            ph1.close()
            if stop == 1:
                ph12.close()
                return

            KhT = [[sb12("kh%d_%d" % (i, g), [128, NOWN], BF) for g in range(2)] for i in range(1)]
            Vh = [[sb12("vh%d_%d" % (i, g), [128, 16, 128], BF) for g in range(2)] for i in range(1)]
            Qh = [sb12("qh%d" % i, [128, NOWN], BF) for i in range(1)]
            Qr = [sb12("qr%d" % i, [128, NOWN], BF) for i in range(1)]
            G(lambda e: e.memset(Qr[0][64:128, :], 0.0), writes=[Qr[0]])
            Pt = [sb12("pt%d" % i, [128, 512], BF) for i in range(4)]
            pt_rr = [0]
            SB_ = [0, 1, 2]
            s_rr = [0]
            OB = [3, 5]
            LB = [4, 6]
            HBS = [7, 3, 4]
            hb_rr = [0]

            def nhb():
                b_ = HBS[hb_rr[0] % 3]
                hb_rr[0] += 1
                return b_
            evac_rr = [0]

            def evac_copy(dst_ap, src_bank_ap, reads, writes):
                if evac_rr[0] % 2 == 0:
                    V(lambda e: e.tensor_copy(dst_ap, src_bank_ap), reads=reads, writes=writes)
                else:
                    A(lambda e: e.activation(dst_ap, src_bank_ap, AF.Copy), reads=reads, writes=writes)
                evac_rr[0] += 1

            def build_head(h):
                i = 0
                for t in range(4):
                    HB = nhb()
                    for k in range(3):
                        mm(bap(HB)[0:64, :], wuq[:, k, h * 192 + 128:h * 192 + 192], qn[:, k, t * 512:(t + 1) * 512],
                           k == 0, k == 2, [wuq, qn], bk(HB), k == 2)
                    ta = nxt(tmpf, tmpf_rr)
                    V(lambda e, ta=ta, t=t: e.tensor_tensor(ta[0:64, :], bap(HB)[0:64, :], CS[:, 0, t * 512:(t + 1) * 512], ALU.mult),
                      reads=[bk(HB), CS], writes=[ta])
                    HB = nhb()
                    for k in range(3):
                        mm(bap(HB)[0:64, :], wuq[:, k, 1536 + h * 64:1536 + (h + 1) * 64], qn[:, k, t * 512:(t + 1) * 512],
                           k == 0, k == 2, [wuq, qn], bk(HB), k == 2)
                    tb_ = nxt(tmpf, tmpf_rr)
                    V(lambda e, tb_=tb_, t=t: e.tensor_tensor(tb_[0:64, :], bap(HB)[0:64, :], CS[:, 1, t * 512:(t + 1) * 512], ALU.mult),
                      reads=[bk(HB), CS], writes=[tb_])
                    V(lambda e, ta=ta, tb_=tb_, t=t: e.tensor_tensor(Qr[i][0:64, t * 512:(t + 1) * 512], ta[0:64, :], tb_[0:64, :], ALU.add),
                      reads=[ta, tb_], writes=[(Qr[i], t)])
                for t in range(4):
                    HB = nhb()
                    for k in range(3):
                        mm(bap(HB), wuq[:, k, h * 192:h * 192 + 128], qn[:, k, t * 512:(t + 1) * 512],
                           k == 0, k == 2, [wuq, qn], bk(HB), k == 2)
                    evac_copy(Qh[i][:, t * 512:(t + 1) * 512], bap(HB), [bk(HB)], [(Qh[i], t)])
                for grp in range(2):
                    for t in range(4):
                        HB = nhb()
                        for k in range(2):
                            mm(bap(HB), wukv[:, k, h * 256:h * 256 + 128], kvn[grp][:, k, t * 512:(t + 1) * 512],
                               k == 0, k == 1, [wukv, kvn[grp]], bk(HB), k == 1)
                        evac_copy(KhT[i][grp][:, t * 512:(t + 1) * 512], bap(HB), [bk(HB)], [(KhT[i][grp], t)])
                    for t in range(4):
                        HB = nhb()
                        for b in range(4):
                            blk = t * 4 + b
                            for k in range(2):
                                mm(bap(HB, b * 128, (b + 1) * 128), kvn[grp][:, k, blk * 128:(blk + 1) * 128],
                                   wukv[:, k, h * 256 + 128:h * 256 + 256],
                                   k == 0, k == 1, [wukv, kvn[grp]], bk(HB), (k == 1 and b == 3))
                        evac_copy(Vh[i][grp][:, t * 4:(t + 1) * 4, :], bap(HB).rearrange("p (b d) -> p b d", d=128),
                                  [bk(HB)], [(Vh[i][grp], t)])

            def attend_head(h):
                i = 0
                for g in range(4):
                    ob = OB[g % 2]
                    lb = LB[g % 2]
                    visits = [(J, grp) for J in range(4 * g + 4) for grp in range(2)]
                    pend = []

                    def do_pv(v, first, last):
                        J, grp, c0, pt = v
                        mm(bap(ob, c0, 512), Vh[i][grp][:, J, :], pt[:, c0:512], first, last,
                           [Vh[i][grp], pt], bk(ob), True)
                        mm(bap(lb, c0, 512), ones_b[:], pt[:, c0:512], first, last,
                           [ones_b, pt], bk(lb), True)

                    npv = [0]
                    for vi, (J, grp) in enumerate(visits):
                        j = J - 4 * g
                        c0 = 128 * max(j, 0)
                        sbk = SB_[s_rr[0] % 3]
                        s_rr[0] += 1
                        q0 = g * 512 + c0
                        q1 = (g + 1) * 512
                        masked = j >= 0
                        mm(bap(sbk, c0, 512), KhT[i][grp][:, J * 128:(J + 1) * 128], Qh[i][:, q0:q1],
                           True, False, [KhT[i][grp], Qh[i]], bk(sbk), False)
                        mm(bap(sbk, c0, 512), krT[grp][:, J * 128:(J + 1) * 128], Qr[i][:, q0:q1],
                           False, not masked, [krT[grp], Qr[i]], bk(sbk), not masked)
                        if masked:
                            mk = tri_b if grp == 0 else pair_b
                            mm(bap(sbk, c0, c0 + 128), ident_b[:], mk[:], False, True, [ident_b, mk], bk(sbk), True)
                        pt = nxt(Pt, pt_rr)
                        A(lambda e, pt=pt, sbk=sbk, c0=c0: e.activation(pt[:, c0:512], bap(sbk, c0, 512), AF.Exp, scale=SCALE),
                          reads=[bk(sbk)], writes=[pt])
                        pend.append((J, grp, c0, pt))
                        if len(pend) > 2:
                            v = pend.pop(0)
                            do_pv(v, npv[0] == 0, False)
                            npv[0] += 1
                    while pend:
                        v = pend.pop(0)
                        do_pv(v, npv[0] == 0, len(pend) == 0)
                        npv[0] += 1
                    rl = nxt(rstd_t, rstd_rr)
                    V(lambda e, rl=rl, lb=lb: e.reciprocal(rl[:], bap(lb)), reads=[bk(lb)], writes=[rl])
                    V(lambda e, rl=rl, ob=ob, g=g: e.tensor_tensor(oT[:, h, g * 512:(g + 1) * 512], bap(ob), rl[:], ALU.mult),
                      reads=[bk(ob), rl], writes=[(oT, (h, g))])

            prep = []
            for pc in range(4):
                prep += [(w_in, pc * 256, 0), (w_in, 1024 + pc * 256, 0)]
            for pc in range(4):
                prep += [(w_conv_out, pc * 256, 0)]
            for pc in range(4):
                prep += [(w_attn_out, pc * 256, 0), (w_in, 2752 + pc * 256, 0), (w_in, 3776 + pc * 256, 0)]
            for pc in range(4):
                prep += [(w_out, pc * 256, 0)]
            for pc in range(16):
                prep += [(w_mlp_in, pc * 256, 0)]
            for pc in range(4):
                for rr_ in range(4):
                    prep += [(w_mlp_out, pc * 256, rr_ * 1024)]
            prep_tok = {}

            def do_prep():
                prev = None
                for i, (src, c0, r0) in enumerate(prep):
                    wb = load_w(src, D, c0, 256, r0=r0, cast="pool")
                    if prev is not None:
                        pw, pi = prev
                        S.dma("sp", lambda e, pw=pw, pi=pi: e.dma_start(out=wq[pi], in_=pw[:].rearrange("p k c -> p (k c)")),
                              reads=[pw], writes=[(wq_key, pi)])
                    prev = (wb, i)
                pw, pi = prev
                S.dma("sp", lambda e: e.dma_start(out=wq[pi], in_=pw[:].rearrange("p k c -> p (k c)")),
                      reads=[pw], writes=[(wq_key, pi)])

            do_prep()
            build_head(0)
            for h in range(8):
                attend_head(h)
                if h + 1 < 8:
                    build_head(h + 1)
            mod_late()

            S.barrier()
            ph12.close()
            if stop == 2:
                return
            wpool = [wbf[0][:], wbf[1][:]]
            for i_ in range(NST):
                fl = wst[i_].bitcast(BF)[:].rearrange("p k c -> p (k c)")
                wpool += [fl[:, 0:2048].rearrange("p (k c) -> p k c", c=256), fl[:, 2048:4096].rearrange("p (k c) -> p k c", c=256)]
            wp_rr = [0]
            pidx = {(src_.tensor.name, c0_, r0_): i_ for i_, (src_, c0_, r0_) in enumerate(prep)}

            def load_wq(src, rows, c0, ncols, r0=0):
                i = pidx[(src.tensor.name, c0, r0)]
                buf = wpool[wp_rr[0] % len(wpool)]
                wp_rr[0] += 1
                S.dma("sp", lambda e: e.dma_start(out=buf.rearrange("p k c -> p (k c)"), in_=wq[i]), writes=[buf])
                return buf

            xT = sb("xT", [128, 8, 512])
            yT = sb("yT", [128, 8, 512])
            hTe = sb("hTe", [128, 8, 640], BF)
            uext = sb("uext", [128, 8, 640], BF)
            arena = sb("arena", [128, 16384], BF)
            hid = arena[:, :].rearrange("p (j t) -> p j t", t=512)
            ucv = arena.bitcast(F32)[:, 0:4096].rearrange("p (c t) -> p c t", t=512)
            diag = [arena[:, 8192 + i * 3968:8192 + (i + 1) * 3968].rearrange("p (k m) -> p k m", m=128) for i in range(2)]
            sh8 = sb("sh8", [128, 8, 512], BF)
            actT = sh8
            mT = sh8
            h2T = sh8
            yaT = sb("yaT", [128, 8, 512], BF)
            oblk = xblk
            stat_s = sb("stat_s", [128, 512])
            stat_n = sb("stat_n", [128, 512])
            sb_sig = sb("sb_sig", [128, 640])

            deferred = []

            def flush_def():
                for f_ in deferred:
                    f_()
                deferred.clear()

            def stats_accum(ps_b, src_ap, reads, idx, n, defer=False):
                s_ = nxt(tmpb, tmpb_rr)
                A(lambda e: e.activation(s_[:], src_ap, AF.Square), reads=reads, writes=[s_])
                f_ = lambda s_=s_: mm(bap(ps_b), ones_b[:], s_[:], idx == 0, idx == n - 1, [ones_b, s_], bk(ps_b), True)
                if defer:
                    deferred.append(f_)
                else:
                    f_()

            def fh_blocks(g_):
                lst = []
                for b in range(4):
                    blk = g_ * 4 + b
                    lst.append((x_halo[blk * 32:(blk + 1) * 32, :], 32, b * 160))
                    lst.append((x_own[blk * 128:(blk + 1) * 128, :], 128, b * 160 + 32))
                return lst

            def front_h(g_):
                lst = fh_blocks(g_)
                hd = xb_prep(lst[0][0], lst[0][1])
                for i_ in range(8):
                    nh = xb_prep(lst[i_ + 1][0], lst[i_ + 1][1]) if i_ + 1 < 8 else None
                    xb_trans(hd, hTe, lst[i_][2], 0, 8)
                    hd = nh

            front_h(0)
            for g in range(4):
                for b in range(4):
                    blk = g * 4 + b
                    xb = xblk[xb_rr[0] % 2]
                    xb_rr[0] += 1
                    dma_in(xb[:], x_own[blk * 128:(blk + 1) * 128, :], xb)
                    for half in range(2):
                        tb_i = 2 + half
                        for kk in range(4):
                            k = half * 4 + kk
                            P(lambda e, k=k, kk=kk, tb_i=tb_i, xb=xb: e.transpose(bap(tb_i, kk * 128, (kk + 1) * 128),
                                                                                    xb[:, k * 128:(k + 1) * 128], ident_f[:]),
                              reads=[xb, ident_f], writes=[bk(tb_i)], inc=(kk == 3))
                        evac_copy(xT[:, half * 4:(half + 1) * 4, b * 128:(b + 1) * 128],
                                  bap(tb_i).rearrange("p (k t) -> p k t", t=128), [bk(tb_i)], [(xT, (half, b))])
                hk = [(hTe, k) for k in range(8)]
                def glu_chunk(c, wa, wb2, cc):
                    for (wt, d) in ((wa, 0), (wb2, 1)):
                        for (n0, n1, hb) in ((0, 512, 0), (512, 640, 1)):
                            for k in range(8):
                                mm(PD[d][:, hb * 512:hb * 512 + (n1 - n0)], wt[:, k, cc * 128:(cc + 1) * 128],
                                   hTe[:, k, n0:n1], k == 0, k == 7, hk + [wt], (PD[d], hb), k == 7)
                    sg = sb_sig
                    A(lambda e: e.activation(sg[:, 0:640], PD[1][:, 0:640], AF.Sigmoid),
                      reads=[(PD[1], 0), (PD[1], 1)], writes=[sg])
                    V(lambda e: e.tensor_tensor(uext[:, c, :], PD[0][:, 0:640], sg[:, 0:640], ALU.mult),
                      reads=[(PD[0], 0), (PD[0], 1), sg], writes=[(uext, c)])
                    if g == 0:
                        V(lambda e: e.tensor_scalar(uext[:, c, 0:32], uext[:, c, 0:32], halom[:, 0:1], None, ALU.mult),
                          reads=[(uext, c), halom], writes=[(uext, c)])

                def conv_chunk(c):
                    dg = diag[c % 2]
                    for k in range(31):
                        G(lambda e: e.tensor_scalar(dg[:, k, :], ident_b[:], cwT[:, c, k:k + 1], 1.0, ALU.mult, ALU.mult),
                          reads=[ident_b, cwT], writes=[(arena, None) if (c == 0 and k == 0) else (arena, ("d", c % 2, k))])
                    uv = uext[:, c, :].rearrange("p (b w) -> p b w", w=160)
                    cb = 4 + (c % 2)
                    for k in range(31):
                        mm(bap(cb).rearrange("p (b w) -> p b w", w=128), dg[:, k, :], uv[:, :, 2 + k:2 + k + 128],
                           k == 0, k == 30, [(arena, ("d", c % 2, k)), (uext, c)], bk(cb), k == 30)
                    flush_def()
                    A(lambda e: e.activation(ucv[:, c, :], bap(cb), AF.Identity, bias=vecs[:, V_CB + c:V_CB + c + 1]),
                      reads=[bk(cb), vecs], writes=[(arena, ("u", c))])
                    ub_ = nxt(tmpb, tmpb_rr)
                    V(lambda e: e.tensor_copy(ub_[:], ucv[:, c, :]), reads=[(arena, ("u", c))], writes=[ub_])
                    deferred.append(lambda ub_=ub_, c=c: mm(bap(6), ones_b[:], ub_[:], c == 0, c == 7, [ones_b, ub_], bk(6), True))
                    stats_accum(7, ucv[:, c, :], [(arena, ("u", c))], c, 8, defer=True)

                for pc in range(4):
                    wa = load_wq(w_in, D, pc * 256, 256)
                    wb2 = load_wq(w_in, D, 1024 + pc * 256, 256)
                    for cc in range(2):
                        c = pc * 2 + cc
                        glu_chunk(c, wa, wb2, cc)
                        if c >= 1:
                            conv_chunk(c - 1)
                conv_chunk(7)
                flush_def()
                mean = stat_s
                nmr = stat_n
                A(lambda e: e.activation(mean[:], bap(6), AF.Copy, scale=1.0 / D), reads=[bk(6)], writes=[mean])
                jt = nxt(tmpf, tmpf_rr)
                V(lambda e, jt=jt: e.tensor_tensor(jt[:], mean[:], mean[:], ALU.mult), reads=[mean], writes=[jt])
                jt2 = nxt(tmpf, tmpf_rr)
                V(lambda e, jt=jt, jt2=jt2: e.scalar_tensor_tensor(jt2[:], bap(7), 1.0 / D, jt[:], ALU.mult, ALU.subtract),
                  reads=[bk(7), jt], writes=[jt2])
                V(lambda e, jt2=jt2: e.tensor_scalar(jt2[:], jt2[:], 0.0, None, ALU.max), reads=[jt2], writes=[jt2])
                A(lambda e, jt=jt, jt2=jt2: e.activation(jt[:], jt2[:], AF.Sqrt, bias=EPS), reads=[jt2], writes=[jt])
                rln = nxt(rstd_t, rstd_rr)
                V(lambda e, jt=jt, rln=rln: e.reciprocal(rln[:], jt[:]), reads=[jt], writes=[rln])
                V(lambda e, rln=rln: e.scalar_tensor_tensor(nmr[:], mean[:], -1.0, rln[:], ALU.mult, ALU.mult),
                  reads=[mean, rln], writes=[nmr])
                for c in range(8):
                    jt = nxt(tmpf, tmpf_rr)
                    V(lambda e, c=c, jt=jt, rln=rln: e.tensor_tensor(jt[:], ucv[:, c, :], rln[:], ALU.mult),
                      reads=[(arena, ("u", c)), rln], writes=[jt])
                    V(lambda e, jt=jt: e.tensor_tensor(jt[:], jt[:], nmr[:], ALU.add), reads=[jt, nmr], writes=[jt])
                    A(lambda e, c=c, jt=jt: e.activation(actT[:, c, :], jt[:], AF.Silu,
                                                         bias=vecs[:, V_CNB + c:V_CNB + c + 1], scale=vecs[:, V_CG + c:V_CG + c + 1]),
                      reads=[jt, vecs, vecs], writes=[(actT, c)])
                ak = [(actT, k) for k in range(8)]
                for pc in range(4):
                    wb = load_wq(w_conv_out, D, pc * 256, 256)
                    for cc in range(2):
                        m = pc * 2 + cc
                        bb = 4 + (m % 2)
                        for k in range(8):
                            mm(bap(bb), wb[:, k, cc * 128:(cc + 1) * 128], actT[:, k, :], k == 0, k == 7, ak + [wb], bk(bb), k == 7)
                        evac_copy(yaT[:, m, :], bap(bb), [bk(bb)], [(yaT, m)])
                hown = lambda k: hTe[:, k, :].rearrange("p (b w) -> p b w", w=160)[:, :, 32:160]
                for pc in range(4):
                    wao = load_wq(w_attn_out, D, pc * 256, 256)
                    for cc in range(2):
                        for hh in range(8):
                            mm(bap(2 + cc), wao[:, hh, cc * 128:(cc + 1) * 128], oT[:, hh, g * 512:(g + 1) * 512],
                               hh == 0, hh == 7, [oT, wao], bk(2 + cc), hh == 7)
                    wga = load_wq(w_in, D, 2752 + pc * 256, 256)
                    sas = []
                    for cc in range(2):
                        m = pc * 2 + cc
                        for k in range(8):
                            mm(bap(cc).rearrange("p (b w) -> p b w", w=128), wga[:, k, cc * 128:(cc + 1) * 128], hown(k),
                               k == 0, k == 7, hk + [wga], bk(cc), k == 7)
                        sa = nxt(tmpf, tmpf_rr)
                        A(lambda e, sa=sa, cc=cc: e.activation(sa[:], bap(cc), AF.Sigmoid), reads=[bk(cc)], writes=[sa])
                        V(lambda e, sa=sa, m=m: e.tensor_tensor(sa[:], sa[:], yaT[:, m, :], ALU.mult),
                          reads=[sa, (yaT, m)], writes=[sa])
                        sas.append(sa)
                    wgb = load_wq(w_in, D, 3776 + pc * 256, 256)
                    for cc in range(2):
                        m = pc * 2 + cc
                        for k in range(8):
                            mm(bap(cc).rearrange("p (b w) -> p b w", w=128), wgb[:, k, cc * 128:(cc + 1) * 128], hown(k),
                               k == 0, k == 7, hk + [wgb], bk(cc), k == 7)
                        sb2 = nxt(tmpf, tmpf_rr)
                        A(lambda e, sb2=sb2, cc=cc: e.activation(sb2[:], bap(cc), AF.Sigmoid), reads=[bk(cc)], writes=[sb2])
                        V(lambda e, sb2=sb2, cc=cc: e.tensor_tensor(sb2[:], bap(2 + cc), sb2[:], ALU.mult),
                          reads=[bk(2 + cc), sb2], writes=[sb2])
                        V(lambda e, sa=sas[cc], sb2=sb2, m=m: e.tensor_tensor(mT[:, m, :], sa[:], sb2[:], ALU.add),
                          reads=[sas[cc], sb2], writes=[(mT, m)])
                mk_ = [(mT, k) for k in range(8)]
                for pc in range(4):
                    wb = load_wq(w_out, D, pc * 256, 256)
                    for cc in range(2):
                        m = pc * 2 + cc
                        bb = 4 + (m % 2)
                        for k in range(8):
                            mm(bap(bb), wb[:, k, cc * 128:(cc + 1) * 128], mT[:, k, :], k == 0, k == 7, mk_ + [wb], bk(bb), k == 7)
                        flush_def()
                        A(lambda e, m=m, bb=bb: e.activation(yT[:, m, :], bap(bb), AF.Copy), reads=[bk(bb)], writes=[(yT, m)])
                        stats_accum(6, yT[:, m, :], [(yT, m)], m, 8, defer=True)
                flush_def()
                if debug == "m" and g == 3:
                    V(lambda e: e.tensor_copy(hTe[:, :, 0:512], mT[:]), reads=[mT], writes=[hTe])
                    V(lambda e: e.tensor_copy(uext[:, :, 0:512], yT[:]), reads=[yT], writes=[uext])
                r1 = rstd_from_ps(6, D)
                for k in range(8):
                    jt = nxt(tmpf, tmpf_rr)
                    V(lambda e, k=k, jt=jt, r1=r1: e.tensor_tensor(jt[:], yT[:, k, :], r1[:], ALU.mult),
                      reads=[(yT, k), r1], writes=[jt])
                    V(lambda e, k=k, jt=jt: e.scalar_tensor_tensor(xT[:, k, :], jt[:], der[:, 16 + k:17 + k], xT[:, k, :], ALU.mult, ALU.add),
                      reads=[jt, (der, 16), xT], writes=[xT])
                    stats_accum(7, xT[:, k, :], [xT], k, 8)
                r2 = rstd_from_ps(7, D)
                for k in range(8):
                    jt = nxt(tmpf, tmpf_rr)
                    V(lambda e, k=k, jt=jt, r2=r2: e.scalar_tensor_tensor(jt[:], xT[:, k, :], der[:, 24 + k:25 + k], r2[:], ALU.mult, ALU.mult),
                      reads=[xT, (der, 24), r2], writes=[jt])
                    A(lambda e, k=k, jt=jt: e.activation(h2T[:, k, :], jt[:], AF.Identity, bias=der[:, 32 + k:33 + k]),
                      reads=[jt, (der, 32)], writes=[(h2T, k)])
                h2k = [(h2T, k) for k in range(8)]
                nlst = fh_blocks(g + 1) if g + 1 < 4 else None
                nhd = {}
                for pc in range(16):
                    if nlst is not None and pc % 2 == 0:
                        if pc >= 2:
                            xb_trans(nhd[pc // 2 - 1], hTe, nlst[pc // 2 - 1][2], 0, 8)
                        nhd[pc // 2] = xb_prep(nlst[pc // 2][0], nlst[pc // 2][1])
                    wb = load_wq(w_mlp_in, D, pc * 256, 256)
                    for cc in range(2):
                        j = pc * 2 + cc
                        bb = 4 + (j % 4)
                        for k in range(8):
                            mm(bap(bb), wb[:, k, cc * 128:(cc + 1) * 128], h2T[:, k, :], k == 0, k == 7, h2k + [wb], bk(bb), k == 7)
                        jt = nxt(tmpf, tmpf_rr)
                        A(lambda e, jt=jt, bb=bb: e.activation(jt[:], bap(bb), AF.Relu), reads=[bk(bb)], writes=[jt])
                        V(lambda e, jt=jt, j=j: e.tensor_tensor(hid[:, j, :], jt[:], jt[:], ALU.mult), reads=[jt],
                          writes=[(arena, None) if j == 0 else (arena, ("h", j))])
                if nlst is not None:
                    xb_trans(nhd[7], hTe, nlst[7][2], 0, 8)
                for pc in range(4):
                    for cc in range(2):
                        pass
                    wbs = []
                    for rr_ in range(4):
                        wb = load_wq(w_mlp_out, D, pc * 256, 256, r0=rr_ * 1024)
                        for cc in range(2):
                            bb = 4 + cc
                            for k in range(8):
                                kk = rr_ * 8 + k
                                mm(bap(bb), wb[:, k, cc * 128:(cc + 1) * 128], hid[:, kk, :], kk == 0, kk == 31,
                                   [arena, wb], bk(bb), (k == 7))
                    flush_def()
                    for cc in range(2):
                        m = pc * 2 + cc
                        bb = 4 + cc
                        A(lambda e, m=m, bb=bb: e.activation(yT[:, m, :], bap(bb), AF.Copy), reads=[bk(bb)], writes=[(yT, m)])
                        stats_accum(6, yT[:, m, :], [(yT, m)], m, 8, defer=True)
                flush_def()
                r3 = rstd_from_ps(6, D)
                for k in range(8):
                    jt = nxt(tmpf, tmpf_rr)
                    V(lambda e, k=k, jt=jt, r3=r3: e.tensor_tensor(jt[:], yT[:, k, :], r3[:], ALU.mult),
                      reads=[(yT, k), r3], writes=[jt])
                    V(lambda e, k=k, jt=jt: e.scalar_tensor_tensor(yT[:, k, :], jt[:], der[:, 40 + k:41 + k], xT[:, k, :], ALU.mult, ALU.add),
                      reads=[jt, (der, 40), xT], writes=[(yT, k)])
                for b in range(4):
                    ob_ = oblk[b % 2]
                    for half in range(2):
                        tb_i = 2 + half
                        for kk in range(4):
                            k = half * 4 + kk
                            P(lambda e, k=k, kk=kk, tb_i=tb_i, b=b: e.transpose(bap(tb_i, kk * 128, (kk + 1) * 128),
                                                                                 yT[:, k, b * 128:(b + 1) * 128], ident_f[:]),
                              reads=[yT, ident_f], writes=[bk(tb_i)], inc=(kk == 3))
                        evac_copy(ob_[:, half * 512:(half + 1) * 512], bap(tb_i), [bk(tb_i)], [(ob_, half)])
                    blk = g * 4 + b
                    tok = S.dma("act", lambda e, ob_=ob_, blk=blk: e.dma_start(out=out[blk * 128:(blk + 1) * 128, :], in_=ob_[:]),
                                reads=[ob_])
                    out_toks.append(tok)


        run_phases()
        if stop is not None:
            out_toks.append(S.dma("act", lambda e: e.dma_start(out=out[0:128, :], in_=xblk[0][:]), reads=[xblk[0]]))
        last = {}
        for (s, v) in out_toks:
            last[s] = max(last.get(s, 0), v)
        S.wait_all("act", list(last.items()))

        with nc.Block() as block:
            def emit(engname, eng):
                for (waits, fn, inc) in S.q[engname]:
                    for (s, v) in waits:
                        eng.wait_ge(sems[s], v)
                    if fn is None:
                        continue
                    ins = fn(eng)
                    if inc is not None:
                        ins.then_inc(sems[inc[0]], inc[1])

            @block.sync
            def _(e):
                emit("sp", e)

            @block.tensor
            def _(e):
                emit("pe", e)

            @block.scalar
            def _(e):
                emit("act", e)

            @block.vector
            def _(e):
                emit("dve", e)

            @block.gpsimd
            def _(e):
                emit("pool", e)
    return nc


def _prep_inputs(inputs):
    x = np.asarray(inputs["x"], np.float32)
    pos = np.asarray(inputs["positions"], np.int32)
    c = np.asarray(inputs["c"], np.float32)
    w_in = np.ascontiguousarray(np.asarray(inputs["w_in"], np.float32)[0])
    w_uq = np.ascontiguousarray(np.asarray(inputs["w_uq"], np.float32)[0])
    kr = w_in[:, 2688:2752]
    w_kr_sw = np.ascontiguousarray(np.concatenate([kr[:, 32:64], kr[:, 0:32]], axis=1))
    uq3 = w_uq.reshape(384, 8, 192)[:, :, 128:192]
    w_uq_sw = np.ascontiguousarray(np.concatenate([uq3[:, :, 32:64], uq3[:, :, 0:32]], axis=2).reshape(384, 512))
    k_idx = np.arange(128)[:, None]
    q_idx = np.arange(128)[None, :]
    trimask = np.where(k_idx <= q_idx, 0.0, NEG).astype(np.float32)
    ident = np.eye(128, dtype=np.float32)
    inv = (1.0 / (np.float32(10000.0) ** (np.arange(0, 64, 2, dtype=np.float32) / np.float32(64)))).astype(np.float32)
    invf = np.concatenate([inv, inv]).reshape(64, 1).astype(np.float32)
    sgn = np.concatenate([-np.ones(32), np.ones(32)]).reshape(64, 1).astype(np.float32)
    shared = {
        "trimask": trimask, "ident": ident, "invf": invf, "sgn": sgn,
        "w_ada": np.ascontiguousarray(inputs["w_ada"][0], np.float32),
        "b_ada": np.ascontiguousarray(inputs["b_ada"][0], np.float32),
        "g_pre_mix": np.ascontiguousarray(inputs["g_pre_mix"][0], np.float32),
        "g_post_mix": np.ascontiguousarray(inputs["g_post_mix"][0], np.float32),
        "g_pre_mlp": np.ascontiguousarray(inputs["g_pre_mlp"][0], np.float32),
        "g_post_mlp": np.ascontiguousarray(inputs["g_post_mlp"][0], np.float32),
        "w_in": w_in, "w_kr_sw": w_kr_sw,
        "conv_w": np.ascontiguousarray(inputs["conv_w"][0], np.float32),
        "conv_b": np.ascontiguousarray(inputs["conv_b"][0], np.float32),
        "conv_norm_g": np.ascontiguousarray(inputs["conv_norm_g"][0], np.float32),
        "conv_norm_b": np.ascontiguousarray(inputs["conv_norm_b"][0], np.float32),
        "w_conv_out": np.ascontiguousarray(inputs["w_conv_out"][0], np.float32),
        "q_norm_g": np.ascontiguousarray(inputs["q_norm_g"][0], np.float32),
        "w_uq": w_uq, "w_uq_sw": w_uq_sw,
        "kv_norm_g": np.ascontiguousarray(inputs["kv_norm_g"][0], np.float32),
        "w_ukv": np.ascontiguousarray(inputs["w_ukv"][0], np.float32),
        "w_attn_out": np.ascontiguousarray(inputs["w_attn_out"][0], np.float32),
        "w_out": np.ascontiguousarray(inputs["w_out"][0], np.float32),
        "w_mlp_in": np.ascontiguousarray(inputs["w_mlp_in"][0], np.float32),
        "w_mlp_out": np.ascontiguousarray(inputs["w_mlp_out"][0], np.float32),
    }
    in_maps = []
    for core in range(8):
        b, p = core // 2, core % 2
        xb = x[b].reshape(32, 128, D)
        pb = pos[b].reshape(32, 128)
        own = [2 * i + p for i in range(16)]
        oth = [2 * i + 1 - p for i in range(16)]
        halo = np.zeros((16, 32, D), np.float32)
        for i in range(16):
            st = own[i] * 128
            if st > 0:
                halo[i] = x[b, st - 32:st]
        m = dict(shared)
        m["x_own"] = np.ascontiguousarray(xb[own].reshape(NOWN, D))
        m["x_oth"] = np.ascontiguousarray(xb[oth].reshape(NOWN, D))
        m["x_halo"] = np.ascontiguousarray(halo.reshape(512, D))
        m["pos_own"] = np.ascontiguousarray(pb[own].reshape(NOWN))
        m["pos_oth"] = np.ascontiguousarray(pb[oth].reshape(NOWN))
        m["c"] = np.ascontiguousarray(c[b])
        m["pairmask"] = np.full((128, 128), 0.0 if p == 1 else NEG, np.float32)
        m["halomask"] = np.full((128, 1), 1.0 if p == 1 else 0.0, np.float32)
        in_maps.append(m)
    return in_maps


def kernel(**inputs):
    in_maps = _prep_inputs(inputs)
    nc = build_nc()
    res = run_bass_kernel_spmd(nc, in_maps, core_ids=list(range(8)))
    outf = np.zeros((4, 32, 128, D), np.float32)
    for core in range(8):
        b, p = core // 2, core % 2
        o = np.asarray(res.results[core]["out"]).reshape(16, 128, D)
        for i in range(16):
            outf[b, 2 * i + p] = o[i]
    return outf.reshape(4, 4096, D)
```

```python
import contextlib
import math
import numpy as np
import concourse.bass as bass
import concourse.mybir as mybir
from concourse.bass_utils import run_bass_kernel_spmd

F32 = mybir.dt.float32
BF = mybir.dt.bfloat16
I32 = mybir.dt.int32
AF = mybir.ActivationFunctionType
ALU = mybir.AluOpType

D = 1024
KC = 8
NOWN = 2048
EPS = 1e-6
NEG = -30000.0
SCALE = 1.0 / math.sqrt(192.0)
TWO_PI = 2.0 * math.pi
C1 = 6.28125
C2 = TWO_PI - 6.28125

ENGS = ("pe", "act", "dve", "pool", "sp")
NDMA = 12


class _Rec:
    def __init__(self):
        self.call = None

    def __getattr__(self, name):
        def f(*a, **k):
            self.call = (name, a, k)
            return self
        return f


def _bind(fn):
    if fn is None:
        return None
    r = _Rec()
    fn(r)
    name, a, k = r.call
    return lambda eng: getattr(eng, name)(*a, **k)


class Sched:
    def __init__(self):
        self.q = {e: [] for e in ENGS}
        self.cnt = {e: 0 for e in ENGS}
        self.waited = {e: {} for e in ENGS}
        self.state = {}
        self.dma_tot = [0] * NDMA
        self.dma_rr = 0
        self.all_dma_tokens = {}

    def _entries(self, buf, key, create):
        d = self.state.setdefault(id(buf), {})
        if key is None:
            if create and None not in d:
                d[None] = {"w": None, "r": {}}
            return list(d.values()) if not create else list(d.values())
        out = []
        if key not in d and create:
            d[key] = {"w": None, "r": {}}
        if key in d:
            out.append(d[key])
        if None in d:
            out.append(d[None])
        return out

    def _deps(self, reads, writes):
        deps = {}

        def add(tok):
            if tok is None:
                return
            s, v = tok
            if deps.get(s, 0) < v:
                deps[s] = v

        for (b, k) in reads:
            for st in self._entries(b, k, False):
                add(st["w"])
        for (b, k) in writes:
            for st in self._entries(b, k, False):
                add(st["w"])
                for s, v in st["r"].items():
                    add((s, v))
        return deps

    def _commit(self, reads, writes, tok):
        for (b, k) in reads:
            d = self.state.setdefault(id(b), {})
            if k not in d:
                d[k] = {"w": None, "r": {}}
            st = d[k]
            s, v = tok
            if st["r"].get(s, 0) < v:
                st["r"][s] = v
        for (b, k) in writes:
            d = self.state.setdefault(id(b), {})
            if k is None:
                d.clear()
            d[k] = {"w": tok, "r": {}}

    def _norm(self, lst):
        out = []
        for x in lst:
            if isinstance(x, tuple):
                out.append(x)
            else:
                out.append((x, None))
        return out

    def op(self, eng, fn, reads=(), writes=(), inc=True):
        fn = _bind(fn)
        reads = self._norm(reads)
        writes = self._norm(writes)
        deps = self._deps(reads, writes)
        waits = []
        for s, v in deps.items():
            if s == eng and eng == "pe":
                continue
            if self.waited[eng].get(s, 0) >= v:
                continue
            self.waited[eng][s] = v
            waits.append((s, v))
        if inc:
            self.cnt[eng] += 1
            tok = (eng, self.cnt[eng])
            self.q[eng].append((waits, fn, (eng, 1)))
        else:
            tok = (eng, self.cnt[eng] + 1)
            self.q[eng].append((waits, fn, None))
        self._commit(reads, writes, tok)
        return tok

    def dma(self, eng, fn, reads=(), writes=()):
        fn = _bind(fn)
        reads = self._norm(reads)
        writes = self._norm(writes)
        j = self.dma_rr
        self.dma_rr = (self.dma_rr + 1) % NDMA
        sem = "d%d" % j
        deps = self._deps(reads, writes)
        if self.dma_tot[j] > 0:
            if deps.get(sem, 0) < self.dma_tot[j]:
                deps[sem] = self.dma_tot[j]
        waits = []
        for s, v in deps.items():
            if self.waited[eng].get(s, 0) >= v:
                continue
            self.waited[eng][s] = v
            waits.append((s, v))
        self.dma_tot[j] += 16
        tok = (sem, self.dma_tot[j])
        self.q[eng].append((waits, fn, (sem, 16)))
        self._commit(reads, writes, tok)
        return tok

    def barrier(self):
        toks = [(e, self.cnt[e]) for e in ENGS if self.cnt[e] > 0]
        toks += [("d%d" % j, self.dma_tot[j]) for j in range(NDMA) if self.dma_tot[j] > 0]
        for e in ENGS:
            self.wait_all(e, [t for t in toks if t[0] != e])
        self.state = {}

    def wait_all(self, eng, toks):
        waits = []
        for (s, v) in toks:
            if self.waited[eng].get(s, 0) >= v:
                continue
            self.waited[eng][s] = v
            waits.append((s, v))
        self.q[eng].append((waits, None, None))


def build_nc(debug=None, stop=None):
    nc = bass.Bass("TRN2", target_bir_lowering=False)
    S = Sched()

    def din(name, shape, dt=F32):
        return nc.dram_tensor(name, list(shape), dt, kind="ExternalInput").ap()

    x_own = din("x_own", [NOWN, D])
    x_oth = din("x_oth", [NOWN, D])
    x_halo = din("x_halo", [512, D])
    pos_own = nc.dram_tensor("pos_own", [NOWN], I32, kind="ExternalInput")
    pos_oth = nc.dram_tensor("pos_oth", [NOWN], I32, kind="ExternalInput")
    c_in = din("c", [D])
    pairmask_in = din("pairmask", [128, 128])
    trimask_in = din("trimask", [128, 128])
    ident_in = din("ident", [128, 128])
    halomask_in = din("halomask", [128, 1])
    invf_in = din("invf", [64, 1])
    sgn_in = din("sgn", [64, 1])
    w_ada = din("w_ada", [D, 6 * D])
    b_ada = din("b_ada", [6 * D])
    g_pre_mix = din("g_pre_mix", [D])
    g_post_mix = din("g_post_mix", [D])
    g_pre_mlp = din("g_pre_mlp", [D])
    g_post_mlp = din("g_post_mlp", [D])
    w_in = din("w_in", [D, 4800])
    w_kr_sw = din("w_kr_sw", [D, 64])
    conv_w = din("conv_w", [31, D])
    conv_b = din("conv_b", [D])
    conv_norm_g = din("conv_norm_g", [D])
    conv_norm_b = din("conv_norm_b", [D])
    w_conv_out = din("w_conv_out", [D, D])
    q_norm_g = din("q_norm_g", [384])
    w_uq = din("w_uq", [384, 1536])
    w_uq_sw = din("w_uq_sw", [384, 512])
    kv_norm_g = din("kv_norm_g", [256])
    w_ukv = din("w_ukv", [256, 2048])
    w_attn_out = din("w_attn_out", [D, D])
    w_out = din("w_out", [D, D])
    w_mlp_in = din("w_mlp_in", [D, 4 * D])
    w_mlp_out = din("w_mlp_out", [4 * D, D])
    out = nc.dram_tensor("out", [NOWN, D], F32, kind="ExternalOutput").ap()
    wq = nc.dram_tensor("wq", [60, 128, 2048], BF, kind="Internal").ap()
    wq_key = object()
    dbg = None

    es = contextlib.ExitStack()
    with es:
        def sb(name, shape, dt=F32):
            return es.enter_context(nc.sbuf_tensor("s_" + name, list(shape), dt))

        sems = {}
        for e in ENGS:
            sems[e] = es.enter_context(nc.semaphore("sem_" + e))
        for j in range(NDMA):
            sems["d%d" % j] = es.enter_context(nc.semaphore("sem_d%d" % j))

        PD = [es.enter_context(nc.psum_tensor("pd%d" % i, [128, 1024], F32)) for i in range(4)]

        def bank(i):
            t = PD[i // 2]
            h = i % 2
            return t, h

        def bk(i):
            t, h = bank(i)
            return (t, h)

        def bap(i, c0=0, c1=512):
            t, h = bank(i)
            return t[:, h * 512 + c0: h * 512 + c1]

        def bap_bf(i):
            t, h = bank(i)
            return t.bitcast(BF)[:, h * 1024:(h + 1) * 1024]

        ident_f = sb("ident_f", [128, 128])
        ident_b = sb("ident_b", [128, 128], BF)
        ones_b = sb("ones_b", [128, 128], BF)
        tri_b = sb("tri_b", [128, 128], BF)
        pair_b = sb("pair_b", [128, 128], BF)
        halom = sb("halom", [128, 1])
        invf = sb("invf", [64, 1])
        sgn = sb("sgn", [64, 1])
        modT = sb("modT", [128, 48])
        vecs = sb("vecs", [128, 128])
        cwT = sb("cwT", [128, 8, 31])
        V_GPRE, V_GPOST, V_GPRE2, V_GPOST2 = 0, 8, 16, 24
        V_CB, V_CG, V_CNB = 32, 40, 48
        V_QG, V_KVG = 56, 59
        der = sb("der", [128, 48])
        NST = 2
        wst = [sb("wst%d" % i, [128, 8, 256]) for i in range(NST)]
        wbf = [sb("wbf%d" % i, [128, 8, 256], BF) for i in range(NST)]
        wrr = [0]
        xblk = [sb("xblk%d" % i, [128, D]) for i in range(2)]
        xnb = [sb("xnb%d" % i, [128, D], BF) for i in range(2)]
        small = [sb("small%d" % i, [128, 4]) for i in range(4)]
        small_rr = [0]
        tmpf = [sb("tmpf%d" % i, [128, 512]) for i in range(4)]
        tmpf_rr = [0]
        tmpb = [sb("tmpb%d" % i, [128, 512], BF) for i in range(6)]
        tmpb_rr = [0]
        rstd_t = [sb("rstd%d" % i, [128, 512]) for i in range(2)]
        rstd_rr = [0]

        def nxt(lst, rr):
            t = lst[rr[0] % len(lst)]
            rr[0] += 1
            return t

        def A(fn, **kw):
            return S.op("act", fn, **kw)

        def V(fn, **kw):
            return S.op("dve", fn, **kw)

        def G(fn, **kw):
            return S.op("pool", fn, **kw)

        def P(fn, **kw):
            return S.op("pe", fn, **kw)

        def dma_in(dst_ap, src_ap, dst_buf, key=None, eng="sp", nonc=False):
            def f(e, dst_ap=dst_ap, src_ap=src_ap):
                if nonc:
                    return e.dma_start(out=dst_ap, in_=src_ap, allow_slow_non_contiguous=True)
                return e.dma_start(out=dst_ap, in_=src_ap)
            return S.dma(eng, f, writes=[(dst_buf, key)])

        def mm(out_ap, lhsT, rhs, start, stop, reads, wkey, last):
            def f(e):
                return e.matmul(out_ap, lhsT, rhs, start=start, stop=stop)
            return S.op("pe", f, reads=reads, writes=[wkey], inc=last)

        def load_w(src, rows, c0, ncols, r0=0, cast=None):
            i = wrr[0] % NST
            wrr[0] += 1
            kc = rows // 128
            st, wb = wst[i], wbf[i]
            src_ap = src[r0:r0 + rows, c0:c0 + ncols].rearrange("(k p) c -> p k c", p=128)
            dma_in(st[:, 0:kc, 0:ncols], src_ap, st)
            if cast == "pool" or (cast is None and wrr[0] % 2 == 0):
                G(lambda e: e.tensor_copy(wb[:, 0:kc, 0:ncols], st[:, 0:kc, 0:ncols]), reads=[st], writes=[wb])
            else:
                V(lambda e: e.tensor_copy(wb[:, 0:kc, 0:ncols], st[:, 0:kc, 0:ncols]), reads=[st], writes=[wb])
            return wb

        def _unused():
            pass

        out_toks = []

        def run_phases():
            dma_in(ident_f[:], ident_in, ident_f)
            dma_in(halom[:], halomask_in, halom)
            dma_in(invf[:], invf_in, invf)
            dma_in(sgn[:], sgn_in, sgn)
            t0 = tmpf[0]
            t1 = tmpf[1]
            dma_in(t0[:, 0:128], trimask_in, t0)
            dma_in(t1[:, 0:128], pairmask_in, t1)
            V(lambda e: e.tensor_copy(ident_b[:], ident_f[:]), reads=[ident_f], writes=[ident_b])
            V(lambda e: e.memset(ones_b[:], 1.0), writes=[ones_b])
            V(lambda e: e.tensor_copy(tri_b[:], t0[:, 0:128]), reads=[t0], writes=[tri_b])
            V(lambda e: e.tensor_copy(pair_b[:], t1[:, 0:128]), reads=[t1], writes=[pair_b])
            tmpf_rr[0] = 2
            if stop == -1:
                return
            stg = tmpf[2]
            V(lambda e: e.memset(stg[:, 0:128], 0.0), writes=[stg])
            for col, src, n in ((V_GPRE, g_pre_mix, D), (V_GPOST, g_post_mix, D), (V_GPRE2, g_pre_mlp, D),
                                (V_GPOST2, g_post_mlp, D), (V_CB, conv_b, D), (V_CG, conv_norm_g, D),
                                (V_CNB, conv_norm_b, D), (V_QG, q_norm_g, 384), (V_KVG, kv_norm_g, 256)):
                dma_in(stg[col:col + n // 128, 0:128], src.rearrange("(k p) -> k p", p=128), stg)
            dma_in(stg[64:112, 0:128], b_ada.rearrange("(k p) -> k p", p=128), stg)
            dma_in(stg[112:120, 0:128], c_in.rearrange("(k p) -> k p", p=128), stg)
            P(lambda e: e.transpose(bap(1, 0, 120), stg[0:120, 0:128], ident_f[0:120, 0:120]),
              reads=[stg, ident_f], writes=[bk(1)])
            V(lambda e: e.tensor_copy(vecs[:, 0:120], bap(1, 0, 120)), reads=[bk(1)], writes=[vecs])
            if stop == -2:
                return
            badaT = vecs[:, 64:112]
            cT = vecs[:, 112:120]
            cwn = xblk[0]
            dma_in(cwn[0:31, :], conv_w, cwn)
            for c in range(8):
                P(lambda e, c=c: e.transpose(bap(2, c * 32, c * 32 + 31), cwn[0:31, c * 128:(c + 1) * 128], ident_f[0:31, 0:31]),
                  reads=[cwn, ident_f], writes=[bk(2)])
            V(lambda e: e.tensor_copy(cwT[:], bap(2, 0, 256).rearrange("p (c k) -> p c k", k=32)[:, :, 0:31]),
              reads=[bk(2)], writes=[cwT])
            if stop == -3:
                return
            scb = sb("scb", [128, 8])
            A(lambda e: e.activation(scb[:], cT, AF.Silu), reads=[vecs], writes=[scb])
            if stop == -4:
                return
            def load_w32(src, rows, c0, ncols):
                i = wrr[0] % NST
                wrr[0] += 1
                st = wst[i]
                dma_in(st[:, 0:rows // 128, 0:ncols], src[0:rows, c0:c0 + ncols].rearrange("(k p) c -> p k c", p=128), st)
                return st

            def mod_part(p0, p1, MODB, cast=None):
                for pc in range(p0, p1):
                    wb = load_w32(w_ada, D, pc * 256, 256)
                    for jj in range(2):
                        j = pc * 2 + jj
                        for k in range(8):
                            mm(bap(MODB, j, j + 1), wb[:, k, jj * 128:(jj + 1) * 128], scb[:, k:k + 1],
                               k == 0, k == 7, [wb, scb], bk(MODB), k == 7)
                V(lambda e: e.tensor_tensor(modT[:, p0 * 2:p1 * 2], bap(MODB, p0 * 2, p1 * 2), vecs[:, 64 + p0 * 2:64 + p1 * 2], ALU.add),
                  reads=[bk(MODB), vecs], writes=[(modT, p0)])

            def mod_late():
                mod_part(8, 24, 7, cast="pool")
                V(lambda e: e.tensor_tensor(der[:, 16:24], modT[:, 16:24], vecs[:, V_GPOST:V_GPOST + 8], ALU.mult),
                  reads=[modT, vecs], writes=[(der, 16)])
                V(lambda e: e.scalar_tensor_tensor(der[:, 24:32], modT[:, 32:40], 1.0, vecs[:, V_GPRE2:V_GPRE2 + 8], ALU.add, ALU.mult),
                  reads=[modT, vecs], writes=[(der, 24)])
                V(lambda e: e.tensor_copy(der[:, 32:40], modT[:, 24:32]), reads=[modT], writes=[(der, 32)])
                V(lambda e: e.tensor_tensor(der[:, 40:48], modT[:, 40:48], vecs[:, V_GPOST2:V_GPOST2 + 8], ALU.mult),
                  reads=[modT, vecs], writes=[(der, 40)])
            DER_ALL = [(der, 0), (der, 8), (der, 16), (der, 24), (der, 32), (der, 40)]

            TPB = [0, 1]
            tp_rr = [0]
            xb_rr = [0]

            def xb_prep(src_rows_ap, nrows):
                i = xb_rr[0] % 2
                xb_rr[0] += 1
                xb = xblk[i]
                xn = xnb[i]
                dma_in(xb[0:nrows, :], src_rows_ap, xb)
                sm = nxt(small, small_rr)
                A(lambda e: e.activation(xn[0:nrows, :], xb[0:nrows, :], AF.Square, accum_out=sm[0:nrows, 0:1]),
                  reads=[xb], writes=[xn, (sm, 0)])
                A(lambda e: e.activation(sm[0:nrows, 3:4], sm[0:nrows, 0:1], AF.Sqrt, bias=EPS, scale=1.0 / D),
                  reads=[(sm, 0)], writes=[(sm, 3)])
                V(lambda e: e.reciprocal(sm[0:nrows, 2:3], sm[0:nrows, 3:4]), reads=[(sm, 3)], writes=[(sm, 2)])
                V(lambda e: e.tensor_scalar(xn[0:nrows, :], xb[0:nrows, :], sm[0:nrows, 2:3], None, ALU.mult),
                  reads=[xb, (sm, 2)], writes=[xn])
                return (xn, nrows)

            def xb_trans(hd, hT, col0, gsc, shc):
                xn, nrows = hd
                b = TPB[tp_rr[0] % 2]
                tp_rr[0] += 1
                tpv = bap_bf(b)
                for k in range(8):
                    P(lambda e, k=k: e.transpose(tpv[:, k * 128:k * 128 + nrows], xn[0:nrows, k * 128:(k + 1) * 128],
                                                  ident_b[0:nrows, 0:nrows]),
                      reads=[xn, ident_b], writes=[bk(b)], inc=(k == 7))
                for k in range(8):
                    if b == TPB[0]:
                        V(lambda e, k=k: e.tensor_scalar(hT[:, k, col0:col0 + nrows], tpv[:, k * 128:k * 128 + nrows],
                                                         der[:, gsc + k:gsc + k + 1], der[:, shc + k:shc + k + 1],
                                                         ALU.mult, ALU.add),
                          reads=[bk(b), (der, gsc), (der, shc)], writes=[(hT, k)])
                    else:
                        A(lambda e, k=k: e.activation(hT[:, k, col0:col0 + nrows], tpv[:, k * 128:k * 128 + nrows],
                                                      AF.Identity, bias=der[:, shc + k:shc + k + 1],
                                                      scale=der[:, gsc + k:gsc + k + 1]),
                          reads=[bk(b), (der, gsc), (der, shc)], writes=[(hT, k)])

            def x_block_to_hT(src_rows_ap, nrows, hT, col0, gsc, shc):
                xb_trans(xb_prep(src_rows_ap, nrows), hT, col0, gsc, shc)

            def rstd_from_ps(ps_bank, nfeat, ncols=512):
                r = nxt(rstd_t, rstd_rr)
                jt = nxt(tmpf, tmpf_rr)
                A(lambda e: e.activation(jt[:, 0:ncols], bap(ps_bank, 0, ncols), AF.Sqrt, bias=EPS, scale=1.0 / nfeat),
                  reads=[bk(ps_bank)], writes=[jt])
                V(lambda e: e.reciprocal(r[:, 0:ncols], jt[:, 0:ncols]), reads=[jt], writes=[r])
                return r

            hd_first = xb_prep(x_own[0:128, :], 128)
            mod_part(0, 8, 0)
            if stop == -5:
                return
            V(lambda e: e.scalar_tensor_tensor(der[:, 0:8], modT[:, 8:16], 1.0, vecs[:, V_GPRE:V_GPRE + 8], ALU.add, ALU.mult),
              reads=[modT, vecs], writes=[(der, 0)])
            V(lambda e: e.tensor_copy(der[:, 8:16], modT[:, 0:8]), reads=[modT], writes=[(der, 8)])

            if stop == 0:
                return
            oT = sb("oT", [128, 8, NOWN], BF)
            ph12 = es.enter_context(contextlib.ExitStack())
            ph1 = es.enter_context(contextlib.ExitStack())

            def sb12(name, shape, dt=F32):
                return ph12.enter_context(nc.sbuf_tensor("s_" + name, list(shape), dt))

            def sb1(name, shape, dt=F32):
                return ph1.enter_context(nc.sbuf_tensor("s_" + name, list(shape), dt))

            kvn = [sb12("kvn_own", [128, 2, NOWN], BF), sb12("kvn_oth", [128, 2, NOWN], BF)]
            krT = [sb12("kr_own", [128, NOWN], BF), sb12("kr_oth", [128, NOWN], BF)]
            for kr_ in krT:
                G(lambda e, kr_=kr_: e.memset(kr_[64:128, :], 0.0), writes=[kr_])
            qn = sb12("qn", [128, 3, NOWN], BF)
            CS = sb12("cs_own", [64, 2, NOWN])
            wukv = sb12("wukv", [128, 2, 2048], BF)
            wuq = sb12("wuq", [128, 3, 2048], BF)
            hTs = [sb1("hT%d" % i, [128, 8, 640], BF) for i in range(2)]
            wlat = sb1("wlat", [128, 8, 768], BF)
            cs_tmp = sb1("cs_tmp", [64, 2, 512])
            posi = sb1("posi", [64, 512], I32)
            angs = [sb1("ang%d" % i, [64, 512]) for i in range(3)]
            ni_t = sb1("ni_t", [64, 512], I32)

            for pc, (src, c0, n, d0) in enumerate(((w_in, 2048, 256, 0), (w_in, 2304, 256, 256), (w_in, 2560, 192, 512),
                                                   (w_kr_sw, 0, 64, 704))):
                wb = load_w(src, D, c0, n)
                G(lambda e, wb=wb, n=n, d0=d0: e.tensor_copy(wlat[:, :, d0:d0 + n], wb[:, :, 0:n]),
                  reads=[wb], writes=[(wlat, pc)])
            WL = [(wlat, i) for i in range(4)]
            if stop == 10:
                return

            def rope_tables(pos_t, c0, dst, dcol):
                src = bass.AP(pos_t, c0, [[0, 64], [1, 512]])
                dma_in(posi[:], src, posi)
                a0, a1, a2 = angs
                a3 = posi.bitcast(F32)
                V(lambda e: e.tensor_copy(a0[:], posi[:]), reads=[posi], writes=[a0])
                V(lambda e: e.tensor_scalar(a0[:], a0[:], invf[:, 0:1], None, ALU.mult), reads=[a0, invf], writes=[a0])
                V(lambda e: e.tensor_scalar(a1[:], a0[:], 1.0 / TWO_PI, None, ALU.mult), reads=[a0], writes=[a1])
                V(lambda e: e.tensor_copy(ni_t[:], a1[:]), reads=[a1], writes=[ni_t])
                V(lambda e: e.tensor_copy(a1[:], ni_t[:]), reads=[ni_t], writes=[a1])
                V(lambda e: e.scalar_tensor_tensor(a2[:], a1[:], -C1, a0[:], ALU.mult, ALU.add), reads=[a1, a0], writes=[a2])
                V(lambda e: e.scalar_tensor_tensor(a2[:], a1[:], -C2, a2[:], ALU.mult, ALU.add), reads=[a1, a2], writes=[a2])
                V(lambda e: e.tensor_scalar(a3[:], a2[:], math.pi, -TWO_PI, ALU.is_gt, ALU.mult), reads=[a2], writes=[posi])
                V(lambda e: e.tensor_tensor(a2[:], a2[:], a3[:], ALU.add), reads=[a2, posi], writes=[a2])
                V(lambda e: e.tensor_scalar(a3[:], a2[:], -math.pi, TWO_PI, ALU.is_lt, ALU.mult), reads=[a2], writes=[posi])
                V(lambda e: e.tensor_tensor(a2[:], a2[:], a3[:], ALU.add), reads=[a2, posi], writes=[a2])
                V(lambda e: e.tensor_scalar(a1[:], a2[:], math.pi / 2, None, ALU.add), reads=[a2], writes=[a1])
                V(lambda e: e.tensor_scalar(a3[:], a1[:], math.pi, -TWO_PI, ALU.is_gt, ALU.mult), reads=[a1], writes=[posi])
                V(lambda e: e.tensor_tensor(a1[:], a1[:], a3[:], ALU.add), reads=[a1, posi], writes=[a1])
                V(lambda e: e.tensor_scalar(a1[:], a1[:], math.pi, -math.pi, ALU.min, ALU.max), reads=[a1], writes=[a1])
                V(lambda e: e.tensor_scalar(a2[:], a2[:], math.pi, -math.pi, ALU.min, ALU.max), reads=[a2], writes=[a2])
                A(lambda e: e.activation(dst[:, 0, dcol:dcol + 512], a1[:], AF.Sin), reads=[a1], writes=[(dst, dcol)])
                A(lambda e: e.activation(dst[:, 1, dcol:dcol + 512], a2[:], AF.Sin, scale=sgn[:, 0:1]),
                  reads=[a2, sgn], writes=[(dst, dcol)])

            def load_wbig(src, rows, c0, ncols):
                i = wrr[0] % NST
                wrr[0] += 1
                kc = rows // 128
                st, wb = wst[i], wbf[i]
                stv = st[:].rearrange("p k c -> p (k c)")[:, 0:kc * ncols].rearrange("p (k c) -> p k c", c=ncols)
                wbv = wb[:].rearrange("p k c -> p (k c)")[:, 0:kc * ncols].rearrange("p (k c) -> p k c", c=ncols)
                dma_in(stv, src[0:rows, c0:c0 + ncols].rearrange("(k p) c -> p k c", p=128), st)
                G(lambda e: e.tensor_copy(wbv, stv), reads=[st], writes=[wb])
                return wb, wbv

            def _job_uq(pc):
                def f():
                    wb, wbv = load_wbig(w_uq, 384, pc * 512, 512)
                    G(lambda e: e.tensor_copy(wuq[:, :, pc * 512:(pc + 1) * 512], wbv), reads=[wb], writes=[(wuq, pc)])
                return f

            def _job_uqsw():
                wb, wbv = load_wbig(w_uq_sw, 384, 0, 512)
                G(lambda e: e.tensor_copy(wuq[:, :, 1536:2048], wbv), reads=[wb], writes=[(wuq, 3)])

            def _job_ukv(pc):
                def f():
                    wb, wbv = load_wbig(w_ukv, 256, pc * 1024, 1024)
                    G(lambda e: e.tensor_copy(wukv[:, :, pc * 1024:(pc + 1) * 1024], wbv), reads=[wb], writes=[(wukv, pc)])
                return f

            wjobs = [_job_uq(0), _job_uq(1), _job_uq(2), _job_uqsw, _job_ukv(0), _job_ukv(1)]

            def prep1(j_):
                i_, b_ = j_ // 4, j_ % 4
                grp_, t_ = i_ // 4, i_ % 4
                xsrc = x_own if grp_ == 0 else x_oth
                r0 = t_ * 512 + b_ * 128
                return xb_prep(xsrc[r0:r0 + 128, :], 128)

            hd1 = [hd_first]
            for grp in range(2):
                pos_t = pos_own if grp == 0 else pos_oth
                for t in range(4):
                    hT = hTs[(grp * 4 + t) % 2]
                    for b in range(4):
                        j_ = (grp * 4 + t) * 4 + b
                        nh = prep1(j_ + 1) if j_ + 1 < 32 else None
                        xb_trans(hd1[0], hT, b * 128, 0, 8)
                        hd1[0] = nh
                    if grp * 4 + t >= 2 and wjobs:
                        wjobs.pop(0)()
                    if grp == 0:
                        rope_tables(pos_t, t * 512, CS, t * 512)
                        cs, cc = CS, t * 512
                    else:
                        rope_tables(pos_t, t * 512, cs_tmp, 0)
                        cs, cc = cs_tmp, 0
                    if stop == 12:
                        return
                    hk = [(hT, k) for k in range(8)]
                    for m in range(2):
                        for k in range(8):
                            mm(bap(2 + m), wlat[:, k, 384 + m * 128:384 + (m + 1) * 128], hT[:, k, 0:512],
                               k == 0, k == 7, hk + WL, bk(2 + m), k == 7)
                    for m in range(2):
                        for k in range(8):
                            mm(bap(5 + m)[0:64, :], wlat[:, k, 640 + m * 64:640 + (m + 1) * 64], hT[:, k, 0:512],
                               k == 0, k == 7, hk + WL, bk(5 + m), k == 7)
                    sq = []
                    for m in range(2):
                        s_ = nxt(tmpb, tmpb_rr)
                        A(lambda e, m=m, s_=s_: e.activation(s_[:], bap(2 + m), AF.Square), reads=[bk(2 + m)], writes=[s_])
                        sq.append(s_)
                    for m in range(2):
                        mm(bap(4), ones_b[:], sq[m][:], m == 0, m == 1, [ones_b, sq[m]], bk(4), True)
                    r = rstd_from_ps(4, 256)
                    for m in range(2):
                        V(lambda e, m=m, r=r: e.scalar_tensor_tensor(kvn[grp][:, m, t * 512:(t + 1) * 512], bap(2 + m),
                                                                     vecs[:, V_KVG + m:V_KVG + m + 1], r[:], ALU.mult, ALU.mult),
                          reads=[bk(2 + m), r, vecs], writes=[(kvn[grp], t)])
                    ta = nxt(tmpf, tmpf_rr)
                    tb_ = nxt(tmpf, tmpf_rr)
                    V(lambda e, ta=ta, cs=cs, cc=cc: e.tensor_tensor(ta[0:64, :], bap(5)[0:64, :], cs[:, 0, cc:cc + 512], ALU.mult),
                      reads=[bk(5), (cs, cc)], writes=[ta])
                    V(lambda e, tb_=tb_, cs=cs, cc=cc: e.tensor_tensor(tb_[0:64, :], bap(6)[0:64, :], cs[:, 1, cc:cc + 512], ALU.mult),
                      reads=[bk(6), (cs, cc)], writes=[tb_])
                    V(lambda e, ta=ta, tb_=tb_: e.tensor_tensor(krT[grp][0:64, t * 512:(t + 1) * 512], ta[0:64, :], tb_[0:64, :], ALU.add),
                      reads=[ta, tb_], writes=[(krT[grp], t)])
                    if stop == 13:
                        return
                    if grp == 0:
                        QB = [2, 3, 7]
                        for m in range(3):
                            for k in range(8):
                                mm(bap(QB[m]), wlat[:, k, m * 128:(m + 1) * 128], hT[:, k, 0:512],
                                   k == 0, k == 7, hk + WL, bk(QB[m]), k == 7)
                        sq = []
                        for m in range(3):
                            s_ = nxt(tmpb, tmpb_rr)
                            A(lambda e, m=m, s_=s_: e.activation(s_[:], bap(QB[m]), AF.Square), reads=[bk(QB[m])], writes=[s_])
                            sq.append(s_)
                        for m in range(3):
                            mm(bap(4), ones_b[:], sq[m][:], m == 0, m == 2, [ones_b, sq[m]], bk(4), True)
                        r = rstd_from_ps(4, 384)
                        for m in range(3):
                            V(lambda e, m=m, r=r: e.scalar_tensor_tensor(qn[:, m, t * 512:(t + 1) * 512], bap(QB[m]),
                                                                         vecs[:, V_QG + m:V_QG + m + 1], r[:], ALU.mult, ALU.mult),
                              reads=[bk(QB[m]), r, vecs], writes=[(qn, t)])
                    if stop == 14:
                        return

            while wjobs:
                wjobs.pop(0)()
            S.barrier()
            ph1.close()
            if stop == 1:
                ph12.close()
                return

            KhT = [[sb12("kh%d_%d" % (i, g), [128, NOWN], BF) for g in range(2)] for i in range(1)]
            Vh = [[sb12("vh%d_%d" % (i, g), [128, 16, 128], BF) for g in range(2)] for i in range(1)]
            Qh = [sb12("qh%d" % i, [128, NOWN], BF) for i in range(1)]
            Qr = [sb12("qr%d" % i, [128, NOWN], BF) for i in range(1)]
            G(lambda e: e.memset(Qr[0][64:128, :], 0.0), writes=[Qr[0]])
            Pt = [sb12("pt%d" % i, [128, 512], BF) for i in range(4)]
            pt_rr = [0]
            SB_ = [0, 1, 2]
            s_rr = [0]
            OB = [3, 5]
            LB = [4, 6]
            HBS = [7, 3, 4]
            hb_rr = [0]

            def nhb():
                b_ = HBS[hb_rr[0] % 3]
                hb_rr[0] += 1
                return b_
            evac_rr = [0]

            def evac_copy(dst_ap, src_bank_ap, reads, writes):
                if evac_rr[0] % 2 == 0:
                    V(lambda e: e.tensor_copy(dst_ap, src_bank_ap), reads=reads, writes=writes)
                else:
                    A(lambda e: e.activation(dst_ap, src_bank_ap, AF.Copy), reads=reads, writes=writes)
                evac_rr[0] += 1

            def build_head(h):
                i = 0
                for t in range(4):
                    HB = nhb()
                    for k in range(3):
                        mm(bap(HB)[0:64, :], wuq[:, k, h * 192 + 128:h * 192 + 192], qn[:, k, t * 512:(t + 1) * 512],
                           k == 0, k == 2, [wuq, qn], bk(HB), k == 2)
                    ta = nxt(tmpf, tmpf_rr)
                    V(lambda e, ta=ta, t=t: e.tensor_tensor(ta[0:64, :], bap(HB)[0:64, :], CS[:, 0, t * 512:(t + 1) * 512], ALU.mult),
                      reads=[bk(HB), CS], writes=[ta])
                    HB = nhb()
                    for k in range(3):
                        mm(bap(HB)[0:64, :], wuq[:, k, 1536 + h * 64:1536 + (h + 1) * 64], qn[:, k, t * 512:(t + 1) * 512],
                           k == 0, k == 2, [wuq, qn], bk(HB), k == 2)
                    tb_ = nxt(tmpf, tmpf_rr)
                    V(lambda e, tb_=tb_, t=t: e.tensor_tensor(tb_[0:64, :], bap(HB)[0:64, :], CS[:, 1, t * 512:(t + 1) * 512], ALU.mult),
                      reads=[bk(HB), CS], writes=[tb_])
                    V(lambda e, ta=ta, tb_=tb_, t=t: e.tensor_tensor(Qr[i][0:64, t * 512:(t + 1) * 512], ta[0:64, :], tb_[0:64, :], ALU.add),
                      reads=[ta, tb_], writes=[(Qr[i], t)])
                for t in range(4):
                    HB = nhb()
                    for k in range(3):
                        mm(bap(HB), wuq[:, k, h * 192:h * 192 + 128], qn[:, k, t * 512:(t + 1) * 512],
                           k == 0, k == 2, [wuq, qn], bk(HB), k == 2)
                    evac_copy(Qh[i][:, t * 512:(t + 1) * 512], bap(HB), [bk(HB)], [(Qh[i], t)])
                for grp in range(2):
                    for t in range(4):
                        HB = nhb()
                        for k in range(2):
                            mm(bap(HB), wukv[:, k, h * 256:h * 256 + 128], kvn[grp][:, k, t * 512:(t + 1) * 512],
                               k == 0, k == 1, [wukv, kvn[grp]], bk(HB), k == 1)
                        evac_copy(KhT[i][grp][:, t * 512:(t + 1) * 512], bap(HB), [bk(HB)], [(KhT[i][grp], t)])
                    for t in range(4):
                        HB = nhb()
                        for b in range(4):
                            blk = t * 4 + b
                            for k in range(2):
                                mm(bap(HB, b * 128, (b + 1) * 128), kvn[grp][:, k, blk * 128:(blk + 1) * 128],
                                   wukv[:, k, h * 256 + 128:h * 256 + 256],
                                   k == 0, k == 1, [wukv, kvn[grp]], bk(HB), (k == 1 and b == 3))
                        evac_copy(Vh[i][grp][:, t * 4:(t + 1) * 4, :], bap(HB).rearrange("p (b d) -> p b d", d=128),
                                  [bk(HB)], [(Vh[i][grp], t)])

            def attend_head(h):
                i = 0
                for g in range(4):
                    ob = OB[g % 2]
                    lb = LB[g % 2]
                    visits = [(J, grp) for J in range(4 * g + 4) for grp in range(2)]
                    pend = []

                    def do_pv(v, first, last):
                        J, grp, c0, pt = v
                        mm(bap(ob, c0, 512), Vh[i][grp][:, J, :], pt[:, c0:512], first, last,
                           [Vh[i][grp], pt], bk(ob), True)
                        mm(bap(lb, c0, 512), ones_b[:], pt[:, c0:512], first, last,
                           [ones_b, pt], bk(lb), True)

                    npv = [0]
                    for vi, (J, grp) in enumerate(visits):
                        j = J - 4 * g
                        c0 = 128 * max(j, 0)
                        sbk = SB_[s_rr[0] % 3]
                        s_rr[0] += 1
                        q0 = g * 512 + c0
                        q1 = (g + 1) * 512
                        masked = j >= 0
                        mm(bap(sbk, c0, 512), KhT[i][grp][:, J * 128:(J + 1) * 128], Qh[i][:, q0:q1],
                           True, False, [KhT[i][grp], Qh[i]], bk(sbk), False)
                        mm(bap(sbk, c0, 512), krT[grp][:, J * 128:(J + 1) * 128], Qr[i][:, q0:q1],
                           False, not masked, [krT[grp], Qr[i]], bk(sbk), not masked)
                        if masked:
                            mk = tri_b if grp == 0 else pair_b
                            mm(bap(sbk, c0, c0 + 128), ident_b[:], mk[:], False, True, [ident_b, mk], bk(sbk), True)
                        pt = nxt(Pt, pt_rr)
                        A(lambda e, pt=pt, sbk=sbk, c0=c0: e.activation(pt[:, c0:512], bap(sbk, c0, 512), AF.Exp, scale=SCALE),
                          reads=[bk(sbk)], writes=[pt])
                        pend.append((J, grp, c0, pt))
                        if len(pend) > 2:
                            v = pend.pop(0)
                            do_pv(v, npv[0] == 0, False)
                            npv[0] += 1
                    while pend:
                        v = pend.pop(0)
                        do_pv(v, npv[0] == 0, len(pend) == 0)
                        npv[0] += 1
                    rl = nxt(rstd_t, rstd_rr)
                    V(lambda e, rl=rl, lb=lb: e.reciprocal(rl[:], bap(lb)), reads=[bk(lb)], writes=[rl])
                    V(lambda e, rl=rl, ob=ob, g=g: e.tensor_tensor(oT[:, h, g * 512:(g + 1) * 512], bap(ob), rl[:], ALU.mult),
                      reads=[bk(ob), rl], writes=[(oT, (h, g))])

            prep = []
            for pc in range(4):
                prep += [(w_in, pc * 256, 0), (w_in, 1024 + pc * 256, 0)]
            for pc in range(4):
                prep += [(w_conv_out, pc * 256, 0)]
            for pc in range(4):
                prep += [(w_attn_out, pc * 256, 0), (w_in, 2752 + pc * 256, 0), (w_in, 3776 + pc * 256, 0)]
            for pc in range(4):
                prep += [(w_out, pc * 256, 0)]
            for pc in range(16):
                prep += [(w_mlp_in, pc * 256, 0)]
            for pc in range(4):
                for rr_ in range(4):
                    prep += [(w_mlp_out, pc * 256, rr_ * 1024)]
            prep_tok = {}

            def do_prep():
                prev = None
                for i, (src, c0, r0) in enumerate(prep):
                    wb = load_w(src, D, c0, 256, r0=r0, cast="pool")
                    if prev is not None:
                        pw, pi = prev
                        S.dma("sp", lambda e, pw=pw, pi=pi: e.dma_start(out=wq[pi], in_=pw[:].rearrange("p k c -> p (k c)")),
                              reads=[pw], writes=[(wq_key, pi)])
                    prev = (wb, i)
                pw, pi = prev
                S.dma("sp", lambda e: e.dma_start(out=wq[pi], in_=pw[:].rearrange("p k c -> p (k c)")),
                      reads=[pw], writes=[(wq_key, pi)])

            do_prep()
            build_head(0)
            for h in range(8):
                attend_head(h)
                if h + 1 < 8:
                    build_head(h + 1)
            mod_late()

            S.barrier()
            ph12.close()
            if stop == 2:
                return
            wpool = [wbf[0][:], wbf[1][:]]
            for i_ in range(NST):
                fl = wst[i_].bitcast(BF)[:].rearrange("p k c -> p (k c)")
                wpool += [fl[:, 0:2048].rearrange("p (k c) -> p k c", c=256), fl[:, 2048:4096].rearrange("p (k c) -> p k c", c=256)]
            wp_rr = [0]
            pidx = {(src_.tensor.name, c0_, r0_): i_ for i_, (src_, c0_, r0_) in enumerate(prep)}

            def load_wq(src, rows, c0, ncols, r0=0):
                i = pidx[(src.tensor.name, c0, r0)]
                buf = wpool[wp_rr[0] % len(wpool)]
                wp_rr[0] += 1
                S.dma("sp", lambda e: e.dma_start(out=buf.rearrange("p k c -> p (k c)"), in_=wq[i]), writes=[buf])
                return buf

            xT = sb("xT", [128, 8, 512])
            yT = sb("yT", [128, 8, 512])
            hTe = sb("hTe", [128, 8, 640], BF)
            uext = sb("uext", [128, 8, 640], BF)
            arena = sb("arena", [128, 16384], BF)
            hid = arena[:, :].rearrange("p (j t) -> p j t", t=512)
            ucv = arena.bitcast(F32)[:, 0:4096].rearrange("p (c t) -> p c t", t=512)
            diag = [arena[:, 8192 + i * 3968:8192 + (i + 1) * 3968].rearrange("p (k m) -> p k m", m=128) for i in range(2)]
            sh8 = sb("sh8", [128, 8, 512], BF)
            actT = sh8
            mT = sh8
            h2T = sh8
            yaT = sb("yaT", [128, 8, 512], BF)
            oblk = xblk
            stat_s = sb("stat_s", [128, 512])
            stat_n = sb("stat_n", [128, 512])
            sb_sig = sb("sb_sig", [128, 640])

            deferred = []

            def flush_def():
                for f_ in deferred:
                    f_()
                deferred.clear()

            def stats_accum(ps_b, src_ap, reads, idx, n, defer=False):
                s_ = nxt(tmpb, tmpb_rr)
                A(lambda e: e.activation(s_[:], src_ap, AF.Square), reads=reads, writes=[s_])
                f_ = lambda s_=s_: mm(bap(ps_b), ones_b[:], s_[:], idx == 0, idx == n - 1, [ones_b, s_], bk(ps_b), True)
                if defer:
                    deferred.append(f_)
                else:
                    f_()

            def fh_blocks(g_):
                lst = []
                for b in range(4):
                    blk = g_ * 4 + b
                    lst.append((x_halo[blk * 32:(blk + 1) * 32, :], 32, b * 160))
                    lst.append((x_own[blk * 128:(blk + 1) * 128, :], 128, b * 160 + 32))
                return lst

            def front_h(g_):
                lst = fh_blocks(g_)
                hd = xb_prep(lst[0][0], lst[0][1])
                for i_ in range(8):
                    nh = xb_prep(lst[i_ + 1][0], lst[i_ + 1][1]) if i_ + 1 < 8 else None
                    xb_trans(hd, hTe, lst[i_][2], 0, 8)
                    hd = nh

            front_h(0)
            for g in range(4):
                for b in range(4):
                    blk = g * 4 + b
                    xb = xblk[xb_rr[0] % 2]
                    xb_rr[0] += 1
                    dma_in(xb[:], x_own[blk * 128:(blk + 1) * 128, :], xb)
                    for half in range(2):
                        tb_i = 2 + half
                        for kk in range(4):
                            k = half * 4 + kk
                            P(lambda e, k=k, kk=kk, tb_i=tb_i, xb=xb: e.transpose(bap(tb_i, kk * 128, (kk + 1) * 128),
                                                                                    xb[:, k * 128:(k + 1) * 128], ident_f[:]),
                              reads=[xb, ident_f], writes=[bk(tb_i)], inc=(kk == 3))
                        evac_copy(xT[:, half * 4:(half + 1) * 4, b * 128:(b + 1) * 128],
                                  bap(tb_i).rearrange("p (k t) -> p k t", t=128), [bk(tb_i)], [(xT, (half, b))])
                hk = [(hTe, k) for k in range(8)]
                def glu_chunk(c, wa, wb2, cc):
                    for (wt, d) in ((wa, 0), (wb2, 1)):
                        for (n0, n1, hb) in ((0, 512, 0), (512, 640, 1)):
                            for k in range(8):
                                mm(PD[d][:, hb * 512:hb * 512 + (n1 - n0)], wt[:, k, cc * 128:(cc + 1) * 128],
                                   hTe[:, k, n0:n1], k == 0, k == 7, hk + [wt], (PD[d], hb), k == 7)
                    sg = sb_sig
                    A(lambda e: e.activation(sg[:, 0:640], PD[1][:, 0:640], AF.Sigmoid),
                      reads=[(PD[1], 0), (PD[1], 1)], writes=[sg])
                    V(lambda e: e.tensor_tensor(uext[:, c, :], PD[0][:, 0:640], sg[:, 0:640], ALU.mult),
                      reads=[(PD[0], 0), (PD[0], 1), sg], writes=[(uext, c)])
                    if g == 0:
                        V(lambda e: e.tensor_scalar(uext[:, c, 0:32], uext[:, c, 0:32], halom[:, 0:1], None, ALU.mult),
                          reads=[(uext, c), halom], writes=[(uext, c)])

                def conv_chunk(c):
                    dg = diag[c % 2]
                    for k in range(31):
                        G(lambda e: e.tensor_scalar(dg[:, k, :], ident_b[:], cwT[:, c, k:k + 1], 1.0, ALU.mult, ALU.mult),
                          reads=[ident_b, cwT], writes=[(arena, None) if (c == 0 and k == 0) else (arena, ("d", c % 2, k))])
                    uv = uext[:, c, :].rearrange("p (b w) -> p b w", w=160)
                    cb = 4 + (c % 2)
                    for k in range(31):
                        mm(bap(cb).rearrange("p (b w) -> p b w", w=128), dg[:, k, :], uv[:, :, 2 + k:2 + k + 128],
                           k == 0, k == 30, [(arena, ("d", c % 2, k)), (uext, c)], bk(cb), k == 30)
                    flush_def()
                    A(lambda e: e.activation(ucv[:, c, :], bap(cb), AF.Identity, bias=vecs[:, V_CB + c:V_CB + c + 1]),
                      reads=[bk(cb), vecs], writes=[(arena, ("u", c))])
                    ub_ = nxt(tmpb, tmpb_rr)
                    V(lambda e: e.tensor_copy(ub_[:], ucv[:, c, :]), reads=[(arena, ("u", c))], writes=[ub_])
                    deferred.append(lambda ub_=ub_, c=c: mm(bap(6), ones_b[:], ub_[:], c == 0, c == 7, [ones_b, ub_], bk(6), True))
                    stats_accum(7, ucv[:, c, :], [(arena, ("u", c))], c, 8, defer=True)

                for pc in range(4):
                    wa = load_wq(w_in, D, pc * 256, 256)
                    wb2 = load_wq(w_in, D, 1024 + pc * 256, 256)
                    for cc in range(2):
                        c = pc * 2 + cc
                        glu_chunk(c, wa, wb2, cc)
                        if c >= 1:
                            conv_chunk(c - 1)
                conv_chunk(7)
                flush_def()
                mean = stat_s
                nmr = stat_n
                A(lambda e: e.activation(mean[:], bap(6), AF.Copy, scale=1.0 / D), reads=[bk(6)], writes=[mean])
                jt = nxt(tmpf, tmpf_rr)
                V(lambda e, jt=jt: e.tensor_tensor(jt[:], mean[:], mean[:], ALU.mult), reads=[mean], writes=[jt])
                jt2 = nxt(tmpf, tmpf_rr)
                V(lambda e, jt=jt, jt2=jt2: e.scalar_tensor_tensor(jt2[:], bap(7), 1.0 / D, jt[:], ALU.mult, ALU.subtract),
                  reads=[bk(7), jt], writes=[jt2])
                V(lambda e, jt2=jt2: e.tensor_scalar(jt2[:], jt2[:], 0.0, None, ALU.max), reads=[jt2], writes=[jt2])
                A(lambda e, jt=jt, jt2=jt2: e.activation(jt[:], jt2[:], AF.Sqrt, bias=EPS), reads=[jt2], writes=[jt])
                rln = nxt(rstd_t, rstd_rr)
                V(lambda e, jt=jt, rln=rln: e.reciprocal(rln[:], jt[:]), reads=[jt], writes=[rln])
                V(lambda e, rln=rln: e.scalar_tensor_tensor(nmr[:], mean[:], -1.0, rln[:], ALU.mult, ALU.mult),
                  reads=[mean, rln], writes=[nmr])
                for c in range(8):
                    jt = nxt(tmpf, tmpf_rr)
                    V(lambda e, c=c, jt=jt, rln=rln: e.tensor_tensor(jt[:], ucv[:, c, :], rln[:], ALU.mult),
                      reads=[(arena, ("u", c)), rln], writes=[jt])
                    V(lambda e, jt=jt: e.tensor_tensor(jt[:], jt[:], nmr[:], ALU.add), reads=[jt, nmr], writes=[jt])
                    A(lambda e, c=c, jt=jt: e.activation(actT[:, c, :], jt[:], AF.Silu,
                                                         bias=vecs[:, V_CNB + c:V_CNB + c + 1], scale=vecs[:, V_CG + c:V_CG + c + 1]),
                      reads=[jt, vecs, vecs], writes=[(actT, c)])
                ak = [(actT, k) for k in range(8)]
                wbs = [load_wq(w_conv_out, D, 0, 256), load_wq(w_conv_out, D, 256, 256)]
                for k in range(8):
                    for m in range(4):
                        wb = wbs[m // 2]
                        cc = m % 2
                        mm(bap(4 + m), wb[:, k, cc * 128:(cc + 1) * 128], actT[:, k, :], k == 0, k == 7,
                           [(actT, k), wb], bk(4 + m), True)
                for m in range(4):
                    evac_copy(yaT[:, m, :], bap(4 + m), [bk(4 + m)], [(yaT, m)])
                for pc in range(2, 4):
                    wb = load_wq(w_conv_out, D, pc * 256, 256)
                    for cc in range(2):
                        m = pc * 2 + cc
                        bb = 4 + (m % 2)
                        for k in range(8):
                            mm(bap(bb), wb[:, k, cc * 128:(cc + 1) * 128], actT[:, k, :], k == 0, k == 7, ak + [wb], bk(bb), k == 7)
                        evac_copy(yaT[:, m, :], bap(bb), [bk(bb)], [(yaT, m)])
                hown = lambda k: hTe[:, k, :].rearrange("p (b w) -> p b w", w=160)[:, :, 32:160]
                for pc in range(4):
                    wao = load_wq(w_attn_out, D, pc * 256, 256)
                    for cc in range(2):
                        for hh in range(8):
                            mm(bap(2 + cc), wao[:, hh, cc * 128:(cc + 1) * 128], oT[:, hh, g * 512:(g + 1) * 512],
                               hh == 0, hh == 7, [oT, wao], bk(2 + cc), hh == 7)
                    wga = load_wq(w_in, D, 2752 + pc * 256, 256)
                    sas = []
                    for cc in range(2):
                        m = pc * 2 + cc
                        for k in range(8):
                            mm(bap(cc).rearrange("p (b w) -> p b w", w=128), wga[:, k, cc * 128:(cc + 1) * 128], hown(k),
                               k == 0, k == 7, hk + [wga], bk(cc), k == 7)
                        sa = nxt(tmpf, tmpf_rr)
                        A(lambda e, sa=sa, cc=cc: e.activation(sa[:], bap(cc), AF.Sigmoid), reads=[bk(cc)], writes=[sa])
                        V(lambda e, sa=sa, m=m: e.tensor_tensor(sa[:], sa[:], yaT[:, m, :], ALU.mult),
                          reads=[sa, (yaT, m)], writes=[sa])
                        sas.append(sa)
                    wgb = load_wq(w_in, D, 3776 + pc * 256, 256)
                    for cc in range(2):
                        m = pc * 2 + cc
                        for k in range(8):
                            mm(bap(cc).rearrange("p (b w) -> p b w", w=128), wgb[:, k, cc * 128:(cc + 1) * 128], hown(k),
                               k == 0, k == 7, hk + [wgb], bk(cc), k == 7)
                        sb2 = nxt(tmpf, tmpf_rr)
                        A(lambda e, sb2=sb2, cc=cc: e.activation(sb2[:], bap(cc), AF.Sigmoid), reads=[bk(cc)], writes=[sb2])
                        V(lambda e, sb2=sb2, cc=cc: e.tensor_tensor(sb2[:], bap(2 + cc), sb2[:], ALU.mult),
                          reads=[bk(2 + cc), sb2], writes=[sb2])
                        V(lambda e, sa=sas[cc], sb2=sb2, m=m: e.tensor_tensor(mT[:, m, :], sa[:], sb2[:], ALU.add),
                          reads=[sas[cc], sb2], writes=[(mT, m)])
                mk_ = [(mT, k) for k in range(8)]
                for pc in range(4):
                    wb = load_wq(w_out, D, pc * 256, 256)
                    for cc in range(2):
                        m = pc * 2 + cc
                        bb = 4 + (m % 2)
                        for k in range(8):
                            mm(bap(bb), wb[:, k, cc * 128:(cc + 1) * 128], mT[:, k, :], k == 0, k == 7, mk_ + [wb], bk(bb), k == 7)
                        flush_def()
                        A(lambda e, m=m, bb=bb: e.activation(yT[:, m, :], bap(bb), AF.Copy), reads=[bk(bb)], writes=[(yT, m)])
                        stats_accum(6, yT[:, m, :], [(yT, m)], m, 8, defer=True)
                flush_def()
                if debug == "m" and g == 3:
                    V(lambda e: e.tensor_copy(hTe[:, :, 0:512], mT[:]), reads=[mT], writes=[hTe])
                    V(lambda e: e.tensor_copy(uext[:, :, 0:512], yT[:]), reads=[yT], writes=[uext])
                r1 = rstd_from_ps(6, D)
                for k in range(8):
                    jt = nxt(tmpf, tmpf_rr)
                    V(lambda e, k=k, jt=jt, r1=r1: e.tensor_tensor(jt[:], yT[:, k, :], r1[:], ALU.mult),
                      reads=[(yT, k), r1], writes=[jt])
                    V(lambda e, k=k, jt=jt: e.scalar_tensor_tensor(xT[:, k, :], jt[:], der[:, 16 + k:17 + k], xT[:, k, :], ALU.mult, ALU.add),
                      reads=[jt, (der, 16), xT], writes=[xT])
                    stats_accum(7, xT[:, k, :], [xT], k, 8)
                r2 = rstd_from_ps(7, D)
                for k in range(8):
                    jt = nxt(tmpf, tmpf_rr)
                    V(lambda e, k=k, jt=jt, r2=r2: e.scalar_tensor_tensor(jt[:], xT[:, k, :], der[:, 24 + k:25 + k], r2[:], ALU.mult, ALU.mult),
                      reads=[xT, (der, 24), r2], writes=[jt])
                    A(lambda e, k=k, jt=jt: e.activation(h2T[:, k, :], jt[:], AF.Identity, bias=der[:, 32 + k:33 + k]),
                      reads=[jt, (der, 32)], writes=[(h2T, k)])
                h2k = [(h2T, k) for k in range(8)]
                nlst = fh_blocks(g + 1) if g + 1 < 4 else None
                nhd = {}
                for pc in range(16):
                    if nlst is not None and pc % 2 == 0:
                        if pc >= 2:
                            xb_trans(nhd[pc // 2 - 1], hTe, nlst[pc // 2 - 1][2], 0, 8)
                        nhd[pc // 2] = xb_prep(nlst[pc // 2][0], nlst[pc // 2][1])
                    wb = load_wq(w_mlp_in, D, pc * 256, 256)
                    for cc in range(2):
                        j = pc * 2 + cc
                        bb = 4 + (j % 4)
                        for k in range(8):
                            mm(bap(bb), wb[:, k, cc * 128:(cc + 1) * 128], h2T[:, k, :], k == 0, k == 7, h2k + [wb], bk(bb), k == 7)
                        jt = nxt(tmpf, tmpf_rr)
                        A(lambda e, jt=jt, bb=bb: e.activation(jt[:], bap(bb), AF.Relu), reads=[bk(bb)], writes=[jt])
                        V(lambda e, jt=jt, j=j: e.tensor_tensor(hid[:, j, :], jt[:], jt[:], ALU.mult), reads=[jt],
                          writes=[(arena, None) if j == 0 else (arena, ("h", j))])
                if nlst is not None:
                    xb_trans(nhd[7], hTe, nlst[7][2], 0, 8)
                for pc in range(4):
                    for cc in range(2):
                        pass
                    wbs = []
                    for rr_ in range(4):
                        wb = load_wq(w_mlp_out, D, pc * 256, 256, r0=rr_ * 1024)
                        for cc in range(2):
                            bb = 4 + cc
                            for k in range(8):
                                kk = rr_ * 8 + k
                                mm(bap(bb), wb[:, k, cc * 128:(cc + 1) * 128], hid[:, kk, :], kk == 0, kk == 31,
                                   [arena, wb], bk(bb), (k == 7))
                    flush_def()
                    for cc in range(2):
                        m = pc * 2 + cc
                        bb = 4 + cc
                        A(lambda e, m=m, bb=bb: e.activation(yT[:, m, :], bap(bb), AF.Copy), reads=[bk(bb)], writes=[(yT, m)])
                        stats_accum(6, yT[:, m, :], [(yT, m)], m, 8, defer=True)
                flush_def()
                r3 = rstd_from_ps(6, D)
                for k in range(8):
                    jt = nxt(tmpf, tmpf_rr)
                    V(lambda e, k=k, jt=jt, r3=r3: e.tensor_tensor(jt[:], yT[:, k, :], r3[:], ALU.mult),
                      reads=[(yT, k), r3], writes=[jt])
                    V(lambda e, k=k, jt=jt: e.scalar_tensor_tensor(yT[:, k, :], jt[:], der[:, 40 + k:41 + k], xT[:, k, :], ALU.mult, ALU.add),
                      reads=[jt, (der, 40), xT], writes=[(yT, k)])
                for b in range(4):
                    ob_ = oblk[b % 2]
                    for half in range(2):
                        tb_i = 2 + half
                        for kk in range(4):
                            k = half * 4 + kk
                            P(lambda e, k=k, kk=kk, tb_i=tb_i, b=b: e.transpose(bap(tb_i, kk * 128, (kk + 1) * 128),
                                                                                 yT[:, k, b * 128:(b + 1) * 128], ident_f[:]),
                              reads=[yT, ident_f], writes=[bk(tb_i)], inc=(kk == 3))
                        evac_copy(ob_[:, half * 512:(half + 1) * 512], bap(tb_i), [bk(tb_i)], [(ob_, half)])
                    blk = g * 4 + b
                    tok = S.dma("act", lambda e, ob_=ob_, blk=blk: e.dma_start(out=out[blk * 128:(blk + 1) * 128, :], in_=ob_[:]),
                                reads=[ob_])
                    out_toks.append(tok)


        run_phases()
        if stop is not None:
            out_toks.append(S.dma("act", lambda e: e.dma_start(out=out[0:128, :], in_=xblk[0][:]), reads=[xblk[0]]))
        last = {}
        for (s, v) in out_toks:
            last[s] = max(last.get(s, 0), v)
        S.wait_all("act", list(last.items()))

        with nc.Block() as block:
            def emit(engname, eng):
                for (waits, fn, inc) in S.q[engname]:
                    for (s, v) in waits:
                        eng.wait_ge(sems[s], v)
                    if fn is None:
                        continue
                    ins = fn(eng)
                    if inc is not None:
                        ins.then_inc(sems[inc[0]], inc[1])

            @block.sync
            def _(e):
                emit("sp", e)

            @block.tensor
            def _(e):
                emit("pe", e)

            @block.scalar
            def _(e):
                emit("act", e)

            @block.vector
            def _(e):
                emit("dve", e)

            @block.gpsimd
            def _(e):
                emit("pool", e)
    return nc


def _prep_inputs(inputs):
    x = np.asarray(inputs["x"], np.float32)
    pos = np.asarray(inputs["positions"], np.int32)
    c = np.asarray(inputs["c"], np.float32)
    w_in = np.ascontiguousarray(np.asarray(inputs["w_in"], np.float32)[0])
    w_uq = np.ascontiguousarray(np.asarray(inputs["w_uq"], np.float32)[0])
    kr = w_in[:, 2688:2752]
    w_kr_sw = np.ascontiguousarray(np.concatenate([kr[:, 32:64], kr[:, 0:32]], axis=1))
    uq3 = w_uq.reshape(384, 8, 192)[:, :, 128:192]
    w_uq_sw = np.ascontiguousarray(np.concatenate([uq3[:, :, 32:64], uq3[:, :, 0:32]], axis=2).reshape(384, 512))
    k_idx = np.arange(128)[:, None]
    q_idx = np.arange(128)[None, :]
    trimask = np.where(k_idx <= q_idx, 0.0, NEG).astype(np.float32)
    ident = np.eye(128, dtype=np.float32)
    inv = (1.0 / (np.float32(10000.0) ** (np.arange(0, 64, 2, dtype=np.float32) / np.float32(64)))).astype(np.float32)
    invf = np.concatenate([inv, inv]).reshape(64, 1).astype(np.float32)
    sgn = np.concatenate([-np.ones(32), np.ones(32)]).reshape(64, 1).astype(np.float32)
    shared = {
        "trimask": trimask, "ident": ident, "invf": invf, "sgn": sgn,
        "w_ada": np.ascontiguousarray(inputs["w_ada"][0], np.float32),
        "b_ada": np.ascontiguousarray(inputs["b_ada"][0], np.float32),
        "g_pre_mix": np.ascontiguousarray(inputs["g_pre_mix"][0], np.float32),
        "g_post_mix": np.ascontiguousarray(inputs["g_post_mix"][0], np.float32),
        "g_pre_mlp": np.ascontiguousarray(inputs["g_pre_mlp"][0], np.float32),
        "g_post_mlp": np.ascontiguousarray(inputs["g_post_mlp"][0], np.float32),
        "w_in": w_in, "w_kr_sw": w_kr_sw,
        "conv_w": np.ascontiguousarray(inputs["conv_w"][0], np.float32),
        "conv_b": np.ascontiguousarray(inputs["conv_b"][0], np.float32),
        "conv_norm_g": np.ascontiguousarray(inputs["conv_norm_g"][0], np.float32),
        "conv_norm_b": np.ascontiguousarray(inputs["conv_norm_b"][0], np.float32),
        "w_conv_out": np.ascontiguousarray(inputs["w_conv_out"][0], np.float32),
        "q_norm_g": np.ascontiguousarray(inputs["q_norm_g"][0], np.float32),
        "w_uq": w_uq, "w_uq_sw": w_uq_sw,
        "kv_norm_g": np.ascontiguousarray(inputs["kv_norm_g"][0], np.float32),
        "w_ukv": np.ascontiguousarray(inputs["w_ukv"][0], np.float32),
        "w_attn_out": np.ascontiguousarray(inputs["w_attn_out"][0], np.float32),
        "w_out": np.ascontiguousarray(inputs["w_out"][0], np.float32),
        "w_mlp_in": np.ascontiguousarray(inputs["w_mlp_in"][0], np.float32),
        "w_mlp_out": np.ascontiguousarray(inputs["w_mlp_out"][0], np.float32),
    }
    in_maps = []
    for core in range(8):
        b, p = core // 2, core % 2
        xb = x[b].reshape(32, 128, D)
        pb = pos[b].reshape(32, 128)
        own = [2 * i + p for i in range(16)]
        oth = [2 * i + 1 - p for i in range(16)]
        halo = np.zeros((16, 32, D), np.float32)
        for i in range(16):
            st = own[i] * 128
            if st > 0:
                halo[i] = x[b, st - 32:st]
        m = dict(shared)
        m["x_own"] = np.ascontiguousarray(xb[own].reshape(NOWN, D))
        m["x_oth"] = np.ascontiguousarray(xb[oth].reshape(NOWN, D))
        m["x_halo"] = np.ascontiguousarray(halo.reshape(512, D))
        m["pos_own"] = np.ascontiguousarray(pb[own].reshape(NOWN))
        m["pos_oth"] = np.ascontiguousarray(pb[oth].reshape(NOWN))
        m["c"] = np.ascontiguousarray(c[b])
        m["pairmask"] = np.full((128, 128), 0.0 if p == 1 else NEG, np.float32)
        m["halomask"] = np.full((128, 1), 1.0 if p == 1 else 0.0, np.float32)
        in_maps.append(m)
    return in_maps


def kernel(**inputs):
    in_maps = _prep_inputs(inputs)
    nc = build_nc()
    res = run_bass_kernel_spmd(nc, in_maps, core_ids=list(range(8)))
    outf = np.zeros((4, 32, 128, D), np.float32)
    for core in range(8):
        b, p = core // 2, core % 2
        o = np.asarray(res.results[core]["out"]).reshape(16, 128, D)
        for i in range(16):
            outf[b, 2 * i + p] = o[i]
    return outf.reshape(4, 4096, D)
```
